# Optimizing a Trainium2 kernel written in Bass

```python
import math
import jax
import jax.numpy as jnp
from jax import lax
import numpy as np

D_MODEL = 4096
BATCH = 2
SEQ = 4096
DEPTH = 4

N_MIXERS = 4
EPS = 1e-6
NEG = -1e30
FORCE = 1e4
CONV_W = 4

A_HEADS = 8
A_DQK = D_MODEL // 16
A_DV = D_MODEL // 8
A_QK = A_HEADS * A_DQK
A_INNER = A_HEADS * A_DV
A_CHUNK = 128
A_COLS = 2 * A_QK + 3 * A_INNER + 2 * A_HEADS

B_HD = 128
B_HEADS = D_MODEL // B_HD
B_KV = 4
B_REP = B_HEADS // B_KV
B_INNER = B_HEADS * B_HD
B_KVW = B_KV * B_HD
B_CMP_LEN = 32
B_CMP_STRIDE = 16
B_SEL_LEN = 64
B_N_SEL = 16
B_WIN = 512
B_QBLK = 64
B_COLS = 2 * B_INNER + 6 * B_KVW + 3 * B_HEADS

C_WIDTH = (4 * D_MODEL // 3) // 256 * 256
C_BLOCKS = 16
C_BLK = C_WIDTH // C_BLOCKS
C_EXP = 8.0
C_COLS = 2 * C_WIDTH

D_HD = 128
D_SLOTS = D_MODEL // 256
D_INNER = D_SLOTS * D_HD
D_PATTERNS = ((128, 1), (512, 4), (2048, 16))
D_QBLK = 128
D_COLS = 3 * len(D_PATTERNS) * D_INNER + D_INNER

N_A = (DEPTH + N_MIXERS - 1) // N_MIXERS
N_B = (DEPTH + N_MIXERS - 2) // N_MIXERS
N_C = (DEPTH + N_MIXERS - 3) // N_MIXERS
N_D = (DEPTH + N_MIXERS - 4) // N_MIXERS

kernel_name = "hybrid_mlstm_nsa_rglru_dilated"


def rms_norm(x, w):
    xf = x.astype(jnp.float32)
    y = xf * lax.rsqrt(jnp.mean(xf * xf, axis=-1, keepdims=True) + EPS)
    return (y * w.astype(jnp.float32)).astype(x.dtype)


def causal_dwconv(x, w, b):
    ch = x.shape[-1]
    y = lax.conv_general_dilated(x, w[:, None, :].astype(x.dtype), window_strides=(1,),
                                 padding=[(CONV_W - 1, 0)],
                                 dimension_numbers=("NWC", "WIO", "NWC"),
                                 feature_group_count=ch)
    return y + b.astype(x.dtype)


def alibi_slopes(n):
    return jnp.asarray(2.0 ** (-8.0 * np.arange(1, n + 1) / n), dtype=jnp.float32)


def masked_softmax(s, mask):
    s = jnp.where(mask, s, NEG)
    m = jnp.max(s, axis=-1, keepdims=True)
    p = jnp.where(mask, jnp.exp(s - m), 0.0)
    den = jnp.sum(p, axis=-1, keepdims=True)
    return p / jnp.where(den > 0, den, 1.0)


def mlstm_mixer(h, w_in, conv_w, conv_b, gate_b, out_norm_w, w_out):
    bsz, seq, _ = h.shape
    n_chunks = seq // A_CHUNK
    u = h @ w_in
    o0 = 2 * A_QK
    o1 = o0 + A_INNER
    o2 = o1 + A_INNER
    o3 = o2 + A_INNER
    o4 = o3 + A_HEADS
    qk, v, og, z, ig, fg = jnp.split(u, [o0, o1, o2, o3, o4], axis=-1)
    qk = jax.nn.silu(causal_dwconv(qk, conv_w, conv_b)).astype(jnp.float32)
    q = qk[..., :A_QK].reshape(bsz, seq, A_HEADS, A_DQK) * (A_DQK ** -0.5)
    k = qk[..., A_QK:].reshape(bsz, seq, A_HEADS, A_DQK)
    v = v.astype(jnp.float32).reshape(bsz, seq, A_HEADS, A_DV)
    log_i = ig.astype(jnp.float32) + gate_b[0].astype(jnp.float32)
    log_f = jax.nn.log_sigmoid(fg.astype(jnp.float32) + gate_b[1].astype(jnp.float32))

    def to_chunks(t):
        t = t.reshape((bsz, n_chunks, A_CHUNK) + t.shape[2:])
        return jnp.swapaxes(jnp.moveaxis(t, 1, 0), 2, 3)

    causal = jnp.tril(jnp.ones((A_CHUNK, A_CHUNK), dtype=bool))

    def step(carry, xs):
        c_st, n_st, m_st = carry
        qc, kc, vc, li, lf = xs
        b = jnp.cumsum(lf, axis=-1)
        dmat = b[..., :, None] - b[..., None, :] + li[..., None, :]
        dmat = jnp.where(causal, dmat, NEG)
        inter = b + m_st[..., None]
        m_t = jnp.maximum(inter, jnp.max(dmat, axis=-1))
        w_intra = jnp.exp(dmat - m_t[..., None])
        w_inter = jnp.exp(inter - m_t)
        a = w_intra * jnp.einsum("bhtd,bhsd->bhts", qc, kc)
        num = (jnp.einsum("bhts,bhsv->bhtv", a, vc)
               + w_inter[..., None] * jnp.einsum("bhtd,bhdv->bhtv", qc, c_st))
        qn = jnp.sum(a, axis=-1) + w_inter * jnp.einsum("bhtd,bhd->bht", qc, n_st)
        hc = num / jnp.maximum(jnp.abs(qn), jnp.exp(-m_t))[..., None]
        b_last = b[..., -1]
        g = b_last[..., None] - b + li
        m_new = jnp.maximum(b_last + m_st, jnp.max(g, axis=-1))
        ws = jnp.exp(g - m_new[..., None])
        decay = jnp.exp(b_last + m_st - m_new)
        c_new = decay[..., None, None] * c_st + jnp.einsum("bhs,bhsd,bhsv->bhdv", ws, kc, vc)
        n_new = decay[..., None] * n_st + jnp.einsum("bhs,bhsd->bhd", ws, kc)
        return (c_new, n_new, m_new), hc

    init = (jnp.zeros((bsz, A_HEADS, A_DQK, A_DV), jnp.float32),
            jnp.zeros((bsz, A_HEADS, A_DQK), jnp.float32),
            jnp.zeros((bsz, A_HEADS), jnp.float32))
    xs = (to_chunks(q), to_chunks(k), to_chunks(v), to_chunks(log_i), to_chunks(log_f))
    _, hc = lax.scan(step, init, xs)
    hc = jnp.swapaxes(jnp.moveaxis(hc, 0, 1), 2, 3).reshape(bsz, seq, A_HEADS, A_DV)
    hn = hc * lax.rsqrt(jnp.mean(hc * hc, axis=-1, keepdims=True) + EPS)
    hn = hn.reshape(bsz, seq, A_INNER) * out_norm_w.astype(jnp.float32)
    y = hn * jax.nn.sigmoid(og.astype(jnp.float32)) * jax.nn.silu(z.astype(jnp.float32))
    return y.astype(h.dtype) @ w_out


def nsa_mixer(h, w_in, cmp_pe, cmp_wk, cmp_wv, q_norm_w, k_norm_w, w_out):
    bsz, seq, _ = h.shape
    u = h @ w_in
    q, z, kv, gates = jnp.split(u, [B_INNER, 2 * B_INNER, 2 * B_INNER + 6 * B_KVW], axis=-1)
    q = rms_norm(q.reshape(bsz, seq, B_HEADS, B_HD), q_norm_w)
    kv = kv.reshape(bsz, seq, 3, 2, B_KV, B_HD)
    gates = jax.nn.sigmoid(gates.astype(jnp.float32)).reshape(bsz, seq, B_HEADS, 3)
    scale = B_HD ** -0.5
    slopes = alibi_slopes(B_HEADS).reshape(B_KV, B_REP)

    n_cmp = (seq - B_CMP_LEN) // B_CMP_STRIDE + 1
    cmp_start = np.arange(n_cmp) * B_CMP_STRIDE
    cmp_idx = cmp_start[:, None] + np.arange(B_CMP_LEN)[None, :]
    cmp_end = jnp.asarray(cmp_start + B_CMP_LEN - 1, dtype=jnp.int32)
    kb = kv[:, :, 0, 0][:, cmp_idx] + cmp_pe[0][:, None, :]
    vb = kv[:, :, 0, 1][:, cmp_idx] + cmp_pe[1][:, None, :]
    k_cmp = jnp.einsum("bclgd,lde->bcge", kb, cmp_wk.reshape(B_CMP_LEN, B_HD, B_HD))
    k_cmp = rms_norm(k_cmp, k_norm_w[0])
    v_cmp = jnp.einsum("bclgd,lde->bcge", vb, cmp_wv.reshape(B_CMP_LEN, B_HD, B_HD))

    n_blk = seq // B_SEL_LEN
    n_top = min(B_N_SEL, n_blk)
    ks_blocks = rms_norm(kv[:, :, 1, 0], k_norm_w[1]).reshape(bsz, n_blk, B_SEL_LEN, B_KV, B_HD)
    ks_blocks = jnp.transpose(ks_blocks, (0, 3, 1, 2, 4))
    vs_blocks = jnp.transpose(kv[:, :, 1, 1].reshape(bsz, n_blk, B_SEL_LEN, B_KV, B_HD), (0, 3, 1, 2, 4))
    sel_start = np.arange(n_blk) * B_SEL_LEN
    ov = np.clip(np.minimum(cmp_start[:, None] + B_CMP_LEN, sel_start[None, :] + B_SEL_LEN)
                 - np.maximum(cmp_start[:, None], sel_start[None, :]), 0, None) / B_CMP_LEN
    overlap = jnp.asarray(ov, dtype=jnp.float32)

    kw_pad = jnp.pad(rms_norm(kv[:, :, 2, 0], k_norm_w[2]), ((0, 0), (B_WIN, 0), (0, 0), (0, 0)))
    vw_pad = jnp.pad(kv[:, :, 2, 1], ((0, 0), (B_WIN, 0), (0, 0), (0, 0)))

    bi = jnp.arange(bsz)[:, None, None, None]
    gi = jnp.arange(B_KV)[None, :, None, None]
    blk_ids = jnp.arange(n_blk)

    def block(i):
        t0 = i * B_QBLK
        t = t0 + jnp.arange(B_QBLK)
        qb = lax.dynamic_slice_in_dim(q, t0, B_QBLK, axis=1).reshape(bsz, B_QBLK, B_KV, B_REP, B_HD)
        dist_c = (t[:, None] - cmp_end[None, :]).astype(jnp.float32)
        s_c = jnp.einsum("bqgrd,bcgd->bgrqc", qb, k_cmp).astype(jnp.float32) * scale
        s_c = s_c - slopes[:, :, None, None] * dist_c
        p_c = masked_softmax(s_c, dist_c >= 0)
        o_c = jnp.einsum("bgrqc,bcgd->bqgrd", p_c, v_cmp.astype(jnp.float32))
        imp = jnp.einsum("bgrqc,cn->bgqn", p_c, overlap)
        cur = t // B_SEL_LEN
        forced = ((blk_ids[None, :] == 0) | (blk_ids[None, :] == cur[:, None])
                  | (blk_ids[None, :] == cur[:, None] - 1))
        causal_b = blk_ids[None, :] * B_SEL_LEN <= t[:, None]
        imp = jnp.where(forced, imp + FORCE, imp)
        imp = jnp.where(causal_b, imp, NEG)
        top_v, sel = lax.top_k(imp, n_top)
        ks = ks_blocks[bi, gi, sel].reshape(bsz, B_KV, B_QBLK, n_top * B_SEL_LEN, B_HD)
        vs = vs_blocks[bi, gi, sel].reshape(bsz, B_KV, B_QBLK, n_top * B_SEL_LEN, B_HD)
        pos = sel[..., None] * B_SEL_LEN + jnp.arange(B_SEL_LEN)
        valid = (top_v > NEG / 2)[..., None] & (pos <= t[None, None, :, None, None])
        pos = pos.reshape(bsz, B_KV, B_QBLK, n_top * B_SEL_LEN)
        valid = valid.reshape(bsz, B_KV, B_QBLK, n_top * B_SEL_LEN)
        dist_s = (t[None, None, :, None] - pos).astype(jnp.float32)
        s_s = jnp.einsum("bqgrd,bgqkd->bgrqk", qb, ks).astype(jnp.float32) * scale
        s_s = s_s - slopes[None, :, :, None, None] * dist_s[:, :, None]
        p_s = masked_softmax(s_s, valid[:, :, None])
        o_s = jnp.einsum("bgrqk,bgqkd->bqgrd", p_s, vs.astype(jnp.float32))
        kw = lax.dynamic_slice_in_dim(kw_pad, t0, B_QBLK + B_WIN, axis=1)
        vw = lax.dynamic_slice_in_dim(vw_pad, t0, B_QBLK + B_WIN, axis=1)
        pos_w = t0 - B_WIN + jnp.arange(B_QBLK + B_WIN)
        dist_w = t[:, None] - pos_w[None, :]
        mask_w = (dist_w >= 0) & (dist_w < B_WIN) & (pos_w[None, :] >= 0)
        s_w = jnp.einsum("bqgrd,bkgd->bgrqk", qb, kw).astype(jnp.float32) * scale
        s_w = s_w - slopes[:, :, None, None] * dist_w.astype(jnp.float32)
        p_w = masked_softmax(s_w, mask_w)
        o_w = jnp.einsum("bgrqk,bkgd->bqgrd", p_w, vw.astype(jnp.float32))
        g = lax.dynamic_slice_in_dim(gates, t0, B_QBLK, axis=1).reshape(bsz, B_QBLK, B_KV, B_REP, 3)
        o = g[..., 0:1] * o_c + g[..., 1:2] * o_s + g[..., 2:3] * o_w
        return o.reshape(bsz, B_QBLK, B_INNER)

    o = lax.map(block, jnp.arange(seq // B_QBLK))
    o = jnp.moveaxis(o, 0, 1).reshape(bsz, seq, B_INNER)
    y = o * jax.nn.silu(z.astype(jnp.float32))
    return y.astype(h.dtype) @ w_out


def rglru_mixer(h, w_in, conv_w, conv_b, w_a, b_a, w_x, b_x, lam, w_out):
    bsz, seq, _ = h.shape
    u = h @ w_in
    xb, z = jnp.split(u, [C_WIDTH], axis=-1)
    xb = causal_dwconv(xb, conv_w, conv_b).astype(jnp.float32)
    xblk = xb.reshape(bsz, seq, C_BLOCKS, C_BLK)
    r = jax.nn.sigmoid(jnp.einsum("bsnc,ncd->bsnd", xblk, w_a.astype(jnp.float32)).reshape(bsz, seq, C_WIDTH)
                       + b_a.astype(jnp.float32))
    ig = jax.nn.sigmoid(jnp.einsum("bsnc,ncd->bsnd", xblk, w_x.astype(jnp.float32)).reshape(bsz, seq, C_WIDTH)
                        + b_x.astype(jnp.float32))
    log_a = -C_EXP * jax.nn.softplus(-lam.astype(jnp.float32)) * r
    a = jnp.exp(log_a)
    inp = jnp.sqrt(-jnp.expm1(2.0 * log_a)) * (ig * xb)

    def combine(left, right):
        a1, b1 = left
        a2, b2 = right
        return a1 * a2, a2 * b1 + b2

    _, hs = lax.associative_scan(combine, (a, inp), axis=1)
    y = hs * jax.nn.silu(z.astype(jnp.float32))
    return y.astype(h.dtype) @ w_out


def dilated_window_attention(q, k, v, window, dil, slopes):
    bsz, seq, nh, hd = q.shape
    n_back = window // dil
    s_sub = seq // dil
    qb = math.gcd(D_QBLK, s_sub)
    n_blk = s_sub // qb

    def strided(t):
        return jnp.swapaxes(t.reshape(bsz, s_sub, dil, nh, hd), 1, 2)

    qs = strided(q).reshape(bsz, dil, n_blk, qb, nh, hd)
    kp = jnp.pad(strided(k), ((0, 0), (0, 0), (n_back, 0), (0, 0), (0, 0)))
    vp = jnp.pad(strided(v), ((0, 0), (0, 0), (n_back, 0), (0, 0), (0, 0)))
    key_idx = np.arange(n_blk)[:, None] * qb + np.arange(qb + n_back)[None, :]
    kb = kp[:, :, key_idx]
    vb = vp[:, :, key_idx]
    steps = np.arange(qb)[:, None] + n_back - np.arange(qb + n_back)[None, :]
    mask = jnp.asarray(((steps >= 0) & (steps <= n_back))[None] & ((key_idx - n_back) >= 0)[:, None, :])
    dist = jnp.asarray((steps * dil).astype(np.float32))
    s = jnp.einsum("bdnqhe,bdnkhe->bdnhqk", qs, kb).astype(jnp.float32) * (hd ** -0.5)
    s = s - slopes[:, None, None] * dist
    s = jnp.where(mask[:, None], s, NEG)
    m = jnp.max(s, axis=-1, keepdims=True)
    p = jnp.exp(s - m)
    den = jnp.sum(p, axis=-1, keepdims=True)
    o = jnp.einsum("bdnhqk,bdnkhe->bdnqhe", p / den, vb.astype(jnp.float32))
    lse = (m + jnp.log(den))[..., 0]
    o = jnp.swapaxes(o.reshape(bsz, dil, s_sub, nh, hd), 1, 2).reshape(bsz, seq, nh, hd)
    lse = jnp.swapaxes(jnp.swapaxes(lse, 3, 4).reshape(bsz, dil, s_sub, nh), 1, 2).reshape(bsz, seq, nh)
    return o, lse


def dilated_mixer(h, w_in, q_norm_w, k_norm_w, w_out):
    bsz, seq, _ = h.shape
    u = h @ w_in
    n_pat = len(D_PATTERNS)
    qkv, z = jnp.split(u, [3 * n_pat * D_INNER], axis=-1)
    qkv = qkv.reshape(bsz, seq, n_pat, 3, D_SLOTS, D_HD)
    slopes = alibi_slopes(D_SLOTS)
    outs = []
    lses = []
    for g, (window, dil) in enumerate(D_PATTERNS):
        qg = rms_norm(qkv[:, :, g, 0], q_norm_w)
        kg = rms_norm(qkv[:, :, g, 1], k_norm_w)
        og, lg = dilated_window_attention(qg, kg, qkv[:, :, g, 2], window, dil, slopes)
        outs.append(og)
        lses.append(lg)
    wts = jax.nn.softmax(jnp.stack(lses, axis=0), axis=0)
    o = jnp.sum(wts[..., None] * jnp.stack(outs, axis=0), axis=0).reshape(bsz, seq, D_INNER)
    y = o * jax.nn.silu(z.astype(jnp.float32))
    return y.astype(h.dtype) @ w_out


def _normal(k, shape, scale):
    return jax.random.normal(k, shape, jnp.float32) * scale


def setup_inputs(seed: int = 0) -> dict:
    key = jax.random.key(seed)
    ks = jax.random.split(key, 30)
    lam_u = jax.random.uniform(ks[22], (N_C, C_WIDTH), jnp.float32, 0.9, 0.999) ** (1.0 / C_EXP)
    gate_b = jnp.stack([
        _normal(ks[5], (N_A, A_HEADS), 0.1),
        jnp.broadcast_to(jnp.linspace(3.0, 6.0, A_HEADS), (N_A, A_HEADS)) + _normal(ks[26], (N_A, A_HEADS), 0.1),
    ], axis=1)
    return {
        "x": _normal(ks[0], (BATCH, SEQ, D_MODEL), 1.0),
        "norm_w": 1.0 + _normal(ks[1], (DEPTH, D_MODEL), 0.02),
        "a_w_in": _normal(ks[2], (N_A, D_MODEL, A_COLS), D_MODEL ** -0.5),
        "a_conv_w": _normal(ks[3], (N_A, CONV_W, 2 * A_QK), CONV_W ** -0.5),
        "a_conv_b": _normal(ks[4], (N_A, 2 * A_QK), 0.02),
        "a_gate_b": gate_b,
        "a_out_norm_w": 1.0 + _normal(ks[6], (N_A, A_INNER), 0.02),
        "a_w_out": _normal(ks[7], (N_A, A_INNER, D_MODEL), A_INNER ** -0.5),
        "b_w_in": _normal(ks[8], (N_B, D_MODEL, B_COLS), D_MODEL ** -0.5),
        "b_cmp_pe": _normal(ks[9], (N_B, 2, B_CMP_LEN, B_HD), 0.1),
        "b_cmp_wk": _normal(ks[10], (N_B, B_CMP_LEN * B_HD, B_HD), (B_CMP_LEN * B_HD) ** -0.5),
        "b_cmp_wv": _normal(ks[11], (N_B, B_CMP_LEN * B_HD, B_HD), (B_CMP_LEN * B_HD) ** -0.5),
        "b_q_norm_w": 1.0 + _normal(ks[12], (N_B, B_HD), 0.02),
        "b_k_norm_w": 1.0 + _normal(ks[13], (N_B, 3, B_HD), 0.02),
        "b_w_out": _normal(ks[14], (N_B, B_INNER, D_MODEL), B_INNER ** -0.5),
        "c_w_in": _normal(ks[15], (N_C, D_MODEL, C_COLS), D_MODEL ** -0.5),
        "c_conv_w": _normal(ks[16], (N_C, CONV_W, C_WIDTH), CONV_W ** -0.5),
        "c_conv_b": _normal(ks[17], (N_C, C_WIDTH), 0.02),
        "c_w_a": _normal(ks[18], (N_C, C_BLOCKS, C_BLK, C_BLK), C_BLK ** -0.5),
        "c_b_a": _normal(ks[19], (N_C, C_WIDTH), 0.02),
        "c_w_x": _normal(ks[20], (N_C, C_BLOCKS, C_BLK, C_BLK), C_BLK ** -0.5),
        "c_b_x": _normal(ks[21], (N_C, C_WIDTH), 0.02),
        "c_lambda": jnp.log(lam_u) - jnp.log1p(-lam_u),
        "c_w_out": _normal(ks[23], (N_C, C_WIDTH, D_MODEL), C_WIDTH ** -0.5),
        "d_w_in": _normal(ks[24], (N_D, D_MODEL, D_COLS), D_MODEL ** -0.5),
        "d_q_norm_w": 1.0 + _normal(ks[25], (N_D, D_HD), 0.02),
        "d_k_norm_w": 1.0 + _normal(ks[27], (N_D, D_HD), 0.02),
        "d_w_out": _normal(ks[28], (N_D, D_INNER, D_MODEL), D_INNER ** -0.5),
    }


def reference(x, norm_w, a_w_in, a_conv_w, a_conv_b, a_gate_b, a_out_norm_w, a_w_out,
              b_w_in, b_cmp_pe, b_cmp_wk, b_cmp_wv, b_q_norm_w, b_k_norm_w, b_w_out,
              c_w_in, c_conv_w, c_conv_b, c_w_a, c_b_a, c_w_x, c_b_x, c_lambda, c_w_out,
              d_w_in, d_q_norm_w, d_k_norm_w, d_w_out):
    for layer in range(DEPTH):
        kind = layer % N_MIXERS
        j = layer // N_MIXERS
        hn = rms_norm(x, norm_w[layer])
        if kind == 0:
            y = mlstm_mixer(hn, a_w_in[j], a_conv_w[j], a_conv_b[j], a_gate_b[j], a_out_norm_w[j], a_w_out[j])
        elif kind == 1:
            y = nsa_mixer(hn, b_w_in[j], b_cmp_pe[j], b_cmp_wk[j], b_cmp_wv[j], b_q_norm_w[j],
                          b_k_norm_w[j], b_w_out[j])
        elif kind == 2:
            y = rglru_mixer(hn, c_w_in[j], c_conv_w[j], c_conv_b[j], c_w_a[j], c_b_a[j], c_w_x[j],
                            c_b_x[j], c_lambda[j], c_w_out[j])
        else:
            y = dilated_mixer(hn, d_w_in[j], d_q_norm_w[j], d_k_norm_w[j], d_w_out[j])
        x = x + y.astype(x.dtype)
    return x
```

```python
import math
from contextlib import ExitStack

import numpy as np
import concourse.bass as bass
import concourse.mybir as mybir
from concourse.bass_utils import run_bass_kernel_spmd

F32 = mybir.dt.float32
BF16 = mybir.dt.bfloat16
AF = mybir.ActivationFunctionType
ALU = mybir.AluOpType
AX = mybir.AxisListType

D = 4096
SEQ = 4096
NCORES = 8
EPS = 1e-6
TT = 512
NTT = SEQ // TT


class Res:
    __slots__ = ("name", "w", "r", "p", "dsem", "dcnt", "excl")

    def __init__(self, name):
        self.name = name
        self.p = {}
        self.excl = False
        self.w = {}
        self.r = {}
        self.dsem = None
        self.dcnt = 0


class Sched:
    def __init__(self, nc, stack):
        self.nc = nc
        self.stack = stack
        self.E = {"pe": nc.tensor, "dve": nc.vector, "act": nc.scalar,
                  "pool": nc.gpsimd, "sp": nc.sync}
        self.sem = {}
        self.cnt = {}
        for k in ("pe", "dve", "act", "pool"):
            self.sem[k] = stack.enter_context(nc.semaphore("cs_" + k))
            self.cnt[k] = 0
        self.waited = {k: {} for k in self.E}
        self.byname = {}
        self.dsems = []
        self.nwaits = 0
        self.nins = 0

    def res(self, name, dma=False):
        if name in self.byname:
            r = self.byname[name]
        else:
            r = Res(name)
            self.byname[name] = r
        if dma and r.dsem is None:
            r.dsem = self.stack.enter_context(self.nc.semaphore("ds_" + name))
            self.dsems.append(r)
        return r

    def _wait(self, e, key, tok):
        sem, val, teng = tok
        if self.waited[e].get(key, 0) >= val:
            return
        self.E[e].wait_ge(sem, val)
        self.waited[e][key] = val
        self.nwaits += 1

    def _deps(self, e, reads, writes, pwrites):
        for r in reads:
            for key, tok in r.w.items():
                self._wait(e, key, tok)
            if r.excl:
                for key, tok in r.r.items():
                    if tok[2] != e:
                        self._wait(e, key, tok)
        for w in writes:
            for key, tok in w.w.items():
                if tok[2] != e:
                    self._wait(e, key, tok)
            for key, tok in w.r.items():
                if tok[2] != e:
                    self._wait(e, key, tok)
        for w in pwrites:
            for key, tok in w.r.items():
                if tok[2] != e:
                    self._wait(e, key, tok)
            for key, tok in w.p.items():
                if tok[2] != e:
                    self._wait(e, key, tok)

    def _commit(self, key, tok, reads, writes, pwrites):
        for r in reads:
            r.r[key] = tok
        for w in writes:
            prev = dict(w.w)
            for k2, t2 in w.r.items():
                if k2 not in prev or prev[k2][1] < t2[1]:
                    prev[k2] = t2
            w.p = prev
            w.w = {key: tok}
            w.r = {}
        for w in pwrites:
            w.w[key] = tok

    def op(self, e, fn, reads=(), writes=(), pwrites=(), inc=True):
        self._deps(e, reads, writes, pwrites)
        ins = fn(self.E[e])
        self.nins += 1
        if inc:
            self.cnt[e] += 1
            ins.then_inc(self.sem[e], 1)
            tok = (self.sem[e], self.cnt[e], e)
        else:
            tok = (self.sem[e], self.cnt[e] + 1, e)
        self._commit("c_" + e, tok, reads, writes, pwrites)
        return ins

    def dma(self, q, out, in_, reads=(), writes=(), pwrites=(), sem_res=None, **kw):
        self._deps(q, reads, writes, pwrites)
        ins = self.E[q].dma_start(out=out, in_=in_, **kw)
        self.nins += 1
        if sem_res is None:
            for c in list(writes) + list(pwrites) + list(reads):
                if c.dsem is not None:
                    sem_res = c
                    break
        assert sem_res is not None and sem_res.dsem is not None
        sem_res.dcnt += 16
        ins.then_inc(sem_res.dsem, 16)
        tok = (sem_res.dsem, sem_res.dcnt, "dma")
        self._commit("d_" + sem_res.name, tok, reads, writes, pwrites)
        return ins

    def barrier(self, engines=("pe", "dve", "act", "pool", "sp")):
        for e in engines:
            for f in ("pe", "dve", "act", "pool"):
                if f != e and self.cnt[f] > 0:
                    self._wait(e, "c_" + f, (self.sem[f], self.cnt[f], f))
            for r in self.dsems:
                if r.dcnt > 0:
                    self._wait(e, "d_" + r.name, (r.dsem, r.dcnt, "dma"))

    def finish(self):
        self.barrier(engines=("sp",))


class KB:
    def __init__(self, nc, st):
        self.nc = nc
        self.S = Sched(nc, st)
        self.top = st
        self.uid = 0

    def scope(self):
        return _Scope(self)


class _Scope:
    def __init__(self, kb):
        self.kb = kb
        self.st = ExitStack()

    def __enter__(self):
        self.st.__enter__()
        return self

    def __exit__(self, *a):
        self.kb.S.barrier()
        return self.st.__exit__(*a)

    def sb(self, name, shape, dt, dma=False):
        self.kb.uid += 1
        t = self.st.enter_context(self.kb.nc.sbuf_tensor(f"{name}_{self.kb.uid}", shape, dt))
        return t, self.kb.S.res(name, dma)

    def ps(self, name, shape, dt):
        self.kb.uid += 1
        t = self.st.enter_context(self.kb.nc.psum_tensor(f"{name}_{self.kb.uid}", shape, dt))
        r = self.kb.S.res(name)
        r.excl = True
        return t, r


def make_ident(S, sc):
    identf, r_if = sc.sb("identf", [128, 128], F32)
    ident, r_id = sc.sb("ident", [128, 128], BF16)
    S.op("pool", lambda e: e.memset(identf[:], 0.0), writes=[r_if])
    S.op("pool", lambda e: e.affine_select(
        out=identf[:], in_=identf[:], pattern=[[-1, 128]], compare_op=ALU.not_equal,
        fill=1.0, base=0, channel_multiplier=1), reads=[r_if], writes=[r_if])
    S.op("dve", lambda e: e.tensor_copy(out=ident[:], in_=identf[:]), reads=[r_if], writes=[r_id])
    return ident, r_id, identf, r_if


def phase1(kb, x_src, nw_ap, groups):
    S = kb.S
    with kb.scope() as sc:
        ident, r_id, _, _ = make_ident(S, sc)
        nwb, r_nwb = sc.sb("p1_nwb", [128, D], F32, dma=True)
        xt = [sc.sb(f"p1_xt{i}", [128, D], F32, dma=True) for i in range(2)]
        xn = [sc.sb(f"p1_xn{i}", [128, D], BF16) for i in range(2)]
        ss = [sc.sb(f"p1_ss{i}", [128, 2], F32) for i in range(2)]
        hT, r_hT = sc.sb("p1_hT", [128, 32, TT], BF16)
        NWB = 3
        wb = [sc.sb(f"p1_wb{i}", [128, 32, 256], BF16, dma=True) for i in range(NWB)]
        NOB = 4
        ob = [sc.sb(f"p1_ob{i}", [128, 512], F32, dma=True) for i in range(NOB)]
        obh = [sc.sb(f"p1_obh{i}", [128, 512], BF16, dma=True) for i in range(NOB)]
        pt = [sc.ps(f"p1_pt{i}", [128, 1024], BF16) for i in range(2)]
        pm = [sc.ps(f"p1_pm{i}", [128, 512], F32) for i in range(4)]
        S.dma("sp", nwb[:], nw_ap.partition_broadcast(128), writes=[r_nwb])
        wi = 0
        oi = 0
        pi = 0
        ev = 0
        for T in range(NTT):
            for s in range(TT // 128):
                b = s % 2
                t0 = T * TT + s * 128
                xt_t, xt_r = xt[b]
                xn_t, xn_r = xn[b]
                ss_t, ss_r = ss[b]
                S.dma("sp", xt_t[:], x_src[t0:t0 + 128, :], writes=[xt_r])
                S.op("act", lambda e: e.activation(out=xn_t[:], in_=xt_t[:], func=AF.Square,
                                                   accum_out=ss_t[:, 0:1]),
                     reads=[xt_r], writes=[xn_r, ss_r])
                S.op("act", lambda e: e.activation(out=ss_t[:, 1:2], in_=ss_t[:, 0:1], func=AF.Sqrt,
                                                   scale=1.0 / D, bias=EPS),
                     reads=[ss_r], writes=[ss_r])
                S.op("dve", lambda e: e.reciprocal(out=ss_t[:, 0:1], in_=ss_t[:, 1:2]),
                     reads=[ss_r], writes=[ss_r])
                S.op("dve", lambda e: e.scalar_tensor_tensor(out=xn_t[:], in0=xt_t[:], scalar=ss_t[:, 0:1],
                                                             in1=nwb[:], op0=ALU.mult, op1=ALU.mult),
                     reads=[xt_r, ss_r, r_nwb], writes=[xn_r])
                for g4 in range(4):
                    pt_t, pt_r = pt[g4 % 2]
                    for j in range(8):
                        kc = g4 * 8 + j
                        S.op("pe", lambda e: e.transpose(out=pt_t[:, j * 128:(j + 1) * 128],
                                                         in_=xn_t[:, kc * 128:(kc + 1) * 128], identity=ident[:]),
                             reads=[xn_r, r_id], writes=[pt_r] if j == 0 else [],
                             pwrites=[] if j == 0 else [pt_r], inc=(j == 7))
                    dst = hT[:, g4 * 8:(g4 + 1) * 8, s * 128:(s + 1) * 128]
                    src = pt_t[:].rearrange("p (j t) -> p j t", j=8)
                    if g4 % 2 == 0:
                        S.op("dve", lambda e: e.tensor_copy(out=dst, in_=src), reads=[pt_r], pwrites=[r_hT])
                    else:
                        S.op("act", lambda e: e.copy(out=dst, in_=src), reads=[pt_r], pwrites=[r_hT])
            for g in groups:
                W = g["W"]
                odt = g["dst"].dtype
                for c in range(g["nchunk"]):
                    wb_t, wb_r = wb[wi % NWB]
                    wi += 1
                    S.dma("pool", wb_t[:, :, :W], g["w"][c], writes=[wb_r])
                    if g["layout"] == "fm":
                        pm_t, pm_r = pm[pi % 4]
                        pi += 1
                        for kc in range(32):
                            S.op("pe", lambda e: e.matmul(pm_t[:W, :TT], lhsT=wb_t[:, kc, :W], rhs=hT[:, kc, :],
                                                          start=(kc == 0), stop=(kc == 31)),
                                 reads=[wb_r, r_hT], writes=[pm_r] if kc == 0 else [],
                                 pwrites=[] if kc == 0 else [pm_r], inc=(kc == 31))
                        ob_t, ob_r = (ob if odt == F32 else obh)[oi % NOB]
                        oi += 1
                        if ev % 2 == 0:
                            S.op("dve", lambda e: e.tensor_copy(out=ob_t[:W, :TT], in_=pm_t[:W, :TT]),
                                 reads=[pm_r], writes=[ob_r])
                        else:
                            S.op("act", lambda e: e.copy(out=ob_t[:W, :TT], in_=pm_t[:W, :TT]),
                                 reads=[pm_r], writes=[ob_r])
                        ev += 1
                        S.dma("sp", g["dst"][c * W:(c + 1) * W, T * TT:(T + 1) * TT], ob_t[:W, :TT],
                              reads=[ob_r], pwrites=[g["rdst"]])
                    else:
                        for s in range(TT // 128):
                            pm_t, pm_r = pm[pi % 4]
                            pi += 1
                            for kc in range(32):
                                S.op("pe", lambda e: e.matmul(pm_t[:, :W], lhsT=hT[:, kc, s * 128:(s + 1) * 128],
                                                              rhs=wb_t[:, kc, :W],
                                                              start=(kc == 0), stop=(kc == 31)),
                                     reads=[wb_r, r_hT], writes=[pm_r] if kc == 0 else [],
                                     pwrites=[] if kc == 0 else [pm_r], inc=(kc == 31))
                            ob_t, ob_r = (ob if odt == F32 else obh)[oi % NOB]
                            oi += 1
                            if ev % 2 == 0:
                                S.op("dve", lambda e: e.tensor_copy(out=ob_t[:, :W], in_=pm_t[:, :W]),
                                     reads=[pm_r], writes=[ob_r])
                            else:
                                S.op("act", lambda e: e.copy(out=ob_t[:, :W], in_=pm_t[:, :W]),
                                     reads=[pm_r], writes=[ob_r])
                            ev += 1
                            t0 = T * TT + s * 128
                            S.dma("sp", g["dst"][t0:t0 + 128, c * W:(c + 1) * W], ob_t[:, :W],
                                  reads=[ob_r], pwrites=[g["rdst"]])


def phase3(kb, o_scr, r_o, P, n_ic, wout_ap, y_dst, r_y, accum=None):
    S = kb.S
    with kb.scope() as sc:
        w, r_w = sc.sb("p3_w", [P, n_ic, D], BF16, dma=True)
        for ic in range(n_ic):
            S.dma("pool", w[:, ic, :], wout_ap[:, ic, :], pwrites=[r_w])
        ot = [sc.sb(f"p3_ot{i}", [P, n_ic, TT], BF16, dma=True) for i in range(2)]
        yb = [sc.sb(f"p3_yb{i}", [128, D], F32, dma=True) for i in range(2)]
        if accum is not None:
            xd, r_xd = sc.sb("p3_xd", [128, D], F32, dma=True)
        pm = [sc.ps(f"p3_pm{i}", [128, 512], F32) for i in range(4)]
        pi = 0
        yi = 0
        o_v = o_scr.rearrange("(ic p) t -> p ic t", p=P)
        for T in range(NTT):
            ot_t, ot_r = ot[T % 2]
            S.dma("sp", ot_t[:], o_v[:, :, T * TT:(T + 1) * TT], reads=[r_o], writes=[ot_r])
            for s in range(TT // 128):
                yb_t, yb_r = yb[yi % 2]
                yi += 1
                for cc in range(D // 512):
                    pm_t, pm_r = pm[pi % 4]
                    pi += 1
                    for ic in range(n_ic):
                        S.op("pe", lambda e: e.matmul(pm_t[:, :], lhsT=ot_t[:, ic, s * 128:(s + 1) * 128],
                                                      rhs=w[:, ic, cc * 512:(cc + 1) * 512],
                                                      start=(ic == 0), stop=(ic == n_ic - 1)),
                             reads=[ot_r, r_w], writes=[pm_r] if ic == 0 else [],
                             pwrites=[] if ic == 0 else [pm_r], inc=(ic == n_ic - 1))
                    if cc % 2 == 0:
                        S.op("dve", lambda e: e.tensor_copy(out=yb_t[:, cc * 512:(cc + 1) * 512], in_=pm_t[:, :]),
                             reads=[pm_r], writes=[yb_r] if cc == 0 else [], pwrites=[] if cc == 0 else [yb_r])
                    else:
                        S.op("act", lambda e: e.copy(out=yb_t[:, cc * 512:(cc + 1) * 512], in_=pm_t[:, :]),
                             reads=[pm_r], pwrites=[yb_r])
                t0 = T * TT + s * 128
                if accum is not None:
                    S.dma("sp", xd[:], accum[t0:t0 + 128, :], writes=[r_xd], sem_res=r_xd)
                    hD = D // 2
                    S.op("dve", lambda e: e.tensor_tensor(out=yb_t[:, 0:hD], in0=yb_t[:, 0:hD], in1=xd[:, 0:hD], op=ALU.add),
                         reads=[yb_r, r_xd], writes=[yb_r])
                    S.op("pool", lambda e: e.tensor_tensor(out=yb_t[:, hD:D], in0=yb_t[:, hD:D], in1=xd[:, hD:D], op=ALU.add),
                         reads=[yb_r, r_xd], writes=[yb_r])
                S.dma("sp", y_dst[t0:t0 + 128, :], yb_t[:], reads=[yb_r], pwrites=[r_y], sem_res=yb_r)


C_P = 112
C_NCH = 12


def mixer_rglru(kb, xbT, r_xb, zT, r_z, prm, o_scr, r_o):
    S = kb.S
    P = C_P
    with kb.scope() as sc:
        cw, r_cw = sc.sb("c_cw", [P, C_NCH, 4], F32, dma=True)
        vec, r_vec = sc.sb("c_vec", [P, 4, C_NCH], F32, dma=True)
        c1, r_c1 = sc.sb("c_c1", [P, C_NCH], F32)
        wa, r_wa = sc.sb("c_wa", [P, 4, 3, 336], BF16, dma=True)
        wx, r_wx = sc.sb("c_wx", [P, 4, 3, 336], BF16, dma=True)
        S.dma("sp", cw[:], prm["cw"], writes=[r_cw])
        S.dma("sp", vec[:], prm["vec"], writes=[r_vec])
        S.dma("pool", wa[:], prm["wa"], writes=[r_wa])
        S.dma("pool", wx[:], prm["wx"], writes=[r_wx])
        S.op("act", lambda e: e.activation(out=c1[:], in_=vec[:, 3, :], func=AF.Exp, scale=-1.0),
             reads=[r_vec], writes=[r_c1])
        S.op("act", lambda e: e.activation(out=c1[:], in_=c1[:], func=AF.Ln, bias=1.0, scale=1.0),
             reads=[r_c1], writes=[r_c1])
        S.op("dve", lambda e: e.tensor_scalar(out=c1[:], in0=c1[:], scalar1=-8.0, scalar2=None, op0=ALU.mult),
             reads=[r_c1], writes=[r_c1])
        xb, r_xbs = sc.sb("c_xb", [P, SEQ + 4], F32, dma=True)
        xc = [sc.sb(f"c_xc{i}", [P, SEQ], F32) for i in range(3)]
        xcb, r_xcb = sc.sb("c_xcb", [P, 3, SEQ], BF16)
        ra, r_ra = sc.sb("c_ra", [P, SEQ], F32)
        gi, r_gi = sc.sb("c_gi", [P, SEQ], F32)
        tmp, r_tmp = sc.sb("c_tmp", [P, SEQ], F32)
        zt, r_zt = sc.sb("c_zt", [P, SEQ], BF16, dma=True)
        ob, r_ob = sc.sb("c_ob", [P, SEQ], BF16, dma=True)
        pm = [sc.ps(f"c_pm{i}", [128, 512], F32) for i in range(4)]
        pi = 0
        S.op("pool", lambda e: e.memset(xb[:, 0:4], 0.0), writes=[r_xbs])
        for n in range(4):
            for c in range(3):
                ch = n * 3 + c
                xc_t, xc_r = xc[c]
                S.dma("sp", xb[:, 4:], xbT[ch * P:(ch + 1) * P, :], reads=[r_xb], writes=[r_xbs])
                S.op("dve", lambda e: e.tensor_scalar(out=xc_t[:], in0=xb[:, 1:1 + SEQ], scalar1=cw[:, ch, 0:1],
                                                      scalar2=vec[:, 0, ch:ch + 1], op0=ALU.mult, op1=ALU.add),
                     reads=[r_xbs, r_cw, r_vec], writes=[xc_r])
                for j in range(1, 4):
                    S.op("dve", lambda e: e.scalar_tensor_tensor(out=xc_t[:], in0=xb[:, 1 + j:1 + j + SEQ],
                                                                 scalar=cw[:, ch, j:j + 1], in1=xc_t[:],
                                                                 op0=ALU.mult, op1=ALU.add),
                         reads=[r_xbs, r_cw, xc_r], writes=[xc_r])
                S.op("act", lambda e: e.copy(out=xcb[:, c, :], in_=xc_t[:]), reads=[xc_r],
                     writes=[r_xcb] if c == 0 else [], pwrites=[] if c == 0 else [r_xcb])
            for d in range(3):
                ch = n * 3 + d
                xc_t, xc_r = xc[d]
                S.dma("sp", zt[:], zT[ch * P:(ch + 1) * P, :], reads=[r_z], writes=[r_zt])
                for (wt, wr, dst, dr, bi) in ((wa, r_wa, ra, r_ra, 1), (wx, r_wx, gi, r_gi, 2)):
                    for T in range(NTT):
                        pm_t, pm_r = pm[pi % 4]
                        pi += 1
                        for c in range(3):
                            S.op("pe", lambda e: e.matmul(pm_t[:P, :TT], lhsT=wt[:, n, c, d * P:(d + 1) * P],
                                                          rhs=xcb[:, c, T * TT:(T + 1) * TT],
                                                          start=(c == 0), stop=(c == 2)),
                                 reads=[wr, r_xcb], writes=[pm_r] if c == 0 else [],
                                 pwrites=[] if c == 0 else [pm_r], inc=(c == 2))
                        S.op("act", lambda e: e.activation(out=dst[:, T * TT:(T + 1) * TT], in_=pm_t[:P, :TT],
                                                           func=AF.Sigmoid, bias=vec[:, bi, ch:ch + 1], scale=1.0),
                             reads=[pm_r, r_vec], writes=[dr] if T == 0 else [], pwrites=[] if T == 0 else [dr])
                S.op("act", lambda e: e.activation(out=ra[:], in_=ra[:], func=AF.Exp, scale=c1[:, ch:ch + 1]),
                     reads=[r_ra, r_c1], writes=[r_ra])
                S.op("pool", lambda e: e.tensor_tensor(out=tmp[:], in0=ra[:], in1=ra[:], op=ALU.mult),
                     reads=[r_ra], writes=[r_tmp])
                S.op("act", lambda e: e.activation(out=tmp[:], in_=tmp[:], func=AF.Sqrt, scale=-1.0, bias=1.0),
                     reads=[r_tmp], writes=[r_tmp])
                S.op("dve", lambda e: e.tensor_tensor(out=gi[:], in0=gi[:], in1=xc_t[:], op=ALU.mult),
                     reads=[r_gi, xc_r], writes=[r_gi])
                S.op("pool", lambda e: e.tensor_tensor(out=gi[:], in0=gi[:], in1=tmp[:], op=ALU.mult),
                     reads=[r_gi, r_tmp], writes=[r_gi])
                S.op("dve", lambda e: e.tensor_tensor_scan(out=tmp[:], data0=ra[:], data1=gi[:], initial=0.0,
                                                           op0=ALU.mult, op1=ALU.add),
                     reads=[r_ra, r_gi], writes=[r_tmp])
                S.op("act", lambda e: e.activation(out=zt[:], in_=zt[:], func=AF.Silu), reads=[r_zt], writes=[r_zt])
                S.op("pool", lambda e: e.tensor_tensor(out=ob[:], in0=tmp[:], in1=zt[:], op=ALU.mult),
                     reads=[r_tmp, r_zt], writes=[r_ob])
                S.dma("sp", o_scr[ch * P:(ch + 1) * P, :], ob[:], reads=[r_ob], pwrites=[r_o])


D_PATS = ((128, 1), (512, 4), (2048, 16))


def mixer_dilated(kb, u, r_u, prm, o_scr, r_o, nd_scr, r_nd):
    S = kb.S
    scale = 128.0 ** -0.5
    with kb.scope() as sc:
        ident, r_id, _, _ = make_ident(S, sc)
        wqk, r_wqk = sc.sb("d_wqk", [128, 2, 128], F32, dma=True)
        bm, r_bm = sc.sb("d_bm", [128, 3, 2, 512], F32, dma=True)
        S.dma("sp", wqk[:].rearrange("p a d -> p (a d)"), prm["qkw"].partition_broadcast(128), writes=[r_wqk])
        S.dma("sp", bm[:].rearrange("p a b c -> p (a b c)"), prm["bm"], writes=[r_bm])
        blk = [sc.sb(f"d_blk{i}", [128, 1536], BF16, dma=True) for i in range(2)]
        sq, r_sq = sc.sb("d_sq", [128, 1024], F32)
        ssq, r_ssq = sc.sb("d_ssq", [128, 8], F32)
        rstd, r_rstd = sc.sb("d_rstd", [128, 8], F32)
        qkn, r_qkn = sc.sb("d_qkn", [128, 1024], BF16)
        qT, r_qT = sc.sb("d_qT", [128, 512], BF16)
        kT = [sc.sb(f"d_kT{i}", [128, 512], BF16) for i in range(2)]
        va = [sc.sb(f"d_va{i}", [128, 4, 129], BF16) for i in range(2)]
        pex = [sc.sb(f"d_pex{i}", [128, 512], F32) for i in range(2)]
        ptm = [sc.sb(f"d_ptm{i}", [128, 512], BF16) for i in range(2)]
        ndst = [sc.sb(f"d_ndst{i}", [128, 4, 129], F32, dma=True) for i in range(2)]
        pt, r_pt = sc.ps("d_pt", [128, 1024], BF16)
        pss = [sc.ps(f"d_pss{i}", [128, 512], F32) for i in range(4)]
        accs = [sc.ps(f"d_acc{i}", [128, 512], F32) for i in range(2)]
        for i in range(2):
            S.op("pool", lambda e: e.memset(va[i][0][:], 1.0), writes=[va[i][1]])
        bi = 0
        si = 0
        for p, (window, dil) in enumerate(D_PATS):
            uv = u.rearrange("(n dl) c -> dl n c", dl=dil)
            ndv = nd_scr[p].rearrange("(n dl) c -> dl n c", dl=dil)
            for r in range(dil):
                for i in range(SEQ // dil // 128):
                    blk_t, blk_r = blk[bi % 2]
                    nd_t, nd_r = ndst[bi % 2]
                    bi += 1
                    cur = i % 2
                    kT_t, kT_r = kT[cur]
                    va_t, va_r = va[cur]
                    S.dma("sp", blk_t[:], uv[r, i * 128:(i + 1) * 128, p * 1536:(p + 1) * 1536],
                          reads=[r_u], writes=[blk_r])
                    S.op("dve", lambda e: e.tensor_tensor(out=sq[:], in0=blk_t[:, 0:1024], in1=blk_t[:, 0:1024],
                                                          op=ALU.mult), reads=[blk_r], writes=[r_sq])
                    S.op("dve", lambda e: e.tensor_reduce(out=ssq[:], in_=sq[:].rearrange("p (h d) -> p h d", h=8),
                                                          axis=AX.X, op=ALU.add), reads=[r_sq], writes=[r_ssq])
                    S.op("act", lambda e: e.activation(out=rstd[:], in_=ssq[:], func=AF.Sqrt, scale=1.0 / 128, bias=EPS),
                         reads=[r_ssq], writes=[r_rstd])
                    S.op("dve", lambda e: e.reciprocal(out=rstd[:], in_=rstd[:]), reads=[r_rstd], writes=[r_rstd])
                    S.op("dve", lambda e: e.tensor_tensor(
                        out=sq[:].rearrange("p (h d) -> p h d", h=8),
                        in0=blk_t[:, 0:1024].rearrange("p (h d) -> p h d", h=8),
                        in1=rstd[:, :].unsqueeze(2).to_broadcast([128, 8, 128]), op=ALU.mult),
                        reads=[blk_r, r_rstd], writes=[r_sq])
                    S.op("pool", lambda e: e.tensor_tensor(
                        out=qkn[:].rearrange("p (a h d) -> p a h d", a=2, h=4),
                        in0=sq[:].rearrange("p (a h d) -> p a h d", a=2, h=4),
                        in1=wqk[:, :, :].unsqueeze(2).to_broadcast([128, 2, 4, 128]), op=ALU.mult),
                        reads=[r_sq, r_wqk], writes=[r_qkn])
                    for j in range(8):
                        S.op("pe", lambda e: e.transpose(out=pt[:, j * 128:(j + 1) * 128],
                                                         in_=qkn[:, j * 128:(j + 1) * 128], identity=ident[:]),
                             reads=[r_qkn, r_id], writes=[r_pt] if j == 0 else [],
                             pwrites=[] if j == 0 else [r_pt], inc=(j == 7))
                    S.op("dve", lambda e: e.tensor_copy(out=qT[:], in_=pt[:, 0:512]), reads=[r_pt], writes=[r_qT])
                    S.op("act", lambda e: e.copy(out=kT_t[:], in_=pt[:, 512:1024]), reads=[r_pt], writes=[kT_r])
                    S.op("pool", lambda e: e.tensor_copy(out=va_t[:, :, 0:128],
                                                         in_=blk_t[:, 1024:1536].rearrange("p (h d) -> p h d", h=4)),
                         reads=[blk_r], writes=[va_r])
                    tiles = ([(1 - cur, 0)] if i > 0 else []) + [(cur, 1)]
                    pts = []
                    for ti, (slot, kind) in enumerate(tiles):
                        ps_t, ps_r = pss[si % 4]
                        pe_t, pe_r = pex[si % 2]
                        pm_t, pm_r = ptm[si % 2]
                        si += 1
                        for h in range(4):
                            S.op("pe", lambda e: e.matmul(ps_t[:, h * 128:(h + 1) * 128],
                                                          lhsT=kT[slot][0][:, h * 128:(h + 1) * 128],
                                                          rhs=qT[:, h * 128:(h + 1) * 128], start=True, stop=True),
                                 reads=[kT[slot][1], r_qT], writes=[ps_r] if h == 0 else [],
                                 pwrites=[] if h == 0 else [ps_r], inc=(h == 3))
                        S.op("act", lambda e: e.activation(out=pe_t[:], in_=ps_t[:], func=AF.Exp, scale=scale),
                             reads=[ps_r], writes=[pe_r])
                        S.op("dve", lambda e: e.tensor_tensor(out=pm_t[:], in0=pe_t[:], in1=bm[:, p, kind, :],
                                                              op=ALU.mult), reads=[pe_r, r_bm], writes=[pm_r])
                        pts.append((pm_t, pm_r, slot))
                    for h in range(4):
                        off = (h % 2) * 129
                        acc, r_acc = accs[h // 2]
                        for ti, (pm_t, pm_r, slot) in enumerate(pts):
                            last = (ti == len(pts) - 1)
                            S.op("pe", lambda e: e.matmul(acc[:, off:off + 129], lhsT=pm_t[:, h * 128:(h + 1) * 128],
                                                          rhs=va[slot][0][:, h, :], start=(ti == 0), stop=last),
                                 reads=[pm_r, va[slot][1]],
                                 writes=[r_acc] if (ti == 0 and h % 2 == 0) else [],
                                 pwrites=[] if (ti == 0 and h % 2 == 0) else [r_acc],
                                 inc=(last and h % 2 == 1))
                    S.op("dve", lambda e: e.tensor_copy(
                        out=nd_t[:, 0:2, :], in_=accs[0][0][:, 0:258].rearrange("p (h c) -> p h c", h=2)),
                        reads=[accs[0][1]], writes=[nd_r])
                    S.op("act", lambda e: e.copy(
                        out=nd_t[:, 2:4, :], in_=accs[1][0][:, 0:258].rearrange("p (h c) -> p h c", h=2)),
                        reads=[accs[1][1]], pwrites=[nd_r])
                    S.dma("sp", ndv[r, i * 128:(i + 1) * 128, :], nd_t[:].rearrange("p h c -> p (h c)"),
                          reads=[nd_r], pwrites=[r_nd])
    with kb.scope() as sc:
        ident, r_id, _, _ = make_ident(S, sc)
        nds = [[sc.sb(f"d2_nd{i}_{j}", [128, 4, 129], F32, dma=True) for j in range(3)] for i in range(2)]
        zt = [sc.sb(f"d2_z{i}", [128, 512], BF16, dma=True) for i in range(2)]
        rden, r_rden = sc.sb("d2_rden", [128, 4], F32)
        of, r_of = sc.sb("d2_of", [128, 4, 128], F32)
        zs, r_zs = sc.sb("d2_zs", [128, 512], F32)
        ob, r_ob = sc.sb("d2_ob", [128, 512], BF16)
        oT = [sc.sb(f"d2_oT{i}", [128, 4, TT], BF16, dma=True) for i in range(2)]
        pt, r_pt = sc.ps("d2_pt", [128, 1024], BF16)
        for t in range(SEQ // 128):
            a = nds[t % 2]
            z_t, z_r = zt[t % 2]
            oT_t, oT_r = oT[(t // 4) % 2]
            for j in range(3):
                S.dma("sp", a[j][0][:].rearrange("p h c -> p (h c)"), nd_scr[j, t * 128:(t + 1) * 128, :],
                      reads=[r_nd], writes=[a[j][1]])
            S.dma("sp", z_t[:], u[t * 128:(t + 1) * 128, 4608:5120], reads=[r_u], writes=[z_r])
            s_t, s_r = a[0]
            S.op("dve", lambda e: e.tensor_tensor(out=s_t[:], in0=s_t[:], in1=a[1][0][:], op=ALU.add),
                 reads=[s_r, a[1][1]], writes=[s_r])
            S.op("pool", lambda e: e.tensor_tensor(out=s_t[:], in0=s_t[:], in1=a[2][0][:], op=ALU.add),
                 reads=[s_r, a[2][1]], writes=[s_r])
            S.op("dve", lambda e: e.reciprocal(out=rden[:], in_=s_t[:, :, 128]), reads=[s_r], writes=[r_rden])
            S.op("dve", lambda e: e.tensor_tensor(out=of[:], in0=s_t[:, :, 0:128],
                                                  in1=rden[:, :].unsqueeze(2).to_broadcast([128, 4, 128]), op=ALU.mult),
                 reads=[s_r, r_rden], writes=[r_of])
            S.op("act", lambda e: e.activation(out=zs[:], in_=z_t[:], func=AF.Silu), reads=[z_r], writes=[r_zs])
            S.op("pool", lambda e: e.tensor_tensor(out=ob[:], in0=of[:].rearrange("p h d -> p (h d)"), in1=zs[:],
                                                   op=ALU.mult), reads=[r_of, r_zs], writes=[r_ob])
            for h in range(4):
                S.op("pe", lambda e: e.transpose(out=pt[:, h * 128:(h + 1) * 128], in_=ob[:, h * 128:(h + 1) * 128],
                                                 identity=ident[:]),
                     reads=[r_ob, r_id], writes=[r_pt] if h == 0 else [], pwrites=[] if h == 0 else [r_pt],
                     inc=(h == 3))
            q4 = t % 4
            S.op("act", lambda e: e.copy(out=oT_t[:, :, q4 * 128:(q4 + 1) * 128],
                                         in_=pt[:, 0:512].rearrange("p (h t) -> p h t", h=4)),
                 reads=[r_pt], writes=[oT_r] if q4 == 0 else [], pwrites=[] if q4 == 0 else [oT_r])
            if q4 == 3:
                T = t // 4
                S.dma("sp", o_scr.rearrange("(h p) t -> p h t", p=128)[:, :, T * TT:(T + 1) * TT], oT_t[:],
                      reads=[oT_r], pwrites=[r_o])


def mixer_mlstm(kb, qkT, r_qk, utm, r_utm, gates, r_g, prm, o_scr, r_o):
    S = kb.S
    NCH = SEQ // 128
    with kb.scope() as sc0:
        qkb, r_qkb = sc0.sb("a_qkb", [128, 8, SEQ], BF16)
        with kb.scope() as sc:
            cw, r_cw = sc.sb("a_cw", [128, 8, 4], F32, dma=True)
            cb, r_cb = sc.sb("a_cb", [128, 8], F32, dma=True)
            S.dma("sp", cw[:], prm["cw"], writes=[r_cw])
            S.dma("sp", cb[:], prm["cb"], writes=[r_cb])
            xb = [sc.sb(f"a_xb{i}", [128, SEQ + 4], F32, dma=True) for i in range(2)]
            xc, r_xc = sc.sb("a_xc", [128, SEQ], F32)
            for i in range(2):
                S.op("pool", lambda e: e.memset(xb[i][0][:, 0:4], 0.0), writes=[xb[i][1]])
            for ch in range(8):
                xb_t, xb_r = xb[ch % 2]
                S.dma("sp", xb_t[:, 4:], qkT[ch * 128:(ch + 1) * 128, :], reads=[r_qk], writes=[xb_r])
                S.op("dve", lambda e: e.tensor_scalar(out=xc[:], in0=xb_t[:, 1:1 + SEQ], scalar1=cw[:, ch, 0:1],
                                                      scalar2=cb[:, ch:ch + 1], op0=ALU.mult, op1=ALU.add),
                     reads=[xb_r, r_cw, r_cb], writes=[r_xc])
                for j in range(1, 4):
                    S.op("dve", lambda e: e.scalar_tensor_tensor(out=xc[:], in0=xb_t[:, 1 + j:1 + j + SEQ],
                                                                 scalar=cw[:, ch, j:j + 1], in1=xc[:],
                                                                 op0=ALU.mult, op1=ALU.add),
                         reads=[xb_r, r_cw, r_xc], writes=[r_xc])
                S.op("act", lambda e: e.activation(out=qkb[:, ch, :], in_=xc[:], func=AF.Silu),
                     reads=[r_xc], writes=[r_qkb] if ch == 0 else [], pwrites=[] if ch == 0 else [r_qkb])
        with kb.scope() as sc:
            ident, r_id, identf, r_if = make_ident(S, sc)
            triu, r_tri = sc.sb("a_triu", [128, 128], F32)
            ones, r_ones = sc.sb("a_ones", [128, 128], F32)
            onesb, r_onesb = sc.sb("a_onesb", [128, 1], BF16)
            S.op("pool", lambda e: e.memset(triu[:], 1.0), writes=[r_tri])
            S.op("pool", lambda e: e.affine_select(out=triu[:], in_=triu[:], pattern=[[1, 128]],
                                                   compare_op=ALU.is_ge, fill=0.0, base=0, channel_multiplier=-1),
                 reads=[r_tri], writes=[r_tri])
            S.op("pool", lambda e: e.memset(ones[:], 1.0), writes=[r_ones])
            S.op("pool", lambda e: e.memset(onesb[:], 1.0), writes=[r_onesb])
            onw, r_onw = sc.sb("a_onw", [128, 1024], F32, dma=True)
            S.dma("sp", onw[:], prm["onw"].partition_broadcast(128), writes=[r_onw])
            gb, r_gb = sc.sb("a_gb", [128, 4], F32, dma=True)
            S.dma("sp", gb[:], prm["gb"].partition_broadcast(128), writes=[r_gb])
            G, r_G = sc.sb("a_G", [128, NCH, 4], F32, dma=True)
            gv = gates.rearrange("(c p) g -> p c g", p=128)
            for i4 in range(4):
                S.dma("sp", G[:, i4 * 8:(i4 + 1) * 8, :], gv[:, i4 * 8:(i4 + 1) * 8, :], reads=[r_g],
                      writes=[r_G] if i4 == 0 else [], pwrites=[] if i4 == 0 else [r_G])
            S.op("dve", lambda e: e.tensor_tensor(out=G[:], in0=G[:], in1=gb[:, :].unsqueeze(1).to_broadcast([128, NCH, 4]),
                                                  op=ALU.add), reads=[r_G, r_gb], writes=[r_G])
            lf, r_lf = sc.sb("a_lf", [128, NCH, 2], F32)
            S.op("act", lambda e: e.activation(out=lf[:], in_=G[:, :, 2:4], func=AF.Exp, scale=-1.0),
                 reads=[r_G], writes=[r_lf])
            S.op("act", lambda e: e.activation(out=lf[:], in_=lf[:], func=AF.Ln, bias=1.0, scale=1.0),
                 reads=[r_lf], writes=[r_lf])
            S.op("dve", lambda e: e.tensor_scalar(out=lf[:], in0=lf[:], scalar1=-1.0, scalar2=None, op0=ALU.mult),
                 reads=[r_lf], writes=[r_lf])
            ps_b, r_psb = sc.ps("a_psb", [128, 512], F32)
            S.op("pe", lambda e: e.matmul(ps_b[:, 0:2 * NCH], lhsT=triu[:], rhs=lf[:].rearrange("p c h -> p (c h)"),
                                          start=True, stop=True), reads=[r_tri, r_lf], writes=[r_psb])
            cs, r_cs = sc.sb("a_cs", [128, NCH, 2], F32)
            ecs, r_ecs = sc.sb("a_ecs", [128, NCH, 2], F32)
            S.op("dve", lambda e: e.tensor_tensor(out=cs[:], in0=G[:, :, 0:2],
                                                  in1=ps_b[:, 0:2 * NCH].rearrange("p (c h) -> p c h", h=2),
                                                  op=ALU.subtract), reads=[r_G, r_psb], writes=[r_cs])
            S.op("act", lambda e: e.activation(out=ecs[:], in_=cs[:], func=AF.Exp), reads=[r_cs], writes=[r_ecs])
            Cst, r_C = sc.sb("a_C", [128, 2, 2, 512], F32)
            Cb, r_Cb = sc.sb("a_Cb", [128, 2, 2, 512], BF16)
            nst, r_n = sc.sb("a_n", [128, 2, 2], F32)
            nb, r_nb = sc.sb("a_nb", [128, 2, 2], BF16)
            S.op("pool", lambda e: e.memset(Cst[:], 0.0), writes=[r_C])
            S.op("pool", lambda e: e.memset(nst[:], 0.0), writes=[r_n])
            LT, r_LT = sc.sb("a_LT", [128, 4, 128], F32)
            EB = [sc.sb(f"a_EB{i}", [128, 4, 128], F32) for i in range(2)]
            DT = [sc.sb(f"a_DT{i}", [128, 128], F32) for i in range(4)]
            ws, r_ws = sc.sb("a_ws", [128, 4], F32)
            wsb, r_wsb = sc.sb("a_wsb", [128, 4], BF16)
            vt = [sc.sb(f"a_vt{i}", [128, 3072], BF16, dma=True) for i in range(2)]
            aT, r_aT = sc.sb("a_aT", [128, 128], BF16)
            qp, r_qp = sc.sb("a_qp", [128, 2, 128], BF16)
            vp, r_vp = sc.sb("a_vp", [128, 512], BF16)
            ktm, r_ktm = sc.sb("a_ktm", [128, 256], BF16)
            den, r_den = sc.sb("a_den", [128, 4], F32)
            hc, r_hc = sc.sb("a_hc", [128, 512], F32)
            junk, r_junk = sc.sb("a_junk", [128, 512], F32)
            t1, r_t1 = sc.sb("a_t1", [128, 512], F32)
            sg, r_sg = sc.sb("a_sg", [128, 512], F32)
            sz, r_sz = sc.sb("a_sz", [128, 512], F32)
            yb, r_yb = sc.sb("a_yb", [128, 512], BF16)
            oT = [sc.sb(f"a_oT{i}", [128, 8, TT], BF16, dma=True) for i in range(2)]
            ps_row, r_prow = sc.ps("a_prow", [128, 512], F32)
            ps_st, r_pst = sc.ps("a_pst", [128, 512], F32)
            ps_num, r_pnum = sc.ps("a_pnum", [128, 512], F32)
            ps_sm, r_psm = sc.ps("a_psm", [128, 512], F32)
            ps_cu = [sc.ps(f"a_pcu{i}", [128, 512], F32) for i in range(2)]
            ps_kt, r_pkt = sc.ps("a_pkt", [128, 1024], BF16)
            o_v = o_scr.rearrange("(ic p) t -> p ic t", p=128)
            for c in range(NCH):
                vt_t, vt_r = vt[c % 2]
                S.dma("sp", vt_t[:], utm[c * 128:(c + 1) * 128, :], reads=[r_utm], writes=[vt_r])
                if c % 2 == 0:
                    EB_t, EB_r = EB[(c // 2) % 2]
                    S.op("dve", lambda e: e.tensor_tensor(
                        out=LT[:], in0=triu[:, :].unsqueeze(1).to_broadcast([128, 4, 128]),
                        in1=lf[:, c:c + 2, :].rearrange("p c h -> p (c h)").unsqueeze(2).to_broadcast([128, 4, 128]),
                        op=ALU.mult), reads=[r_tri, r_lf], writes=[r_LT])
                    S.op("pe", lambda e: e.matmul(ps_row[:, :], lhsT=ones[:], rhs=LT[:].rearrange("p a t -> p (a t)"),
                                                  start=True, stop=True), reads=[r_ones, r_LT], writes=[r_prow])
                    S.op("act", lambda e: e.activation(out=EB_t[:].rearrange("p a t -> p (a t)"), in_=ps_row[:, :],
                                                       func=AF.Exp), reads=[r_prow], writes=[EB_r])
                    for pr in range(4):
                        idx = c * 2 + pr
                        S.op("act", lambda e: e.activation(out=DT[pr][0][:], in_=ps_row[:, pr * 128:(pr + 1) * 128],
                                                           func=AF.Exp,
                                                           bias=cs[:, idx // 2, (idx % 2):(idx % 2) + 1], scale=1.0),
                             reads=[r_prow, r_cs], writes=[DT[pr][1]])
                        S.op("pool", lambda e: e.tensor_tensor(out=DT[pr][0][:], in0=DT[pr][0][:], in1=triu[:],
                                                               op=ALU.mult), reads=[DT[pr][1], r_tri], writes=[DT[pr][1]])
                    S.op("dve", lambda e: e.tensor_tensor(out=ws[:], in0=ecs[:, c:c + 2, :].rearrange("p c h -> p (c h)"),
                                                          in1=EB_t[:, :, 127], op=ALU.mult),
                         reads=[r_ecs, EB_r], writes=[r_ws])
                    S.op("act", lambda e: e.copy(out=wsb[:], in_=ws[:]), reads=[r_ws], writes=[r_wsb])
                EB_t, EB_r = EB[(c // 2) % 2]
                oT_t, oT_r = oT[(c // 4) % 2]
                for h in range(2):
                    pr = (c % 2) * 2 + h
                    DT_t, DT_r = DT[pr]
                    qo = h * 2
                    ko = 4 + h * 2
                    tok = slice(c * 128, (c + 1) * 128)
                    for dc in range(2):
                        S.op("pe", lambda e: e.transpose(out=ps_kt[:, dc * 128:(dc + 1) * 128],
                                                         in_=qkb[:, ko + dc, tok], identity=ident[:]),
                             reads=[r_qkb, r_id], writes=[r_pkt] if dc == 0 else [], pwrites=[] if dc == 0 else [r_pkt],
                             inc=(dc == 1))
                    S.op("act", lambda e: e.copy(out=ktm[:], in_=ps_kt[:, 0:256]), reads=[r_pkt], writes=[r_ktm])
                    for dc in range(2):
                        S.op("pe", lambda e: e.matmul(ps_st[:, 0:128], lhsT=qkb[:, ko + dc, tok], rhs=qkb[:, qo + dc, tok],
                                                      start=(dc == 0), stop=(dc == 1)),
                             reads=[r_qkb], writes=[r_pst] if dc == 0 else [], pwrites=[] if dc == 0 else [r_pst],
                             inc=(dc == 1))
                    S.op("dve", lambda e: e.scalar_tensor_tensor(out=aT[:], in0=ps_st[:, 0:128], scalar=1.0 / 16.0,
                                                                 in1=DT_t[:], op0=ALU.mult, op1=ALU.mult),
                         reads=[r_pst, DT_r], writes=[r_aT])
                    S.op("dve", lambda e: e.scalar_tensor_tensor(
                        out=qp[:], in0=qkb[:, qo:qo + 2, tok], scalar=1.0 / 16.0,
                        in1=EB_t[:, pr, :].unsqueeze(1).to_broadcast([128, 2, 128]), op0=ALU.mult, op1=ALU.mult),
                        reads=[r_qkb, EB_r], writes=[r_qp])
                    vs = vt_t[:, h * 512:(h + 1) * 512]
                    nmm = 1 if c == 0 else 3
                    S.op("pe", lambda e: e.matmul(ps_num[:, :], lhsT=aT[:], rhs=vs, start=True, stop=(nmm == 1)),
                         reads=[r_aT, vt_r], writes=[r_pnum], inc=(nmm == 1))
                    if c > 0:
                        for dc in range(2):
                            S.op("pe", lambda e: e.matmul(ps_num[:, :], lhsT=qp[:, dc, :], rhs=Cb[:, h, dc, :],
                                                          start=False, stop=(dc == 1)),
                                 reads=[r_qp, r_Cb], pwrites=[r_pnum], inc=(dc == 1))
                    S.op("pe", lambda e: e.matmul(ps_sm[:, 0:1], lhsT=aT[:], rhs=onesb[:], start=True, stop=(nmm == 1)),
                         reads=[r_aT, r_onesb], writes=[r_psm], inc=(nmm == 1))
                    if c > 0:
                        for dc in range(2):
                            S.op("pe", lambda e: e.matmul(ps_sm[:, 0:1], lhsT=qp[:, dc, :], rhs=nb[:, h, dc:dc + 1],
                                                          start=False, stop=(dc == 1)),
                                 reads=[r_qp, r_nb], pwrites=[r_psm], inc=(dc == 1))
                    S.op("act", lambda e: e.activation(out=den[:, 0:1], in_=ps_sm[:, 0:1], func=AF.Abs),
                         reads=[r_psm], writes=[r_den])
                    S.op("dve", lambda e: e.tensor_scalar(out=den[:, 0:1], in0=den[:, 0:1], scalar1=1.0, scalar2=None,
                                                          op0=ALU.max), reads=[r_den], writes=[r_den])
                    S.op("dve", lambda e: e.reciprocal(out=den[:, 1:2], in_=den[:, 0:1]), reads=[r_den], writes=[r_den])
                    S.op("dve", lambda e: e.tensor_scalar(out=hc[:], in0=ps_num[:, :], scalar1=den[:, 1:2], scalar2=None,
                                                          op0=ALU.mult), reads=[r_pnum, r_den], writes=[r_hc])
                    S.op("pool", lambda e: e.tensor_scalar(out=vp[:], in0=vs, scalar1=ws[:, pr:pr + 1], scalar2=None,
                                                           op0=ALU.mult), reads=[vt_r, r_ws], writes=[r_vp])
                    for dc in range(2):
                        S.op("pe", lambda e: e.matmul(ps_cu[dc][0][:, :], lhsT=ktm[:, dc * 128:(dc + 1) * 128], rhs=vp[:],
                                                      start=True, stop=True),
                             reads=[r_ktm, r_vp], writes=[ps_cu[dc][1]])
                    for dc in range(2):
                        S.op("pe", lambda e: e.matmul(ps_sm[:, 2 + dc:3 + dc], lhsT=ktm[:, dc * 128:(dc + 1) * 128],
                                                      rhs=wsb[:, pr:pr + 1], start=True, stop=True),
                             reads=[r_ktm, r_wsb], pwrites=[r_psm])
                    dec = EB_t[:, pr, 127:128]
                    for dc in range(2):
                        S.op("dve", lambda e: e.scalar_tensor_tensor(out=Cst[:, h, dc, :], in0=Cst[:, h, dc, :], scalar=dec,
                                                                     in1=ps_cu[dc][0][:, :], op0=ALU.mult, op1=ALU.add),
                             reads=[r_C, EB_r, ps_cu[dc][1]], writes=[r_C])
                    S.op("act", lambda e: e.copy(out=Cb[:, h, :, :], in_=Cst[:, h, :, :]), reads=[r_C], writes=[r_Cb])
                    S.op("dve", lambda e: e.scalar_tensor_tensor(out=nst[:, h, :], in0=nst[:, h, :], scalar=dec,
                                                                 in1=ps_sm[:, 2:4], op0=ALU.mult, op1=ALU.add),
                         reads=[r_n, EB_r, r_psm], writes=[r_n])
                    S.op("act", lambda e: e.copy(out=nb[:, h, :], in_=nst[:, h, :]), reads=[r_n], writes=[r_nb])
                    S.op("act", lambda e: e.activation(out=junk[:], in_=hc[:], func=AF.Square, accum_out=den[:, 2:3]),
                         reads=[r_hc], writes=[r_junk, r_den])
                    S.op("act", lambda e: e.activation(out=den[:, 3:4], in_=den[:, 2:3], func=AF.Sqrt, scale=1.0 / 512,
                                                       bias=EPS), reads=[r_den], writes=[r_den])
                    S.op("dve", lambda e: e.reciprocal(out=den[:, 2:3], in_=den[:, 3:4]), reads=[r_den], writes=[r_den])
                    S.op("dve", lambda e: e.scalar_tensor_tensor(out=t1[:], in0=hc[:], scalar=den[:, 2:3],
                                                                 in1=onw[:, h * 512:(h + 1) * 512], op0=ALU.mult,
                                                                 op1=ALU.mult), reads=[r_hc, r_den, r_onw], writes=[r_t1])
                    S.op("act", lambda e: e.activation(out=sg[:], in_=vt_t[:, 1024 + h * 512:1024 + (h + 1) * 512],
                                                       func=AF.Sigmoid), reads=[vt_r], writes=[r_sg])
                    S.op("act", lambda e: e.activation(out=sz[:], in_=vt_t[:, 2048 + h * 512:2048 + (h + 1) * 512],
                                                       func=AF.Silu), reads=[vt_r], writes=[r_sz])
                    S.op("pool", lambda e: e.tensor_tensor(out=sg[:], in0=sg[:], in1=sz[:], op=ALU.mult),
                         reads=[r_sg, r_sz], writes=[r_sg])
                    S.op("pool", lambda e: e.tensor_tensor(out=yb[:], in0=t1[:], in1=sg[:], op=ALU.mult),
                         reads=[r_t1, r_sg], writes=[r_yb])
                    for ic in range(4):
                        S.op("pe", lambda e: e.transpose(out=ps_kt[:, 512 + ic * 128:512 + (ic + 1) * 128],
                                                         in_=yb[:, ic * 128:(ic + 1) * 128], identity=ident[:]),
                             reads=[r_yb, r_id], pwrites=[r_pkt], inc=(ic == 3))
                    q4 = c % 4
                    first = (q4 == 0 and h == 0)
                    S.op("act", lambda e: e.copy(out=oT_t[:, h * 4:(h + 1) * 4, q4 * 128:(q4 + 1) * 128],
                                                 in_=ps_kt[:, 512:1024].rearrange("p (i t) -> p i t", i=4)),
                         reads=[r_pkt], writes=[oT_r] if first else [], pwrites=[] if first else [oT_r])
                if c % 4 == 3:
                    T = c // 4
                    S.dma("sp", o_v[:, :, T * TT:(T + 1) * TT], oT_t[:], reads=[oT_r], pwrites=[r_o])


B_FORCE = 1e4
B_NEG = -1e30


def _alibi(n):
    return 2.0 ** (-8.0 * np.arange(1, n + 1) / n)


def mixer_nsa(kb, utm, r_utm, kv0T, r_kv0, ug, r_ug, prm, o_scr, r_o):
    S = kb.S
    scale = 128.0 ** -0.5
    NQ = SEQ // 128
    with kb.scope() as sc0:
        kselT, r_kselT = sc0.sb("b_kselT", [128, SEQ], BF16)
        kwinT, r_kwinT = sc0.sb("b_kwinT", [128, SEQ], BF16)
        vsel, r_vsel = sc0.sb("b_vsel", [128, NQ, 129], BF16)
        vwin, r_vwin = sc0.sb("b_vwin", [128, NQ, 129], BF16)
        kcmpT, r_kcmpT = sc0.sb("b_kcmpT", [128, 256], BF16)
        vcmp, r_vcmp = sc0.sb("b_vcmp", [128, 2, 129], BF16)
        ovl, r_ovl = sc0.sb("b_ovl", [128, 2, 64], BF16, dma=True)
        S.op("pool", lambda e: e.memset(vsel[:], 1.0), writes=[r_vsel])
        S.op("pool", lambda e: e.memset(vwin[:], 1.0), writes=[r_vwin])
        S.op("pool", lambda e: e.memset(kcmpT[:], 0.0), writes=[r_kcmpT])
        S.op("pool", lambda e: e.memset(vcmp[:], 0.0), writes=[r_vcmp])
        S.op("pool", lambda e: e.memset(vcmp[:, :, 128:129], 1.0), writes=[r_vcmp])
        S.dma("pool", ovl[:], prm["ovl"], writes=[r_ovl])
        with kb.scope() as sc:
            ident, r_id, _, _ = make_ident(S, sc)
            knw, r_knw = sc.sb("b_knw", [128, 3, 128], F32, dma=True)
            S.dma("sp", knw[:].rearrange("p a d -> p (a d)"), prm["knw"].partition_broadcast(128), writes=[r_knw])
            kvt = [sc.sb(f"b_kvt{i}", [128, 512], BF16, dma=True) for i in range(2)]
            sq, r_sq = sc.sb("b_sq", [128, 2, 128], F32)
            ssq, r_ssq = sc.sb("b_ssq", [128, 2], F32)
            rstd, r_rstd = sc.sb("b_rstd", [128, 2], F32)
            kn, r_kn = sc.sb("b_kn", [128, 2, 128], BF16)
            pt, r_pt = sc.ps("b_pt", [128, 1024], BF16)
            for kt in range(NQ):
                kv_t, kv_r = kvt[kt % 2]
                S.dma("sp", kv_t[:], utm[kt * 128:(kt + 1) * 128, 2304:2816], reads=[r_utm], writes=[kv_r])
                kview = kv_t[:].rearrange("p (a b d) -> p a b d", a=2, b=2)
                S.op("dve", lambda e: e.tensor_tensor(out=sq[:], in0=kview[:, :, 0, :], in1=kview[:, :, 0, :], op=ALU.mult),
                     reads=[kv_r], writes=[r_sq])
                S.op("dve", lambda e: e.tensor_reduce(out=ssq[:], in_=sq[:], axis=AX.X, op=ALU.add),
                     reads=[r_sq], writes=[r_ssq])
                S.op("act", lambda e: e.activation(out=rstd[:], in_=ssq[:], func=AF.Sqrt, scale=1.0 / 128, bias=EPS),
                     reads=[r_ssq], writes=[r_rstd])
                S.op("dve", lambda e: e.reciprocal(out=rstd[:], in_=rstd[:]), reads=[r_rstd], writes=[r_rstd])
                S.op("dve", lambda e: e.tensor_tensor(out=sq[:], in0=kview[:, :, 0, :],
                                                      in1=rstd[:, :].unsqueeze(2).to_broadcast([128, 2, 128]), op=ALU.mult),
                     reads=[kv_r, r_rstd], writes=[r_sq])
                S.op("pool", lambda e: e.tensor_tensor(out=kn[:], in0=sq[:], in1=knw[:, 1:3, :], op=ALU.mult),
                     reads=[r_sq, r_knw], writes=[r_kn])
                for a in range(2):
                    S.op("pe", lambda e: e.transpose(out=pt[:, a * 128:(a + 1) * 128], in_=kn[:, a, :], identity=ident[:]),
                         reads=[r_kn, r_id], writes=[r_pt] if a == 0 else [], pwrites=[] if a == 0 else [r_pt], inc=(a == 1))
                S.op("dve", lambda e: e.tensor_copy(out=kselT[:, kt * 128:(kt + 1) * 128], in_=pt[:, 0:128]),
                     reads=[r_pt], pwrites=[r_kselT])
                S.op("act", lambda e: e.copy(out=kwinT[:, kt * 128:(kt + 1) * 128], in_=pt[:, 128:256]),
                     reads=[r_pt], pwrites=[r_kwinT])
                S.op("pool", lambda e: e.tensor_copy(out=vsel[:, kt, 0:128], in_=kview[:, 0, 1, :]),
                     reads=[kv_r], pwrites=[r_vsel])
                S.op("pool", lambda e: e.tensor_copy(out=vwin[:, kt, 0:128], in_=kview[:, 1, 1, :]),
                     reads=[kv_r], pwrites=[r_vwin])
            k0, r_k0 = sc.sb("b_k0", [128, 2, SEQ], BF16, dma=True)
            S.dma("sp", k0[:], kv0T.rearrange("(a p) t -> p a t", p=128), reads=[r_kv0], writes=[r_k0])
            peT, r_peT = sc.sb("b_peT", [128, 2, 32], F32, dma=True)
            S.dma("sp", peT[:], prm["peT"], writes=[r_peT])
            wkv, r_wkv = sc.sb("b_wkv", [128, 2, 32, 128], BF16, dma=True)
            S.dma("pool", wkv[:, 0], prm["wk"], writes=[r_wkv])
            S.dma("pool", wkv[:, 1], prm["wv"], pwrites=[r_wkv])
            kg, r_kg = sc.sb("b_kg", [128, 2, 32, 256], BF16)
            S.op("pool", lambda e: e.memset(kg[:], 0.0), writes=[r_kg])
            for a in range(2):
                for l in range(32):
                    eng = "dve" if (l % 2 == 0) else "pool"
                    S.op(eng, lambda e: e.tensor_scalar(out=kg[:, a, l, 0:255], in0=k0[:, a, l:l + 16 * 254 + 1:16],
                                                        scalar1=peT[:, a, l:l + 1], scalar2=None, op0=ALU.add),
                         reads=[r_k0, r_peT], pwrites=[r_kg])
            pc = [sc.ps(f"b_pc{i}", [128, 512], F32) for i in range(2)]
            for ct in range(2):
                M = 128 if ct == 0 else 127
                for a in range(2):
                    pc_t, pc_r = pc[a]
                    for l in range(32):
                        S.op("pe", lambda e: e.matmul(pc_t[:M, 0:128], lhsT=kg[:, a, l, ct * 128:ct * 128 + M],
                                                      rhs=wkv[:, a, l, :], start=(l == 0), stop=(l == 31)),
                             reads=[r_kg, r_wkv], writes=[pc_r] if l == 0 else [], pwrites=[] if l == 0 else [pc_r],
                             inc=(l == 31))
                pk, pk_r = pc[0]
                S.op("act", lambda e: e.activation(out=sq[:M, 0, :], in_=pk[:M, 0:128], func=AF.Square,
                                                   accum_out=ssq[:M, 0:1]), reads=[pk_r], writes=[r_sq, r_ssq])
                S.op("act", lambda e: e.activation(out=rstd[:M, 0:1], in_=ssq[:M, 0:1], func=AF.Sqrt, scale=1.0 / 128,
                                                   bias=EPS), reads=[r_ssq], writes=[r_rstd])
                S.op("dve", lambda e: e.reciprocal(out=rstd[:M, 0:1], in_=rstd[:M, 0:1]), reads=[r_rstd], writes=[r_rstd])
                S.op("dve", lambda e: e.scalar_tensor_tensor(out=kn[:M, 0, :], in0=pk[:M, 0:128], scalar=rstd[:M, 0:1],
                                                             in1=knw[:M, 0, :], op0=ALU.mult, op1=ALU.mult),
                     reads=[pk_r, r_rstd, r_knw], writes=[r_kn])
                S.op("pe", lambda e: e.transpose(out=pt[:, 0:M], in_=kn[:M, 0, :], identity=ident[:M, :M]),
                     reads=[r_kn, r_id], writes=[r_pt])
                S.op("dve", lambda e: e.tensor_copy(out=kcmpT[:, ct * 128:ct * 128 + M], in_=pt[:, 0:M]),
                     reads=[r_pt], pwrites=[r_kcmpT])
                S.op("act", lambda e: e.copy(out=vcmp[:M, ct, 0:128], in_=pc[1][0][:M, 0:128]),
                     reads=[pc[1][1]], pwrites=[r_vcmp])
        with kb.scope() as sc:
            ident, r_id, _, _ = make_ident(S, sc)
            qnw, r_qnw = sc.sb("b_qnw", [128, 128], F32, dma=True)
            S.dma("sp", qnw[:], prm["qnw"].partition_broadcast(128), writes=[r_qnw])
            BQ, r_BQ = sc.sb("b_BQ", [4, 1024], BF16, dma=True)
            AK, r_AK = sc.sb("b_AK", [4, 2, 128], BF16, dma=True)
            Eall, r_E = sc.sb("b_E", [64, SEQ], BF16, dma=True)
            Wadd, r_Wadd = sc.sb("b_Wadd", [128, 128], F32, dma=True)
            Wkeep, r_Wkeep = sc.sb("b_Wkeep", [128, 128], F32, dma=True)
            btab, r_btab = sc.sb("b_btab", [128, 8, 64], F32, dma=True)
            S.dma("sp", btab[:].rearrange("p h m -> p (h m)"), prm["btab"].partition_broadcast(128), writes=[r_btab])
            S.dma("pool", BQ[:], prm["BQ"], writes=[r_BQ])
            S.dma("pool", AK[:], prm["AK"], writes=[r_AK])
            S.dma("pool", Eall[:], prm["Eall"], writes=[r_E])
            S.dma("sp", Wadd[:], prm["Wadd"], writes=[r_Wadd])
            S.dma("sp", Wkeep[:], prm["Wkeep"], writes=[r_Wkeep])
            qt = [sc.sb(f"b_qt{i}", [128, 2048], BF16, dma=True) for i in range(2)]
            gt = [sc.sb(f"b_gt{i}", [128, 24], F32, dma=True) for i in range(2)]
            sq, r_sq = sc.sb("b2_sq", [128, 1024], F32)
            ssq, r_ssq = sc.sb("b2_ssq", [128, 8], F32)
            rstd, r_rstd = sc.sb("b2_rstd", [128, 8], F32)
            qn, r_qn = sc.sb("b2_qn", [128, 1024], BF16)
            qT, r_qT = sc.sb("b2_qT", [128, 1024], BF16)
            PT = [sc.sb(f"b2_PT{i}", [128, 1024], BF16) for i in range(2)]
            msk, r_msk = sc.sb("b2_msk", [128, 128], BF16)
            oacc, r_oacc = sc.sb("b2_oacc", [128, 8, 128], F32)
            imp, r_imp = sc.sb("b2_imp", [128, 64], F32)
            imp2, r_imp2 = sc.sb("b2_imp2", [128, 64], F32)
            imp3, r_imp3 = sc.sb("b2_imp3", [128, 64], F32)
            m8, r_m8 = sc.sb("b2_m8", [128, 16], F32)
            selb, r_selb = sc.sb("b2_selb", [128, 64], BF16)
            selT, r_selT = sc.sb("b2_selT", [64, 128], BF16)
            den, r_den = sc.sb("b2_den", [128, 8], F32)
            rg, r_rg = sc.sb("b2_rg", [128, 8], F32)
            zs, r_zs = sc.sb("b2_zs", [128, 1024], F32)
            yb, r_yb = sc.sb("b2_yb", [128, 1024], BF16)
            oT = [sc.sb(f"b2_oT{i}", [128, 8, TT], BF16, dma=True) for i in range(2)]
            pst = [sc.ps(f"b2_pst{i}", [128, 512], F32) for i in range(2)]
            acc = [sc.ps(f"b2_acc{i}", [128, 512], F32) for i in range(3)]
            pmk, r_pmk = sc.ps("b2_pmk", [128, 512], F32)
            ptr, r_ptr = sc.ps("b2_ptr", [128, 1024], BF16)
            pti = 0

            def acc_of(h):
                return acc[h // 3][0][:, (h % 3) * 129:(h % 3) * 129 + 129], acc[h // 3][1]

            def attend(tiles, br, first_branch, gt_t, gt_r, want_imp):
                nonlocal pti
                nt = len(tiles)
                for ti, tl in enumerate(tiles):
                    PT_t, PT_r = PT[pti % 2]
                    pti += 1
                    for half in range(2):
                        ps_t, ps_r = pst[half]
                        S.op("pe", lambda e: e.matmul(ps_t[:, :], lhsT=tl["kT"], rhs=qT[:, half * 512:(half + 1) * 512],
                                                      start=True, stop=False),
                             reads=[tl["r_k"], r_qT], writes=[ps_r], inc=False)
                        S.op("pe", lambda e: e.matmul(ps_t[:, :], lhsT=AK[:, tl["ak"], :],
                                                      rhs=BQ[:, half * 512:(half + 1) * 512], start=False, stop=True),
                             reads=[r_AK, r_BQ], pwrites=[ps_r])
                    for h in range(8):
                        ps_t, ps_r = pst[h // 4]
                        S.op("act", lambda e: e.activation(out=PT_t[:, h * 128:(h + 1) * 128],
                                                           in_=ps_t[:, (h % 4) * 128:(h % 4 + 1) * 128], func=AF.Exp,
                                                           scale=scale, bias=btab[:, h, tl["bidx"]:tl["bidx"] + 1]),
                             reads=[ps_r, r_btab], writes=[PT_r] if h == 0 else [], pwrites=[] if h == 0 else [PT_r])
                    if tl["aff"] is not None:
                        pat, cm, base = tl["aff"]
                        S.op("pool", lambda e: e.affine_select(out=PT_t[:].rearrange("p (h q) -> p h q", h=8),
                                                               in_=PT_t[:].rearrange("p (h q) -> p h q", h=8),
                                                               pattern=[[0, 8], [pat, 128]], compare_op=ALU.is_ge, fill=0.0,
                                                               base=base, channel_multiplier=cm),
                             reads=[PT_r], writes=[PT_r])
                    if tl["selmask_kt"] is not None:
                        kt = tl["selmask_kt"]
                        S.op("pe", lambda e: e.matmul(pmk[:, 0:128], lhsT=Eall[:, kt * 128:(kt + 1) * 128], rhs=selT[:, :],
                                                      start=True, stop=True), reads=[r_E, r_selT], writes=[r_pmk])
                        S.op("act", lambda e: e.copy(out=msk[:], in_=pmk[:, 0:128]), reads=[r_pmk], writes=[r_msk])
                        S.op("dve", lambda e: e.tensor_tensor(out=PT_t[:].rearrange("p (h q) -> p h q", h=8),
                                                              in0=PT_t[:].rearrange("p (h q) -> p h q", h=8),
                                                              in1=msk[:, :].unsqueeze(1).to_broadcast([128, 8, 128]),
                                                              op=ALU.mult), reads=[PT_r, r_msk], writes=[PT_r])
                    for h in range(8):
                        a_ap, a_r = acc_of(h)
                        first_in_bank = (ti == 0 and h % 3 == 0)
                        last = (ti == nt - 1)
                        S.op("pe", lambda e: e.matmul(a_ap, lhsT=PT_t[:, h * 128:(h + 1) * 128], rhs=tl["vaug"],
                                                      start=first_in_bank, stop=last, skip_group_check=True),
                             reads=[PT_r, tl["r_v"]], writes=[a_r] if first_in_bank else [],
                             pwrites=[] if first_in_bank else [a_r], inc=(last and (h % 3 == 2 or h == 7)))
                    if want_imp:
                        for h in range(8):
                            S.op("pe", lambda e: e.matmul(pmk[:, h * 64:(h + 1) * 64], lhsT=PT_t[:, h * 128:(h + 1) * 128],
                                                          rhs=tl["ovl"], start=(ti == 0 and h == 0), stop=(ti == nt - 1),
                                                          skip_group_check=True),
                                 reads=[PT_r, r_ovl], writes=[r_pmk] if (ti == 0 and h == 0) else [],
                                 pwrites=[] if (ti == 0 and h == 0) else [r_pmk], inc=(ti == nt - 1 and h == 7))
                for bk in range(3):
                    nh = 3 if bk < 2 else 2
                    S.op("dve", lambda e: e.tensor_scalar(
                        out=den[:, bk * 3:bk * 3 + nh],
                        in0=acc[bk][0][:, 0:nh * 129].rearrange("p (h c) -> p h c", c=129)[:, :, 128],
                        scalar1=1e-30, scalar2=None, op0=ALU.max), reads=[acc[bk][1]],
                        writes=[r_den] if bk == 0 else [], pwrites=[] if bk == 0 else [r_den])
                S.op("dve", lambda e: e.reciprocal(out=den[:], in_=den[:]), reads=[r_den], writes=[r_den])
                S.op("dve", lambda e: e.tensor_tensor(out=rg[:], in0=den[:],
                                                      in1=gt_t[:].rearrange("p (h b) -> p h b", b=3)[:, :, br],
                                                      op=ALU.mult), reads=[r_den, gt_r], writes=[r_rg])
                for h in range(8):
                    a_ap, a_r = acc_of(h)
                    if first_branch:
                        S.op("dve", lambda e: e.tensor_scalar(out=oacc[:, h, :], in0=a_ap[:, 0:128], scalar1=rg[:, h:h + 1],
                                                              scalar2=None, op0=ALU.mult),
                             reads=[a_r, r_rg], writes=[r_oacc] if h == 0 else [], pwrites=[] if h == 0 else [r_oacc])
                    else:
                        S.op("dve", lambda e: e.scalar_tensor_tensor(out=oacc[:, h, :], in0=a_ap[:, 0:128],
                                                                     scalar=rg[:, h:h + 1], in1=oacc[:, h, :],
                                                                     op0=ALU.mult, op1=ALU.add),
                             reads=[a_r, r_rg, r_oacc], writes=[r_oacc])
                    if want_imp:
                        if h == 0:
                            S.op("dve", lambda e: e.tensor_scalar(out=imp[:], in0=pmk[:, 0:64], scalar1=den[:, 0:1],
                                                                  scalar2=None, op0=ALU.mult),
                                 reads=[r_pmk, r_den], writes=[r_imp])
                        else:
                            S.op("dve", lambda e: e.scalar_tensor_tensor(out=imp[:], in0=pmk[:, h * 64:(h + 1) * 64],
                                                                         scalar=den[:, h:h + 1], in1=imp[:],
                                                                         op0=ALU.mult, op1=ALU.add),
                                 reads=[r_pmk, r_den, r_imp], writes=[r_imp])

            for i in range(NQ):
                t0 = i * 128
                q_t, q_r = qt[i % 2]
                gt_t, gt_r = gt[i % 2]
                oT_t, oT_r = oT[(i // 4) % 2]
                S.dma("sp", q_t[:], utm[t0:t0 + 128, 0:2048], reads=[r_utm], writes=[q_r])
                S.dma("sp", gt_t[:], ug[t0:t0 + 128, :], reads=[r_ug], writes=[gt_r])
                S.op("act", lambda e: e.activation(out=gt_t[:], in_=gt_t[:], func=AF.Sigmoid), reads=[gt_r], writes=[gt_r])
                S.op("dve", lambda e: e.tensor_tensor(out=sq[:], in0=q_t[:, 0:1024], in1=q_t[:, 0:1024], op=ALU.mult),
                     reads=[q_r], writes=[r_sq])
                S.op("dve", lambda e: e.tensor_reduce(out=ssq[:], in_=sq[:].rearrange("p (h d) -> p h d", h=8), axis=AX.X,
                                                      op=ALU.add), reads=[r_sq], writes=[r_ssq])
                S.op("act", lambda e: e.activation(out=rstd[:], in_=ssq[:], func=AF.Sqrt, scale=1.0 / 128, bias=EPS),
                     reads=[r_ssq], writes=[r_rstd])
                S.op("dve", lambda e: e.reciprocal(out=rstd[:], in_=rstd[:]), reads=[r_rstd], writes=[r_rstd])
                S.op("dve", lambda e: e.tensor_tensor(out=sq[:].rearrange("p (h d) -> p h d", h=8),
                                                      in0=q_t[:, 0:1024].rearrange("p (h d) -> p h d", h=8),
                                                      in1=rstd[:, :].unsqueeze(2).to_broadcast([128, 8, 128]), op=ALU.mult),
                     reads=[q_r, r_rstd], writes=[r_sq])
                S.op("pool", lambda e: e.tensor_tensor(out=qn[:].rearrange("p (h d) -> p h d", h=8),
                                                       in0=sq[:].rearrange("p (h d) -> p h d", h=8),
                                                       in1=qnw[:, :].unsqueeze(1).to_broadcast([128, 8, 128]), op=ALU.mult),
                     reads=[r_sq, r_qnw], writes=[r_qn])
                for h in range(8):
                    S.op("pe", lambda e: e.transpose(out=ptr[:, h * 128:(h + 1) * 128], in_=qn[:, h * 128:(h + 1) * 128],
                                                     identity=ident[:]),
                         reads=[r_qn, r_id], writes=[r_ptr] if h == 0 else [], pwrites=[] if h == 0 else [r_ptr],
                         inc=(h == 7))
                S.op("dve", lambda e: e.tensor_copy(out=qT[:], in_=ptr[:, :]), reads=[r_ptr], writes=[r_qT])
                tiles = []
                for ct in range(2):
                    P0 = 2048 * ct + 31
                    if t0 + 127 < P0:
                        continue
                    tiles.append(dict(kT=kcmpT[:, ct * 128:(ct + 1) * 128], r_k=r_kcmpT, vaug=vcmp[:, ct, :], r_v=r_vcmp,
                                      ak=1, bidx=32 + i - 16 * ct, aff=(1, -16, t0 - P0), selmask_kt=None, ovl=ovl[:, ct, :]))
                attend(tiles, 0, True, gt_t, gt_r, True)
                c0 = 62 - 2 * i
                S.op("dve", lambda e: e.tensor_tensor(out=imp2[:], in0=imp[:], in1=Wkeep[:, c0:c0 + 64], op=ALU.mult),
                     reads=[r_imp, r_Wkeep], writes=[r_imp2])
                S.op("dve", lambda e: e.tensor_tensor(out=imp2[:], in0=imp2[:], in1=Wadd[:, c0:c0 + 64], op=ALU.add),
                     reads=[r_imp2, r_Wadd], writes=[r_imp2])
                if i >= 1:
                    S.op("dve", lambda e: e.tensor_scalar(out=imp2[:, 0:1], in0=imp2[:, 0:1], scalar1=B_FORCE, scalar2=None,
                                                          op0=ALU.add), reads=[r_imp2], writes=[r_imp2])
                S.op("dve", lambda e: e.max(out=m8[:, 0:8], in_=imp2[:]), reads=[r_imp2], writes=[r_m8])
                S.op("dve", lambda e: e.match_replace(out=imp3[:], in_to_replace=m8[:, 0:8], in_values=imp2[:],
                                                      imm_value=-3.0e38), reads=[r_m8, r_imp2], writes=[r_imp3])
                S.op("dve", lambda e: e.max(out=m8[:, 8:16], in_=imp3[:]), reads=[r_imp3], writes=[r_m8])
                S.op("dve", lambda e: e.tensor_scalar(out=imp3[:], in0=imp2[:], scalar1=m8[:, 15:16], scalar2=None,
                                                      op0=ALU.is_ge), reads=[r_imp2, r_m8], writes=[r_imp3])
                S.op("dve", lambda e: e.tensor_tensor(out=selb[:], in0=imp3[:], in1=Wkeep[:, c0:c0 + 64], op=ALU.mult),
                     reads=[r_imp3, r_Wkeep], writes=[r_selb])
                S.op("pe", lambda e: e.transpose(out=ptr[:64, 0:128], in_=selb[:, :], identity=ident[:]),
                     reads=[r_selb, r_id], writes=[r_ptr])
                S.op("act", lambda e: e.copy(out=selT[:], in_=ptr[:64, 0:128]), reads=[r_ptr], writes=[r_selT])
                tiles = []
                for kt in range(i + 1):
                    tiles.append(dict(kT=kselT[:, kt * 128:(kt + 1) * 128], r_k=r_kselT, vaug=vsel[:, kt, :], r_v=r_vsel,
                                      ak=0, bidx=i - kt, aff=((1, -1, 0) if kt == i else None), selmask_kt=kt,
                                      ovl=None))
                attend(tiles, 1, False, gt_t, gt_r, False)
                tiles = []
                for kt in range(max(0, i - 4), i + 1):
                    aff = None
                    if kt == i:
                        aff = (1, -1, 0)
                    elif kt == i - 4:
                        aff = (-1, 1, -1)
                    tiles.append(dict(kT=kwinT[:, kt * 128:(kt + 1) * 128], r_k=r_kwinT, vaug=vwin[:, kt, :], r_v=r_vwin,
                                      ak=0, bidx=i - kt, aff=aff, selmask_kt=None, ovl=None))
                attend(tiles, 2, False, gt_t, gt_r, False)
                S.op("act", lambda e: e.activation(out=zs[:], in_=q_t[:, 1024:2048], func=AF.Silu), reads=[q_r], writes=[r_zs])
                S.op("pool", lambda e: e.tensor_tensor(out=yb[:], in0=oacc[:].rearrange("p h d -> p (h d)"), in1=zs[:],
                                                       op=ALU.mult), reads=[r_oacc, r_zs], writes=[r_yb])
                for h in range(8):
                    S.op("pe", lambda e: e.transpose(out=ptr[:, h * 128:(h + 1) * 128], in_=yb[:, h * 128:(h + 1) * 128],
                                                     identity=ident[:]),
                         reads=[r_yb, r_id], writes=[r_ptr] if h == 0 else [], pwrites=[] if h == 0 else [r_ptr],
                         inc=(h == 7))
                q4 = i % 4
                S.op("act", lambda e: e.copy(out=oT_t[:, :, q4 * 128:(q4 + 1) * 128],
                                             in_=ptr[:, :].rearrange("p (h t) -> p h t", h=8)),
                     reads=[r_ptr], writes=[oT_r] if q4 == 0 else [], pwrites=[] if q4 == 0 else [oT_r])
                if q4 == 3:
                    T = i // 4
                    S.dma("sp", o_scr.rearrange("(h p) t -> p h t", p=128)[:, :, T * TT:(T + 1) * TT], oT_t[:],
                          reads=[oT_r], pwrites=[r_o])


def _dram_in(nc, name, shape, dt=F32):
    return nc.dram_tensor(name, list(shape), dt, kind="ExternalInput").ap()


_SCRATCH = {
    0: (("s_qkT", [1024, SEQ], F32), ("s_utm", [SEQ, 3072], BF16), ("s_gates", [SEQ, 4], F32), ("s_o", [1024, SEQ], BF16)),
    1: (("s_utm", [SEQ, 2816], BF16), ("s_kv0T", [256, SEQ], BF16), ("s_ug", [SEQ, 24], F32), ("s_o", [1024, SEQ], BF16)),
    2: (("s_xbT", [1344, SEQ], F32), ("s_zT", [1344, SEQ], BF16), ("s_o", [1344, SEQ], BF16)),
    3: (("s_u", [SEQ, 5120], BF16), ("s_nd", [3, SEQ, 516], F32), ("s_o", [512, SEQ], BF16)),
}


def alloc_scratch(nc, S, kind, tag=""):
    scr = {}
    for name, shape, dt in _SCRATCH[kind]:
        scr[name] = (nc.dram_tensor(f"{name}{tag}", shape, dt, kind="Internal").ap(), S.res(f"{name}{tag}"))
    return scr


def emit_layer(kb, kind, x, nw, win, y, r_y, scr, accum=None):
    if kind == 2:
        (xbT, r_xb), (zT, r_z), (o_scr, r_o) = scr["s_xbT"], scr["s_zT"], scr["s_o"]
        groups = [
            dict(layout="fm", w=win["w_xb"], W=112, nchunk=12, dst=xbT, rdst=r_xb),
            dict(layout="fm", w=win["w_z"], W=112, nchunk=12, dst=zT, rdst=r_z),
        ]
        phase1(kb, x, nw, groups)
        mixer_rglru(kb, xbT, r_xb, zT, r_z, win, o_scr, r_o)
        phase3(kb, o_scr, r_o, 112, 12, win["w_out"], y, r_y, accum)
    elif kind == 0:
        (qkT, r_qk), (utm, r_utm), (gts, r_g), (o_scr, r_o) = scr["s_qkT"], scr["s_utm"], scr["s_gates"], scr["s_o"]
        groups = [
            dict(layout="fm", w=win["w_qk"], W=128, nchunk=8, dst=qkT, rdst=r_qk),
            dict(layout="tm", w=win["w_vgz"], W=256, nchunk=12, dst=utm, rdst=r_utm),
            dict(layout="tm", w=win["w_g"], W=4, nchunk=1, dst=gts, rdst=r_g),
        ]
        phase1(kb, x, nw, groups)
        mixer_mlstm(kb, qkT, r_qk, utm, r_utm, gts, r_g, win, o_scr, r_o)
        phase3(kb, o_scr, r_o, 128, 8, win["w_out"], y, r_y, accum)
    elif kind == 1:
        (utm, r_utm), (kv0T, r_kv0), (ug, r_ug), (o_scr, r_o) = scr["s_utm"], scr["s_kv0T"], scr["s_ug"], scr["s_o"]
        groups = [
            dict(layout="tm", w=win["w_tm"], W=256, nchunk=11, dst=utm, rdst=r_utm),
            dict(layout="fm", w=win["w_kv0"], W=128, nchunk=2, dst=kv0T, rdst=r_kv0),
            dict(layout="tm", w=win["w_g"], W=24, nchunk=1, dst=ug, rdst=r_ug),
        ]
        phase1(kb, x, nw, groups)
        mixer_nsa(kb, utm, r_utm, kv0T, r_kv0, ug, r_ug, win, o_scr, r_o)
        phase3(kb, o_scr, r_o, 128, 8, win["w_out"], y, r_y, accum)
    elif kind == 3:
        (u, r_u), (nd, r_nd), (o_scr, r_o) = scr["s_u"], scr["s_nd"], scr["s_o"]
        groups = [dict(layout="tm", w=win["w_u"], W=256, nchunk=20, dst=u, rdst=r_u)]
        phase1(kb, x, nw, groups)
        mixer_dilated(kb, u, r_u, win, o_scr, r_o, nd, r_nd)
        phase3(kb, o_scr, r_o, 128, 4, win["w_out"], y, r_y, accum)
    else:
        raise NotImplementedError


def build_layer(kind, shapes):
    nc = bass.Bass("TRN2", target_bir_lowering=False)
    x = _dram_in(nc, "x", [SEQ, D])
    nw = _dram_in(nc, "nw", [1, D])
    win = {k: _dram_in(nc, k, v) for k, v in shapes.items()}
    y = nc.dram_tensor("y", [SEQ, D], F32, kind="ExternalOutput").ap()
    with ExitStack() as st:
        kb = KB(nc, st)
        S = kb.S
        emit_layer(kb, kind, x, nw, win, y, S.res("y_out"), alloc_scratch(nc, S, kind))
        S.finish()
    return nc


def build_fused(shapes, layers=(0, 1, 2, 3), groups=(0, 1, 2, 3)):
    nc = bass.Bass("TRN2", target_bir_lowering=False)
    x = _dram_in(nc, "x", [SEQ, D])
    nwall = _dram_in(nc, "nw", [4, D])
    out = nc.dram_tensor("out", [SEQ, D], F32, kind="ExternalOutput").ap()
    xs = [nc.dram_tensor(f"s_x{i}", [SEQ, D], F32, kind="Internal").ap() for i in range(2)]
    with ExitStack() as st:
        kb = KB(nc, st)
        S = kb.S
        for li, L in enumerate(layers):
            src_ap = x if li == 0 else xs[(li - 1) % 2]
            dst_ap = out if li == len(layers) - 1 else xs[li % 2]
            r_dst = S.res(f"xres{L}")
            scr = alloc_scratch(nc, S, L, tag=f"_L{L}")
            for gi, g in enumerate(groups):
                win = {k: _dram_in(nc, f"L{L}g{g}_{k}", v) for k, v in shapes[(L, g)].items()}
                emit_layer(kb, L, src_ap, nwall[L:L + 1, :], win, dst_ap, r_dst, scr, accum=(src_ap if gi == 0 else dst_ap))
        S.finish()
        print("fused program: nins", S.nins, "nwaits", S.nwaits, flush=True)
    return nc


def build_reduce():
    nc = bass.Bass("TRN2", target_bir_lowering=False)
    R = 1024
    x = _dram_in(nc, "x", [R, D])
    ys = [_dram_in(nc, f"y{i}", [R, D]) for i in range(4)]
    out = nc.dram_tensor("out", [R, D], F32, kind="ExternalOutput").ap()
    with ExitStack() as st:
        kb = KB(nc, st)
        S = kb.S
        r_out = S.res("out")
        with kb.scope() as sc:
            bufs = [[sc.sb(f"r_b{i}_{j}", [128, D], F32, dma=True) for j in range(5)] for i in range(2)]
            for t in range(R // 128):
                bb = bufs[t % 2]
                for j, src in enumerate([x] + ys):
                    S.dma("sp" if j % 2 == 0 else "act", bb[j][0][:], src[t * 128:(t + 1) * 128, :], writes=[bb[j][1]])
                a_t, a_r = bb[0]
                for j in range(1, 5):
                    eng = "dve" if j % 2 == 1 else "pool"
                    S.op(eng, lambda e: e.tensor_tensor(out=a_t[:], in0=a_t[:], in1=bb[j][0][:], op=ALU.add),
                         reads=[a_r, bb[j][1]], writes=[a_r])
                S.dma("sp", out[t * 128:(t + 1) * 128, :], a_t[:], reads=[a_r], pwrites=[r_out])
        S.finish()
    return nc


def _chunk_w(wcols, W):
    n = wcols.shape[1] // W
    a = wcols.reshape(32, 128, n, W).transpose(2, 1, 0, 3)
    return np.ascontiguousarray(a)


def _layer2_inputs(inp, g):
    CW = 5376
    lo, hi = g * 1344, (g + 1) * 1344
    w_in = inp["c_w_in"][0]
    d = {}
    d["w_xb"] = _chunk_w(w_in[:, lo:hi], 112)
    d["w_z"] = _chunk_w(w_in[:, CW + lo:CW + hi], 112)
    d["cw"] = np.ascontiguousarray(inp["c_conv_w"][0][:, lo:hi].reshape(4, 12, 112).transpose(2, 1, 0))
    vec = np.stack([inp["c_conv_b"][0][lo:hi], inp["c_b_a"][0][lo:hi], inp["c_b_x"][0][lo:hi],
                    inp["c_lambda"][0][lo:hi]], axis=0)
    d["vec"] = np.ascontiguousarray(vec.reshape(4, 12, 112).transpose(2, 0, 1))
    for nm, key in (("wa", "c_w_a"), ("wx", "c_w_x")):
        wblk = inp[key][0][4 * g:4 * g + 4]
        d[nm] = np.ascontiguousarray(wblk.reshape(4, 3, 112, 336).transpose(2, 0, 1, 3))
    d["w_out"] = np.ascontiguousarray(inp["c_w_out"][0][lo:hi, :].reshape(12, 112, D).transpose(1, 0, 2))
    return d


def _layer0_inputs(inp, g):
    w_in = inp["a_w_in"][0]
    hs = (2 * g, 2 * g + 1)
    d = {}
    qk_cols = [w_in[:, h * 256:(h + 1) * 256] for h in hs] + [w_in[:, 2048 + h * 256:2048 + (h + 1) * 256] for h in hs]
    d["w_qk"] = _chunk_w(np.concatenate(qk_cols, axis=1), 128)
    vgz = []
    for base in (4096, 8192, 12288):
        for h in hs:
            vgz.append(w_in[:, base + h * 512:base + (h + 1) * 512])
    d["w_vgz"] = _chunk_w(np.concatenate(vgz, axis=1), 256)
    gcols = [w_in[:, 16384 + h:16385 + h] for h in hs] + [w_in[:, 16392 + h:16393 + h] for h in hs]
    d["w_g"] = _chunk_w(np.concatenate(gcols, axis=1), 4)
    idx = np.concatenate([np.arange(h * 256, (h + 1) * 256) for h in hs] +
                         [2048 + np.arange(h * 256, (h + 1) * 256) for h in hs])
    d["cw"] = np.ascontiguousarray(inp["a_conv_w"][0][:, idx].reshape(4, 8, 128).transpose(2, 1, 0))
    d["cb"] = np.ascontiguousarray(inp["a_conv_b"][0][idx].reshape(8, 128).T)
    gbv = inp["a_gate_b"][0]
    d["gb"] = np.ascontiguousarray(np.array([[gbv[0, hs[0]], gbv[0, hs[1]], gbv[1, hs[0]], gbv[1, hs[1]]]], np.float32))
    d["onw"] = np.ascontiguousarray(inp["a_out_norm_w"][0][g * 1024:(g + 1) * 1024][None, :])
    d["w_out"] = np.ascontiguousarray(inp["a_w_out"][0][g * 1024:(g + 1) * 1024, :].reshape(8, 128, D).transpose(1, 0, 2))
    return d


def _bf16_split(x):
    import ml_dtypes
    x = np.asarray(x, np.float32)
    hi = x.astype(ml_dtypes.bfloat16).astype(np.float32)
    lo = (x - hi).astype(ml_dtypes.bfloat16).astype(np.float32)
    return hi, lo


def _layer1_inputs(inp, g):
    w_in = inp["b_w_in"][0]
    d = {}
    kvc = lambda br, kvt: w_in[:, 8192 + ((br * 2 + kvt) * 4 + g) * 128: 8192 + ((br * 2 + kvt) * 4 + g) * 128 + 128]
    cols = [w_in[:, g * 1024:(g + 1) * 1024], w_in[:, 4096 + g * 1024:4096 + (g + 1) * 1024]]
    cols += [kvc(br, kvt) for br in range(3) for kvt in range(2)]
    d["w_tm"] = _chunk_w(np.concatenate(cols, axis=1), 256)
    d["w_kv0"] = _chunk_w(np.concatenate([kvc(0, 0), kvc(0, 1)], axis=1), 128)
    d["w_g"] = _chunk_w(w_in[:, 11264 + g * 24:11264 + (g + 1) * 24], 24)
    d["knw"] = np.ascontiguousarray(inp["b_k_norm_w"][0].reshape(1, 384))
    d["qnw"] = np.ascontiguousarray(inp["b_q_norm_w"][0].reshape(1, 128))
    d["peT"] = np.ascontiguousarray(inp["b_cmp_pe"][0].transpose(2, 0, 1))
    d["wk"] = np.ascontiguousarray(inp["b_cmp_wk"][0].reshape(32, 128, 128).transpose(1, 0, 2))
    d["wv"] = np.ascontiguousarray(inp["b_cmp_wv"][0].reshape(32, 128, 128).transpose(1, 0, 2))
    d["w_out"] = np.ascontiguousarray(inp["b_w_out"][0][g * 1024:(g + 1) * 1024, :].reshape(8, 128, D).transpose(1, 0, 2))
    scale = 128.0 ** -0.5
    slopes = _alibi(32)[8 * g:8 * g + 8]
    sl = (slopes / scale).astype(np.float32)
    qq = np.arange(128, dtype=np.float32)
    BQ = np.zeros((4, 8, 128), np.float32)
    for h in range(8):
        hi, lo = _bf16_split(sl[h])
        BQ[0, h, :] = hi
        BQ[1, h, :] = lo
        phi, plo = _bf16_split(sl[h] * qq)
        BQ[2, h, :] = -phi
        BQ[3, h, :] = -plo
    d["BQ"] = np.ascontiguousarray(BQ.reshape(4, 1024))
    deltas = np.concatenate([128.0 * np.arange(32), 128.0 * np.arange(32) - 31.0])
    d["btab"] = np.ascontiguousarray((-slopes[:, None] * deltas[None, :]).astype(np.float32).reshape(1, 512))
    AK = np.zeros((4, 2, 128), np.float32)
    AK[0, 0] = AK[1, 0] = np.arange(128)
    AK[0, 1] = AK[1, 1] = 16 * np.arange(128)
    AK[2:, :, :] = 1.0
    d["AK"] = AK
    n_cmp = 255
    cmp_start = np.arange(n_cmp) * 16
    sel_start = np.arange(64) * 64
    ov = np.clip(np.minimum(cmp_start[:, None] + 32, sel_start[None, :] + 64)
                 - np.maximum(cmp_start[:, None], sel_start[None, :]), 0, None) / 32.0
    ovl = np.zeros((256, 64), np.float32)
    ovl[:255] = ov
    d["ovl"] = np.ascontiguousarray(ovl.reshape(2, 128, 64).transpose(1, 0, 2))
    E = np.zeros((64, SEQ), np.float32)
    E[np.arange(SEQ) // 64, np.arange(SEQ)] = 1.0
    d["Eall"] = E
    jj = np.arange(128)[None, :] - 62
    cur = (np.arange(128)[:, None] >= 64).astype(np.int64)
    keep = (jj <= cur)
    forced = (jj == cur) | (jj == cur - 1)
    d["Wkeep"] = keep.astype(np.float32)
    d["Wadd"] = np.where(~keep, B_NEG, np.where(forced, B_FORCE, 0.0)).astype(np.float32)
    return d


def _layer3_inputs(inp, g):
    w_in = inp["d_w_in"][0]
    cols = []
    for pat in range(3):
        for typ in range(3):
            c0 = ((pat * 3 + typ) * 16 + 4 * g) * 128
            cols.append(w_in[:, c0:c0 + 512])
    z0 = 9 * 2048 + 4 * g * 128
    cols.append(w_in[:, z0:z0 + 512])
    d = {}
    d["w_u"] = _chunk_w(np.concatenate(cols, axis=1), 256)
    d["qkw"] = np.ascontiguousarray(np.concatenate([inp["d_q_norm_w"][0], inp["d_k_norm_w"][0]])[None, :])
    slopes = _alibi(16)[4 * g:4 * g + 4]
    kk = np.arange(128)[:, None]
    qq = np.arange(128)[None, :]
    bm = np.zeros((128, 3, 2, 4, 128), np.float64)
    for p, (window, dil) in enumerate(D_PATS):
        for kind in range(2):
            steps = qq + 128 - kk if kind == 0 else qq - kk
            valid = (steps >= 0) & (steps <= 128)
            for h in range(4):
                bm[:, p, kind, h, :] = np.where(valid, np.exp(-slopes[h] * np.where(valid, steps, 0) * dil), 0.0)
    d["bm"] = np.ascontiguousarray(bm.reshape(128, -1).astype(np.float32))
    d["w_out"] = np.ascontiguousarray(inp["d_w_out"][0][g * 512:(g + 1) * 512, :].reshape(4, 128, D).transpose(1, 0, 2))
    return d


_LAYER_INPUTS = {0: _layer0_inputs, 1: _layer1_inputs, 2: _layer2_inputs, 3: _layer3_inputs}


def run_layer(kind, xcur, inp, layer_idx):
    fn = _LAYER_INPUTS[kind]
    maps = []
    nw = np.ascontiguousarray(inp["norm_w"][layer_idx][None, :])
    for c in range(NCORES):
        b, g = c // 4, c % 4
        m = fn(inp, g)
        m["x"] = xcur[b]
        m["nw"] = nw
        maps.append(m)
    shapes = {k: v.shape for k, v in maps[0].items() if k not in ("x", "nw")}
    nc = build_layer(kind, shapes)
    res = run_bass_kernel_spmd(nc, maps, core_ids=list(range(NCORES)))
    return [r["y"] for r in res.results]


def run_reduce(xcur, ys):
    flat = xcur.reshape(2 * SEQ, D)
    maps = []
    for c in range(NCORES):
        b, q = c // 4, c % 4
        m = {"x": flat[c * 1024:(c + 1) * 1024]}
        for j in range(4):
            m[f"y{j}"] = ys[b * 4 + j][q * 1024:(q + 1) * 1024]
        maps.append(m)
    nc = build_reduce()
    res = run_bass_kernel_spmd(nc, maps, core_ids=list(range(NCORES)))
    return np.concatenate([r["out"] for r in res.results], axis=0).reshape(2, SEQ, D)


def run_fused(inp, layers=(0, 1, 2, 3), groups=(0, 1, 2, 3)):
    x = np.ascontiguousarray(inp["x"], dtype=np.float32)
    nw = np.ascontiguousarray(inp["norm_w"], dtype=np.float32)
    base = {}
    for L in layers:
        for g in range(4):
            for k, v in _LAYER_INPUTS[L](inp, g).items():
                base[f"L{L}g{g}_{k}"] = v
    shapes = {(L, g): {k[len(f"L{L}g{g}_"):]: v.shape for k, v in base.items() if k.startswith(f"L{L}g{g}_")}
              for L in layers for g in range(4)}
    nc = build_fused(shapes, layers, groups)
    maps = []
    for b in range(2):
        m = dict(base)
        m["x"] = x[b]
        m["nw"] = nw
        maps.append(m)
    res = run_bass_kernel_spmd(nc, maps, core_ids=[0, 1])
    return np.stack([res.results[b]["out"] for b in range(2)], axis=0)


def kernel(**inputs):
    inp = {k: np.asarray(v) for k, v in inputs.items()}
    return run_fused(inp)


def kernel_unfused(**inputs):
    inp = {k: np.asarray(v) for k, v in inputs.items()}
    x = np.ascontiguousarray(inp["x"], dtype=np.float32)
    for layer in range(4):
        ys = run_layer(layer % 4, x, inp, layer)
        x = run_reduce(x, ys)
    return x
```

```python
import math
from contextlib import ExitStack

import numpy as np
import concourse.bass as bass
import concourse.mybir as mybir
from concourse.bass_utils import run_bass_kernel_spmd

F32 = mybir.dt.float32
BF16 = mybir.dt.bfloat16
AF = mybir.ActivationFunctionType
ALU = mybir.AluOpType
AX = mybir.AxisListType

D = 4096
SEQ = 4096
NCORES = 8
EPS = 1e-6
TT = 512
NTT = SEQ // TT
P1TT = 1024


class Res:
    __slots__ = ("name", "w", "r", "p", "fk", "dsem", "dcnt", "excl")

    def __init__(self, name):
        self.name = name
        self.p = {}
        self.fk = set()
        self.excl = False
        self.w = {}
        self.r = {}
        self.dsem = None
        self.dcnt = 0


class Sched:
    def __init__(self, nc, stack):
        self.nc = nc
        self.stack = stack
        self.E = {"pe": nc.tensor, "dve": nc.vector, "act": nc.scalar,
                  "pool": nc.gpsimd, "sp": nc.sync}
        self.sem = {}
        self.cnt = {}
        for k in ("pe", "dve", "act", "pool"):
            self.sem[k] = stack.enter_context(nc.semaphore("cs_" + k))
            self.cnt[k] = 0
        self.waited = {k: {} for k in self.E}
        self.byname = {}
        self.dsems = []
        self.nwaits = 0
        self.nins = 0

    def res(self, name, dma=False):
        if name in self.byname:
            r = self.byname[name]
        else:
            r = Res(name)
            self.byname[name] = r
        if dma and r.dsem is None:
            r.dsem = self.stack.enter_context(self.nc.semaphore("ds_" + name))
            self.dsems.append(r)
        return r

    def _wait(self, e, key, tok):
        sem, val, teng = tok
        if self.waited[e].get(key, 0) >= val:
            return
        self.E[e].wait_ge(sem, val)
        self.waited[e][key] = val
        self.nwaits += 1

    def _deps(self, e, reads, writes, pwrites):
        for r in reads:
            for key, tok in r.w.items():
                self._wait(e, key, tok)
            if r.excl:
                for key, tok in r.r.items():
                    if tok[2] != e:
                        self._wait(e, key, tok)
        for w in writes:
            for key, tok in w.w.items():
                if tok[2] != e:
                    self._wait(e, key, tok)
            for key, tok in w.r.items():
                if tok[2] != e:
                    self._wait(e, key, tok)
        for w in pwrites:
            for key, tok in w.r.items():
                if tok[2] != e:
                    self._wait(e, key, tok)
            for key, tok in w.p.items():
                if tok[2] != e:
                    self._wait(e, key, tok)
            for key in w.fk:
                tok = w.w.get(key)
                if tok is not None and tok[2] != e:
                    self._wait(e, key, tok)

    def _commit(self, key, tok, reads, writes, pwrites):
        for r in reads:
            r.r[key] = tok
        for w in writes:
            prev = dict(w.w)
            for k2, t2 in w.r.items():
                if k2 not in prev or prev[k2][1] < t2[1]:
                    prev[k2] = t2
            w.p = prev
            w.w = {key: tok}
            w.fk = {key}
            w.r = {}
        for w in pwrites:
            w.w[key] = tok

    def op(self, e, fn, reads=(), writes=(), pwrites=(), inc=True):
        self._deps(e, reads, writes, pwrites)
        ins = fn(self.E[e])
        self.nins += 1
        if inc:
            self.cnt[e] += 1
            ins.then_inc(self.sem[e], 1)
            tok = (self.sem[e], self.cnt[e], e)
        else:
            tok = (self.sem[e], self.cnt[e] + 1, e)
        self._commit("c_" + e, tok, reads, writes, pwrites)
        return ins

    def dma(self, q, out, in_, reads=(), writes=(), pwrites=(), sem_res=None, **kw):
        self._deps(q, reads, writes, pwrites)
        ins = self.E[q].dma_start(out=out, in_=in_, **kw)
        self.nins += 1
        if sem_res is None:
            for c in list(writes) + list(pwrites) + list(reads):
                if c.dsem is not None:
                    sem_res = c
                    break
        assert sem_res is not None and sem_res.dsem is not None
        sem_res.dcnt += 16
        ins.then_inc(sem_res.dsem, 16)
        tok = (sem_res.dsem, sem_res.dcnt, "dma")
        self._commit("d_" + sem_res.name, tok, reads, writes, pwrites)
        return ins

    def barrier(self, engines=("pe", "dve", "act", "pool", "sp")):
        for e in engines:
            for f in ("pe", "dve", "act", "pool"):
                if f != e and self.cnt[f] > 0:
                    self._wait(e, "c_" + f, (self.sem[f], self.cnt[f], f))
            for r in self.dsems:
                if r.dcnt > 0:
                    self._wait(e, "d_" + r.name, (r.dsem, r.dcnt, "dma"))

    def finish(self):
        self.barrier(engines=("sp",))


class KB:
    def __init__(self, nc, st):
        self.nc = nc
        self.S = Sched(nc, st)
        self.top = st
        self.uid = 0

    def scope(self):
        return _Scope(self)


class _Scope:
    def __init__(self, kb):
        self.kb = kb
        self.st = ExitStack()

    def __enter__(self):
        self.st.__enter__()
        return self

    def __exit__(self, *a):
        self.kb.S.barrier()
        return self.st.__exit__(*a)

    def sb(self, name, shape, dt, dma=False):
        self.kb.uid += 1
        t = self.st.enter_context(self.kb.nc.sbuf_tensor(f"{name}_{self.kb.uid}", shape, dt))
        return t, self.kb.S.res(name, dma)

    def ps(self, name, shape, dt):
        self.kb.uid += 1
        t = self.st.enter_context(self.kb.nc.psum_tensor(f"{name}_{self.kb.uid}", shape, dt))
        r = self.kb.S.res(name)
        r.excl = True
        return t, r


def make_ident(S, sc):
    identf, r_if = sc.sb("identf", [128, 128], F32)
    ident, r_id = sc.sb("ident", [128, 128], BF16)
    S.op("pool", lambda e: e.memset(identf[:], 0.0), writes=[r_if])
    S.op("pool", lambda e: e.affine_select(
        out=identf[:], in_=identf[:], pattern=[[-1, 128]], compare_op=ALU.not_equal,
        fill=1.0, base=0, channel_multiplier=1), reads=[r_if], writes=[r_if])
    S.op("dve", lambda e: e.tensor_copy(out=ident[:], in_=identf[:]), reads=[r_if], writes=[r_id])
    return ident, r_id, identf, r_if


def phase1(kb, x_src, nw_ap, groups):
    S = kb.S
    with kb.scope() as sc:
        ident, r_id, _, _ = make_ident(S, sc)
        nwb, r_nwb = sc.sb("p1_nwb", [128, D], F32, dma=True)
        xt = [sc.sb(f"p1_xt{i}", [128, D], F32, dma=True) for i in range(2)]
        xn = [sc.sb(f"p1_xn{i}", [128, D], BF16) for i in range(2)]
        ss = [sc.sb(f"p1_ss{i}", [128, 2], F32) for i in range(2)]
        hT, r_hT = sc.sb("p1_hT", [128, 32, P1TT], BF16)
        NWB = 2
        wb = [sc.sb(f"p1_wb{i}", [128, 32, 256], BF16, dma=True) for i in range(NWB)]
        NOB = 4
        ob = [sc.sb(f"p1_ob{i}", [128, 512], F32, dma=True) for i in range(NOB)]
        obh = [sc.sb(f"p1_obh{i}", [128, 512], BF16, dma=True) for i in range(NOB)]
        pt = [sc.ps(f"p1_pt{i}", [128, 1024], BF16) for i in range(2)]
        pm = [sc.ps(f"p1_pm{i}", [128, 512], F32) for i in range(4)]
        S.dma("sp", nwb[:], nw_ap.partition_broadcast(128), writes=[r_nwb])
        wi = 0
        oi = 0
        pi = 0
        ev = 0
        for T in range(SEQ // P1TT):
            for s in range(P1TT // 128):
                b = s % 2
                t0 = T * P1TT + s * 128
                xt_t, xt_r = xt[b]
                xn_t, xn_r = xn[b]
                ss_t, ss_r = ss[b]
                S.dma("sp", xt_t[:], x_src[t0:t0 + 128, :], writes=[xt_r])
                S.op("act", lambda e: e.activation(out=xn_t[:], in_=xt_t[:], func=AF.Square,
                                                   accum_out=ss_t[:, 0:1]),
                     reads=[xt_r], writes=[xn_r, ss_r])
                S.op("act", lambda e: e.activation(out=ss_t[:, 1:2], in_=ss_t[:, 0:1], func=AF.Sqrt,
                                                   scale=1.0 / D, bias=EPS),
                     reads=[ss_r], writes=[ss_r])
                S.op("dve", lambda e: e.reciprocal(out=ss_t[:, 0:1], in_=ss_t[:, 1:2]),
                     reads=[ss_r], writes=[ss_r])
                S.op("dve", lambda e: e.scalar_tensor_tensor(out=xn_t[:], in0=xt_t[:], scalar=ss_t[:, 0:1],
                                                             in1=nwb[:], op0=ALU.mult, op1=ALU.mult),
                     reads=[xt_r, ss_r, r_nwb], writes=[xn_r])
                for g4 in range(4):
                    pt_t, pt_r = pt[g4 % 2]
                    for j in range(8):
                        kc = g4 * 8 + j
                        S.op("pe", lambda e: e.transpose(out=pt_t[:, j * 128:(j + 1) * 128],
                                                         in_=xn_t[:, kc * 128:(kc + 1) * 128], identity=ident[:]),
                             reads=[xn_r, r_id], writes=[pt_r] if j == 0 else [],
                             pwrites=[] if j == 0 else [pt_r], inc=(j == 7))
                    dst = hT[:, g4 * 8:(g4 + 1) * 8, s * 128:(s + 1) * 128]
                    src = pt_t[:].rearrange("p (j t) -> p j t", j=8)
                    if g4 % 2 == 0:
                        S.op("dve", lambda e: e.tensor_copy(out=dst, in_=src), reads=[pt_r], pwrites=[r_hT])
                    else:
                        S.op("act", lambda e: e.copy(out=dst, in_=src), reads=[pt_r], pwrites=[r_hT])
            for g in groups:
                W = g["W"]
                odt = g["dst"].dtype
                for c in range(g["nchunk"]):
                    wb_t, wb_r = wb[wi % NWB]
                    wi += 1
                    S.dma("pool", wb_t[:, :, :W], g["w"][c], writes=[wb_r])
                    if g["layout"] == "fm":
                        for hh in range(P1TT // 512):
                            pm_t, pm_r = pm[pi % 4]
                            pi += 1
                            for kc in range(32):
                                S.op("pe", lambda e: e.matmul(pm_t[:W, :512], lhsT=wb_t[:, kc, :W],
                                                              rhs=hT[:, kc, hh * 512:(hh + 1) * 512],
                                                              start=(kc == 0), stop=(kc == 31)),
                                     reads=[wb_r, r_hT], writes=[pm_r] if kc == 0 else [],
                                     pwrites=[] if kc == 0 else [pm_r], inc=(kc == 31))
                            ob_t, ob_r = (ob if odt == F32 else obh)[oi % NOB]
                            oi += 1
                            if ev % 2 == 0:
                                S.op("dve", lambda e: e.tensor_copy(out=ob_t[:W, :512], in_=pm_t[:W, :512]),
                                     reads=[pm_r], writes=[ob_r])
                            else:
                                S.op("act", lambda e: e.copy(out=ob_t[:W, :512], in_=pm_t[:W, :512]),
                                     reads=[pm_r], writes=[ob_r])
                            ev += 1
                            c0 = T * P1TT + hh * 512
                            S.dma("sp", g["dst"][c * W:(c + 1) * W, c0:c0 + 512], ob_t[:W, :512],
                                  reads=[ob_r], pwrites=[g["rdst"]])
                    else:
                        for s in range(P1TT // 128):
                            pm_t, pm_r = pm[pi % 4]
                            pi += 1
                            for kc in range(32):
                                S.op("pe", lambda e: e.matmul(pm_t[:, :W], lhsT=hT[:, kc, s * 128:(s + 1) * 128],
                                                              rhs=wb_t[:, kc, :W],
                                                              start=(kc == 0), stop=(kc == 31)),
                                     reads=[wb_r, r_hT], writes=[pm_r] if kc == 0 else [],
                                     pwrites=[] if kc == 0 else [pm_r], inc=(kc == 31))
                            ob_t, ob_r = (ob if odt == F32 else obh)[oi % NOB]
                            oi += 1
                            if ev % 2 == 0:
                                S.op("dve", lambda e: e.tensor_copy(out=ob_t[:, :W], in_=pm_t[:, :W]),
                                     reads=[pm_r], writes=[ob_r])
                            else:
                                S.op("act", lambda e: e.copy(out=ob_t[:, :W], in_=pm_t[:, :W]),
                                     reads=[pm_r], writes=[ob_r])
                            ev += 1
                            t0 = T * P1TT + s * 128
                            S.dma("sp", g["dst"][t0:t0 + 128, c * W:(c + 1) * W], ob_t[:, :W],
                                  reads=[ob_r], pwrites=[g["rdst"]])


def phase3(kb, o_scr, r_o, P, n_ic, wout_ap, y_dst, r_y, accum=None):
    S = kb.S
    with kb.scope() as sc:
        w, r_w = sc.sb("p3_w", [P, n_ic, D], BF16, dma=True)
        for ic in range(n_ic):
            S.dma("pool", w[:, ic, :], wout_ap[:, ic, :], pwrites=[r_w])
        ot = [sc.sb(f"p3_ot{i}", [P, n_ic, TT], BF16, dma=True) for i in range(2)]
        yb = [sc.sb(f"p3_yb{i}", [128, D], F32, dma=True) for i in range(2)]
        if accum is not None:
            xds = [sc.sb(f"p3_xd{i}", [128, D], F32, dma=True) for i in range(2 if n_ic <= 8 else 1)]
        pm = [sc.ps(f"p3_pm{i}", [128, 512], F32) for i in range(4)]
        pi = 0
        yi = 0
        o_v = o_scr.rearrange("(ic p) t -> p ic t", p=P)
        for T in range(NTT):
            ot_t, ot_r = ot[T % 2]
            S.dma("sp", ot_t[:], o_v[:, :, T * TT:(T + 1) * TT], reads=[r_o], writes=[ot_r])
            for s in range(TT // 128):
                yb_t, yb_r = yb[yi % 2]
                yi += 1
                for cc in range(D // 512):
                    pm_t, pm_r = pm[pi % 4]
                    pi += 1
                    for ic in range(n_ic):
                        S.op("pe", lambda e: e.matmul(pm_t[:, :], lhsT=ot_t[:, ic, s * 128:(s + 1) * 128],
                                                      rhs=w[:, ic, cc * 512:(cc + 1) * 512],
                                                      start=(ic == 0), stop=(ic == n_ic - 1)),
                             reads=[ot_r, r_w], writes=[pm_r] if ic == 0 else [],
                             pwrites=[] if ic == 0 else [pm_r], inc=(ic == n_ic - 1))
                    if cc % 2 == 0:
                        S.op("dve", lambda e: e.tensor_copy(out=yb_t[:, cc * 512:(cc + 1) * 512], in_=pm_t[:, :]),
                             reads=[pm_r], writes=[yb_r] if cc == 0 else [], pwrites=[] if cc == 0 else [yb_r])
                    else:
                        S.op("act", lambda e: e.copy(out=yb_t[:, cc * 512:(cc + 1) * 512], in_=pm_t[:, :]),
                             reads=[pm_r], pwrites=[yb_r])
                t0 = T * TT + s * 128
                if accum is not None:
                    xd, r_xd = xds[yi % len(xds)]
                    S.dma("sp", xd[:], accum[t0:t0 + 128, :], writes=[r_xd], sem_res=r_xd)
                    hD = D // 2
                    S.op("dve", lambda e: e.tensor_tensor(out=yb_t[:, 0:hD], in0=yb_t[:, 0:hD], in1=xd[:, 0:hD], op=ALU.add),
                         reads=[yb_r, r_xd], writes=[yb_r])
                    S.op("pool", lambda e: e.tensor_tensor(out=yb_t[:, hD:D], in0=yb_t[:, hD:D], in1=xd[:, hD:D], op=ALU.add),
                         reads=[yb_r, r_xd], writes=[yb_r])
                S.dma("sp", y_dst[t0:t0 + 128, :], yb_t[:], reads=[yb_r], pwrites=[r_y], sem_res=yb_r)


C_P = 112
C_NCH = 12


def mixer_rglru(kb, xbT, r_xb, zT, r_z, prm, o_scr, r_o):
    S = kb.S
    P = C_P
    with kb.scope() as sc:
        cw, r_cw = sc.sb("c_cw", [P, C_NCH, 4], F32, dma=True)
        vec, r_vec = sc.sb("c_vec", [P, 4, C_NCH], F32, dma=True)
        c1, r_c1 = sc.sb("c_c1", [P, C_NCH], F32)
        wa, r_wa = sc.sb("c_wa", [P, 4, 3, 336], BF16, dma=True)
        wx, r_wx = sc.sb("c_wx", [P, 4, 3, 336], BF16, dma=True)
        S.dma("sp", cw[:], prm["cw"], writes=[r_cw])
        S.dma("sp", vec[:], prm["vec"], writes=[r_vec])
        S.dma("pool", wa[:], prm["wa"], writes=[r_wa])
        S.dma("pool", wx[:], prm["wx"], writes=[r_wx])
        S.op("act", lambda e: e.activation(out=c1[:], in_=vec[:, 3, :], func=AF.Exp, scale=-1.0),
             reads=[r_vec], writes=[r_c1])
        S.op("act", lambda e: e.activation(out=c1[:], in_=c1[:], func=AF.Ln, bias=1.0, scale=1.0),
             reads=[r_c1], writes=[r_c1])
        S.op("dve", lambda e: e.tensor_scalar(out=c1[:], in0=c1[:], scalar1=-8.0, scalar2=None, op0=ALU.mult),
             reads=[r_c1], writes=[r_c1])
        xb, r_xbs = sc.sb("c_xb", [P, SEQ + 4], F32, dma=True)
        xc = [sc.sb(f"c_xc{i}", [P, SEQ], F32) for i in range(3)]
        xcb, r_xcb = sc.sb("c_xcb", [P, 3, SEQ], BF16)
        ra, r_ra = sc.sb("c_ra", [P, SEQ], F32)
        gi, r_gi = sc.sb("c_gi", [P, SEQ], F32)
        tmp, r_tmp = sc.sb("c_tmp", [P, SEQ], F32)
        zt, r_zt = sc.sb("c_zt", [P, SEQ], BF16, dma=True)
        ob, r_ob = sc.sb("c_ob", [P, SEQ], BF16, dma=True)
        pm = [sc.ps(f"c_pm{i}", [128, 512], F32) for i in range(4)]
        pi = 0
        S.op("pool", lambda e: e.memset(xb[:, 0:4], 0.0), writes=[r_xbs])
        for n in range(4):
            for c in range(3):
                ch = n * 3 + c
                xc_t, xc_r = xc[c]
                S.dma("sp", xb[:, 4:], xbT[ch * P:(ch + 1) * P, :], reads=[r_xb], writes=[r_xbs])
                S.op("dve", lambda e: e.tensor_scalar(out=xc_t[:], in0=xb[:, 1:1 + SEQ], scalar1=cw[:, ch, 0:1],
                                                      scalar2=vec[:, 0, ch:ch + 1], op0=ALU.mult, op1=ALU.add),
                     reads=[r_xbs, r_cw, r_vec], writes=[xc_r])
                for j in range(1, 4):
                    S.op("dve", lambda e: e.scalar_tensor_tensor(out=xc_t[:], in0=xb[:, 1 + j:1 + j + SEQ],
                                                                 scalar=cw[:, ch, j:j + 1], in1=xc_t[:],
                                                                 op0=ALU.mult, op1=ALU.add),
                         reads=[r_xbs, r_cw, xc_r], writes=[xc_r])
                S.op("act", lambda e: e.copy(out=xcb[:, c, :], in_=xc_t[:]), reads=[xc_r],
                     writes=[r_xcb] if c == 0 else [], pwrites=[] if c == 0 else [r_xcb])
            for d in range(3):
                ch = n * 3 + d
                xc_t, xc_r = xc[d]
                S.dma("sp", zt[:], zT[ch * P:(ch + 1) * P, :], reads=[r_z], writes=[r_zt])
                for (wt, wr, dst, dr, bi) in ((wa, r_wa, ra, r_ra, 1), (wx, r_wx, gi, r_gi, 2)):
                    for T in range(NTT):
                        pm_t, pm_r = pm[pi % 4]
                        pi += 1
                        for c in range(3):
                            S.op("pe", lambda e: e.matmul(pm_t[:P, :TT], lhsT=wt[:, n, c, d * P:(d + 1) * P],
                                                          rhs=xcb[:, c, T * TT:(T + 1) * TT],
                                                          start=(c == 0), stop=(c == 2)),
                                 reads=[wr, r_xcb], writes=[pm_r] if c == 0 else [],
                                 pwrites=[] if c == 0 else [pm_r], inc=(c == 2))
                        S.op("act", lambda e: e.activation(out=dst[:, T * TT:(T + 1) * TT], in_=pm_t[:P, :TT],
                                                           func=AF.Sigmoid, bias=vec[:, bi, ch:ch + 1], scale=1.0),
                             reads=[pm_r, r_vec], writes=[dr] if T == 0 else [], pwrites=[] if T == 0 else [dr])
                S.op("act", lambda e: e.activation(out=ra[:], in_=ra[:], func=AF.Exp, scale=c1[:, ch:ch + 1]),
                     reads=[r_ra, r_c1], writes=[r_ra])
                S.op("pool", lambda e: e.tensor_tensor(out=tmp[:], in0=ra[:], in1=ra[:], op=ALU.mult),
                     reads=[r_ra], writes=[r_tmp])
                S.op("act", lambda e: e.activation(out=tmp[:], in_=tmp[:], func=AF.Sqrt, scale=-1.0, bias=1.0),
                     reads=[r_tmp], writes=[r_tmp])
                S.op("dve", lambda e: e.tensor_tensor(out=gi[:], in0=gi[:], in1=xc_t[:], op=ALU.mult),
                     reads=[r_gi, xc_r], writes=[r_gi])
                S.op("pool", lambda e: e.tensor_tensor(out=gi[:], in0=gi[:], in1=tmp[:], op=ALU.mult),
                     reads=[r_gi, r_tmp], writes=[r_gi])
                S.op("dve", lambda e: e.tensor_tensor_scan(out=tmp[:], data0=ra[:], data1=gi[:], initial=0.0,
                                                           op0=ALU.mult, op1=ALU.add),
                     reads=[r_ra, r_gi], writes=[r_tmp])
                S.op("act", lambda e: e.activation(out=zt[:], in_=zt[:], func=AF.Silu), reads=[r_zt], writes=[r_zt])
                S.op("pool", lambda e: e.tensor_tensor(out=ob[:], in0=tmp[:], in1=zt[:], op=ALU.mult),
                     reads=[r_tmp, r_zt], writes=[r_ob])
                S.dma("sp", o_scr[ch * P:(ch + 1) * P, :], ob[:], reads=[r_ob], pwrites=[r_o])


D_PATS = ((128, 1), (512, 4), (2048, 16))


def mixer_dilated(kb, u, r_u, prm, o_scr, r_o, nd_scr, r_nd):
    S = kb.S
    scale = 128.0 ** -0.5
    with kb.scope() as sc:
        ident, r_id, _, _ = make_ident(S, sc)
        wqk, r_wqk = sc.sb("d_wqk", [128, 2, 128], F32, dma=True)
        bm, r_bm = sc.sb("d_bm", [128, 3, 2, 512], F32, dma=True)
        S.dma("sp", wqk[:].rearrange("p a d -> p (a d)"), prm["qkw"].partition_broadcast(128), writes=[r_wqk])
        S.dma("sp", bm[:].rearrange("p a b c -> p (a b c)"), prm["bm"], writes=[r_bm])
        blk = [sc.sb(f"d_blk{i}", [128, 1536], BF16, dma=True) for i in range(2)]
        sq, r_sq = sc.sb("d_sq", [128, 1024], F32)
        ssq, r_ssq = sc.sb("d_ssq", [128, 8], F32)
        rstd, r_rstd = sc.sb("d_rstd", [128, 8], F32)
        qkn, r_qkn = sc.sb("d_qkn", [128, 1024], BF16)
        qT, r_qT = sc.sb("d_qT", [128, 512], BF16)
        kT = [sc.sb(f"d_kT{i}", [128, 512], BF16) for i in range(2)]
        va = [sc.sb(f"d_va{i}", [128, 4, 129], BF16) for i in range(2)]
        pex = [sc.sb(f"d_pex{i}", [128, 512], F32) for i in range(2)]
        ptm = [sc.sb(f"d_ptm{i}", [128, 512], BF16) for i in range(2)]
        ndst = [sc.sb(f"d_ndst{i}", [128, 4, 129], F32, dma=True) for i in range(2)]
        pt, r_pt = sc.ps("d_pt", [128, 1024], BF16)
        pss = [sc.ps(f"d_pss{i}", [128, 512], F32) for i in range(4)]
        accs = [sc.ps(f"d_acc{i}", [128, 512], F32) for i in range(2)]
        for i in range(2):
            S.op("pool", lambda e: e.memset(va[i][0][:], 1.0), writes=[va[i][1]])
        bi = 0
        si = 0
        for p, (window, dil) in enumerate(D_PATS):
            uv = u.rearrange("(n dl) c -> dl n c", dl=dil)
            ndv = nd_scr[p].rearrange("(n dl) c -> dl n c", dl=dil)
            for r in range(dil):
                for i in range(SEQ // dil // 128):
                    blk_t, blk_r = blk[bi % 2]
                    nd_t, nd_r = ndst[bi % 2]
                    bi += 1
                    cur = i % 2
                    kT_t, kT_r = kT[cur]
                    va_t, va_r = va[cur]
                    S.dma("sp", blk_t[:], uv[r, i * 128:(i + 1) * 128, p * 1536:(p + 1) * 1536],
                          reads=[r_u], writes=[blk_r])
                    S.op("dve", lambda e: e.tensor_tensor(out=sq[:], in0=blk_t[:, 0:1024], in1=blk_t[:, 0:1024],
                                                          op=ALU.mult), reads=[blk_r], writes=[r_sq])
                    S.op("dve", lambda e: e.tensor_reduce(out=ssq[:], in_=sq[:].rearrange("p (h d) -> p h d", h=8),
                                                          axis=AX.X, op=ALU.add), reads=[r_sq], writes=[r_ssq])
                    S.op("act", lambda e: e.activation(out=rstd[:], in_=ssq[:], func=AF.Sqrt, scale=1.0 / 128, bias=EPS),
                         reads=[r_ssq], writes=[r_rstd])
                    S.op("dve", lambda e: e.reciprocal(out=rstd[:], in_=rstd[:]), reads=[r_rstd], writes=[r_rstd])
                    S.op("dve", lambda e: e.tensor_tensor(
                        out=sq[:].rearrange("p (h d) -> p h d", h=8),
                        in0=blk_t[:, 0:1024].rearrange("p (h d) -> p h d", h=8),
                        in1=rstd[:, :].unsqueeze(2).to_broadcast([128, 8, 128]), op=ALU.mult),
                        reads=[blk_r, r_rstd], writes=[r_sq])
                    S.op("pool", lambda e: e.tensor_tensor(
                        out=qkn[:].rearrange("p (a h d) -> p a h d", a=2, h=4),
                        in0=sq[:].rearrange("p (a h d) -> p a h d", a=2, h=4),
                        in1=wqk[:, :, :].unsqueeze(2).to_broadcast([128, 2, 4, 128]), op=ALU.mult),
                        reads=[r_sq, r_wqk], writes=[r_qkn])
                    for j in range(8):
                        S.op("pe", lambda e: e.transpose(out=pt[:, j * 128:(j + 1) * 128],
                                                         in_=qkn[:, j * 128:(j + 1) * 128], identity=ident[:]),
                             reads=[r_qkn, r_id], writes=[r_pt] if j == 0 else [],
                             pwrites=[] if j == 0 else [r_pt], inc=(j == 7))
                    S.op("dve", lambda e: e.tensor_copy(out=qT[:], in_=pt[:, 0:512]), reads=[r_pt], writes=[r_qT])
                    S.op("act", lambda e: e.copy(out=kT_t[:], in_=pt[:, 512:1024]), reads=[r_pt], writes=[kT_r])
                    S.op("pool", lambda e: e.tensor_copy(out=va_t[:, :, 0:128],
                                                         in_=blk_t[:, 1024:1536].rearrange("p (h d) -> p h d", h=4)),
                         reads=[blk_r], writes=[va_r])
                    tiles = ([(1 - cur, 0)] if i > 0 else []) + [(cur, 1)]
                    pts = []
                    for ti, (slot, kind) in enumerate(tiles):
                        ps_t, ps_r = pss[si % 4]
                        pe_t, pe_r = pex[si % 2]
                        pm_t, pm_r = ptm[si % 2]
                        si += 1
                        for h in range(4):
                            S.op("pe", lambda e: e.matmul(ps_t[:, h * 128:(h + 1) * 128],
                                                          lhsT=kT[slot][0][:, h * 128:(h + 1) * 128],
                                                          rhs=qT[:, h * 128:(h + 1) * 128], start=True, stop=True),
                                 reads=[kT[slot][1], r_qT], writes=[ps_r] if h == 0 else [],
                                 pwrites=[] if h == 0 else [ps_r], inc=(h == 3))
                        S.op("act", lambda e: e.activation(out=pe_t[:], in_=ps_t[:], func=AF.Exp, scale=scale),
                             reads=[ps_r], writes=[pe_r])
                        S.op("dve", lambda e: e.tensor_tensor(out=pm_t[:], in0=pe_t[:], in1=bm[:, p, kind, :],
                                                              op=ALU.mult), reads=[pe_r, r_bm], writes=[pm_r])
                        pts.append((pm_t, pm_r, slot))
                    for h in range(4):
                        off = (h % 2) * 129
                        acc, r_acc = accs[h // 2]
                        for ti, (pm_t, pm_r, slot) in enumerate(pts):
                            last = (ti == len(pts) - 1)
                            S.op("pe", lambda e: e.matmul(acc[:, off:off + 129], lhsT=pm_t[:, h * 128:(h + 1) * 128],
                                                          rhs=va[slot][0][:, h, :], start=(ti == 0), stop=last),
                                 reads=[pm_r, va[slot][1]],
                                 writes=[r_acc] if (ti == 0 and h % 2 == 0) else [],
                                 pwrites=[] if (ti == 0 and h % 2 == 0) else [r_acc],
                                 inc=(last and h % 2 == 1))
                    S.op("dve", lambda e: e.tensor_copy(
                        out=nd_t[:, 0:2, :], in_=accs[0][0][:, 0:258].rearrange("p (h c) -> p h c", h=2)),
                        reads=[accs[0][1]], writes=[nd_r])
                    S.op("act", lambda e: e.copy(
                        out=nd_t[:, 2:4, :], in_=accs[1][0][:, 0:258].rearrange("p (h c) -> p h c", h=2)),
                        reads=[accs[1][1]], pwrites=[nd_r])
                    S.dma("sp", ndv[r, i * 128:(i + 1) * 128, :], nd_t[:].rearrange("p h c -> p (h c)"),
                          reads=[nd_r], pwrites=[r_nd])
    with kb.scope() as sc:
        ident, r_id, _, _ = make_ident(S, sc)
        nds = [[sc.sb(f"d2_nd{i}_{j}", [128, 4, 129], F32, dma=True) for j in range(3)] for i in range(2)]
        zt = [sc.sb(f"d2_z{i}", [128, 512], BF16, dma=True) for i in range(2)]
        rden, r_rden = sc.sb("d2_rden", [128, 4], F32)
        of, r_of = sc.sb("d2_of", [128, 4, 128], F32)
        zs, r_zs = sc.sb("d2_zs", [128, 512], F32)
        ob, r_ob = sc.sb("d2_ob", [128, 512], BF16)
        oT = [sc.sb(f"d2_oT{i}", [128, 4, TT], BF16, dma=True) for i in range(2)]
        pt, r_pt = sc.ps("d2_pt", [128, 1024], BF16)
        for t in range(SEQ // 128):
            a = nds[t % 2]
            z_t, z_r = zt[t % 2]
            oT_t, oT_r = oT[(t // 4) % 2]
            for j in range(3):
                S.dma("sp", a[j][0][:].rearrange("p h c -> p (h c)"), nd_scr[j, t * 128:(t + 1) * 128, :],
                      reads=[r_nd], writes=[a[j][1]])
            S.dma("sp", z_t[:], u[t * 128:(t + 1) * 128, 4608:5120], reads=[r_u], writes=[z_r])
            s_t, s_r = a[0]
            S.op("dve", lambda e: e.tensor_tensor(out=s_t[:], in0=s_t[:], in1=a[1][0][:], op=ALU.add),
                 reads=[s_r, a[1][1]], writes=[s_r])
            S.op("pool", lambda e: e.tensor_tensor(out=s_t[:], in0=s_t[:], in1=a[2][0][:], op=ALU.add),
                 reads=[s_r, a[2][1]], writes=[s_r])
            S.op("dve", lambda e: e.reciprocal(out=rden[:], in_=s_t[:, :, 128]), reads=[s_r], writes=[r_rden])
            S.op("dve", lambda e: e.tensor_tensor(out=of[:], in0=s_t[:, :, 0:128],
                                                  in1=rden[:, :].unsqueeze(2).to_broadcast([128, 4, 128]), op=ALU.mult),
                 reads=[s_r, r_rden], writes=[r_of])
            S.op("act", lambda e: e.activation(out=zs[:], in_=z_t[:], func=AF.Silu), reads=[z_r], writes=[r_zs])
            S.op("pool", lambda e: e.tensor_tensor(out=ob[:], in0=of[:].rearrange("p h d -> p (h d)"), in1=zs[:],
                                                   op=ALU.mult), reads=[r_of, r_zs], writes=[r_ob])
            for h in range(4):
                S.op("pe", lambda e: e.transpose(out=pt[:, h * 128:(h + 1) * 128], in_=ob[:, h * 128:(h + 1) * 128],
                                                 identity=ident[:]),
                     reads=[r_ob, r_id], writes=[r_pt] if h == 0 else [], pwrites=[] if h == 0 else [r_pt],
                     inc=(h == 3))
            q4 = t % 4
            S.op("act", lambda e: e.copy(out=oT_t[:, :, q4 * 128:(q4 + 1) * 128],
                                         in_=pt[:, 0:512].rearrange("p (h t) -> p h t", h=4)),
                 reads=[r_pt], writes=[oT_r] if q4 == 0 else [], pwrites=[] if q4 == 0 else [oT_r])
            if q4 == 3:
                T = t // 4
                S.dma("sp", o_scr.rearrange("(h p) t -> p h t", p=128)[:, :, T * TT:(T + 1) * TT], oT_t[:],
                      reads=[oT_r], pwrites=[r_o])


def mixer_mlstm(kb, qkT, r_qk, utm, r_utm, gates, r_g, prm, o_scr, r_o):
    S = kb.S
    NCH = SEQ // 128
    with kb.scope() as sc0:
        qkb, r_qkb = sc0.sb("a_qkb", [128, 8, SEQ], BF16)
        with kb.scope() as sc:
            cw, r_cw = sc.sb("a_cw", [128, 8, 4], F32, dma=True)
            cb, r_cb = sc.sb("a_cb", [128, 8], F32, dma=True)
            S.dma("sp", cw[:], prm["cw"], writes=[r_cw])
            S.dma("sp", cb[:], prm["cb"], writes=[r_cb])
            xb = [sc.sb(f"a_xb{i}", [128, SEQ + 4], F32, dma=True) for i in range(2)]
            xc, r_xc = sc.sb("a_xc", [128, SEQ], F32)
            for i in range(2):
                S.op("pool", lambda e: e.memset(xb[i][0][:, 0:4], 0.0), writes=[xb[i][1]])
            for ch in range(8):
                xb_t, xb_r = xb[ch % 2]
                S.dma("sp", xb_t[:, 4:], qkT[ch * 128:(ch + 1) * 128, :], reads=[r_qk], writes=[xb_r])
                S.op("dve", lambda e: e.tensor_scalar(out=xc[:], in0=xb_t[:, 1:1 + SEQ], scalar1=cw[:, ch, 0:1],
                                                      scalar2=cb[:, ch:ch + 1], op0=ALU.mult, op1=ALU.add),
                     reads=[xb_r, r_cw, r_cb], writes=[r_xc])
                for j in range(1, 4):
                    S.op("dve", lambda e: e.scalar_tensor_tensor(out=xc[:], in0=xb_t[:, 1 + j:1 + j + SEQ],
                                                                 scalar=cw[:, ch, j:j + 1], in1=xc[:],
                                                                 op0=ALU.mult, op1=ALU.add),
                         reads=[xb_r, r_cw, r_xc], writes=[r_xc])
                S.op("act", lambda e: e.activation(out=qkb[:, ch, :], in_=xc[:], func=AF.Silu),
                     reads=[r_xc], writes=[r_qkb] if ch == 0 else [], pwrites=[] if ch == 0 else [r_qkb])
        with kb.scope() as sc:
            ident, r_id, identf, r_if = make_ident(S, sc)
            triu, r_tri = sc.sb("a_triu", [128, 128], F32)
            ones, r_ones = sc.sb("a_ones", [128, 128], F32)
            onesb, r_onesb = sc.sb("a_onesb", [128, 1], BF16)
            S.op("pool", lambda e: e.memset(triu[:], 1.0), writes=[r_tri])
            S.op("pool", lambda e: e.affine_select(out=triu[:], in_=triu[:], pattern=[[1, 128]],
                                                   compare_op=ALU.is_ge, fill=0.0, base=0, channel_multiplier=-1),
                 reads=[r_tri], writes=[r_tri])
            S.op("pool", lambda e: e.memset(ones[:], 1.0), writes=[r_ones])
            S.op("pool", lambda e: e.memset(onesb[:], 1.0), writes=[r_onesb])
            onw, r_onw = sc.sb("a_onw", [128, 1024], F32, dma=True)
            S.dma("sp", onw[:], prm["onw"].partition_broadcast(128), writes=[r_onw])
            gb, r_gb = sc.sb("a_gb", [128, 4], F32, dma=True)
            S.dma("sp", gb[:], prm["gb"].partition_broadcast(128), writes=[r_gb])
            G, r_G = sc.sb("a_G", [128, NCH, 4], F32, dma=True)
            gv = gates.rearrange("(c p) g -> p c g", p=128)
            for i4 in range(4):
                S.dma("sp", G[:, i4 * 8:(i4 + 1) * 8, :], gv[:, i4 * 8:(i4 + 1) * 8, :], reads=[r_g],
                      writes=[r_G] if i4 == 0 else [], pwrites=[] if i4 == 0 else [r_G])
            S.op("dve", lambda e: e.tensor_tensor(out=G[:], in0=G[:], in1=gb[:, :].unsqueeze(1).to_broadcast([128, NCH, 4]),
                                                  op=ALU.add), reads=[r_G, r_gb], writes=[r_G])
            lf, r_lf = sc.sb("a_lf", [128, NCH, 2], F32)
            S.op("act", lambda e: e.activation(out=lf[:], in_=G[:, :, 2:4], func=AF.Exp, scale=-1.0),
                 reads=[r_G], writes=[r_lf])
            S.op("act", lambda e: e.activation(out=lf[:], in_=lf[:], func=AF.Ln, bias=1.0, scale=1.0),
                 reads=[r_lf], writes=[r_lf])
            S.op("dve", lambda e: e.tensor_scalar(out=lf[:], in0=lf[:], scalar1=-1.0, scalar2=None, op0=ALU.mult),
                 reads=[r_lf], writes=[r_lf])
            ps_b, r_psb = sc.ps("a_psb", [128, 512], F32)
            S.op("pe", lambda e: e.matmul(ps_b[:, 0:2 * NCH], lhsT=triu[:], rhs=lf[:].rearrange("p c h -> p (c h)"),
                                          start=True, stop=True), reads=[r_tri, r_lf], writes=[r_psb])
            cs, r_cs = sc.sb("a_cs", [128, NCH, 2], F32)
            ecs, r_ecs = sc.sb("a_ecs", [128, NCH, 2], F32)
            S.op("dve", lambda e: e.tensor_tensor(out=cs[:], in0=G[:, :, 0:2],
                                                  in1=ps_b[:, 0:2 * NCH].rearrange("p (c h) -> p c h", h=2),
                                                  op=ALU.subtract), reads=[r_G, r_psb], writes=[r_cs])
            S.op("act", lambda e: e.activation(out=ecs[:], in_=cs[:], func=AF.Exp), reads=[r_cs], writes=[r_ecs])
            Cst, r_C = sc.sb("a_C", [128, 2, 2, 512], F32)
            Cb, r_Cb = sc.sb("a_Cb", [128, 2, 2, 512], BF16)
            nst, r_n = sc.sb("a_n", [128, 2, 2], F32)
            nb, r_nb = sc.sb("a_nb", [128, 2, 2], BF16)
            S.op("pool", lambda e: e.memset(Cst[:], 0.0), writes=[r_C])
            S.op("pool", lambda e: e.memset(nst[:], 0.0), writes=[r_n])
            LT, r_LT = sc.sb("a_LT", [128, 4, 128], F32)
            EB = [sc.sb(f"a_EB{i}", [128, 4, 128], F32) for i in range(2)]
            DT = [sc.sb(f"a_DT{i}", [128, 128], F32) for i in range(4)]
            ws, r_ws = sc.sb("a_ws", [128, 4], F32)
            wsb, r_wsb = sc.sb("a_wsb", [128, 4], BF16)
            vt = [sc.sb(f"a_vt{i}", [128, 3072], BF16, dma=True) for i in range(2)]
            aT, r_aT = sc.sb("a_aT", [128, 128], BF16)
            qp, r_qp = sc.sb("a_qp", [128, 2, 128], BF16)
            vp, r_vp = sc.sb("a_vp", [128, 512], BF16)
            ktm, r_ktm = sc.sb("a_ktm", [128, 256], BF16)
            den, r_den = sc.sb("a_den", [128, 4], F32)
            hc, r_hc = sc.sb("a_hc", [128, 512], F32)
            junk, r_junk = sc.sb("a_junk", [128, 512], F32)
            t1, r_t1 = sc.sb("a_t1", [128, 512], F32)
            sg, r_sg = sc.sb("a_sg", [128, 512], F32)
            sz, r_sz = sc.sb("a_sz", [128, 512], F32)
            yb, r_yb = sc.sb("a_yb", [128, 512], BF16)
            oT = [sc.sb(f"a_oT{i}", [128, 8, TT], BF16, dma=True) for i in range(2)]
            ps_row, r_prow = sc.ps("a_prow", [128, 512], F32)
            ps_st, r_pst = sc.ps("a_pst", [128, 512], F32)
            ps_num, r_pnum = sc.ps("a_pnum", [128, 512], F32)
            ps_sm, r_psm = sc.ps("a_psm", [128, 512], F32)
            ps_cu = [sc.ps(f"a_pcu{i}", [128, 512], F32) for i in range(2)]
            ps_kt, r_pkt = sc.ps("a_pkt", [128, 1024], BF16)
            o_v = o_scr.rearrange("(ic p) t -> p ic t", p=128)
            for c in range(NCH):
                vt_t, vt_r = vt[c % 2]
                S.dma("sp", vt_t[:], utm[c * 128:(c + 1) * 128, :], reads=[r_utm], writes=[vt_r])
                if c % 2 == 0:
                    EB_t, EB_r = EB[(c // 2) % 2]
                    S.op("dve", lambda e: e.tensor_tensor(
                        out=LT[:], in0=triu[:, :].unsqueeze(1).to_broadcast([128, 4, 128]),
                        in1=lf[:, c:c + 2, :].rearrange("p c h -> p (c h)").unsqueeze(2).to_broadcast([128, 4, 128]),
                        op=ALU.mult), reads=[r_tri, r_lf], writes=[r_LT])
                    S.op("pe", lambda e: e.matmul(ps_row[:, :], lhsT=ones[:], rhs=LT[:].rearrange("p a t -> p (a t)"),
                                                  start=True, stop=True), reads=[r_ones, r_LT], writes=[r_prow])
                    S.op("act", lambda e: e.activation(out=EB_t[:].rearrange("p a t -> p (a t)"), in_=ps_row[:, :],
                                                       func=AF.Exp), reads=[r_prow], writes=[EB_r])
                    for pr in range(4):
                        idx = c * 2 + pr
                        S.op("act", lambda e: e.activation(out=DT[pr][0][:], in_=ps_row[:, pr * 128:(pr + 1) * 128],
                                                           func=AF.Exp,
                                                           bias=cs[:, idx // 2, (idx % 2):(idx % 2) + 1], scale=1.0),
                             reads=[r_prow, r_cs], writes=[DT[pr][1]])
                        S.op("pool", lambda e: e.tensor_tensor(out=DT[pr][0][:], in0=DT[pr][0][:], in1=triu[:],
                                                               op=ALU.mult), reads=[DT[pr][1], r_tri], writes=[DT[pr][1]])
                    S.op("dve", lambda e: e.tensor_tensor(out=ws[:], in0=ecs[:, c:c + 2, :].rearrange("p c h -> p (c h)"),
                                                          in1=EB_t[:, :, 127], op=ALU.mult),
                         reads=[r_ecs, EB_r], writes=[r_ws])
                    S.op("act", lambda e: e.copy(out=wsb[:], in_=ws[:]), reads=[r_ws], writes=[r_wsb])
                EB_t, EB_r = EB[(c // 2) % 2]
                oT_t, oT_r = oT[(c // 4) % 2]
                for h in range(2):
                    pr = (c % 2) * 2 + h
                    DT_t, DT_r = DT[pr]
                    qo = h * 2
                    ko = 4 + h * 2
                    tok = slice(c * 128, (c + 1) * 128)
                    for dc in range(2):
                        S.op("pe", lambda e: e.transpose(out=ps_kt[:, dc * 128:(dc + 1) * 128],
                                                         in_=qkb[:, ko + dc, tok], identity=ident[:]),
                             reads=[r_qkb, r_id], writes=[r_pkt] if dc == 0 else [], pwrites=[] if dc == 0 else [r_pkt],
                             inc=(dc == 1))
                    S.op("act", lambda e: e.copy(out=ktm[:], in_=ps_kt[:, 0:256]), reads=[r_pkt], writes=[r_ktm])
                    for dc in range(2):
                        S.op("pe", lambda e: e.matmul(ps_st[:, 0:128], lhsT=qkb[:, ko + dc, tok], rhs=qkb[:, qo + dc, tok],
                                                      start=(dc == 0), stop=(dc == 1)),
                             reads=[r_qkb], writes=[r_pst] if dc == 0 else [], pwrites=[] if dc == 0 else [r_pst],
                             inc=(dc == 1))
                    S.op("dve", lambda e: e.scalar_tensor_tensor(out=aT[:], in0=ps_st[:, 0:128], scalar=1.0 / 16.0,
                                                                 in1=DT_t[:], op0=ALU.mult, op1=ALU.mult),
                         reads=[r_pst, DT_r], writes=[r_aT])
                    S.op("dve", lambda e: e.scalar_tensor_tensor(
                        out=qp[:], in0=qkb[:, qo:qo + 2, tok], scalar=1.0 / 16.0,
                        in1=EB_t[:, pr, :].unsqueeze(1).to_broadcast([128, 2, 128]), op0=ALU.mult, op1=ALU.mult),
                        reads=[r_qkb, EB_r], writes=[r_qp])
                    vs = vt_t[:, h * 512:(h + 1) * 512]
                    nmm = 1 if c == 0 else 3
                    S.op("pe", lambda e: e.matmul(ps_num[:, :], lhsT=aT[:], rhs=vs, start=True, stop=(nmm == 1)),
                         reads=[r_aT, vt_r], writes=[r_pnum], inc=(nmm == 1))
                    if c > 0:
                        for dc in range(2):
                            S.op("pe", lambda e: e.matmul(ps_num[:, :], lhsT=qp[:, dc, :], rhs=Cb[:, h, dc, :],
                                                          start=False, stop=(dc == 1)),
                                 reads=[r_qp, r_Cb], pwrites=[r_pnum], inc=(dc == 1))
                    S.op("pe", lambda e: e.matmul(ps_sm[:, 0:1], lhsT=aT[:], rhs=onesb[:], start=True, stop=(nmm == 1)),
                         reads=[r_aT, r_onesb], writes=[r_psm], inc=(nmm == 1))
                    if c > 0:
                        for dc in range(2):
                            S.op("pe", lambda e: e.matmul(ps_sm[:, 0:1], lhsT=qp[:, dc, :], rhs=nb[:, h, dc:dc + 1],
                                                          start=False, stop=(dc == 1)),
                                 reads=[r_qp, r_nb], pwrites=[r_psm], inc=(dc == 1))
                    S.op("act", lambda e: e.activation(out=den[:, 0:1], in_=ps_sm[:, 0:1], func=AF.Abs),
                         reads=[r_psm], writes=[r_den])
                    S.op("dve", lambda e: e.tensor_scalar(out=den[:, 0:1], in0=den[:, 0:1], scalar1=1.0, scalar2=None,
                                                          op0=ALU.max), reads=[r_den], writes=[r_den])
                    S.op("dve", lambda e: e.reciprocal(out=den[:, 1:2], in_=den[:, 0:1]), reads=[r_den], writes=[r_den])
                    S.op("dve", lambda e: e.tensor_scalar(out=hc[:], in0=ps_num[:, :], scalar1=den[:, 1:2], scalar2=None,
                                                          op0=ALU.mult), reads=[r_pnum, r_den], writes=[r_hc])
                    S.op("pool", lambda e: e.tensor_scalar(out=vp[:], in0=vs, scalar1=ws[:, pr:pr + 1], scalar2=None,
                                                           op0=ALU.mult), reads=[vt_r, r_ws], writes=[r_vp])
                    for dc in range(2):
                        S.op("pe", lambda e: e.matmul(ps_cu[dc][0][:, :], lhsT=ktm[:, dc * 128:(dc + 1) * 128], rhs=vp[:],
                                                      start=True, stop=True),
                             reads=[r_ktm, r_vp], writes=[ps_cu[dc][1]])
                    for dc in range(2):
                        S.op("pe", lambda e: e.matmul(ps_sm[:, 2 + dc:3 + dc], lhsT=ktm[:, dc * 128:(dc + 1) * 128],
                                                      rhs=wsb[:, pr:pr + 1], start=True, stop=True),
                             reads=[r_ktm, r_wsb], pwrites=[r_psm])
                    dec = EB_t[:, pr, 127:128]
                    for dc in range(2):
                        S.op("dve", lambda e: e.scalar_tensor_tensor(out=Cst[:, h, dc, :], in0=Cst[:, h, dc, :], scalar=dec,
                                                                     in1=ps_cu[dc][0][:, :], op0=ALU.mult, op1=ALU.add),
                             reads=[r_C, EB_r, ps_cu[dc][1]], writes=[r_C])
                    S.op("act", lambda e: e.copy(out=Cb[:, h, :, :], in_=Cst[:, h, :, :]), reads=[r_C], writes=[r_Cb])
                    S.op("dve", lambda e: e.scalar_tensor_tensor(out=nst[:, h, :], in0=nst[:, h, :], scalar=dec,
                                                                 in1=ps_sm[:, 2:4], op0=ALU.mult, op1=ALU.add),
                         reads=[r_n, EB_r, r_psm], writes=[r_n])
                    S.op("act", lambda e: e.copy(out=nb[:, h, :], in_=nst[:, h, :]), reads=[r_n], writes=[r_nb])
                    S.op("act", lambda e: e.activation(out=junk[:], in_=hc[:], func=AF.Square, accum_out=den[:, 2:3]),
                         reads=[r_hc], writes=[r_junk, r_den])
                    S.op("act", lambda e: e.activation(out=den[:, 3:4], in_=den[:, 2:3], func=AF.Sqrt, scale=1.0 / 512,
                                                       bias=EPS), reads=[r_den], writes=[r_den])
                    S.op("dve", lambda e: e.reciprocal(out=den[:, 2:3], in_=den[:, 3:4]), reads=[r_den], writes=[r_den])
                    S.op("dve", lambda e: e.scalar_tensor_tensor(out=t1[:], in0=hc[:], scalar=den[:, 2:3],
                                                                 in1=onw[:, h * 512:(h + 1) * 512], op0=ALU.mult,
                                                                 op1=ALU.mult), reads=[r_hc, r_den, r_onw], writes=[r_t1])
                    S.op("act", lambda e: e.activation(out=sg[:], in_=vt_t[:, 1024 + h * 512:1024 + (h + 1) * 512],
                                                       func=AF.Sigmoid), reads=[vt_r], writes=[r_sg])
                    S.op("act", lambda e: e.activation(out=sz[:], in_=vt_t[:, 2048 + h * 512:2048 + (h + 1) * 512],
                                                       func=AF.Silu), reads=[vt_r], writes=[r_sz])
                    S.op("pool", lambda e: e.tensor_tensor(out=sg[:], in0=sg[:], in1=sz[:], op=ALU.mult),
                         reads=[r_sg, r_sz], writes=[r_sg])
                    S.op("pool", lambda e: e.tensor_tensor(out=yb[:], in0=t1[:], in1=sg[:], op=ALU.mult),
                         reads=[r_t1, r_sg], writes=[r_yb])
                    for ic in range(4):
                        S.op("pe", lambda e: e.transpose(out=ps_kt[:, 512 + ic * 128:512 + (ic + 1) * 128],
                                                         in_=yb[:, ic * 128:(ic + 1) * 128], identity=ident[:]),
                             reads=[r_yb, r_id], pwrites=[r_pkt], inc=(ic == 3))
                    q4 = c % 4
                    first = (q4 == 0 and h == 0)
                    S.op("act", lambda e: e.copy(out=oT_t[:, h * 4:(h + 1) * 4, q4 * 128:(q4 + 1) * 128],
                                                 in_=ps_kt[:, 512:1024].rearrange("p (i t) -> p i t", i=4)),
                         reads=[r_pkt], writes=[oT_r] if first else [], pwrites=[] if first else [oT_r])
                if c % 4 == 3:
                    T = c // 4
                    S.dma("sp", o_v[:, :, T * TT:(T + 1) * TT], oT_t[:], reads=[oT_r], pwrites=[r_o])


B_FORCE = 1e4
B_NEG = -1e30


def _alibi(n):
    return 2.0 ** (-8.0 * np.arange(1, n + 1) / n)


def mixer_nsa(kb, utm, r_utm, kv0T, r_kv0, ug, r_ug, prm, o_scr, r_o, G=3):
    S = kb.S
    scale = 128.0 ** -0.5
    NQ = SEQ // 128
    min_slope = float(_alibi(32)[8 * G + 7])
    with kb.scope() as sc0:
        kselT, r_kselT = sc0.sb("b_kselT", [128, SEQ], BF16)
        kwinT, r_kwinT = sc0.sb("b_kwinT", [128, SEQ], BF16)
        vsel, r_vsel = sc0.sb("b_vsel", [128, NQ, 129], BF16)
        vwin, r_vwin = sc0.sb("b_vwin", [128, NQ, 129], BF16)
        kcmpT, r_kcmpT = sc0.sb("b_kcmpT", [128, 256], BF16)
        vcmp, r_vcmp = sc0.sb("b_vcmp", [128, 2, 129], BF16)
        ovl, r_ovl = sc0.sb("b_ovl", [128, 2, 64], BF16, dma=True)
        S.op("pool", lambda e: e.memset(vsel[:], 1.0), writes=[r_vsel])
        S.op("pool", lambda e: e.memset(vwin[:], 1.0), writes=[r_vwin])
        S.op("pool", lambda e: e.memset(kcmpT[:], 0.0), writes=[r_kcmpT])
        S.op("pool", lambda e: e.memset(vcmp[:], 0.0), writes=[r_vcmp])
        S.op("pool", lambda e: e.memset(vcmp[:, :, 128:129], 1.0), writes=[r_vcmp])
        S.dma("pool", ovl[:], prm["ovl"], writes=[r_ovl])
        with kb.scope() as sc:
            ident, r_id, _, _ = make_ident(S, sc)
            knw, r_knw = sc.sb("b_knw", [128, 3, 128], F32, dma=True)
            S.dma("sp", knw[:].rearrange("p a d -> p (a d)"), prm["knw"].partition_broadcast(128), writes=[r_knw])
            kvt = [sc.sb(f"b_kvt{i}", [128, 512], BF16, dma=True) for i in range(2)]
            sq, r_sq = sc.sb("b_sq", [128, 2, 128], F32)
            ssq, r_ssq = sc.sb("b_ssq", [128, 2], F32)
            rstd, r_rstd = sc.sb("b_rstd", [128, 2], F32)
            kn, r_kn = sc.sb("b_kn", [128, 2, 128], BF16)
            pt, r_pt = sc.ps("b_pt", [128, 1024], BF16)
            for kt in range(NQ):
                kv_t, kv_r = kvt[kt % 2]
                S.dma("sp", kv_t[:], utm[kt * 128:(kt + 1) * 128, 2304:2816], reads=[r_utm], writes=[kv_r])
                kview = kv_t[:].rearrange("p (a b d) -> p a b d", a=2, b=2)
                S.op("dve", lambda e: e.tensor_tensor(out=sq[:], in0=kview[:, :, 0, :], in1=kview[:, :, 0, :], op=ALU.mult),
                     reads=[kv_r], writes=[r_sq])
                S.op("dve", lambda e: e.tensor_reduce(out=ssq[:], in_=sq[:], axis=AX.X, op=ALU.add),
                     reads=[r_sq], writes=[r_ssq])
                S.op("act", lambda e: e.activation(out=rstd[:], in_=ssq[:], func=AF.Sqrt, scale=1.0 / 128, bias=EPS),
                     reads=[r_ssq], writes=[r_rstd])
                S.op("dve", lambda e: e.reciprocal(out=rstd[:], in_=rstd[:]), reads=[r_rstd], writes=[r_rstd])
                S.op("dve", lambda e: e.tensor_tensor(out=sq[:], in0=kview[:, :, 0, :],
                                                      in1=rstd[:, :].unsqueeze(2).to_broadcast([128, 2, 128]), op=ALU.mult),
                     reads=[kv_r, r_rstd], writes=[r_sq])
                S.op("pool", lambda e: e.tensor_tensor(out=kn[:], in0=sq[:], in1=knw[:, 1:3, :], op=ALU.mult),
                     reads=[r_sq, r_knw], writes=[r_kn])
                for a in range(2):
                    S.op("pe", lambda e: e.transpose(out=pt[:, a * 128:(a + 1) * 128], in_=kn[:, a, :], identity=ident[:]),
                         reads=[r_kn, r_id], writes=[r_pt] if a == 0 else [], pwrites=[] if a == 0 else [r_pt], inc=(a == 1))
                S.op("dve", lambda e: e.tensor_copy(out=kselT[:, kt * 128:(kt + 1) * 128], in_=pt[:, 0:128]),
                     reads=[r_pt], pwrites=[r_kselT])
                S.op("act", lambda e: e.copy(out=kwinT[:, kt * 128:(kt + 1) * 128], in_=pt[:, 128:256]),
                     reads=[r_pt], pwrites=[r_kwinT])
                S.op("pool", lambda e: e.tensor_copy(out=vsel[:, kt, 0:128], in_=kview[:, 0, 1, :]),
                     reads=[kv_r], pwrites=[r_vsel])
                S.op("pool", lambda e: e.tensor_copy(out=vwin[:, kt, 0:128], in_=kview[:, 1, 1, :]),
                     reads=[kv_r], pwrites=[r_vwin])
            k0, r_k0 = sc.sb("b_k0", [128, 2, SEQ], BF16, dma=True)
            S.dma("sp", k0[:], kv0T.rearrange("(a p) t -> p a t", p=128), reads=[r_kv0], writes=[r_k0])
            peT, r_peT = sc.sb("b_peT", [128, 2, 32], F32, dma=True)
            S.dma("sp", peT[:], prm["peT"], writes=[r_peT])
            wkv, r_wkv = sc.sb("b_wkv", [128, 2, 32, 128], BF16, dma=True)
            S.dma("pool", wkv[:, 0], prm["wk"], writes=[r_wkv])
            S.dma("pool", wkv[:, 1], prm["wv"], pwrites=[r_wkv])
            kg, r_kg = sc.sb("b_kg", [128, 2, 32, 256], BF16)
            S.op("pool", lambda e: e.memset(kg[:], 0.0), writes=[r_kg])
            for a in range(2):
                for l in range(32):
                    eng = "dve" if (l % 2 == 0) else "pool"
                    S.op(eng, lambda e: e.tensor_scalar(out=kg[:, a, l, 0:255], in0=k0[:, a, l:l + 16 * 254 + 1:16],
                                                        scalar1=peT[:, a, l:l + 1], scalar2=None, op0=ALU.add),
                         reads=[r_k0, r_peT], pwrites=[r_kg])
            pc = [sc.ps(f"b_pc{i}", [128, 512], F32) for i in range(2)]
            for ct in range(2):
                M = 128 if ct == 0 else 127
                for a in range(2):
                    pc_t, pc_r = pc[a]
                    for l in range(32):
                        S.op("pe", lambda e: e.matmul(pc_t[:M, 0:128], lhsT=kg[:, a, l, ct * 128:ct * 128 + M],
                                                      rhs=wkv[:, a, l, :], start=(l == 0), stop=(l == 31)),
                             reads=[r_kg, r_wkv], writes=[pc_r] if l == 0 else [], pwrites=[] if l == 0 else [pc_r],
                             inc=(l == 31))
                pk, pk_r = pc[0]
                S.op("act", lambda e: e.activation(out=sq[:M, 0, :], in_=pk[:M, 0:128], func=AF.Square,
                                                   accum_out=ssq[:M, 0:1]), reads=[pk_r], writes=[r_sq, r_ssq])
                S.op("act", lambda e: e.activation(out=rstd[:M, 0:1], in_=ssq[:M, 0:1], func=AF.Sqrt, scale=1.0 / 128,
                                                   bias=EPS), reads=[r_ssq], writes=[r_rstd])
                S.op("dve", lambda e: e.reciprocal(out=rstd[:M, 0:1], in_=rstd[:M, 0:1]), reads=[r_rstd], writes=[r_rstd])
                S.op("dve", lambda e: e.scalar_tensor_tensor(out=kn[:M, 0, :], in0=pk[:M, 0:128], scalar=rstd[:M, 0:1],
                                                             in1=knw[:M, 0, :], op0=ALU.mult, op1=ALU.mult),
                     reads=[pk_r, r_rstd, r_knw], writes=[r_kn])
                S.op("pe", lambda e: e.transpose(out=pt[:, 0:M], in_=kn[:M, 0, :], identity=ident[:M, :M]),
                     reads=[r_kn, r_id], writes=[r_pt])
                S.op("dve", lambda e: e.tensor_copy(out=kcmpT[:, ct * 128:ct * 128 + M], in_=pt[:, 0:M]),
                     reads=[r_pt], pwrites=[r_kcmpT])
                S.op("act", lambda e: e.copy(out=vcmp[:M, ct, 0:128], in_=pc[1][0][:M, 0:128]),
                     reads=[pc[1][1]], pwrites=[r_vcmp])
        with kb.scope() as sc:
            ident, r_id, _, _ = make_ident(S, sc)
            qnw, r_qnw = sc.sb("b_qnw", [128, 128], F32, dma=True)
            S.dma("sp", qnw[:], prm["qnw"].partition_broadcast(128), writes=[r_qnw])
            BQ, r_BQ = sc.sb("b_BQ", [8, 1024], BF16, dma=True)
            AK, r_AK = sc.sb("b_AK", [8, 2, 32, 128], BF16, dma=True)
            Eall, r_E = sc.sb("b_E", [64, SEQ], BF16, dma=True)
            Wadd, r_Wadd = sc.sb("b_Wadd", [128, 128], F32, dma=True)
            Wkeep, r_Wkeep = sc.sb("b_Wkeep", [128, 128], F32, dma=True)
            S.dma("pool", BQ[:], prm["BQ"], writes=[r_BQ])
            S.dma("pool", AK[:], prm["AK"], writes=[r_AK])
            S.dma("pool", Eall[:], prm["Eall"], writes=[r_E])
            S.dma("sp", Wadd[:], prm["Wadd"], writes=[r_Wadd])
            S.dma("sp", Wkeep[:], prm["Wkeep"], writes=[r_Wkeep])
            qt = [sc.sb(f"b_qt{i}", [128, 2048], BF16, dma=True) for i in range(2)]
            gt = [sc.sb(f"b_gt{i}", [128, 24], F32, dma=True) for i in range(2)]
            sq, r_sq = sc.sb("b2_sq", [128, 1024], F32)
            ssq, r_ssq = sc.sb("b2_ssq", [128, 8], F32)
            rstd, r_rstd = sc.sb("b2_rstd", [128, 8], F32)
            qn, r_qn = sc.sb("b2_qn", [128, 1024], BF16)
            qT, r_qT = sc.sb("b2_qT", [128, 1024], BF16)
            PT = [sc.sb(f"b2_PT{i}", [128, 1024], BF16) for i in range(2)]
            msk, r_msk = sc.sb("b2_msk", [128, 128], BF16)
            oacc, r_oacc = sc.sb("b2_oacc", [128, 8, 128], F32)
            imp, r_imp = sc.sb("b2_imp", [128, 64], F32)
            imp2, r_imp2 = sc.sb("b2_imp2", [128, 64], F32)
            imp3, r_imp3 = sc.sb("b2_imp3", [128, 64], F32)
            m8, r_m8 = sc.sb("b2_m8", [128, 16], F32)
            selb, r_selb = sc.sb("b2_selb", [128, 64], BF16)
            selT, r_selT = sc.sb("b2_selT", [64, 128], BF16)
            den, r_den = sc.sb("b2_den", [128, 8], F32)
            rg, r_rg = sc.sb("b2_rg", [128, 8], F32)
            zs, r_zs = sc.sb("b2_zs", [128, 1024], F32)
            yb, r_yb = sc.sb("b2_yb", [128, 1024], BF16)
            oT = [sc.sb(f"b2_oT{i}", [128, 8, TT], BF16, dma=True) for i in range(2)]
            pst, r_pst = sc.ps("b2_pst", [128, 1024], F32)
            acc = [sc.ps(f"b2_acc{i}", [128, 512], F32) for i in range(3)]
            pmk, r_pmk = sc.ps("b2_pmk", [128, 512], F32)
            ptr, r_ptr = sc.ps("b2_ptr", [128, 1024], BF16)
            pti = 0

            def acc_of(h):
                return acc[h // 3][0][:, (h % 3) * 129:(h % 3) * 129 + 129], acc[h // 3][1]

            def attend(tiles, br, first_branch, gt_t, gt_r, want_imp):
                nonlocal pti
                nt = len(tiles)

                def stage_a(tl):
                    nonlocal pti
                    PT_t, PT_r = PT[pti % 2]
                    pti += 1
                    for half in range(2):
                        S.op("pe", lambda e: e.matmul(pst[:, half * 512:(half + 1) * 512], lhsT=tl["kT"],
                                                      rhs=qT[:, half * 512:(half + 1) * 512], start=True, stop=False),
                             reads=[tl["r_k"], r_qT], writes=[r_pst] if half == 0 else [],
                             pwrites=[] if half == 0 else [r_pst], inc=False)
                        S.op("pe", lambda e: e.matmul(pst[:, half * 512:(half + 1) * 512], lhsT=AK[:, tl["ak"], tl["m"], :],
                                                      rhs=BQ[:, half * 512:(half + 1) * 512], start=False, stop=True),
                             reads=[r_AK, r_BQ], pwrites=[r_pst], inc=(half == 1))
                    S.op("act", lambda e: e.activation(out=PT_t[:], in_=pst[:, :], func=AF.Exp, scale=scale),
                         reads=[r_pst], writes=[PT_r])
                    if tl["aff"] is not None:
                        pat, cm, base = tl["aff"]
                        S.op("pool", lambda e: e.affine_select(out=PT_t[:].rearrange("p (h q) -> p h q", h=8),
                                                               in_=PT_t[:].rearrange("p (h q) -> p h q", h=8),
                                                               pattern=[[0, 8], [pat, 128]], compare_op=ALU.is_ge, fill=0.0,
                                                               base=base, channel_multiplier=cm),
                             reads=[PT_r], writes=[PT_r])
                    return PT_t, PT_r

                def stage_b(ti, tl, PT_t, PT_r):
                    if tl["selmask_kt"] is not None:
                        kt = tl["selmask_kt"]
                        S.op("pe", lambda e: e.matmul(pmk[:, 0:128], lhsT=Eall[:, kt * 128:(kt + 1) * 128], rhs=selT[:, :],
                                                      start=True, stop=True), reads=[r_E, r_selT], writes=[r_pmk])
                        S.op("dve", lambda e: e.tensor_tensor(out=PT_t[:].rearrange("p (h q) -> p h q", h=8),
                                                              in0=PT_t[:].rearrange("p (h q) -> p h q", h=8),
                                                              in1=pmk[:, 0:128].unsqueeze(1).to_broadcast([128, 8, 128]),
                                                              op=ALU.mult), reads=[PT_r, r_pmk], writes=[PT_r])
                    for h in range(8):
                        a_ap, a_r = acc_of(h)
                        first_in_bank = (ti == 0 and h % 3 == 0)
                        last = (ti == nt - 1)
                        S.op("pe", lambda e: e.matmul(a_ap, lhsT=PT_t[:, h * 128:(h + 1) * 128], rhs=tl["vaug"],
                                                      start=first_in_bank, stop=last, skip_group_check=True),
                             reads=[PT_r, tl["r_v"]], writes=[a_r] if first_in_bank else [],
                             pwrites=[] if first_in_bank else [a_r], inc=(last and (h % 3 == 2 or h == 7)))
                    if want_imp:
                        for h in range(8):
                            S.op("pe", lambda e: e.matmul(pmk[:, h * 64:(h + 1) * 64], lhsT=PT_t[:, h * 128:(h + 1) * 128],
                                                          rhs=tl["ovl"], start=(ti == 0 and h == 0), stop=(ti == nt - 1),
                                                          skip_group_check=True),
                                 reads=[PT_r, r_ovl], writes=[r_pmk] if (ti == 0 and h == 0) else [],
                                 pwrites=[] if (ti == 0 and h == 0) else [r_pmk], inc=(ti == nt - 1 and h == 7))

                pend = stage_a(tiles[0])
                for ti in range(nt):
                    nxt = stage_a(tiles[ti + 1]) if ti + 1 < nt else None
                    stage_b(ti, tiles[ti], *pend)
                    pend = nxt
                for bk in range(3):
                    nh = 3 if bk < 2 else 2
                    S.op("dve", lambda e: e.tensor_scalar(
                        out=den[:, bk * 3:bk * 3 + nh],
                        in0=acc[bk][0][:, 0:nh * 129].rearrange("p (h c) -> p h c", c=129)[:, :, 128],
                        scalar1=1e-30, scalar2=None, op0=ALU.max), reads=[acc[bk][1]],
                        writes=[r_den] if bk == 0 else [], pwrites=[] if bk == 0 else [r_den])
                S.op("dve", lambda e: e.reciprocal(out=den[:], in_=den[:]), reads=[r_den], writes=[r_den])
                S.op("dve", lambda e: e.tensor_tensor(out=rg[:], in0=den[:],
                                                      in1=gt_t[:].rearrange("p (h b) -> p h b", b=3)[:, :, br],
                                                      op=ALU.mult), reads=[r_den, gt_r], writes=[r_rg])
                for h in range(8):
                    a_ap, a_r = acc_of(h)
                    if first_branch:
                        S.op("dve", lambda e: e.tensor_scalar(out=oacc[:, h, :], in0=a_ap[:, 0:128], scalar1=rg[:, h:h + 1],
                                                              scalar2=None, op0=ALU.mult),
                             reads=[a_r, r_rg], writes=[r_oacc] if h == 0 else [], pwrites=[] if h == 0 else [r_oacc])
                    else:
                        S.op("dve", lambda e: e.scalar_tensor_tensor(out=oacc[:, h, :], in0=a_ap[:, 0:128],
                                                                     scalar=rg[:, h:h + 1], in1=oacc[:, h, :],
                                                                     op0=ALU.mult, op1=ALU.add),
                             reads=[a_r, r_rg, r_oacc], writes=[r_oacc])
                    if want_imp:
                        if h == 0:
                            S.op("dve", lambda e: e.tensor_scalar(out=imp[:], in0=pmk[:, 0:64], scalar1=den[:, 0:1],
                                                                  scalar2=None, op0=ALU.mult),
                                 reads=[r_pmk, r_den], writes=[r_imp])
                        else:
                            S.op("dve", lambda e: e.scalar_tensor_tensor(out=imp[:], in0=pmk[:, h * 64:(h + 1) * 64],
                                                                         scalar=den[:, h:h + 1], in1=imp[:],
                                                                         op0=ALU.mult, op1=ALU.add),
                                 reads=[r_pmk, r_den, r_imp], writes=[r_imp])

            for i in range(NQ):
                t0 = i * 128
                q_t, q_r = qt[i % 2]
                gt_t, gt_r = gt[i % 2]
                oT_t, oT_r = oT[(i // 4) % 2]
                S.dma("sp", q_t[:], utm[t0:t0 + 128, 0:2048], reads=[r_utm], writes=[q_r])
                S.dma("sp", gt_t[:], ug[t0:t0 + 128, :], reads=[r_ug], writes=[gt_r])
                S.op("act", lambda e: e.activation(out=gt_t[:], in_=gt_t[:], func=AF.Sigmoid), reads=[gt_r], writes=[gt_r])
                S.op("dve", lambda e: e.tensor_tensor(out=sq[:], in0=q_t[:, 0:1024], in1=q_t[:, 0:1024], op=ALU.mult),
                     reads=[q_r], writes=[r_sq])
                S.op("dve", lambda e: e.tensor_reduce(out=ssq[:], in_=sq[:].rearrange("p (h d) -> p h d", h=8), axis=AX.X,
                                                      op=ALU.add), reads=[r_sq], writes=[r_ssq])
                S.op("act", lambda e: e.activation(out=rstd[:], in_=ssq[:], func=AF.Sqrt, scale=1.0 / 128, bias=EPS),
                     reads=[r_ssq], writes=[r_rstd])
                S.op("dve", lambda e: e.reciprocal(out=rstd[:], in_=rstd[:]), reads=[r_rstd], writes=[r_rstd])
                S.op("dve", lambda e: e.tensor_tensor(out=sq[:].rearrange("p (h d) -> p h d", h=8),
                                                      in0=q_t[:, 0:1024].rearrange("p (h d) -> p h d", h=8),
                                                      in1=rstd[:, :].unsqueeze(2).to_broadcast([128, 8, 128]), op=ALU.mult),
                     reads=[q_r, r_rstd], writes=[r_sq])
                S.op("pool", lambda e: e.tensor_tensor(out=qn[:].rearrange("p (h d) -> p h d", h=8),
                                                       in0=sq[:].rearrange("p (h d) -> p h d", h=8),
                                                       in1=qnw[:, :].unsqueeze(1).to_broadcast([128, 8, 128]), op=ALU.mult),
                     reads=[r_sq, r_qnw], writes=[r_qn])
                for h in range(8):
                    S.op("pe", lambda e: e.transpose(out=ptr[:, h * 128:(h + 1) * 128], in_=qn[:, h * 128:(h + 1) * 128],
                                                     identity=ident[:]),
                         reads=[r_qn, r_id], writes=[r_ptr] if h == 0 else [], pwrites=[] if h == 0 else [r_ptr],
                         inc=(h == 7))
                S.op("dve", lambda e: e.tensor_copy(out=qT[:], in_=ptr[:, :]), reads=[r_ptr], writes=[r_qT])
                tiles = []
                for ct in range(2):
                    P0 = 2048 * ct + 31
                    if t0 + 127 < P0:
                        continue
                    tiles.append(dict(kT=kcmpT[:, ct * 128:(ct + 1) * 128], r_k=r_kcmpT, vaug=vcmp[:, ct, :], r_v=r_vcmp,
                                      ak=1, m=i - 16 * ct, aff=(1, -16, t0 - P0), selmask_kt=None, ovl=ovl[:, ct, :]))
                attend(tiles, 0, True, gt_t, gt_r, True)
                c0 = 62 - 2 * i
                S.op("dve", lambda e: e.tensor_tensor(out=imp2[:], in0=imp[:], in1=Wkeep[:, c0:c0 + 64], op=ALU.mult),
                     reads=[r_imp, r_Wkeep], writes=[r_imp2])
                S.op("dve", lambda e: e.tensor_tensor(out=imp2[:], in0=imp2[:], in1=Wadd[:, c0:c0 + 64], op=ALU.add),
                     reads=[r_imp2, r_Wadd], writes=[r_imp2])
                if i >= 1:
                    S.op("dve", lambda e: e.tensor_scalar(out=imp2[:, 0:1], in0=imp2[:, 0:1], scalar1=B_FORCE, scalar2=None,
                                                          op0=ALU.add), reads=[r_imp2], writes=[r_imp2])
                S.op("dve", lambda e: e.max(out=m8[:, 0:8], in_=imp2[:]), reads=[r_imp2], writes=[r_m8])
                S.op("dve", lambda e: e.match_replace(out=imp3[:], in_to_replace=m8[:, 0:8], in_values=imp2[:],
                                                      imm_value=-3.0e38), reads=[r_m8, r_imp2], writes=[r_imp3])
                S.op("dve", lambda e: e.max(out=m8[:, 8:16], in_=imp3[:]), reads=[r_imp3], writes=[r_m8])
                S.op("dve", lambda e: e.tensor_scalar(out=imp3[:], in0=imp2[:], scalar1=m8[:, 15:16], scalar2=None,
                                                      op0=ALU.is_ge), reads=[r_imp2, r_m8], writes=[r_imp3])
                S.op("dve", lambda e: e.tensor_tensor(out=selb[:], in0=imp3[:], in1=Wkeep[:, c0:c0 + 64], op=ALU.mult),
                     reads=[r_imp3, r_Wkeep], writes=[r_selb])
                S.op("pe", lambda e: e.transpose(out=ptr[:64, 0:128], in_=selb[:, :], identity=ident[:]),
                     reads=[r_selb, r_id], writes=[r_ptr])
                S.op("act", lambda e: e.copy(out=selT[:], in_=ptr[:64, 0:128]), reads=[r_ptr], writes=[r_selT])
                tiles = []
                for kt in range(i + 1):
                    if (i - kt - 1) * 128 * min_slope > 160.0:
                        continue
                    tiles.append(dict(kT=kselT[:, kt * 128:(kt + 1) * 128], r_k=r_kselT, vaug=vsel[:, kt, :], r_v=r_vsel,
                                      ak=0, m=i - kt, aff=((1, -1, 0) if kt == i else None), selmask_kt=kt,
                                      ovl=None))
                attend(tiles, 1, False, gt_t, gt_r, False)
                tiles = []
                for kt in range(max(0, i - 4), i + 1):
                    aff = None
                    if kt == i:
                        aff = (1, -1, 0)
                    elif kt == i - 4:
                        aff = (-1, 1, -1)
                    tiles.append(dict(kT=kwinT[:, kt * 128:(kt + 1) * 128], r_k=r_kwinT, vaug=vwin[:, kt, :], r_v=r_vwin,
                                      ak=0, m=i - kt, aff=aff, selmask_kt=None, ovl=None))
                attend(tiles, 2, False, gt_t, gt_r, False)
                S.op("act", lambda e: e.activation(out=zs[:], in_=q_t[:, 1024:2048], func=AF.Silu), reads=[q_r], writes=[r_zs])
                S.op("pool", lambda e: e.tensor_tensor(out=yb[:], in0=oacc[:].rearrange("p h d -> p (h d)"), in1=zs[:],
                                                       op=ALU.mult), reads=[r_oacc, r_zs], writes=[r_yb])
                for h in range(8):
                    S.op("pe", lambda e: e.transpose(out=ptr[:, h * 128:(h + 1) * 128], in_=yb[:, h * 128:(h + 1) * 128],
                                                     identity=ident[:]),
                         reads=[r_yb, r_id], writes=[r_ptr] if h == 0 else [], pwrites=[] if h == 0 else [r_ptr],
                         inc=(h == 7))
                q4 = i % 4
                S.op("act", lambda e: e.copy(out=oT_t[:, :, q4 * 128:(q4 + 1) * 128],
                                             in_=ptr[:, :].rearrange("p (h t) -> p h t", h=8)),
                     reads=[r_ptr], writes=[oT_r] if q4 == 0 else [], pwrites=[] if q4 == 0 else [oT_r])
                if q4 == 3:
                    T = i // 4
                    S.dma("sp", o_scr.rearrange("(h p) t -> p h t", p=128)[:, :, T * TT:(T + 1) * TT], oT_t[:],
                          reads=[oT_r], pwrites=[r_o])


def _dram_in(nc, name, shape, dt=F32):
    return nc.dram_tensor(name, list(shape), dt, kind="ExternalInput").ap()


_SCRATCH = {
    0: (("s_qkT", [1024, SEQ], F32), ("s_utm", [SEQ, 3072], BF16), ("s_gates", [SEQ, 4], F32), ("s_o", [1024, SEQ], BF16)),
    1: (("s_utm", [SEQ, 2816], BF16), ("s_kv0T", [256, SEQ], BF16), ("s_ug", [SEQ, 24], F32), ("s_o", [1024, SEQ], BF16)),
    2: (("s_xbT", [1344, SEQ], F32), ("s_zT", [1344, SEQ], BF16), ("s_o", [1344, SEQ], BF16)),
    3: (("s_u", [SEQ, 5120], BF16), ("s_nd", [3, SEQ, 516], F32), ("s_o", [512, SEQ], BF16)),
}


def alloc_scratch(nc, S, kind, tag=""):
    scr = {}
    for name, shape, dt in _SCRATCH[kind]:
        scr[name] = (nc.dram_tensor(f"{name}{tag}", shape, dt, kind="Internal").ap(), S.res(f"{name}{tag}"))
    return scr


def emit_layer(kb, kind, x, nw, win, y, r_y, scr, accum=None, G=3):
    if kind == 2:
        (xbT, r_xb), (zT, r_z), (o_scr, r_o) = scr["s_xbT"], scr["s_zT"], scr["s_o"]
        groups = [
            dict(layout="fm", w=win["w_xb"], W=112, nchunk=12, dst=xbT, rdst=r_xb),
            dict(layout="fm", w=win["w_z"], W=112, nchunk=12, dst=zT, rdst=r_z),
        ]
        phase1(kb, x, nw, groups)
        mixer_rglru(kb, xbT, r_xb, zT, r_z, win, o_scr, r_o)
        phase3(kb, o_scr, r_o, 112, 12, win["w_out"], y, r_y, accum)
    elif kind == 0:
        (qkT, r_qk), (utm, r_utm), (gts, r_g), (o_scr, r_o) = scr["s_qkT"], scr["s_utm"], scr["s_gates"], scr["s_o"]
        groups = [
            dict(layout="fm", w=win["w_qk"], W=128, nchunk=8, dst=qkT, rdst=r_qk),
            dict(layout="tm", w=win["w_vgz"], W=256, nchunk=12, dst=utm, rdst=r_utm),
            dict(layout="tm", w=win["w_g"], W=4, nchunk=1, dst=gts, rdst=r_g),
        ]
        phase1(kb, x, nw, groups)
        mixer_mlstm(kb, qkT, r_qk, utm, r_utm, gts, r_g, win, o_scr, r_o)
        phase3(kb, o_scr, r_o, 128, 8, win["w_out"], y, r_y, accum)
    elif kind == 1:
        (utm, r_utm), (kv0T, r_kv0), (ug, r_ug), (o_scr, r_o) = scr["s_utm"], scr["s_kv0T"], scr["s_ug"], scr["s_o"]
        groups = [
            dict(layout="tm", w=win["w_tm"], W=256, nchunk=11, dst=utm, rdst=r_utm),
            dict(layout="fm", w=win["w_kv0"], W=128, nchunk=2, dst=kv0T, rdst=r_kv0),
            dict(layout="tm", w=win["w_g"], W=24, nchunk=1, dst=ug, rdst=r_ug),
        ]
        phase1(kb, x, nw, groups)
        mixer_nsa(kb, utm, r_utm, kv0T, r_kv0, ug, r_ug, win, o_scr, r_o, G)
        phase3(kb, o_scr, r_o, 128, 8, win["w_out"], y, r_y, accum)
    elif kind == 3:
        (u, r_u), (nd, r_nd), (o_scr, r_o) = scr["s_u"], scr["s_nd"], scr["s_o"]
        groups = [dict(layout="tm", w=win["w_u"], W=256, nchunk=20, dst=u, rdst=r_u)]
        phase1(kb, x, nw, groups)
        mixer_dilated(kb, u, r_u, win, o_scr, r_o, nd, r_nd)
        phase3(kb, o_scr, r_o, 128, 4, win["w_out"], y, r_y, accum)
    else:
        raise NotImplementedError


def build_layer(kind, shapes):
    nc = bass.Bass("TRN2", target_bir_lowering=False)
    x = _dram_in(nc, "x", [SEQ, D])
    nw = _dram_in(nc, "nw", [1, D])
    win = {k: _dram_in(nc, k, v) for k, v in shapes.items()}
    y = nc.dram_tensor("y", [SEQ, D], F32, kind="ExternalOutput").ap()
    with ExitStack() as st:
        kb = KB(nc, st)
        S = kb.S
        emit_layer(kb, kind, x, nw, win, y, S.res("y_out"), alloc_scratch(nc, S, kind))
        S.finish()
    return nc


def build_fused(shapes, layers=(0, 1, 2, 3), groups=(0, 1, 2, 3)):
    nc = bass.Bass("TRN2", target_bir_lowering=False)
    x = _dram_in(nc, "x", [SEQ, D])
    nwall = _dram_in(nc, "nw", [4, D])
    out = nc.dram_tensor("out", [SEQ, D], F32, kind="ExternalOutput").ap()
    xs = [nc.dram_tensor(f"s_x{i}", [SEQ, D], F32, kind="Internal").ap() for i in range(2)]
    with ExitStack() as st:
        kb = KB(nc, st)
        S = kb.S
        for li, L in enumerate(layers):
            src_ap = x if li == 0 else xs[(li - 1) % 2]
            dst_ap = out if li == len(layers) - 1 else xs[li % 2]
            r_dst = S.res(f"xres{L}")
            scr = alloc_scratch(nc, S, L, tag=f"_L{L}")
            for gi, g in enumerate(groups):
                win = {k: _dram_in(nc, f"L{L}g{g}_{k}", v) for k, v in shapes[(L, g)].items()}
                emit_layer(kb, L, src_ap, nwall[L:L + 1, :], win, dst_ap, r_dst, scr, accum=(src_ap if gi == 0 else dst_ap), G=g)
        S.finish()
        print("fused program: nins", S.nins, "nwaits", S.nwaits, flush=True)
    return nc


def build_reduce():
    nc = bass.Bass("TRN2", target_bir_lowering=False)
    R = 1024
    x = _dram_in(nc, "x", [R, D])
    ys = [_dram_in(nc, f"y{i}", [R, D]) for i in range(4)]
    out = nc.dram_tensor("out", [R, D], F32, kind="ExternalOutput").ap()
    with ExitStack() as st:
        kb = KB(nc, st)
        S = kb.S
        r_out = S.res("out")
        with kb.scope() as sc:
            bufs = [[sc.sb(f"r_b{i}_{j}", [128, D], F32, dma=True) for j in range(5)] for i in range(2)]
            for t in range(R // 128):
                bb = bufs[t % 2]
                for j, src in enumerate([x] + ys):
                    S.dma("sp" if j % 2 == 0 else "act", bb[j][0][:], src[t * 128:(t + 1) * 128, :], writes=[bb[j][1]])
                a_t, a_r = bb[0]
                for j in range(1, 5):
                    eng = "dve" if j % 2 == 1 else "pool"
                    S.op(eng, lambda e: e.tensor_tensor(out=a_t[:], in0=a_t[:], in1=bb[j][0][:], op=ALU.add),
                         reads=[a_r, bb[j][1]], writes=[a_r])
                S.dma("sp", out[t * 128:(t + 1) * 128, :], a_t[:], reads=[a_r], pwrites=[r_out])
        S.finish()
    return nc


def _chunk_w(wcols, W):
    n = wcols.shape[1] // W
    a = wcols.reshape(32, 128, n, W).transpose(2, 1, 0, 3)
    return np.ascontiguousarray(a)


def _layer2_inputs(inp, g):
    CW = 5376
    lo, hi = g * 1344, (g + 1) * 1344
    w_in = inp["c_w_in"][0]
    d = {}
    d["w_xb"] = _chunk_w(w_in[:, lo:hi], 112)
    d["w_z"] = _chunk_w(w_in[:, CW + lo:CW + hi], 112)
    d["cw"] = np.ascontiguousarray(inp["c_conv_w"][0][:, lo:hi].reshape(4, 12, 112).transpose(2, 1, 0))
    vec = np.stack([inp["c_conv_b"][0][lo:hi], inp["c_b_a"][0][lo:hi], inp["c_b_x"][0][lo:hi],
                    inp["c_lambda"][0][lo:hi]], axis=0)
    d["vec"] = np.ascontiguousarray(vec.reshape(4, 12, 112).transpose(2, 0, 1))
    for nm, key in (("wa", "c_w_a"), ("wx", "c_w_x")):
        wblk = inp[key][0][4 * g:4 * g + 4]
        d[nm] = np.ascontiguousarray(wblk.reshape(4, 3, 112, 336).transpose(2, 0, 1, 3))
    d["w_out"] = np.ascontiguousarray(inp["c_w_out"][0][lo:hi, :].reshape(12, 112, D).transpose(1, 0, 2))
    return d


def _layer0_inputs(inp, g):
    w_in = inp["a_w_in"][0]
    hs = (2 * g, 2 * g + 1)
    d = {}
    qk_cols = [w_in[:, h * 256:(h + 1) * 256] for h in hs] + [w_in[:, 2048 + h * 256:2048 + (h + 1) * 256] for h in hs]
    d["w_qk"] = _chunk_w(np.concatenate(qk_cols, axis=1), 128)
    vgz = []
    for base in (4096, 8192, 12288):
        for h in hs:
            vgz.append(w_in[:, base + h * 512:base + (h + 1) * 512])
    d["w_vgz"] = _chunk_w(np.concatenate(vgz, axis=1), 256)
    gcols = [w_in[:, 16384 + h:16385 + h] for h in hs] + [w_in[:, 16392 + h:16393 + h] for h in hs]
    d["w_g"] = _chunk_w(np.concatenate(gcols, axis=1), 4)
    idx = np.concatenate([np.arange(h * 256, (h + 1) * 256) for h in hs] +
                         [2048 + np.arange(h * 256, (h + 1) * 256) for h in hs])
    d["cw"] = np.ascontiguousarray(inp["a_conv_w"][0][:, idx].reshape(4, 8, 128).transpose(2, 1, 0))
    d["cb"] = np.ascontiguousarray(inp["a_conv_b"][0][idx].reshape(8, 128).T)
    gbv = inp["a_gate_b"][0]
    d["gb"] = np.ascontiguousarray(np.array([[gbv[0, hs[0]], gbv[0, hs[1]], gbv[1, hs[0]], gbv[1, hs[1]]]], np.float32))
    d["onw"] = np.ascontiguousarray(inp["a_out_norm_w"][0][g * 1024:(g + 1) * 1024][None, :])
    d["w_out"] = np.ascontiguousarray(inp["a_w_out"][0][g * 1024:(g + 1) * 1024, :].reshape(8, 128, D).transpose(1, 0, 2))
    return d


def _bf16_split(x):
    import ml_dtypes
    x = np.asarray(x, np.float32)
    hi = x.astype(ml_dtypes.bfloat16).astype(np.float32)
    lo = (x - hi).astype(ml_dtypes.bfloat16).astype(np.float32)
    return hi, lo


def _layer1_inputs(inp, g):
    w_in = inp["b_w_in"][0]
    d = {}
    kvc = lambda br, kvt: w_in[:, 8192 + ((br * 2 + kvt) * 4 + g) * 128: 8192 + ((br * 2 + kvt) * 4 + g) * 128 + 128]
    cols = [w_in[:, g * 1024:(g + 1) * 1024], w_in[:, 4096 + g * 1024:4096 + (g + 1) * 1024]]
    cols += [kvc(br, kvt) for br in range(3) for kvt in range(2)]
    d["w_tm"] = _chunk_w(np.concatenate(cols, axis=1), 256)
    d["w_kv0"] = _chunk_w(np.concatenate([kvc(0, 0), kvc(0, 1)], axis=1), 128)
    d["w_g"] = _chunk_w(w_in[:, 11264 + g * 24:11264 + (g + 1) * 24], 24)
    d["knw"] = np.ascontiguousarray(inp["b_k_norm_w"][0].reshape(1, 384))
    d["qnw"] = np.ascontiguousarray(inp["b_q_norm_w"][0].reshape(1, 128))
    d["peT"] = np.ascontiguousarray(inp["b_cmp_pe"][0].transpose(2, 0, 1))
    d["wk"] = np.ascontiguousarray(inp["b_cmp_wk"][0].reshape(32, 128, 128).transpose(1, 0, 2))
    d["wv"] = np.ascontiguousarray(inp["b_cmp_wv"][0].reshape(32, 128, 128).transpose(1, 0, 2))
    d["w_out"] = np.ascontiguousarray(inp["b_w_out"][0][g * 1024:(g + 1) * 1024, :].reshape(8, 128, D).transpose(1, 0, 2))
    scale = 128.0 ** -0.5
    slopes = _alibi(32)[8 * g:8 * g + 8]
    sl = (slopes / scale).astype(np.float32)
    qq = np.arange(128, dtype=np.float32)
    BQ = np.zeros((8, 8, 128), np.float32)
    for h in range(8):
        hi, lo = _bf16_split(sl[h])
        BQ[0, h, :] = hi
        BQ[1, h, :] = lo
        phi, plo = _bf16_split(sl[h] * qq)
        BQ[2, h, :] = -phi
        BQ[3, h, :] = -plo
        BQ[4, h, :] = -128.0 * hi
        BQ[5, h, :] = -128.0 * lo
        chi, clo = _bf16_split(31.0 * sl[h])
        BQ[6, h, :] = chi
        BQ[7, h, :] = clo
    d["BQ"] = np.ascontiguousarray(BQ.reshape(8, 1024))
    AK = np.zeros((8, 2, 32, 128), np.float32)
    AK[0, 0] = AK[1, 0] = np.arange(128)[None, :]
    AK[0, 1] = AK[1, 1] = 16 * np.arange(128)[None, :]
    AK[2:4] = 1.0
    AK[4] = AK[5] = np.arange(32, dtype=np.float32)[None, :, None]
    AK[6:8, 1] = 1.0
    d["AK"] = AK
    n_cmp = 255
    cmp_start = np.arange(n_cmp) * 16
    sel_start = np.arange(64) * 64
    ov = np.clip(np.minimum(cmp_start[:, None] + 32, sel_start[None, :] + 64)
                 - np.maximum(cmp_start[:, None], sel_start[None, :]), 0, None) / 32.0
    ovl = np.zeros((256, 64), np.float32)
    ovl[:255] = ov
    d["ovl"] = np.ascontiguousarray(ovl.reshape(2, 128, 64).transpose(1, 0, 2))
    E = np.zeros((64, SEQ), np.float32)
    E[np.arange(SEQ) // 64, np.arange(SEQ)] = 1.0
    d["Eall"] = E
    jj = np.arange(128)[None, :] - 62
    cur = (np.arange(128)[:, None] >= 64).astype(np.int64)
    keep = (jj <= cur)
    forced = (jj == cur) | (jj == cur - 1)
    d["Wkeep"] = keep.astype(np.float32)
    d["Wadd"] = np.where(~keep, B_NEG, np.where(forced, B_FORCE, 0.0)).astype(np.float32)
    return d


def _layer3_inputs(inp, g):
    w_in = inp["d_w_in"][0]
    cols = []
    for pat in range(3):
        for typ in range(3):
            c0 = ((pat * 3 + typ) * 16 + 4 * g) * 128
            cols.append(w_in[:, c0:c0 + 512])
    z0 = 9 * 2048 + 4 * g * 128
    cols.append(w_in[:, z0:z0 + 512])
    d = {}
    d["w_u"] = _chunk_w(np.concatenate(cols, axis=1), 256)
    d["qkw"] = np.ascontiguousarray(np.concatenate([inp["d_q_norm_w"][0], inp["d_k_norm_w"][0]])[None, :])
    slopes = _alibi(16)[4 * g:4 * g + 4]
    kk = np.arange(128)[:, None]
    qq = np.arange(128)[None, :]
    bm = np.zeros((128, 3, 2, 4, 128), np.float64)
    for p, (window, dil) in enumerate(D_PATS):
        for kind in range(2):
            steps = qq + 128 - kk if kind == 0 else qq - kk
            valid = (steps >= 0) & (steps <= 128)
            for h in range(4):
                bm[:, p, kind, h, :] = np.where(valid, np.exp(-slopes[h] * np.where(valid, steps, 0) * dil), 0.0)
    d["bm"] = np.ascontiguousarray(bm.reshape(128, -1).astype(np.float32))
    d["w_out"] = np.ascontiguousarray(inp["d_w_out"][0][g * 512:(g + 1) * 512, :].reshape(4, 128, D).transpose(1, 0, 2))
    return d


_LAYER_INPUTS = {0: _layer0_inputs, 1: _layer1_inputs, 2: _layer2_inputs, 3: _layer3_inputs}


def run_layer(kind, xcur, inp, layer_idx):
    fn = _LAYER_INPUTS[kind]
    maps = []
    nw = np.ascontiguousarray(inp["norm_w"][layer_idx][None, :])
    for c in range(NCORES):
        b, g = c // 4, c % 4
        m = fn(inp, g)
        m["x"] = xcur[b]
        m["nw"] = nw
        maps.append(m)
    shapes = {k: v.shape for k, v in maps[0].items() if k not in ("x", "nw")}
    nc = build_layer(kind, shapes)
    res = run_bass_kernel_spmd(nc, maps, core_ids=list(range(NCORES)))
    return [r["y"] for r in res.results]


def run_reduce(xcur, ys):
    flat = xcur.reshape(2 * SEQ, D)
    maps = []
    for c in range(NCORES):
        b, q = c // 4, c % 4
        m = {"x": flat[c * 1024:(c + 1) * 1024]}
        for j in range(4):
            m[f"y{j}"] = ys[b * 4 + j][q * 1024:(q + 1) * 1024]
        maps.append(m)
    nc = build_reduce()
    res = run_bass_kernel_spmd(nc, maps, core_ids=list(range(NCORES)))
    return np.concatenate([r["out"] for r in res.results], axis=0).reshape(2, SEQ, D)


def run_fused(inp, layers=(0, 1, 2, 3), groups=(0, 1, 2, 3)):
    x = np.ascontiguousarray(inp["x"], dtype=np.float32)
    nw = np.ascontiguousarray(inp["norm_w"], dtype=np.float32)
    base = {}
    for L in layers:
        for g in range(4):
            for k, v in _LAYER_INPUTS[L](inp, g).items():
                base[f"L{L}g{g}_{k}"] = v
    shapes = {(L, g): {k[len(f"L{L}g{g}_"):]: v.shape for k, v in base.items() if k.startswith(f"L{L}g{g}_")}
              for L in layers for g in range(4)}
    nc = build_fused(shapes, layers, groups)
    maps = []
    for b in range(2):
        m = dict(base)
        m["x"] = x[b]
        m["nw"] = nw
        maps.append(m)
    res = run_bass_kernel_spmd(nc, maps, core_ids=[0, 1])
    return np.stack([res.results[b]["out"] for b in range(2)], axis=0)


def kernel(**inputs):
    inp = {k: np.asarray(v) for k, v in inputs.items()}
    return run_fused(inp)


def kernel_unfused(**inputs):
    inp = {k: np.asarray(v) for k, v in inputs.items()}
    x = np.ascontiguousarray(inp["x"], dtype=np.float32)
    for layer in range(4):
        ys = run_layer(layer % 4, x, inp, layer)
        x = run_reduce(x, ys)
    return x
```

```python
import math
from contextlib import ExitStack

import numpy as np
import concourse.bass as bass
import concourse.mybir as mybir
from concourse.bass_utils import run_bass_kernel_spmd

F32 = mybir.dt.float32
BF16 = mybir.dt.bfloat16
AF = mybir.ActivationFunctionType
ALU = mybir.AluOpType
AX = mybir.AxisListType

D = 4096
SEQ = 4096
NCORES = 8
EPS = 1e-6
TT = 512
NTT = SEQ // TT
P1TT = 1024


class Res:
    __slots__ = ("name", "w", "r", "p", "fk", "dsem", "dcnt", "excl")

    def __init__(self, name):
        self.name = name
        self.p = {}
        self.fk = set()
        self.excl = False
        self.w = {}
        self.r = {}
        self.dsem = None
        self.dcnt = 0


class Sched:
    def __init__(self, nc, stack):
        self.nc = nc
        self.stack = stack
        self.E = {"pe": nc.tensor, "dve": nc.vector, "act": nc.scalar,
                  "pool": nc.gpsimd, "sp": nc.sync}
        self.sem = {}
        self.cnt = {}
        for k in ("pe", "dve", "act", "pool"):
            self.sem[k] = stack.enter_context(nc.semaphore("cs_" + k))
            self.cnt[k] = 0
        self.waited = {k: {} for k in self.E}
        self.byname = {}
        self.dsems = []
        self.nwaits = 0
        self.nins = 0

    def res(self, name, dma=False):
        if name in self.byname:
            r = self.byname[name]
        else:
            r = Res(name)
            self.byname[name] = r
        if dma and r.dsem is None:
            r.dsem = self.stack.enter_context(self.nc.semaphore("ds_" + name))
            self.dsems.append(r)
        return r

    def _wait(self, e, key, tok):
        sem, val, teng = tok
        if self.waited[e].get(key, 0) >= val:
            return
        self.E[e].wait_ge(sem, val)
        self.waited[e][key] = val
        self.nwaits += 1

    def _deps(self, e, reads, writes, pwrites):
        for r in reads:
            for key, tok in r.w.items():
                self._wait(e, key, tok)
            if r.excl:
                for key, tok in r.r.items():
                    if tok[2] != e:
                        self._wait(e, key, tok)
        for w in writes:
            for key, tok in w.w.items():
                if tok[2] != e:
                    self._wait(e, key, tok)
            for key, tok in w.r.items():
                if tok[2] != e:
                    self._wait(e, key, tok)
        for w in pwrites:
            for key, tok in w.r.items():
                if tok[2] != e:
                    self._wait(e, key, tok)
            for key, tok in w.p.items():
                if tok[2] != e:
                    self._wait(e, key, tok)
            for key in w.fk:
                tok = w.w.get(key)
                if tok is not None and tok[2] != e:
                    self._wait(e, key, tok)

    def _commit(self, key, tok, reads, writes, pwrites):
        for r in reads:
            r.r[key] = tok
        for w in writes:
            prev = dict(w.w)
            for k2, t2 in w.r.items():
                if k2 not in prev or prev[k2][1] < t2[1]:
                    prev[k2] = t2
            w.p = prev
            w.w = {key: tok}
            w.fk = {key}
            w.r = {}
        for w in pwrites:
            w.w[key] = tok

    def op(self, e, fn, reads=(), writes=(), pwrites=(), inc=True):
        self._deps(e, reads, writes, pwrites)
        ins = fn(self.E[e])
        self.nins += 1
        if inc:
            self.cnt[e] += 1
            ins.then_inc(self.sem[e], 1)
            tok = (self.sem[e], self.cnt[e], e)
        else:
            tok = (self.sem[e], self.cnt[e] + 1, e)
        self._commit("c_" + e, tok, reads, writes, pwrites)
        return ins

    def dma(self, q, out, in_, reads=(), writes=(), pwrites=(), sem_res=None, **kw):
        self._deps(q, reads, writes, pwrites)
        ins = self.E[q].dma_start(out=out, in_=in_, **kw)
        self.nins += 1
        if sem_res is None:
            for c in list(writes) + list(pwrites) + list(reads):
                if c.dsem is not None:
                    sem_res = c
                    break
        assert sem_res is not None and sem_res.dsem is not None
        sem_res.dcnt += 16
        ins.then_inc(sem_res.dsem, 16)
        tok = (sem_res.dsem, sem_res.dcnt, "dma")
        self._commit("d_" + sem_res.name, tok, reads, writes, pwrites)
        return ins

    def barrier(self, engines=("pe", "dve", "act", "pool", "sp")):
        for e in engines:
            for f in ("pe", "dve", "act", "pool"):
                if f != e and self.cnt[f] > 0:
                    self._wait(e, "c_" + f, (self.sem[f], self.cnt[f], f))
            for r in self.dsems:
                if r.dcnt > 0:
                    self._wait(e, "d_" + r.name, (r.dsem, r.dcnt, "dma"))

    def finish(self):
        self.barrier(engines=("sp",))


class KB:
    def __init__(self, nc, st):
        self.nc = nc
        self.S = Sched(nc, st)
        self.top = st
        self.uid = 0

    def scope(self):
        return _Scope(self)


class _Scope:
    def __init__(self, kb):
        self.kb = kb
        self.st = ExitStack()

    def __enter__(self):
        self.st.__enter__()
        return self

    def __exit__(self, *a):
        self.kb.S.barrier()
        return self.st.__exit__(*a)

    def sb(self, name, shape, dt, dma=False):
        self.kb.uid += 1
        t = self.st.enter_context(self.kb.nc.sbuf_tensor(f"{name}_{self.kb.uid}", shape, dt))
        return t, self.kb.S.res(name, dma)

    def ps(self, name, shape, dt):
        self.kb.uid += 1
        t = self.st.enter_context(self.kb.nc.psum_tensor(f"{name}_{self.kb.uid}", shape, dt))
        r = self.kb.S.res(name)
        r.excl = True
        return t, r


def make_ident(S, sc):
    identf, r_if = sc.sb("identf", [128, 128], F32)
    ident, r_id = sc.sb("ident", [128, 128], BF16)
    S.op("pool", lambda e: e.memset(identf[:], 0.0), writes=[r_if])
    S.op("pool", lambda e: e.affine_select(
        out=identf[:], in_=identf[:], pattern=[[-1, 128]], compare_op=ALU.not_equal,
        fill=1.0, base=0, channel_multiplier=1), reads=[r_if], writes=[r_if])
    S.op("dve", lambda e: e.tensor_copy(out=ident[:], in_=identf[:]), reads=[r_if], writes=[r_id])
    return ident, r_id, identf, r_if


def phase1(kb, x_src, nw_ap, groups):
    S = kb.S
    with kb.scope() as sc:
        ident, r_id, _, _ = make_ident(S, sc)
        nwb, r_nwb = sc.sb("p1_nwb", [128, D], F32, dma=True)
        xt = [sc.sb(f"p1_xt{i}", [128, D], F32, dma=True) for i in range(2)]
        xn = [sc.sb(f"p1_xn{i}", [128, D], BF16) for i in range(2)]
        ss = [sc.sb(f"p1_ss{i}", [128, 2], F32) for i in range(2)]
        hT, r_hT = sc.sb("p1_hT", [128, 32, P1TT], BF16)
        NWB = 2
        wb = [sc.sb(f"p1_wb{i}", [128, 32, 256], BF16, dma=True) for i in range(NWB)]
        NOB = 4
        ob = [sc.sb(f"p1_ob{i}", [128, 512], F32, dma=True) for i in range(NOB)]
        obh = [sc.sb(f"p1_obh{i}", [128, 512], BF16, dma=True) for i in range(NOB)]
        pt = [sc.ps(f"p1_pt{i}", [128, 1024], BF16) for i in range(2)]
        pm = [sc.ps(f"p1_pm{i}", [128, 512], F32) for i in range(4)]
        S.dma("sp", nwb[:], nw_ap.partition_broadcast(128), writes=[r_nwb])
        wi = 0
        oi = 0
        pi = 0
        ev = 0
        for T in range(SEQ // P1TT):
            for s in range(P1TT // 128):
                b = s % 2
                t0 = T * P1TT + s * 128
                xt_t, xt_r = xt[b]
                xn_t, xn_r = xn[b]
                ss_t, ss_r = ss[b]
                S.dma("sp", xt_t[:], x_src[t0:t0 + 128, :], writes=[xt_r])
                S.op("act", lambda e: e.activation(out=xn_t[:], in_=xt_t[:], func=AF.Square,
                                                   accum_out=ss_t[:, 0:1]),
                     reads=[xt_r], writes=[xn_r, ss_r])
                S.op("act", lambda e: e.activation(out=ss_t[:, 1:2], in_=ss_t[:, 0:1], func=AF.Sqrt,
                                                   scale=1.0 / D, bias=EPS),
                     reads=[ss_r], writes=[ss_r])
                S.op("dve", lambda e: e.reciprocal(out=ss_t[:, 0:1], in_=ss_t[:, 1:2]),
                     reads=[ss_r], writes=[ss_r])
                S.op("dve", lambda e: e.scalar_tensor_tensor(out=xn_t[:], in0=xt_t[:], scalar=ss_t[:, 0:1],
                                                             in1=nwb[:], op0=ALU.mult, op1=ALU.mult),
                     reads=[xt_r, ss_r, r_nwb], writes=[xn_r])
                for g4 in range(4):
                    pt_t, pt_r = pt[g4 % 2]
                    for j in range(8):
                        kc = g4 * 8 + j
                        S.op("pe", lambda e: e.transpose(out=pt_t[:, j * 128:(j + 1) * 128],
                                                         in_=xn_t[:, kc * 128:(kc + 1) * 128], identity=ident[:]),
                             reads=[xn_r, r_id], writes=[pt_r] if j == 0 else [],
                             pwrites=[] if j == 0 else [pt_r], inc=(j == 7))
                    dst = hT[:, g4 * 8:(g4 + 1) * 8, s * 128:(s + 1) * 128]
                    src = pt_t[:].rearrange("p (j t) -> p j t", j=8)
                    if g4 % 2 == 0:
                        S.op("dve", lambda e: e.tensor_copy(out=dst, in_=src), reads=[pt_r], pwrites=[r_hT])
                    else:
                        S.op("act", lambda e: e.copy(out=dst, in_=src), reads=[pt_r], pwrites=[r_hT])
            for g in groups:
                W = g["W"]
                odt = g["dst"].dtype
                for c in range(g["nchunk"]):
                    wb_t, wb_r = wb[wi % NWB]
                    wi += 1
                    S.dma("pool", wb_t[:, :, :W], g["w"][c], writes=[wb_r])
                    if g["layout"] == "fm":
                        for hh in range(P1TT // 512):
                            pm_t, pm_r = pm[pi % 4]
                            pi += 1
                            for kc in range(32):
                                S.op("pe", lambda e: e.matmul(pm_t[:W, :512], lhsT=wb_t[:, kc, :W],
                                                              rhs=hT[:, kc, hh * 512:(hh + 1) * 512],
                                                              start=(kc == 0), stop=(kc == 31)),
                                     reads=[wb_r, r_hT], writes=[pm_r] if kc == 0 else [],
                                     pwrites=[] if kc == 0 else [pm_r], inc=(kc == 31))
                            ob_t, ob_r = (ob if odt == F32 else obh)[oi % NOB]
                            oi += 1
                            if ev % 2 == 0:
                                S.op("dve", lambda e: e.tensor_copy(out=ob_t[:W, :512], in_=pm_t[:W, :512]),
                                     reads=[pm_r], writes=[ob_r])
                            else:
                                S.op("act", lambda e: e.copy(out=ob_t[:W, :512], in_=pm_t[:W, :512]),
                                     reads=[pm_r], writes=[ob_r])
                            ev += 1
                            c0 = T * P1TT + hh * 512
                            S.dma("sp", g["dst"][c * W:(c + 1) * W, c0:c0 + 512], ob_t[:W, :512],
                                  reads=[ob_r], pwrites=[g["rdst"]])
                    else:
                        for s in range(P1TT // 128):
                            pm_t, pm_r = pm[pi % 4]
                            pi += 1
                            for kc in range(32):
                                S.op("pe", lambda e: e.matmul(pm_t[:, :W], lhsT=hT[:, kc, s * 128:(s + 1) * 128],
                                                              rhs=wb_t[:, kc, :W],
                                                              start=(kc == 0), stop=(kc == 31)),
                                     reads=[wb_r, r_hT], writes=[pm_r] if kc == 0 else [],
                                     pwrites=[] if kc == 0 else [pm_r], inc=(kc == 31))
                            ob_t, ob_r = (ob if odt == F32 else obh)[oi % NOB]
                            oi += 1
                            if ev % 2 == 0:
                                S.op("dve", lambda e: e.tensor_copy(out=ob_t[:, :W], in_=pm_t[:, :W]),
                                     reads=[pm_r], writes=[ob_r])
                            else:
                                S.op("act", lambda e: e.copy(out=ob_t[:, :W], in_=pm_t[:, :W]),
                                     reads=[pm_r], writes=[ob_r])
                            ev += 1
                            t0 = T * P1TT + s * 128
                            S.dma("sp", g["dst"][t0:t0 + 128, c * W:(c + 1) * W], ob_t[:, :W],
                                  reads=[ob_r], pwrites=[g["rdst"]])


def phase3(kb, o_scr, r_o, P, n_ic, wout_ap, y_dst, r_y, accum=None):
    S = kb.S
    with kb.scope() as sc:
        w, r_w = sc.sb("p3_w", [P, n_ic, D], BF16, dma=True)
        for ic in range(n_ic):
            S.dma("pool", w[:, ic, :], wout_ap[:, ic, :], pwrites=[r_w])
        ot = [sc.sb(f"p3_ot{i}", [P, n_ic, TT], BF16, dma=True) for i in range(2)]
        yb = [sc.sb(f"p3_yb{i}", [128, D], F32, dma=True) for i in range(2)]
        if accum is not None:
            xds = [sc.sb(f"p3_xd{i}", [128, D], F32, dma=True) for i in range(2 if n_ic <= 8 else 1)]
        pm = [sc.ps(f"p3_pm{i}", [128, 512], F32) for i in range(4)]
        pi = 0
        yi = 0
        o_v = o_scr.rearrange("(ic p) t -> p ic t", p=P)
        for T in range(NTT):
            ot_t, ot_r = ot[T % 2]
            S.dma("sp", ot_t[:], o_v[:, :, T * TT:(T + 1) * TT], reads=[r_o], writes=[ot_r])
            for s in range(TT // 128):
                yb_t, yb_r = yb[yi % 2]
                yi += 1
                for cc in range(D // 512):
                    pm_t, pm_r = pm[pi % 4]
                    pi += 1
                    for ic in range(n_ic):
                        S.op("pe", lambda e: e.matmul(pm_t[:, :], lhsT=ot_t[:, ic, s * 128:(s + 1) * 128],
                                                      rhs=w[:, ic, cc * 512:(cc + 1) * 512],
                                                      start=(ic == 0), stop=(ic == n_ic - 1)),
                             reads=[ot_r, r_w], writes=[pm_r] if ic == 0 else [],
                             pwrites=[] if ic == 0 else [pm_r], inc=(ic == n_ic - 1))
                    if cc % 2 == 0:
                        S.op("dve", lambda e: e.tensor_copy(out=yb_t[:, cc * 512:(cc + 1) * 512], in_=pm_t[:, :]),
                             reads=[pm_r], writes=[yb_r] if cc == 0 else [], pwrites=[] if cc == 0 else [yb_r])
                    else:
                        S.op("act", lambda e: e.copy(out=yb_t[:, cc * 512:(cc + 1) * 512], in_=pm_t[:, :]),
                             reads=[pm_r], pwrites=[yb_r])
                t0 = T * TT + s * 128
                if accum is not None:
                    xd, r_xd = xds[yi % len(xds)]
                    S.dma("sp", xd[:], accum[t0:t0 + 128, :], writes=[r_xd], sem_res=r_xd)
                    hD = D // 2
                    S.op("dve", lambda e: e.tensor_tensor(out=yb_t[:, 0:hD], in0=yb_t[:, 0:hD], in1=xd[:, 0:hD], op=ALU.add),
                         reads=[yb_r, r_xd], writes=[yb_r])
                    S.op("pool", lambda e: e.tensor_tensor(out=yb_t[:, hD:D], in0=yb_t[:, hD:D], in1=xd[:, hD:D], op=ALU.add),
                         reads=[yb_r, r_xd], writes=[yb_r])
                S.dma("sp", y_dst[t0:t0 + 128, :], yb_t[:], reads=[yb_r], pwrites=[r_y], sem_res=yb_r)


C_P = 112
C_NCH = 12


def mixer_rglru(kb, xbT, r_xb, zT, r_z, prm, o_scr, r_o):
    S = kb.S
    P = C_P
    with kb.scope() as sc:
        cw, r_cw = sc.sb("c_cw", [P, C_NCH, 4], F32, dma=True)
        vec, r_vec = sc.sb("c_vec", [P, 4, C_NCH], F32, dma=True)
        c1, r_c1 = sc.sb("c_c1", [P, C_NCH], F32)
        wa, r_wa = sc.sb("c_wa", [P, 4, 3, 336], BF16, dma=True)
        wx, r_wx = sc.sb("c_wx", [P, 4, 3, 336], BF16, dma=True)
        S.dma("sp", cw[:], prm["cw"], writes=[r_cw])
        S.dma("sp", vec[:], prm["vec"], writes=[r_vec])
        S.dma("pool", wa[:], prm["wa"], writes=[r_wa])
        S.dma("pool", wx[:], prm["wx"], writes=[r_wx])
        S.op("act", lambda e: e.activation(out=c1[:], in_=vec[:, 3, :], func=AF.Exp, scale=-1.0),
             reads=[r_vec], writes=[r_c1])
        S.op("act", lambda e: e.activation(out=c1[:], in_=c1[:], func=AF.Ln, bias=1.0, scale=1.0),
             reads=[r_c1], writes=[r_c1])
        S.op("dve", lambda e: e.tensor_scalar(out=c1[:], in0=c1[:], scalar1=-8.0, scalar2=None, op0=ALU.mult),
             reads=[r_c1], writes=[r_c1])
        xb, r_xbs = sc.sb("c_xb", [P, SEQ + 4], F32, dma=True)
        xc = [sc.sb(f"c_xc{i}", [P, SEQ], F32) for i in range(3)]
        xcb, r_xcb = sc.sb("c_xcb", [P, 3, SEQ], BF16)
        ra, r_ra = sc.sb("c_ra", [P, SEQ], F32)
        gi, r_gi = sc.sb("c_gi", [P, SEQ], F32)
        tmp, r_tmp = sc.sb("c_tmp", [P, SEQ], F32)
        zt, r_zt = sc.sb("c_zt", [P, SEQ], BF16, dma=True)
        ob, r_ob = sc.sb("c_ob", [P, SEQ], BF16, dma=True)
        pm = [sc.ps(f"c_pm{i}", [128, 512], F32) for i in range(4)]
        pi = 0
        S.op("pool", lambda e: e.memset(xb[:, 0:4], 0.0), writes=[r_xbs])
        for n in range(4):
            for c in range(3):
                ch = n * 3 + c
                xc_t, xc_r = xc[c]
                S.dma("sp", xb[:, 4:], xbT[ch * P:(ch + 1) * P, :], reads=[r_xb], writes=[r_xbs])
                S.op("dve", lambda e: e.tensor_scalar(out=xc_t[:], in0=xb[:, 1:1 + SEQ], scalar1=cw[:, ch, 0:1],
                                                      scalar2=vec[:, 0, ch:ch + 1], op0=ALU.mult, op1=ALU.add),
                     reads=[r_xbs, r_cw, r_vec], writes=[xc_r])
                for j in range(1, 4):
                    S.op("dve", lambda e: e.scalar_tensor_tensor(out=xc_t[:], in0=xb[:, 1 + j:1 + j + SEQ],
                                                                 scalar=cw[:, ch, j:j + 1], in1=xc_t[:],
                                                                 op0=ALU.mult, op1=ALU.add),
                         reads=[r_xbs, r_cw, xc_r], writes=[xc_r])
                S.op("act", lambda e: e.copy(out=xcb[:, c, :], in_=xc_t[:]), reads=[xc_r],
                     writes=[r_xcb] if c == 0 else [], pwrites=[] if c == 0 else [r_xcb])
            for d in range(3):
                ch = n * 3 + d
                xc_t, xc_r = xc[d]
                S.dma("sp", zt[:], zT[ch * P:(ch + 1) * P, :], reads=[r_z], writes=[r_zt])
                for (wt, wr, dst, dr, bi) in ((wa, r_wa, ra, r_ra, 1), (wx, r_wx, gi, r_gi, 2)):
                    for T in range(NTT):
                        pm_t, pm_r = pm[pi % 4]
                        pi += 1
                        for c in range(3):
                            S.op("pe", lambda e: e.matmul(pm_t[:P, :TT], lhsT=wt[:, n, c, d * P:(d + 1) * P],
                                                          rhs=xcb[:, c, T * TT:(T + 1) * TT],
                                                          start=(c == 0), stop=(c == 2)),
                                 reads=[wr, r_xcb], writes=[pm_r] if c == 0 else [],
                                 pwrites=[] if c == 0 else [pm_r], inc=(c == 2))
                        S.op("act", lambda e: e.activation(out=dst[:, T * TT:(T + 1) * TT], in_=pm_t[:P, :TT],
                                                           func=AF.Sigmoid, bias=vec[:, bi, ch:ch + 1], scale=1.0),
                             reads=[pm_r, r_vec], writes=[dr] if T == 0 else [], pwrites=[] if T == 0 else [dr])
                S.op("act", lambda e: e.activation(out=ra[:], in_=ra[:], func=AF.Exp, scale=c1[:, ch:ch + 1]),
                     reads=[r_ra, r_c1], writes=[r_ra])
                S.op("pool", lambda e: e.tensor_tensor(out=tmp[:], in0=ra[:], in1=ra[:], op=ALU.mult),
                     reads=[r_ra], writes=[r_tmp])
                S.op("act", lambda e: e.activation(out=tmp[:], in_=tmp[:], func=AF.Sqrt, scale=-1.0, bias=1.0),
                     reads=[r_tmp], writes=[r_tmp])
                S.op("dve", lambda e: e.tensor_tensor(out=gi[:], in0=gi[:], in1=xc_t[:], op=ALU.mult),
                     reads=[r_gi, xc_r], writes=[r_gi])
                S.op("pool", lambda e: e.tensor_tensor(out=gi[:], in0=gi[:], in1=tmp[:], op=ALU.mult),
                     reads=[r_gi, r_tmp], writes=[r_gi])
                S.op("dve", lambda e: e.tensor_tensor_scan(out=tmp[:], data0=ra[:], data1=gi[:], initial=0.0,
                                                           op0=ALU.mult, op1=ALU.add),
                     reads=[r_ra, r_gi], writes=[r_tmp])
                S.op("act", lambda e: e.activation(out=zt[:], in_=zt[:], func=AF.Silu), reads=[r_zt], writes=[r_zt])
                S.op("pool", lambda e: e.tensor_tensor(out=ob[:], in0=tmp[:], in1=zt[:], op=ALU.mult),
                     reads=[r_tmp, r_zt], writes=[r_ob])
                S.dma("sp", o_scr[ch * P:(ch + 1) * P, :], ob[:], reads=[r_ob], pwrites=[r_o])


D_PATS = ((128, 1), (512, 4), (2048, 16))


def mixer_dilated(kb, u, r_u, prm, o_scr, r_o, nd_scr, r_nd):
    S = kb.S
    scale = 128.0 ** -0.5
    with kb.scope() as sc:
        ident, r_id, _, _ = make_ident(S, sc)
        wqk, r_wqk = sc.sb("d_wqk", [128, 2, 128], F32, dma=True)
        bm, r_bm = sc.sb("d_bm", [128, 3, 2, 512], F32, dma=True)
        S.dma("sp", wqk[:].rearrange("p a d -> p (a d)"), prm["qkw"].partition_broadcast(128), writes=[r_wqk])
        S.dma("sp", bm[:].rearrange("p a b c -> p (a b c)"), prm["bm"], writes=[r_bm])
        blk = [sc.sb(f"d_blk{i}", [128, 1536], BF16, dma=True) for i in range(2)]
        sq, r_sq = sc.sb("d_sq", [128, 1024], F32)
        ssq, r_ssq = sc.sb("d_ssq", [128, 8], F32)
        rstd, r_rstd = sc.sb("d_rstd", [128, 8], F32)
        qkn, r_qkn = sc.sb("d_qkn", [128, 1024], BF16)
        qT, r_qT = sc.sb("d_qT", [128, 512], BF16)
        kT = [sc.sb(f"d_kT{i}", [128, 512], BF16) for i in range(2)]
        va = [sc.sb(f"d_va{i}", [128, 4, 129], BF16) for i in range(2)]
        pex = [sc.sb(f"d_pex{i}", [128, 512], F32) for i in range(2)]
        ptm = [sc.sb(f"d_ptm{i}", [128, 512], BF16) for i in range(2)]
        ndst = [sc.sb(f"d_ndst{i}", [128, 4, 129], F32, dma=True) for i in range(2)]
        pt, r_pt = sc.ps("d_pt", [128, 1024], BF16)
        pss = [sc.ps(f"d_pss{i}", [128, 512], F32) for i in range(4)]
        accs = [sc.ps(f"d_acc{i}", [128, 512], F32) for i in range(2)]
        for i in range(2):
            S.op("pool", lambda e: e.memset(va[i][0][:], 1.0), writes=[va[i][1]])
        bi = 0
        si = 0
        for p, (window, dil) in enumerate(D_PATS):
            uv = u.rearrange("(n dl) c -> dl n c", dl=dil)
            ndv = nd_scr[p].rearrange("(n dl) c -> dl n c", dl=dil)
            for r in range(dil):
                for i in range(SEQ // dil // 128):
                    blk_t, blk_r = blk[bi % 2]
                    nd_t, nd_r = ndst[bi % 2]
                    bi += 1
                    cur = i % 2
                    kT_t, kT_r = kT[cur]
                    va_t, va_r = va[cur]
                    S.dma("sp", blk_t[:], uv[r, i * 128:(i + 1) * 128, p * 1536:(p + 1) * 1536],
                          reads=[r_u], writes=[blk_r])
                    S.op("dve", lambda e: e.tensor_tensor(out=sq[:], in0=blk_t[:, 0:1024], in1=blk_t[:, 0:1024],
                                                          op=ALU.mult), reads=[blk_r], writes=[r_sq])
                    S.op("dve", lambda e: e.tensor_reduce(out=ssq[:], in_=sq[:].rearrange("p (h d) -> p h d", h=8),
                                                          axis=AX.X, op=ALU.add), reads=[r_sq], writes=[r_ssq])
                    S.op("act", lambda e: e.activation(out=rstd[:], in_=ssq[:], func=AF.Sqrt, scale=1.0 / 128, bias=EPS),
                         reads=[r_ssq], writes=[r_rstd])
                    S.op("dve", lambda e: e.reciprocal(out=rstd[:], in_=rstd[:]), reads=[r_rstd], writes=[r_rstd])
                    S.op("dve", lambda e: e.tensor_tensor(
                        out=sq[:].rearrange("p (h d) -> p h d", h=8),
                        in0=blk_t[:, 0:1024].rearrange("p (h d) -> p h d", h=8),
                        in1=rstd[:, :].unsqueeze(2).to_broadcast([128, 8, 128]), op=ALU.mult),
                        reads=[blk_r, r_rstd], writes=[r_sq])
                    S.op("pool", lambda e: e.tensor_tensor(
                        out=qkn[:].rearrange("p (a h d) -> p a h d", a=2, h=4),
                        in0=sq[:].rearrange("p (a h d) -> p a h d", a=2, h=4),
                        in1=wqk[:, :, :].unsqueeze(2).to_broadcast([128, 2, 4, 128]), op=ALU.mult),
                        reads=[r_sq, r_wqk], writes=[r_qkn])
                    for j in range(8):
                        S.op("pe", lambda e: e.transpose(out=pt[:, j * 128:(j + 1) * 128],
                                                         in_=qkn[:, j * 128:(j + 1) * 128], identity=ident[:]),
                             reads=[r_qkn, r_id], writes=[r_pt] if j == 0 else [],
                             pwrites=[] if j == 0 else [r_pt], inc=(j == 7))
                    S.op("dve", lambda e: e.tensor_copy(out=qT[:], in_=pt[:, 0:512]), reads=[r_pt], writes=[r_qT])
                    S.op("act", lambda e: e.copy(out=kT_t[:], in_=pt[:, 512:1024]), reads=[r_pt], writes=[kT_r])
                    S.op("pool", lambda e: e.tensor_copy(out=va_t[:, :, 0:128],
                                                         in_=blk_t[:, 1024:1536].rearrange("p (h d) -> p h d", h=4)),
                         reads=[blk_r], writes=[va_r])
                    tiles = ([(1 - cur, 0)] if i > 0 else []) + [(cur, 1)]
                    pts = []
                    for ti, (slot, kind) in enumerate(tiles):
                        ps_t, ps_r = pss[si % 4]
                        pe_t, pe_r = pex[si % 2]
                        pm_t, pm_r = ptm[si % 2]
                        si += 1
                        for h in range(4):
                            S.op("pe", lambda e: e.matmul(ps_t[:, h * 128:(h + 1) * 128],
                                                          lhsT=kT[slot][0][:, h * 128:(h + 1) * 128],
                                                          rhs=qT[:, h * 128:(h + 1) * 128], start=True, stop=True),
                                 reads=[kT[slot][1], r_qT], writes=[ps_r] if h == 0 else [],
                                 pwrites=[] if h == 0 else [ps_r], inc=(h == 3))
                        S.op("act", lambda e: e.activation(out=pe_t[:], in_=ps_t[:], func=AF.Exp, scale=scale),
                             reads=[ps_r], writes=[pe_r])
                        S.op("dve", lambda e: e.tensor_tensor(out=pm_t[:], in0=pe_t[:], in1=bm[:, p, kind, :],
                                                              op=ALU.mult), reads=[pe_r, r_bm], writes=[pm_r])
                        pts.append((pm_t, pm_r, slot))
                    for h in range(4):
                        off = (h % 2) * 129
                        acc, r_acc = accs[h // 2]
                        for ti, (pm_t, pm_r, slot) in enumerate(pts):
                            last = (ti == len(pts) - 1)
                            S.op("pe", lambda e: e.matmul(acc[:, off:off + 129], lhsT=pm_t[:, h * 128:(h + 1) * 128],
                                                          rhs=va[slot][0][:, h, :], start=(ti == 0), stop=last),
                                 reads=[pm_r, va[slot][1]],
                                 writes=[r_acc] if (ti == 0 and h % 2 == 0) else [],
                                 pwrites=[] if (ti == 0 and h % 2 == 0) else [r_acc],
                                 inc=(last and h % 2 == 1))
                    S.op("dve", lambda e: e.tensor_copy(
                        out=nd_t[:, 0:2, :], in_=accs[0][0][:, 0:258].rearrange("p (h c) -> p h c", h=2)),
                        reads=[accs[0][1]], writes=[nd_r])
                    S.op("act", lambda e: e.copy(
                        out=nd_t[:, 2:4, :], in_=accs[1][0][:, 0:258].rearrange("p (h c) -> p h c", h=2)),
                        reads=[accs[1][1]], pwrites=[nd_r])
                    S.dma("sp", ndv[r, i * 128:(i + 1) * 128, :], nd_t[:].rearrange("p h c -> p (h c)"),
                          reads=[nd_r], pwrites=[r_nd])
    with kb.scope() as sc:
        ident, r_id, _, _ = make_ident(S, sc)
        nds = [[sc.sb(f"d2_nd{i}_{j}", [128, 4, 129], F32, dma=True) for j in range(3)] for i in range(2)]
        zt = [sc.sb(f"d2_z{i}", [128, 512], BF16, dma=True) for i in range(2)]
        rden, r_rden = sc.sb("d2_rden", [128, 4], F32)
        of, r_of = sc.sb("d2_of", [128, 4, 128], F32)
        zs, r_zs = sc.sb("d2_zs", [128, 512], F32)
        ob, r_ob = sc.sb("d2_ob", [128, 512], BF16)
        oT = [sc.sb(f"d2_oT{i}", [128, 4, TT], BF16, dma=True) for i in range(2)]
        pt, r_pt = sc.ps("d2_pt", [128, 1024], BF16)
        for t in range(SEQ // 128):
            a = nds[t % 2]
            z_t, z_r = zt[t % 2]
            oT_t, oT_r = oT[(t // 4) % 2]
            for j in range(3):
                S.dma("sp", a[j][0][:].rearrange("p h c -> p (h c)"), nd_scr[j, t * 128:(t + 1) * 128, :],
                      reads=[r_nd], writes=[a[j][1]])
            S.dma("sp", z_t[:], u[t * 128:(t + 1) * 128, 4608:5120], reads=[r_u], writes=[z_r])
            s_t, s_r = a[0]
            S.op("dve", lambda e: e.tensor_tensor(out=s_t[:], in0=s_t[:], in1=a[1][0][:], op=ALU.add),
                 reads=[s_r, a[1][1]], writes=[s_r])
            S.op("pool", lambda e: e.tensor_tensor(out=s_t[:], in0=s_t[:], in1=a[2][0][:], op=ALU.add),
                 reads=[s_r, a[2][1]], writes=[s_r])
            S.op("dve", lambda e: e.reciprocal(out=rden[:], in_=s_t[:, :, 128]), reads=[s_r], writes=[r_rden])
            S.op("dve", lambda e: e.tensor_tensor(out=of[:], in0=s_t[:, :, 0:128],
                                                  in1=rden[:, :].unsqueeze(2).to_broadcast([128, 4, 128]), op=ALU.mult),
                 reads=[s_r, r_rden], writes=[r_of])
            S.op("act", lambda e: e.activation(out=zs[:], in_=z_t[:], func=AF.Silu), reads=[z_r], writes=[r_zs])
            S.op("pool", lambda e: e.tensor_tensor(out=ob[:], in0=of[:].rearrange("p h d -> p (h d)"), in1=zs[:],
                                                   op=ALU.mult), reads=[r_of, r_zs], writes=[r_ob])
            for h in range(4):
                S.op("pe", lambda e: e.transpose(out=pt[:, h * 128:(h + 1) * 128], in_=ob[:, h * 128:(h + 1) * 128],
                                                 identity=ident[:]),
                     reads=[r_ob, r_id], writes=[r_pt] if h == 0 else [], pwrites=[] if h == 0 else [r_pt],
                     inc=(h == 3))
            q4 = t % 4
            S.op("act", lambda e: e.copy(out=oT_t[:, :, q4 * 128:(q4 + 1) * 128],
                                         in_=pt[:, 0:512].rearrange("p (h t) -> p h t", h=4)),
                 reads=[r_pt], writes=[oT_r] if q4 == 0 else [], pwrites=[] if q4 == 0 else [oT_r])
            if q4 == 3:
                T = t // 4
                S.dma("sp", o_scr.rearrange("(h p) t -> p h t", p=128)[:, :, T * TT:(T + 1) * TT], oT_t[:],
                      reads=[oT_r], pwrites=[r_o])


def mixer_mlstm(kb, qkT, r_qk, utm, r_utm, gates, r_g, prm, o_scr, r_o):
    S = kb.S
    NCH = SEQ // 128
    with kb.scope() as sc0:
        qkb, r_qkb = sc0.sb("a_qkb", [128, 8, SEQ], BF16)
        with kb.scope() as sc:
            cw, r_cw = sc.sb("a_cw", [128, 8, 4], F32, dma=True)
            cb, r_cb = sc.sb("a_cb", [128, 8], F32, dma=True)
            S.dma("sp", cw[:], prm["cw"], writes=[r_cw])
            S.dma("sp", cb[:], prm["cb"], writes=[r_cb])
            xb = [sc.sb(f"a_xb{i}", [128, SEQ + 4], F32, dma=True) for i in range(2)]
            xc, r_xc = sc.sb("a_xc", [128, SEQ], F32)
            for i in range(2):
                S.op("pool", lambda e: e.memset(xb[i][0][:, 0:4], 0.0), writes=[xb[i][1]])
            for ch in range(8):
                xb_t, xb_r = xb[ch % 2]
                S.dma("sp", xb_t[:, 4:], qkT[ch * 128:(ch + 1) * 128, :], reads=[r_qk], writes=[xb_r])
                S.op("dve", lambda e: e.tensor_scalar(out=xc[:], in0=xb_t[:, 1:1 + SEQ], scalar1=cw[:, ch, 0:1],
                                                      scalar2=cb[:, ch:ch + 1], op0=ALU.mult, op1=ALU.add),
                     reads=[xb_r, r_cw, r_cb], writes=[r_xc])
                for j in range(1, 4):
                    S.op("dve", lambda e: e.scalar_tensor_tensor(out=xc[:], in0=xb_t[:, 1 + j:1 + j + SEQ],
                                                                 scalar=cw[:, ch, j:j + 1], in1=xc[:],
                                                                 op0=ALU.mult, op1=ALU.add),
                         reads=[xb_r, r_cw, r_xc], writes=[r_xc])
                S.op("act", lambda e: e.activation(out=qkb[:, ch, :], in_=xc[:], func=AF.Silu),
                     reads=[r_xc], writes=[r_qkb] if ch == 0 else [], pwrites=[] if ch == 0 else [r_qkb])
        with kb.scope() as sc:
            ident, r_id, identf, r_if = make_ident(S, sc)
            triu, r_tri = sc.sb("a_triu", [128, 128], F32)
            ones, r_ones = sc.sb("a_ones", [128, 128], F32)
            onesb, r_onesb = sc.sb("a_onesb", [128, 1], BF16)
            S.op("pool", lambda e: e.memset(triu[:], 1.0), writes=[r_tri])
            S.op("pool", lambda e: e.affine_select(out=triu[:], in_=triu[:], pattern=[[1, 128]],
                                                   compare_op=ALU.is_ge, fill=0.0, base=0, channel_multiplier=-1),
                 reads=[r_tri], writes=[r_tri])
            S.op("pool", lambda e: e.memset(ones[:], 1.0), writes=[r_ones])
            S.op("pool", lambda e: e.memset(onesb[:], 1.0), writes=[r_onesb])
            onw, r_onw = sc.sb("a_onw", [128, 1024], F32, dma=True)
            S.dma("sp", onw[:], prm["onw"].partition_broadcast(128), writes=[r_onw])
            gb, r_gb = sc.sb("a_gb", [128, 4], F32, dma=True)
            S.dma("sp", gb[:], prm["gb"].partition_broadcast(128), writes=[r_gb])
            G, r_G = sc.sb("a_G", [128, NCH, 4], F32, dma=True)
            gv = gates.rearrange("(c p) g -> p c g", p=128)
            for i4 in range(4):
                S.dma("sp", G[:, i4 * 8:(i4 + 1) * 8, :], gv[:, i4 * 8:(i4 + 1) * 8, :], reads=[r_g],
                      writes=[r_G] if i4 == 0 else [], pwrites=[] if i4 == 0 else [r_G])
            S.op("dve", lambda e: e.tensor_tensor(out=G[:], in0=G[:], in1=gb[:, :].unsqueeze(1).to_broadcast([128, NCH, 4]),
                                                  op=ALU.add), reads=[r_G, r_gb], writes=[r_G])
            lf, r_lf = sc.sb("a_lf", [128, NCH, 2], F32)
            S.op("act", lambda e: e.activation(out=lf[:], in_=G[:, :, 2:4], func=AF.Exp, scale=-1.0),
                 reads=[r_G], writes=[r_lf])
            S.op("act", lambda e: e.activation(out=lf[:], in_=lf[:], func=AF.Ln, bias=1.0, scale=1.0),
                 reads=[r_lf], writes=[r_lf])
            S.op("dve", lambda e: e.tensor_scalar(out=lf[:], in0=lf[:], scalar1=-1.0, scalar2=None, op0=ALU.mult),
                 reads=[r_lf], writes=[r_lf])
            ps_b, r_psb = sc.ps("a_psb", [128, 512], F32)
            S.op("pe", lambda e: e.matmul(ps_b[:, 0:2 * NCH], lhsT=triu[:], rhs=lf[:].rearrange("p c h -> p (c h)"),
                                          start=True, stop=True), reads=[r_tri, r_lf], writes=[r_psb])
            cs, r_cs = sc.sb("a_cs", [128, NCH, 2], F32)
            ecs, r_ecs = sc.sb("a_ecs", [128, NCH, 2], F32)
            S.op("dve", lambda e: e.tensor_tensor(out=cs[:], in0=G[:, :, 0:2],
                                                  in1=ps_b[:, 0:2 * NCH].rearrange("p (c h) -> p c h", h=2),
                                                  op=ALU.subtract), reads=[r_G, r_psb], writes=[r_cs])
            S.op("act", lambda e: e.activation(out=ecs[:], in_=cs[:], func=AF.Exp), reads=[r_cs], writes=[r_ecs])
            Cst, r_C = sc.sb("a_C", [128, 2, 2, 512], F32)
            Cb, r_Cb = sc.sb("a_Cb", [128, 2, 2, 512], BF16)
            nst, r_n = sc.sb("a_n", [128, 2, 2], F32)
            nb, r_nb = sc.sb("a_nb", [128, 2, 2], BF16)
            S.op("pool", lambda e: e.memset(Cst[:], 0.0), writes=[r_C])
            S.op("pool", lambda e: e.memset(nst[:], 0.0), writes=[r_n])
            LT, r_LT = sc.sb("a_LT", [128, 4, 128], F32)
            EB = [sc.sb(f"a_EB{i}", [128, 4, 128], F32) for i in range(2)]
            DT = [sc.sb(f"a_DT{i}", [128, 128], F32) for i in range(4)]
            ws, r_ws = sc.sb("a_ws", [128, 4], F32)
            wsb, r_wsb = sc.sb("a_wsb", [128, 4], BF16)
            vt = [sc.sb(f"a_vt{i}", [128, 3072], BF16, dma=True) for i in range(2)]
            aT, r_aT = sc.sb("a_aT", [128, 128], BF16)
            qp, r_qp = sc.sb("a_qp", [128, 2, 128], BF16)
            vp, r_vp = sc.sb("a_vp", [128, 512], BF16)
            ktm, r_ktm = sc.sb("a_ktm", [128, 256], BF16)
            den, r_den = sc.sb("a_den", [128, 4], F32)
            hc, r_hc = sc.sb("a_hc", [128, 512], F32)
            junk, r_junk = sc.sb("a_junk", [128, 512], F32)
            t1, r_t1 = sc.sb("a_t1", [128, 512], F32)
            sg, r_sg = sc.sb("a_sg", [128, 512], F32)
            sz, r_sz = sc.sb("a_sz", [128, 512], F32)
            yb, r_yb = sc.sb("a_yb", [128, 512], BF16)
            oT = [sc.sb(f"a_oT{i}", [128, 8, TT], BF16, dma=True) for i in range(2)]
            ps_row, r_prow = sc.ps("a_prow", [128, 512], F32)
            ps_st, r_pst = sc.ps("a_pst", [128, 512], F32)
            ps_num, r_pnum = sc.ps("a_pnum", [128, 512], F32)
            ps_sm, r_psm = sc.ps("a_psm", [128, 512], F32)
            ps_cu = [sc.ps(f"a_pcu{i}", [128, 512], F32) for i in range(2)]
            ps_kt, r_pkt = sc.ps("a_pkt", [128, 1024], BF16)
            o_v = o_scr.rearrange("(ic p) t -> p ic t", p=128)
            for c in range(NCH):
                vt_t, vt_r = vt[c % 2]
                S.dma("sp", vt_t[:], utm[c * 128:(c + 1) * 128, :], reads=[r_utm], writes=[vt_r])
                if c % 2 == 0:
                    EB_t, EB_r = EB[(c // 2) % 2]
                    S.op("dve", lambda e: e.tensor_tensor(
                        out=LT[:], in0=triu[:, :].unsqueeze(1).to_broadcast([128, 4, 128]),
                        in1=lf[:, c:c + 2, :].rearrange("p c h -> p (c h)").unsqueeze(2).to_broadcast([128, 4, 128]),
                        op=ALU.mult), reads=[r_tri, r_lf], writes=[r_LT])
                    S.op("pe", lambda e: e.matmul(ps_row[:, :], lhsT=ones[:], rhs=LT[:].rearrange("p a t -> p (a t)"),
                                                  start=True, stop=True), reads=[r_ones, r_LT], writes=[r_prow])
                    S.op("act", lambda e: e.activation(out=EB_t[:].rearrange("p a t -> p (a t)"), in_=ps_row[:, :],
                                                       func=AF.Exp), reads=[r_prow], writes=[EB_r])
                    for pr in range(4):
                        idx = c * 2 + pr
                        S.op("act", lambda e: e.activation(out=DT[pr][0][:], in_=ps_row[:, pr * 128:(pr + 1) * 128],
                                                           func=AF.Exp,
                                                           bias=cs[:, idx // 2, (idx % 2):(idx % 2) + 1], scale=1.0),
                             reads=[r_prow, r_cs], writes=[DT[pr][1]])
                        S.op("pool", lambda e: e.tensor_tensor(out=DT[pr][0][:], in0=DT[pr][0][:], in1=triu[:],
                                                               op=ALU.mult), reads=[DT[pr][1], r_tri], writes=[DT[pr][1]])
                    S.op("dve", lambda e: e.tensor_tensor(out=ws[:], in0=ecs[:, c:c + 2, :].rearrange("p c h -> p (c h)"),
                                                          in1=EB_t[:, :, 127], op=ALU.mult),
                         reads=[r_ecs, EB_r], writes=[r_ws])
                    S.op("act", lambda e: e.copy(out=wsb[:], in_=ws[:]), reads=[r_ws], writes=[r_wsb])
                EB_t, EB_r = EB[(c // 2) % 2]
                oT_t, oT_r = oT[(c // 4) % 2]
                for h in range(2):
                    pr = (c % 2) * 2 + h
                    DT_t, DT_r = DT[pr]
                    qo = h * 2
                    ko = 4 + h * 2
                    tok = slice(c * 128, (c + 1) * 128)
                    for dc in range(2):
                        S.op("pe", lambda e: e.transpose(out=ps_kt[:, dc * 128:(dc + 1) * 128],
                                                         in_=qkb[:, ko + dc, tok], identity=ident[:]),
                             reads=[r_qkb, r_id], writes=[r_pkt] if dc == 0 else [], pwrites=[] if dc == 0 else [r_pkt],
                             inc=(dc == 1))
                    S.op("act", lambda e: e.copy(out=ktm[:], in_=ps_kt[:, 0:256]), reads=[r_pkt], writes=[r_ktm])
                    for dc in range(2):
                        S.op("pe", lambda e: e.matmul(ps_st[:, 0:128], lhsT=qkb[:, ko + dc, tok], rhs=qkb[:, qo + dc, tok],
                                                      start=(dc == 0), stop=(dc == 1)),
                             reads=[r_qkb], writes=[r_pst] if dc == 0 else [], pwrites=[] if dc == 0 else [r_pst],
                             inc=(dc == 1))
                    S.op("dve", lambda e: e.scalar_tensor_tensor(out=aT[:], in0=ps_st[:, 0:128], scalar=1.0 / 16.0,
                                                                 in1=DT_t[:], op0=ALU.mult, op1=ALU.mult),
                         reads=[r_pst, DT_r], writes=[r_aT])
                    S.op("dve", lambda e: e.scalar_tensor_tensor(
                        out=qp[:], in0=qkb[:, qo:qo + 2, tok], scalar=1.0 / 16.0,
                        in1=EB_t[:, pr, :].unsqueeze(1).to_broadcast([128, 2, 128]), op0=ALU.mult, op1=ALU.mult),
                        reads=[r_qkb, EB_r], writes=[r_qp])
                    vs = vt_t[:, h * 512:(h + 1) * 512]
                    nmm = 1 if c == 0 else 3
                    S.op("pe", lambda e: e.matmul(ps_num[:, :], lhsT=aT[:], rhs=vs, start=True, stop=(nmm == 1)),
                         reads=[r_aT, vt_r], writes=[r_pnum], inc=(nmm == 1))
                    if c > 0:
                        for dc in range(2):
                            S.op("pe", lambda e: e.matmul(ps_num[:, :], lhsT=qp[:, dc, :], rhs=Cb[:, h, dc, :],
                                                          start=False, stop=(dc == 1)),
                                 reads=[r_qp, r_Cb], pwrites=[r_pnum], inc=(dc == 1))
                    S.op("pe", lambda e: e.matmul(ps_sm[:, 0:1], lhsT=aT[:], rhs=onesb[:], start=True, stop=(nmm == 1)),
                         reads=[r_aT, r_onesb], writes=[r_psm], inc=(nmm == 1))
                    if c > 0:
                        for dc in range(2):
                            S.op("pe", lambda e: e.matmul(ps_sm[:, 0:1], lhsT=qp[:, dc, :], rhs=nb[:, h, dc:dc + 1],
                                                          start=False, stop=(dc == 1)),
                                 reads=[r_qp, r_nb], pwrites=[r_psm], inc=(dc == 1))
                    S.op("act", lambda e: e.activation(out=den[:, 0:1], in_=ps_sm[:, 0:1], func=AF.Abs),
                         reads=[r_psm], writes=[r_den])
                    S.op("dve", lambda e: e.tensor_scalar(out=den[:, 0:1], in0=den[:, 0:1], scalar1=1.0, scalar2=None,
                                                          op0=ALU.max), reads=[r_den], writes=[r_den])
                    S.op("dve", lambda e: e.reciprocal(out=den[:, 1:2], in_=den[:, 0:1]), reads=[r_den], writes=[r_den])
                    S.op("dve", lambda e: e.tensor_scalar(out=hc[:], in0=ps_num[:, :], scalar1=den[:, 1:2], scalar2=None,
                                                          op0=ALU.mult), reads=[r_pnum, r_den], writes=[r_hc])
                    S.op("pool", lambda e: e.tensor_scalar(out=vp[:], in0=vs, scalar1=ws[:, pr:pr + 1], scalar2=None,
                                                           op0=ALU.mult), reads=[vt_r, r_ws], writes=[r_vp])
                    for dc in range(2):
                        S.op("pe", lambda e: e.matmul(ps_cu[dc][0][:, :], lhsT=ktm[:, dc * 128:(dc + 1) * 128], rhs=vp[:],
                                                      start=True, stop=True),
                             reads=[r_ktm, r_vp], writes=[ps_cu[dc][1]])
                    for dc in range(2):
                        S.op("pe", lambda e: e.matmul(ps_sm[:, 2 + dc:3 + dc], lhsT=ktm[:, dc * 128:(dc + 1) * 128],
                                                      rhs=wsb[:, pr:pr + 1], start=True, stop=True),
                             reads=[r_ktm, r_wsb], pwrites=[r_psm])
                    dec = EB_t[:, pr, 127:128]
                    for dc in range(2):
                        S.op("dve", lambda e: e.scalar_tensor_tensor(out=Cst[:, h, dc, :], in0=Cst[:, h, dc, :], scalar=dec,
                                                                     in1=ps_cu[dc][0][:, :], op0=ALU.mult, op1=ALU.add),
                             reads=[r_C, EB_r, ps_cu[dc][1]], writes=[r_C])
                    S.op("act", lambda e: e.copy(out=Cb[:, h, :, :], in_=Cst[:, h, :, :]), reads=[r_C], writes=[r_Cb])
                    S.op("dve", lambda e: e.scalar_tensor_tensor(out=nst[:, h, :], in0=nst[:, h, :], scalar=dec,
                                                                 in1=ps_sm[:, 2:4], op0=ALU.mult, op1=ALU.add),
                         reads=[r_n, EB_r, r_psm], writes=[r_n])
                    S.op("act", lambda e: e.copy(out=nb[:, h, :], in_=nst[:, h, :]), reads=[r_n], writes=[r_nb])
                    S.op("act", lambda e: e.activation(out=junk[:], in_=hc[:], func=AF.Square, accum_out=den[:, 2:3]),
                         reads=[r_hc], writes=[r_junk, r_den])
                    S.op("act", lambda e: e.activation(out=den[:, 3:4], in_=den[:, 2:3], func=AF.Sqrt, scale=1.0 / 512,
                                                       bias=EPS), reads=[r_den], writes=[r_den])
                    S.op("dve", lambda e: e.reciprocal(out=den[:, 2:3], in_=den[:, 3:4]), reads=[r_den], writes=[r_den])
                    S.op("dve", lambda e: e.scalar_tensor_tensor(out=t1[:], in0=hc[:], scalar=den[:, 2:3],
                                                                 in1=onw[:, h * 512:(h + 1) * 512], op0=ALU.mult,
                                                                 op1=ALU.mult), reads=[r_hc, r_den, r_onw], writes=[r_t1])
                    S.op("act", lambda e: e.activation(out=sg[:], in_=vt_t[:, 1024 + h * 512:1024 + (h + 1) * 512],
                                                       func=AF.Sigmoid), reads=[vt_r], writes=[r_sg])
                    S.op("act", lambda e: e.activation(out=sz[:], in_=vt_t[:, 2048 + h * 512:2048 + (h + 1) * 512],
                                                       func=AF.Silu), reads=[vt_r], writes=[r_sz])
                    S.op("pool", lambda e: e.tensor_tensor(out=sg[:], in0=sg[:], in1=sz[:], op=ALU.mult),
                         reads=[r_sg, r_sz], writes=[r_sg])
                    S.op("pool", lambda e: e.tensor_tensor(out=yb[:], in0=t1[:], in1=sg[:], op=ALU.mult),
                         reads=[r_t1, r_sg], writes=[r_yb])
                    for ic in range(4):
                        S.op("pe", lambda e: e.transpose(out=ps_kt[:, 512 + ic * 128:512 + (ic + 1) * 128],
                                                         in_=yb[:, ic * 128:(ic + 1) * 128], identity=ident[:]),
                             reads=[r_yb, r_id], pwrites=[r_pkt], inc=(ic == 3))
                    q4 = c % 4
                    first = (q4 == 0 and h == 0)
                    S.op("act", lambda e: e.copy(out=oT_t[:, h * 4:(h + 1) * 4, q4 * 128:(q4 + 1) * 128],
                                                 in_=ps_kt[:, 512:1024].rearrange("p (i t) -> p i t", i=4)),
                         reads=[r_pkt], writes=[oT_r] if first else [], pwrites=[] if first else [oT_r])
                if c % 4 == 3:
                    T = c // 4
                    S.dma("sp", o_v[:, :, T * TT:(T + 1) * TT], oT_t[:], reads=[oT_r], pwrites=[r_o])


B_FORCE = 1e4
B_NEG = -1e30


def _alibi(n):
    return 2.0 ** (-8.0 * np.arange(1, n + 1) / n)


def mixer_nsa(kb, utm, r_utm, kv0T, r_kv0, ug, r_ug, prm, o_scr, r_o, G=3):
    S = kb.S
    scale = 128.0 ** -0.5
    NQ = SEQ // 128
    min_slope = float(_alibi(32)[8 * G + 7])
    with kb.scope() as sc0:
        kselT, r_kselT = sc0.sb("b_kselT", [128, SEQ], BF16)
        kwinT, r_kwinT = sc0.sb("b_kwinT", [128, SEQ], BF16)
        vsel, r_vsel = sc0.sb("b_vsel", [128, NQ, 129], BF16)
        vwin, r_vwin = sc0.sb("b_vwin", [128, NQ, 129], BF16)
        kcmpT, r_kcmpT = sc0.sb("b_kcmpT", [128, 256], BF16)
        vcmp, r_vcmp = sc0.sb("b_vcmp", [128, 2, 129], BF16)
        ovl, r_ovl = sc0.sb("b_ovl", [128, 2, 64], BF16, dma=True)
        S.op("pool", lambda e: e.memset(vsel[:], 1.0), writes=[r_vsel])
        S.op("pool", lambda e: e.memset(vwin[:], 1.0), writes=[r_vwin])
        S.op("pool", lambda e: e.memset(kcmpT[:], 0.0), writes=[r_kcmpT])
        S.op("pool", lambda e: e.memset(vcmp[:], 0.0), writes=[r_vcmp])
        S.op("pool", lambda e: e.memset(vcmp[:, :, 128:129], 1.0), writes=[r_vcmp])
        S.dma("pool", ovl[:], prm["ovl"], writes=[r_ovl])
        with kb.scope() as sc:
            ident, r_id, _, _ = make_ident(S, sc)
            knw, r_knw = sc.sb("b_knw", [128, 3, 128], F32, dma=True)
            S.dma("sp", knw[:].rearrange("p a d -> p (a d)"), prm["knw"].partition_broadcast(128), writes=[r_knw])
            kvt = [sc.sb(f"b_kvt{i}", [128, 512], BF16, dma=True) for i in range(2)]
            sq, r_sq = sc.sb("b_sq", [128, 2, 128], F32)
            ssq, r_ssq = sc.sb("b_ssq", [128, 2], F32)
            rstd, r_rstd = sc.sb("b_rstd", [128, 2], F32)
            kn, r_kn = sc.sb("b_kn", [128, 2, 128], BF16)
            pt, r_pt = sc.ps("b_pt", [128, 1024], BF16)
            for kt in range(NQ):
                kv_t, kv_r = kvt[kt % 2]
                S.dma("sp", kv_t[:], utm[kt * 128:(kt + 1) * 128, 2304:2816], reads=[r_utm], writes=[kv_r])
                kview = kv_t[:].rearrange("p (a b d) -> p a b d", a=2, b=2)
                S.op("dve", lambda e: e.tensor_tensor(out=sq[:], in0=kview[:, :, 0, :], in1=kview[:, :, 0, :], op=ALU.mult),
                     reads=[kv_r], writes=[r_sq])
                S.op("dve", lambda e: e.tensor_reduce(out=ssq[:], in_=sq[:], axis=AX.X, op=ALU.add),
                     reads=[r_sq], writes=[r_ssq])
                S.op("act", lambda e: e.activation(out=rstd[:], in_=ssq[:], func=AF.Sqrt, scale=1.0 / 128, bias=EPS),
                     reads=[r_ssq], writes=[r_rstd])
                S.op("dve", lambda e: e.reciprocal(out=rstd[:], in_=rstd[:]), reads=[r_rstd], writes=[r_rstd])
                S.op("dve", lambda e: e.tensor_tensor(out=sq[:], in0=kview[:, :, 0, :],
                                                      in1=rstd[:, :].unsqueeze(2).to_broadcast([128, 2, 128]), op=ALU.mult),
                     reads=[kv_r, r_rstd], writes=[r_sq])
                S.op("pool", lambda e: e.tensor_tensor(out=kn[:], in0=sq[:], in1=knw[:, 1:3, :], op=ALU.mult),
                     reads=[r_sq, r_knw], writes=[r_kn])
                for a in range(2):
                    S.op("pe", lambda e: e.transpose(out=pt[:, a * 128:(a + 1) * 128], in_=kn[:, a, :], identity=ident[:]),
                         reads=[r_kn, r_id], writes=[r_pt] if a == 0 else [], pwrites=[] if a == 0 else [r_pt], inc=(a == 1))
                S.op("dve", lambda e: e.tensor_copy(out=kselT[:, kt * 128:(kt + 1) * 128], in_=pt[:, 0:128]),
                     reads=[r_pt], pwrites=[r_kselT])
                S.op("act", lambda e: e.copy(out=kwinT[:, kt * 128:(kt + 1) * 128], in_=pt[:, 128:256]),
                     reads=[r_pt], pwrites=[r_kwinT])
                S.op("pool", lambda e: e.tensor_copy(out=vsel[:, kt, 0:128], in_=kview[:, 0, 1, :]),
                     reads=[kv_r], pwrites=[r_vsel])
                S.op("pool", lambda e: e.tensor_copy(out=vwin[:, kt, 0:128], in_=kview[:, 1, 1, :]),
                     reads=[kv_r], pwrites=[r_vwin])
            k0, r_k0 = sc.sb("b_k0", [128, 2, SEQ], BF16, dma=True)
            S.dma("sp", k0[:], kv0T.rearrange("(a p) t -> p a t", p=128), reads=[r_kv0], writes=[r_k0])
            peT, r_peT = sc.sb("b_peT", [128, 2, 32], F32, dma=True)
            S.dma("sp", peT[:], prm["peT"], writes=[r_peT])
            wkv, r_wkv = sc.sb("b_wkv", [128, 2, 32, 128], BF16, dma=True)
            S.dma("pool", wkv[:, 0], prm["wk"], writes=[r_wkv])
            S.dma("pool", wkv[:, 1], prm["wv"], pwrites=[r_wkv])
            kg, r_kg = sc.sb("b_kg", [128, 2, 32, 256], BF16)
            S.op("pool", lambda e: e.memset(kg[:], 0.0), writes=[r_kg])
            for a in range(2):
                for l in range(32):
                    eng = "dve" if (l % 2 == 0) else "pool"
                    S.op(eng, lambda e: e.tensor_scalar(out=kg[:, a, l, 0:255], in0=k0[:, a, l:l + 16 * 254 + 1:16],
                                                        scalar1=peT[:, a, l:l + 1], scalar2=None, op0=ALU.add),
                         reads=[r_k0, r_peT], pwrites=[r_kg])
            pc = [sc.ps(f"b_pc{i}", [128, 512], F32) for i in range(2)]
            for ct in range(2):
                M = 128 if ct == 0 else 127
                for a in range(2):
                    pc_t, pc_r = pc[a]
                    for l in range(32):
                        S.op("pe", lambda e: e.matmul(pc_t[:M, 0:128], lhsT=kg[:, a, l, ct * 128:ct * 128 + M],
                                                      rhs=wkv[:, a, l, :], start=(l == 0), stop=(l == 31)),
                             reads=[r_kg, r_wkv], writes=[pc_r] if l == 0 else [], pwrites=[] if l == 0 else [pc_r],
                             inc=(l == 31))
                pk, pk_r = pc[0]
                S.op("act", lambda e: e.activation(out=sq[:M, 0, :], in_=pk[:M, 0:128], func=AF.Square,
                                                   accum_out=ssq[:M, 0:1]), reads=[pk_r], writes=[r_sq, r_ssq])
                S.op("act", lambda e: e.activation(out=rstd[:M, 0:1], in_=ssq[:M, 0:1], func=AF.Sqrt, scale=1.0 / 128,
                                                   bias=EPS), reads=[r_ssq], writes=[r_rstd])
                S.op("dve", lambda e: e.reciprocal(out=rstd[:M, 0:1], in_=rstd[:M, 0:1]), reads=[r_rstd], writes=[r_rstd])
                S.op("dve", lambda e: e.scalar_tensor_tensor(out=kn[:M, 0, :], in0=pk[:M, 0:128], scalar=rstd[:M, 0:1],
                                                             in1=knw[:M, 0, :], op0=ALU.mult, op1=ALU.mult),
                     reads=[pk_r, r_rstd, r_knw], writes=[r_kn])
                S.op("pe", lambda e: e.transpose(out=pt[:, 0:M], in_=kn[:M, 0, :], identity=ident[:M, :M]),
                     reads=[r_kn, r_id], writes=[r_pt])
                S.op("dve", lambda e: e.tensor_copy(out=kcmpT[:, ct * 128:ct * 128 + M], in_=pt[:, 0:M]),
                     reads=[r_pt], pwrites=[r_kcmpT])
                S.op("act", lambda e: e.copy(out=vcmp[:M, ct, 0:128], in_=pc[1][0][:M, 0:128]),
                     reads=[pc[1][1]], pwrites=[r_vcmp])
        with kb.scope() as sc:
            ident, r_id, _, _ = make_ident(S, sc)
            qnw, r_qnw = sc.sb("b_qnw", [128, 128], F32, dma=True)
            S.dma("sp", qnw[:], prm["qnw"].partition_broadcast(128), writes=[r_qnw])
            BQ, r_BQ = sc.sb("b_BQ", [8, 1024], BF16, dma=True)
            AK, r_AK = sc.sb("b_AK", [8, 2, 32, 128], BF16, dma=True)
            Eall, r_E = sc.sb("b_E", [64, SEQ], BF16, dma=True)
            Wadd, r_Wadd = sc.sb("b_Wadd", [128, 128], F32, dma=True)
            Wkeep, r_Wkeep = sc.sb("b_Wkeep", [128, 128], F32, dma=True)
            S.dma("pool", BQ[:], prm["BQ"], writes=[r_BQ])
            S.dma("pool", AK[:], prm["AK"], writes=[r_AK])
            S.dma("pool", Eall[:], prm["Eall"], writes=[r_E])
            S.dma("sp", Wadd[:], prm["Wadd"], writes=[r_Wadd])
            S.dma("sp", Wkeep[:], prm["Wkeep"], writes=[r_Wkeep])
            qt = [sc.sb(f"b_qt{i}", [128, 2048], BF16, dma=True) for i in range(2)]
            gt = [sc.sb(f"b_gt{i}", [128, 24], F32, dma=True) for i in range(2)]
            sq, r_sq = sc.sb("b2_sq", [128, 1024], F32)
            ssq, r_ssq = sc.sb("b2_ssq", [128, 8], F32)
            rstd, r_rstd = sc.sb("b2_rstd", [128, 8], F32)
            qn, r_qn = sc.sb("b2_qn", [128, 1024], BF16)
            qT, r_qT = sc.sb("b2_qT", [128, 1024], BF16)
            PT = [sc.sb(f"b2_PT{i}", [128, 1024], BF16) for i in range(2)]
            msk, r_msk = sc.sb("b2_msk", [128, 128], BF16)
            oacc, r_oacc = sc.sb("b2_oacc", [128, 8, 128], F32)
            imp, r_imp = sc.sb("b2_imp", [128, 64], F32)
            imp2, r_imp2 = sc.sb("b2_imp2", [128, 64], F32)
            imp3, r_imp3 = sc.sb("b2_imp3", [128, 64], F32)
            m8, r_m8 = sc.sb("b2_m8", [128, 16], F32)
            selb, r_selb = sc.sb("b2_selb", [128, 64], BF16)
            selT, r_selT = sc.sb("b2_selT", [64, 128], BF16)
            den, r_den = sc.sb("b2_den", [128, 8], F32)
            rg, r_rg = sc.sb("b2_rg", [128, 8], F32)
            zs, r_zs = sc.sb("b2_zs", [128, 1024], F32)
            yb, r_yb = sc.sb("b2_yb", [128, 1024], BF16)
            oT = [sc.sb(f"b2_oT{i}", [128, 8, TT], BF16, dma=True) for i in range(2)]
            pst, r_pst = sc.ps("b2_pst", [128, 1024], F32)
            acc = [sc.ps(f"b2_acc{i}", [128, 512], F32) for i in range(3)]
            pmk, r_pmk = sc.ps("b2_pmk", [128, 512], F32)
            ptr, r_ptr = sc.ps("b2_ptr", [128, 1024], BF16)
            pti = 0

            def acc_of(h):
                return acc[h // 3][0][:, (h % 3) * 129:(h % 3) * 129 + 129], acc[h // 3][1]

            def attend(tiles, br, first_branch, gt_t, gt_r, want_imp):
                nonlocal pti
                nt = len(tiles)

                def stage_a(tl):
                    nonlocal pti
                    PT_t, PT_r = PT[pti % 2]
                    pti += 1
                    for half in range(2):
                        S.op("pe", lambda e: e.matmul(pst[:, half * 512:(half + 1) * 512], lhsT=tl["kT"],
                                                      rhs=qT[:, half * 512:(half + 1) * 512], start=True, stop=False),
                             reads=[tl["r_k"], r_qT], writes=[r_pst] if half == 0 else [],
                             pwrites=[] if half == 0 else [r_pst], inc=False)
                        S.op("pe", lambda e: e.matmul(pst[:, half * 512:(half + 1) * 512], lhsT=AK[:, tl["ak"], tl["m"], :],
                                                      rhs=BQ[:, half * 512:(half + 1) * 512], start=False, stop=True),
                             reads=[r_AK, r_BQ], pwrites=[r_pst], inc=(half == 1))
                    S.op("act", lambda e: e.activation(out=PT_t[:], in_=pst[:, :], func=AF.Exp, scale=scale),
                         reads=[r_pst], writes=[PT_r])
                    if tl["aff"] is not None:
                        pat, cm, base = tl["aff"]
                        S.op("pool", lambda e: e.affine_select(out=PT_t[:].rearrange("p (h q) -> p h q", h=8),
                                                               in_=PT_t[:].rearrange("p (h q) -> p h q", h=8),
                                                               pattern=[[0, 8], [pat, 128]], compare_op=ALU.is_ge, fill=0.0,
                                                               base=base, channel_multiplier=cm),
                             reads=[PT_r], writes=[PT_r])
                    return PT_t, PT_r

                def stage_b(ti, tl, PT_t, PT_r):
                    if tl["selmask_kt"] is not None:
                        kt = tl["selmask_kt"]
                        S.op("pe", lambda e: e.matmul(pmk[:, 0:128], lhsT=Eall[:, kt * 128:(kt + 1) * 128], rhs=selT[:, :],
                                                      start=True, stop=True), reads=[r_E, r_selT], writes=[r_pmk])
                        S.op("dve", lambda e: e.tensor_tensor(out=PT_t[:].rearrange("p (h q) -> p h q", h=8),
                                                              in0=PT_t[:].rearrange("p (h q) -> p h q", h=8),
                                                              in1=pmk[:, 0:128].unsqueeze(1).to_broadcast([128, 8, 128]),
                                                              op=ALU.mult), reads=[PT_r, r_pmk], writes=[PT_r])
                    for h in range(8):
                        a_ap, a_r = acc_of(h)
                        first_in_bank = (ti == 0 and h % 3 == 0)
                        last = (ti == nt - 1)
                        S.op("pe", lambda e: e.matmul(a_ap, lhsT=PT_t[:, h * 128:(h + 1) * 128], rhs=tl["vaug"],
                                                      start=first_in_bank, stop=last, skip_group_check=True),
                             reads=[PT_r, tl["r_v"]], writes=[a_r] if first_in_bank else [],
                             pwrites=[] if first_in_bank else [a_r], inc=(last and (h % 3 == 2 or h == 7)))
                    if want_imp:
                        for h in range(8):
                            S.op("pe", lambda e: e.matmul(pmk[:, h * 64:(h + 1) * 64], lhsT=PT_t[:, h * 128:(h + 1) * 128],
                                                          rhs=tl["ovl"], start=(ti == 0 and h == 0), stop=(ti == nt - 1),
                                                          skip_group_check=True),
                                 reads=[PT_r, r_ovl], writes=[r_pmk] if (ti == 0 and h == 0) else [],
                                 pwrites=[] if (ti == 0 and h == 0) else [r_pmk], inc=(ti == nt - 1 and h == 7))

                pend = stage_a(tiles[0])
                for ti in range(nt):
                    nxt = stage_a(tiles[ti + 1]) if ti + 1 < nt else None
                    stage_b(ti, tiles[ti], *pend)
                    pend = nxt
                for bk in range(3):
                    nh = 3 if bk < 2 else 2
                    S.op("dve", lambda e: e.tensor_scalar(
                        out=den[:, bk * 3:bk * 3 + nh],
                        in0=acc[bk][0][:, 0:nh * 129].rearrange("p (h c) -> p h c", c=129)[:, :, 128],
                        scalar1=1e-30, scalar2=None, op0=ALU.max), reads=[acc[bk][1]],
                        writes=[r_den] if bk == 0 else [], pwrites=[] if bk == 0 else [r_den])
                S.op("dve", lambda e: e.reciprocal(out=den[:], in_=den[:]), reads=[r_den], writes=[r_den])
                S.op("dve", lambda e: e.tensor_tensor(out=rg[:], in0=den[:],
                                                      in1=gt_t[:].rearrange("p (h b) -> p h b", b=3)[:, :, br],
                                                      op=ALU.mult), reads=[r_den, gt_r], writes=[r_rg])
                for h in range(8):
                    a_ap, a_r = acc_of(h)
                    if first_branch:
                        S.op("dve", lambda e: e.tensor_scalar(out=oacc[:, h, :], in0=a_ap[:, 0:128], scalar1=rg[:, h:h + 1],
                                                              scalar2=None, op0=ALU.mult),
                             reads=[a_r, r_rg], writes=[r_oacc] if h == 0 else [], pwrites=[] if h == 0 else [r_oacc])
                    else:
                        S.op("dve", lambda e: e.scalar_tensor_tensor(out=oacc[:, h, :], in0=a_ap[:, 0:128],
                                                                     scalar=rg[:, h:h + 1], in1=oacc[:, h, :],
                                                                     op0=ALU.mult, op1=ALU.add),
                             reads=[a_r, r_rg, r_oacc], writes=[r_oacc])
                    if want_imp:
                        if h == 0:
                            S.op("dve", lambda e: e.tensor_scalar(out=imp[:], in0=pmk[:, 0:64], scalar1=den[:, 0:1],
                                                                  scalar2=None, op0=ALU.mult),
                                 reads=[r_pmk, r_den], writes=[r_imp])
                        else:
                            S.op("dve", lambda e: e.scalar_tensor_tensor(out=imp[:], in0=pmk[:, h * 64:(h + 1) * 64],
                                                                         scalar=den[:, h:h + 1], in1=imp[:],
                                                                         op0=ALU.mult, op1=ALU.add),
                                 reads=[r_pmk, r_den, r_imp], writes=[r_imp])

            for i in range(NQ):
                t0 = i * 128
                q_t, q_r = qt[i % 2]
                gt_t, gt_r = gt[i % 2]
                oT_t, oT_r = oT[(i // 4) % 2]
                S.dma("sp", q_t[:], utm[t0:t0 + 128, 0:2048], reads=[r_utm], writes=[q_r])
                S.dma("sp", gt_t[:], ug[t0:t0 + 128, :], reads=[r_ug], writes=[gt_r])
                S.op("act", lambda e: e.activation(out=gt_t[:], in_=gt_t[:], func=AF.Sigmoid), reads=[gt_r], writes=[gt_r])
                S.op("dve", lambda e: e.tensor_tensor(out=sq[:], in0=q_t[:, 0:1024], in1=q_t[:, 0:1024], op=ALU.mult),
                     reads=[q_r], writes=[r_sq])
                S.op("dve", lambda e: e.tensor_reduce(out=ssq[:], in_=sq[:].rearrange("p (h d) -> p h d", h=8), axis=AX.X,
                                                      op=ALU.add), reads=[r_sq], writes=[r_ssq])
                S.op("act", lambda e: e.activation(out=rstd[:], in_=ssq[:], func=AF.Sqrt, scale=1.0 / 128, bias=EPS),
                     reads=[r_ssq], writes=[r_rstd])
                S.op("dve", lambda e: e.reciprocal(out=rstd[:], in_=rstd[:]), reads=[r_rstd], writes=[r_rstd])
                S.op("dve", lambda e: e.tensor_tensor(out=sq[:].rearrange("p (h d) -> p h d", h=8),
                                                      in0=q_t[:, 0:1024].rearrange("p (h d) -> p h d", h=8),
                                                      in1=rstd[:, :].unsqueeze(2).to_broadcast([128, 8, 128]), op=ALU.mult),
                     reads=[q_r, r_rstd], writes=[r_sq])
                S.op("pool", lambda e: e.tensor_tensor(out=qn[:].rearrange("p (h d) -> p h d", h=8),
                                                       in0=sq[:].rearrange("p (h d) -> p h d", h=8),
                                                       in1=qnw[:, :].unsqueeze(1).to_broadcast([128, 8, 128]), op=ALU.mult),
                     reads=[r_sq, r_qnw], writes=[r_qn])
                for h in range(8):
                    S.op("pe", lambda e: e.transpose(out=ptr[:, h * 128:(h + 1) * 128], in_=qn[:, h * 128:(h + 1) * 128],
                                                     identity=ident[:]),
                         reads=[r_qn, r_id], writes=[r_ptr] if h == 0 else [], pwrites=[] if h == 0 else [r_ptr],
                         inc=(h == 7))
                S.op("dve", lambda e: e.tensor_copy(out=qT[:], in_=ptr[:, :]), reads=[r_ptr], writes=[r_qT])
                tiles = []
                for ct in range(2):
                    P0 = 2048 * ct + 31
                    if t0 + 127 < P0:
                        continue
                    tiles.append(dict(kT=kcmpT[:, ct * 128:(ct + 1) * 128], r_k=r_kcmpT, vaug=vcmp[:, ct, :], r_v=r_vcmp,
                                      ak=1, m=i - 16 * ct, aff=(1, -16, t0 - P0), selmask_kt=None, ovl=ovl[:, ct, :]))
                attend(tiles, 0, True, gt_t, gt_r, True)
                c0 = 62 - 2 * i
                S.op("dve", lambda e: e.tensor_tensor(out=imp2[:], in0=imp[:], in1=Wkeep[:, c0:c0 + 64], op=ALU.mult),
                     reads=[r_imp, r_Wkeep], writes=[r_imp2])
                S.op("dve", lambda e: e.tensor_tensor(out=imp2[:], in0=imp2[:], in1=Wadd[:, c0:c0 + 64], op=ALU.add),
                     reads=[r_imp2, r_Wadd], writes=[r_imp2])
                if i >= 1:
                    S.op("dve", lambda e: e.tensor_scalar(out=imp2[:, 0:1], in0=imp2[:, 0:1], scalar1=B_FORCE, scalar2=None,
                                                          op0=ALU.add), reads=[r_imp2], writes=[r_imp2])
                S.op("dve", lambda e: e.max(out=m8[:, 0:8], in_=imp2[:]), reads=[r_imp2], writes=[r_m8])
                S.op("dve", lambda e: e.match_replace(out=imp3[:], in_to_replace=m8[:, 0:8], in_values=imp2[:],
                                                      imm_value=-3.0e38), reads=[r_m8, r_imp2], writes=[r_imp3])
                S.op("dve", lambda e: e.max(out=m8[:, 8:16], in_=imp3[:]), reads=[r_imp3], writes=[r_m8])
                S.op("dve", lambda e: e.tensor_scalar(out=imp3[:], in0=imp2[:], scalar1=m8[:, 15:16], scalar2=None,
                                                      op0=ALU.is_ge), reads=[r_imp2, r_m8], writes=[r_imp3])
                S.op("dve", lambda e: e.tensor_tensor(out=selb[:], in0=imp3[:], in1=Wkeep[:, c0:c0 + 64], op=ALU.mult),
                     reads=[r_imp3, r_Wkeep], writes=[r_selb])
                S.op("pe", lambda e: e.transpose(out=ptr[:64, 0:128], in_=selb[:, :], identity=ident[:]),
                     reads=[r_selb, r_id], writes=[r_ptr])
                S.op("act", lambda e: e.copy(out=selT[:], in_=ptr[:64, 0:128]), reads=[r_ptr], writes=[r_selT])
                tiles = []
                for kt in range(i + 1):
                    if (i - kt - 1) * 128 * min_slope > 160.0:
                        continue
                    tiles.append(dict(kT=kselT[:, kt * 128:(kt + 1) * 128], r_k=r_kselT, vaug=vsel[:, kt, :], r_v=r_vsel,
                                      ak=0, m=i - kt, aff=((1, -1, 0) if kt == i else None), selmask_kt=kt,
                                      ovl=None))
                attend(tiles, 1, False, gt_t, gt_r, False)
                tiles = []
                for kt in range(max(0, i - 4), i + 1):
                    aff = None
                    if kt == i:
                        aff = (1, -1, 0)
                    elif kt == i - 4:
                        aff = (-1, 1, -1)
                    tiles.append(dict(kT=kwinT[:, kt * 128:(kt + 1) * 128], r_k=r_kwinT, vaug=vwin[:, kt, :], r_v=r_vwin,
                                      ak=0, m=i - kt, aff=aff, selmask_kt=None, ovl=None))
                attend(tiles, 2, False, gt_t, gt_r, False)
                S.op("act", lambda e: e.activation(out=zs[:], in_=q_t[:, 1024:2048], func=AF.Silu), reads=[q_r], writes=[r_zs])
                S.op("pool", lambda e: e.tensor_tensor(out=yb[:], in0=oacc[:].rearrange("p h d -> p (h d)"), in1=zs[:],
                                                       op=ALU.mult), reads=[r_oacc, r_zs], writes=[r_yb])
                for h in range(8):
                    S.op("pe", lambda e: e.transpose(out=ptr[:, h * 128:(h + 1) * 128], in_=yb[:, h * 128:(h + 1) * 128],
                                                     identity=ident[:]),
                         reads=[r_yb, r_id], writes=[r_ptr] if h == 0 else [], pwrites=[] if h == 0 else [r_ptr],
                         inc=(h == 7))
                q4 = i % 4
                S.op("act", lambda e: e.copy(out=oT_t[:, :, q4 * 128:(q4 + 1) * 128],
                                             in_=ptr[:, :].rearrange("p (h t) -> p h t", h=8)),
                     reads=[r_ptr], writes=[oT_r] if q4 == 0 else [], pwrites=[] if q4 == 0 else [oT_r])
                if q4 == 3:
                    T = i // 4
                    S.dma("sp", o_scr.rearrange("(h p) t -> p h t", p=128)[:, :, T * TT:(T + 1) * TT], oT_t[:],
                          reads=[oT_r], pwrites=[r_o])


def _dram_in(nc, name, shape, dt=F32):
    return nc.dram_tensor(name, list(shape), dt, kind="ExternalInput").ap()


_SCRATCH = {
    0: (("s_qkT", [1024, SEQ], F32), ("s_utm", [SEQ, 3072], BF16), ("s_gates", [SEQ, 4], F32), ("s_o", [1024, SEQ], BF16)),
    1: (("s_utm", [SEQ, 2816], BF16), ("s_kv0T", [256, SEQ], BF16), ("s_ug", [SEQ, 24], F32), ("s_o", [1024, SEQ], BF16)),
    2: (("s_xbT", [1344, SEQ], F32), ("s_zT", [1344, SEQ], BF16), ("s_o", [1344, SEQ], BF16)),
    3: (("s_u", [SEQ, 5120], BF16), ("s_nd", [3, SEQ, 516], F32), ("s_o", [512, SEQ], BF16)),
}


def alloc_scratch(nc, S, kind, tag=""):
    scr = {}
    for name, shape, dt in _SCRATCH[kind]:
        scr[name] = (nc.dram_tensor(f"{name}{tag}", shape, dt, kind="Internal").ap(), S.res(f"{name}{tag}"))
    return scr


def layer_groups(kind, win, scr):
    if kind == 2:
        (xbT, r_xb), (zT, r_z) = scr["s_xbT"], scr["s_zT"]
        return [dict(layout="fm", w=win["w_xb"], W=112, nchunk=12, dst=xbT, rdst=r_xb),
                dict(layout="fm", w=win["w_z"], W=112, nchunk=12, dst=zT, rdst=r_z)]
    if kind == 0:
        (qkT, r_qk), (utm, r_utm), (gts, r_g) = scr["s_qkT"], scr["s_utm"], scr["s_gates"]
        return [dict(layout="fm", w=win["w_qk"], W=128, nchunk=8, dst=qkT, rdst=r_qk),
                dict(layout="tm", w=win["w_vgz"], W=256, nchunk=12, dst=utm, rdst=r_utm),
                dict(layout="tm", w=win["w_g"], W=4, nchunk=1, dst=gts, rdst=r_g)]
    if kind == 1:
        (utm, r_utm), (kv0T, r_kv0), (ug, r_ug) = scr["s_utm"], scr["s_kv0T"], scr["s_ug"]
        return [dict(layout="tm", w=win["w_tm"], W=256, nchunk=11, dst=utm, rdst=r_utm),
                dict(layout="fm", w=win["w_kv0"], W=128, nchunk=2, dst=kv0T, rdst=r_kv0),
                dict(layout="tm", w=win["w_g"], W=24, nchunk=1, dst=ug, rdst=r_ug)]
    if kind == 3:
        (u, r_u) = scr["s_u"]
        return [dict(layout="tm", w=win["w_u"], W=256, nchunk=20, dst=u, rdst=r_u)]
    raise NotImplementedError


def layer_mix(kb, kind, win, y, r_y, scr, accum=None, G=3):
    (o_scr, r_o) = scr["s_o"]
    if kind == 2:
        (xbT, r_xb), (zT, r_z) = scr["s_xbT"], scr["s_zT"]
        mixer_rglru(kb, xbT, r_xb, zT, r_z, win, o_scr, r_o)
        phase3(kb, o_scr, r_o, 112, 12, win["w_out"], y, r_y, accum)
    elif kind == 0:
        (qkT, r_qk), (utm, r_utm), (gts, r_g) = scr["s_qkT"], scr["s_utm"], scr["s_gates"]
        mixer_mlstm(kb, qkT, r_qk, utm, r_utm, gts, r_g, win, o_scr, r_o)
        phase3(kb, o_scr, r_o, 128, 8, win["w_out"], y, r_y, accum)
    elif kind == 1:
        (utm, r_utm), (kv0T, r_kv0), (ug, r_ug) = scr["s_utm"], scr["s_kv0T"], scr["s_ug"]
        mixer_nsa(kb, utm, r_utm, kv0T, r_kv0, ug, r_ug, win, o_scr, r_o, G)
        phase3(kb, o_scr, r_o, 128, 8, win["w_out"], y, r_y, accum)
    elif kind == 3:
        (u, r_u), (nd, r_nd) = scr["s_u"], scr["s_nd"]
        mixer_dilated(kb, u, r_u, win, o_scr, r_o, nd, r_nd)
        phase3(kb, o_scr, r_o, 128, 4, win["w_out"], y, r_y, accum)
    else:
        raise NotImplementedError


def emit_layer(kb, kind, x, nw, win, y, r_y, scr, accum=None, G=3):
    phase1(kb, x, nw, layer_groups(kind, win, scr))
    layer_mix(kb, kind, win, y, r_y, scr, accum, G)


def build_layer(kind, shapes):
    nc = bass.Bass("TRN2", target_bir_lowering=False)
    x = _dram_in(nc, "x", [SEQ, D])
    nw = _dram_in(nc, "nw", [1, D])
    win = {k: _dram_in(nc, k, v) for k, v in shapes.items()}
    y = nc.dram_tensor("y", [SEQ, D], F32, kind="ExternalOutput").ap()
    with ExitStack() as st:
        kb = KB(nc, st)
        S = kb.S
        emit_layer(kb, kind, x, nw, win, y, S.res("y_out"), alloc_scratch(nc, S, kind))
        S.finish()
    return nc


def build_fused(shapes, layers=(0, 1, 2, 3), groups=(0, 1, 2, 3)):
    nc = bass.Bass("TRN2", target_bir_lowering=False)
    x = _dram_in(nc, "x", [SEQ, D])
    nwall = _dram_in(nc, "nw", [4, D])
    out = nc.dram_tensor("out", [SEQ, D], F32, kind="ExternalOutput").ap()
    xs = [nc.dram_tensor(f"s_x{i}", [SEQ, D], F32, kind="Internal").ap() for i in range(2)]
    with ExitStack() as st:
        kb = KB(nc, st)
        S = kb.S
        for li, L in enumerate(layers):
            src_ap = x if li == 0 else xs[(li - 1) % 2]
            dst_ap = out if li == len(layers) - 1 else xs[li % 2]
            r_dst = S.res(f"xres{L}")
            scrs, wins, allg = {}, {}, []
            for g in groups:
                scrs[g] = alloc_scratch(nc, S, L, tag=f"_L{L}g{g}")
                wins[g] = {k: _dram_in(nc, f"L{L}g{g}_{k}", v) for k, v in shapes[(L, g)].items()}
                allg += layer_groups(L, wins[g], scrs[g])
            phase1(kb, src_ap, nwall[L:L + 1, :], allg)
            for gi, g in enumerate(groups):
                layer_mix(kb, L, wins[g], dst_ap, r_dst, scrs[g], accum=(src_ap if gi == 0 else dst_ap), G=g)
        S.finish()
        print("fused program: nins", S.nins, "nwaits", S.nwaits, flush=True)
    return nc


def build_reduce():
    nc = bass.Bass("TRN2", target_bir_lowering=False)
    R = 1024
    x = _dram_in(nc, "x", [R, D])
    ys = [_dram_in(nc, f"y{i}", [R, D]) for i in range(4)]
    out = nc.dram_tensor("out", [R, D], F32, kind="ExternalOutput").ap()
    with ExitStack() as st:
        kb = KB(nc, st)
        S = kb.S
        r_out = S.res("out")
        with kb.scope() as sc:
            bufs = [[sc.sb(f"r_b{i}_{j}", [128, D], F32, dma=True) for j in range(5)] for i in range(2)]
            for t in range(R // 128):
                bb = bufs[t % 2]
                for j, src in enumerate([x] + ys):
                    S.dma("sp" if j % 2 == 0 else "act", bb[j][0][:], src[t * 128:(t + 1) * 128, :], writes=[bb[j][1]])
                a_t, a_r = bb[0]
                for j in range(1, 5):
                    eng = "dve" if j % 2 == 1 else "pool"
                    S.op(eng, lambda e: e.tensor_tensor(out=a_t[:], in0=a_t[:], in1=bb[j][0][:], op=ALU.add),
                         reads=[a_r, bb[j][1]], writes=[a_r])
                S.dma("sp", out[t * 128:(t + 1) * 128, :], a_t[:], reads=[a_r], pwrites=[r_out])
        S.finish()
    return nc


def _chunk_w(wcols, W):
    n = wcols.shape[1] // W
    a = wcols.reshape(32, 128, n, W).transpose(2, 1, 0, 3)
    return np.ascontiguousarray(a)


def _layer2_inputs(inp, g):
    CW = 5376
    lo, hi = g * 1344, (g + 1) * 1344
    w_in = inp["c_w_in"][0]
    d = {}
    d["w_xb"] = _chunk_w(w_in[:, lo:hi], 112)
    d["w_z"] = _chunk_w(w_in[:, CW + lo:CW + hi], 112)
    d["cw"] = np.ascontiguousarray(inp["c_conv_w"][0][:, lo:hi].reshape(4, 12, 112).transpose(2, 1, 0))
    vec = np.stack([inp["c_conv_b"][0][lo:hi], inp["c_b_a"][0][lo:hi], inp["c_b_x"][0][lo:hi],
                    inp["c_lambda"][0][lo:hi]], axis=0)
    d["vec"] = np.ascontiguousarray(vec.reshape(4, 12, 112).transpose(2, 0, 1))
    for nm, key in (("wa", "c_w_a"), ("wx", "c_w_x")):
        wblk = inp[key][0][4 * g:4 * g + 4]
        d[nm] = np.ascontiguousarray(wblk.reshape(4, 3, 112, 336).transpose(2, 0, 1, 3))
    d["w_out"] = np.ascontiguousarray(inp["c_w_out"][0][lo:hi, :].reshape(12, 112, D).transpose(1, 0, 2))
    return d


def _layer0_inputs(inp, g):
    w_in = inp["a_w_in"][0]
    hs = (2 * g, 2 * g + 1)
    d = {}
    qk_cols = [w_in[:, h * 256:(h + 1) * 256] for h in hs] + [w_in[:, 2048 + h * 256:2048 + (h + 1) * 256] for h in hs]
    d["w_qk"] = _chunk_w(np.concatenate(qk_cols, axis=1), 128)
    vgz = []
    for base in (4096, 8192, 12288):
        for h in hs:
            vgz.append(w_in[:, base + h * 512:base + (h + 1) * 512])
    d["w_vgz"] = _chunk_w(np.concatenate(vgz, axis=1), 256)
    gcols = [w_in[:, 16384 + h:16385 + h] for h in hs] + [w_in[:, 16392 + h:16393 + h] for h in hs]
    d["w_g"] = _chunk_w(np.concatenate(gcols, axis=1), 4)
    idx = np.concatenate([np.arange(h * 256, (h + 1) * 256) for h in hs] +
                         [2048 + np.arange(h * 256, (h + 1) * 256) for h in hs])
    d["cw"] = np.ascontiguousarray(inp["a_conv_w"][0][:, idx].reshape(4, 8, 128).transpose(2, 1, 0))
    d["cb"] = np.ascontiguousarray(inp["a_conv_b"][0][idx].reshape(8, 128).T)
    gbv = inp["a_gate_b"][0]
    d["gb"] = np.ascontiguousarray(np.array([[gbv[0, hs[0]], gbv[0, hs[1]], gbv[1, hs[0]], gbv[1, hs[1]]]], np.float32))
    d["onw"] = np.ascontiguousarray(inp["a_out_norm_w"][0][g * 1024:(g + 1) * 1024][None, :])
    d["w_out"] = np.ascontiguousarray(inp["a_w_out"][0][g * 1024:(g + 1) * 1024, :].reshape(8, 128, D).transpose(1, 0, 2))
    return d


def _bf16_split(x):
    import ml_dtypes
    x = np.asarray(x, np.float32)
    hi = x.astype(ml_dtypes.bfloat16).astype(np.float32)
    lo = (x - hi).astype(ml_dtypes.bfloat16).astype(np.float32)
    return hi, lo


def _layer1_inputs(inp, g):
    w_in = inp["b_w_in"][0]
    d = {}
    kvc = lambda br, kvt: w_in[:, 8192 + ((br * 2 + kvt) * 4 + g) * 128: 8192 + ((br * 2 + kvt) * 4 + g) * 128 + 128]
    cols = [w_in[:, g * 1024:(g + 1) * 1024], w_in[:, 4096 + g * 1024:4096 + (g + 1) * 1024]]
    cols += [kvc(br, kvt) for br in range(3) for kvt in range(2)]
    d["w_tm"] = _chunk_w(np.concatenate(cols, axis=1), 256)
    d["w_kv0"] = _chunk_w(np.concatenate([kvc(0, 0), kvc(0, 1)], axis=1), 128)
    d["w_g"] = _chunk_w(w_in[:, 11264 + g * 24:11264 + (g + 1) * 24], 24)
    d["knw"] = np.ascontiguousarray(inp["b_k_norm_w"][0].reshape(1, 384))
    d["qnw"] = np.ascontiguousarray(inp["b_q_norm_w"][0].reshape(1, 128))
    d["peT"] = np.ascontiguousarray(inp["b_cmp_pe"][0].transpose(2, 0, 1))
    d["wk"] = np.ascontiguousarray(inp["b_cmp_wk"][0].reshape(32, 128, 128).transpose(1, 0, 2))
    d["wv"] = np.ascontiguousarray(inp["b_cmp_wv"][0].reshape(32, 128, 128).transpose(1, 0, 2))
    d["w_out"] = np.ascontiguousarray(inp["b_w_out"][0][g * 1024:(g + 1) * 1024, :].reshape(8, 128, D).transpose(1, 0, 2))
    scale = 128.0 ** -0.5
    slopes = _alibi(32)[8 * g:8 * g + 8]
    sl = (slopes / scale).astype(np.float32)
    qq = np.arange(128, dtype=np.float32)
    BQ = np.zeros((8, 8, 128), np.float32)
    for h in range(8):
        hi, lo = _bf16_split(sl[h])
        BQ[0, h, :] = hi
        BQ[1, h, :] = lo
        phi, plo = _bf16_split(sl[h] * qq)
        BQ[2, h, :] = -phi
        BQ[3, h, :] = -plo
        BQ[4, h, :] = -128.0 * hi
        BQ[5, h, :] = -128.0 * lo
        chi, clo = _bf16_split(31.0 * sl[h])
        BQ[6, h, :] = chi
        BQ[7, h, :] = clo
    d["BQ"] = np.ascontiguousarray(BQ.reshape(8, 1024))
    AK = np.zeros((8, 2, 32, 128), np.float32)
    AK[0, 0] = AK[1, 0] = np.arange(128)[None, :]
    AK[0, 1] = AK[1, 1] = 16 * np.arange(128)[None, :]
    AK[2:4] = 1.0
    AK[4] = AK[5] = np.arange(32, dtype=np.float32)[None, :, None]
    AK[6:8, 1] = 1.0
    d["AK"] = AK
    n_cmp = 255
    cmp_start = np.arange(n_cmp) * 16
    sel_start = np.arange(64) * 64
    ov = np.clip(np.minimum(cmp_start[:, None] + 32, sel_start[None, :] + 64)
                 - np.maximum(cmp_start[:, None], sel_start[None, :]), 0, None) / 32.0
    ovl = np.zeros((256, 64), np.float32)
    ovl[:255] = ov
    d["ovl"] = np.ascontiguousarray(ovl.reshape(2, 128, 64).transpose(1, 0, 2))
    E = np.zeros((64, SEQ), np.float32)
    E[np.arange(SEQ) // 64, np.arange(SEQ)] = 1.0
    d["Eall"] = E
    jj = np.arange(128)[None, :] - 62
    cur = (np.arange(128)[:, None] >= 64).astype(np.int64)
    keep = (jj <= cur)
    forced = (jj == cur) | (jj == cur - 1)
    d["Wkeep"] = keep.astype(np.float32)
    d["Wadd"] = np.where(~keep, B_NEG, np.where(forced, B_FORCE, 0.0)).astype(np.float32)
    return d


def _layer3_inputs(inp, g):
    w_in = inp["d_w_in"][0]
    cols = []
    for pat in range(3):
        for typ in range(3):
            c0 = ((pat * 3 + typ) * 16 + 4 * g) * 128
            cols.append(w_in[:, c0:c0 + 512])
    z0 = 9 * 2048 + 4 * g * 128
    cols.append(w_in[:, z0:z0 + 512])
    d = {}
    d["w_u"] = _chunk_w(np.concatenate(cols, axis=1), 256)
    d["qkw"] = np.ascontiguousarray(np.concatenate([inp["d_q_norm_w"][0], inp["d_k_norm_w"][0]])[None, :])
    slopes = _alibi(16)[4 * g:4 * g + 4]
    kk = np.arange(128)[:, None]
    qq = np.arange(128)[None, :]
    bm = np.zeros((128, 3, 2, 4, 128), np.float64)
    for p, (window, dil) in enumerate(D_PATS):
        for kind in range(2):
            steps = qq + 128 - kk if kind == 0 else qq - kk
            valid = (steps >= 0) & (steps <= 128)
            for h in range(4):
                bm[:, p, kind, h, :] = np.where(valid, np.exp(-slopes[h] * np.where(valid, steps, 0) * dil), 0.0)
    d["bm"] = np.ascontiguousarray(bm.reshape(128, -1).astype(np.float32))
    d["w_out"] = np.ascontiguousarray(inp["d_w_out"][0][g * 512:(g + 1) * 512, :].reshape(4, 128, D).transpose(1, 0, 2))
    return d


_LAYER_INPUTS = {0: _layer0_inputs, 1: _layer1_inputs, 2: _layer2_inputs, 3: _layer3_inputs}


def run_layer(kind, xcur, inp, layer_idx):
    fn = _LAYER_INPUTS[kind]
    maps = []
    nw = np.ascontiguousarray(inp["norm_w"][layer_idx][None, :])
    for c in range(NCORES):
        b, g = c // 4, c % 4
        m = fn(inp, g)
        m["x"] = xcur[b]
        m["nw"] = nw
        maps.append(m)
    shapes = {k: v.shape for k, v in maps[0].items() if k not in ("x", "nw")}
    nc = build_layer(kind, shapes)
    res = run_bass_kernel_spmd(nc, maps, core_ids=list(range(NCORES)))
    return [r["y"] for r in res.results]


def run_reduce(xcur, ys):
    flat = xcur.reshape(2 * SEQ, D)
    maps = []
    for c in range(NCORES):
        b, q = c // 4, c % 4
        m = {"x": flat[c * 1024:(c + 1) * 1024]}
        for j in range(4):
            m[f"y{j}"] = ys[b * 4 + j][q * 1024:(q + 1) * 1024]
        maps.append(m)
    nc = build_reduce()
    res = run_bass_kernel_spmd(nc, maps, core_ids=list(range(NCORES)))
    return np.concatenate([r["out"] for r in res.results], axis=0).reshape(2, SEQ, D)


def run_fused(inp, layers=(0, 1, 2, 3), groups=(0, 1, 2, 3)):
    x = np.ascontiguousarray(inp["x"], dtype=np.float32)
    nw = np.ascontiguousarray(inp["norm_w"], dtype=np.float32)
    base = {}
    for L in layers:
        for g in range(4):
            for k, v in _LAYER_INPUTS[L](inp, g).items():
                base[f"L{L}g{g}_{k}"] = v
    shapes = {(L, g): {k[len(f"L{L}g{g}_"):]: v.shape for k, v in base.items() if k.startswith(f"L{L}g{g}_")}
              for L in layers for g in range(4)}
    nc = build_fused(shapes, layers, groups)
    maps = []
    for b in range(2):
        m = dict(base)
        m["x"] = x[b]
        m["nw"] = nw
        maps.append(m)
    res = run_bass_kernel_spmd(nc, maps, core_ids=[0, 1])
    return np.stack([res.results[b]["out"] for b in range(2)], axis=0)


def kernel(**inputs):
    inp = {k: np.asarray(v) for k, v in inputs.items()}
    return run_fused(inp)


def kernel_unfused(**inputs):
    inp = {k: np.asarray(v) for k, v in inputs.items()}
    x = np.ascontiguousarray(inp["x"], dtype=np.float32)
    for layer in range(4):
        ys = run_layer(layer % 4, x, inp, layer)
        x = run_reduce(x, ys)
    return x
```

```python
import math
from contextlib import ExitStack

import numpy as np
import concourse.bass as bass
import concourse.mybir as mybir
from concourse.bass_utils import run_bass_kernel_spmd

F32 = mybir.dt.float32
BF16 = mybir.dt.bfloat16
AF = mybir.ActivationFunctionType
ALU = mybir.AluOpType
AX = mybir.AxisListType

D = 4096
SEQ = 4096
NCORES = 8
EPS = 1e-6
TT = 512
NTT = SEQ // TT
P1TT = 1024


class Res:
    __slots__ = ("name", "w", "r", "p", "fk", "dsem", "dcnt", "excl")

    def __init__(self, name):
        self.name = name
        self.p = {}
        self.fk = set()
        self.excl = False
        self.w = {}
        self.r = {}
        self.dsem = None
        self.dcnt = 0


class Sched:
    def __init__(self, nc, stack):
        self.nc = nc
        self.stack = stack
        self.E = {"pe": nc.tensor, "dve": nc.vector, "act": nc.scalar,
                  "pool": nc.gpsimd, "sp": nc.sync}
        self.sem = {}
        self.cnt = {}
        for k in ("pe", "dve", "act", "pool"):
            self.sem[k] = stack.enter_context(nc.semaphore("cs_" + k))
            self.cnt[k] = 0
        self.waited = {k: {} for k in self.E}
        self.byname = {}
        self.dsems = []
        self.nwaits = 0
        self.nins = 0

    def res(self, name, dma=False):
        if name in self.byname:
            r = self.byname[name]
        else:
            r = Res(name)
            self.byname[name] = r
        if dma and r.dsem is None:
            r.dsem = self.stack.enter_context(self.nc.semaphore("ds_" + name))
            self.dsems.append(r)
        return r

    def _wait(self, e, key, tok):
        sem, val, teng = tok
        if self.waited[e].get(key, 0) >= val:
            return
        self.E[e].wait_ge(sem, val)
        self.waited[e][key] = val
        self.nwaits += 1

    def _deps(self, e, reads, writes, pwrites):
        for r in reads:
            for key, tok in r.w.items():
                self._wait(e, key, tok)
            if r.excl:
                for key, tok in r.r.items():
                    if tok[2] != e:
                        self._wait(e, key, tok)
        for w in writes:
            for key, tok in w.w.items():
                if tok[2] != e:
                    self._wait(e, key, tok)
            for key, tok in w.r.items():
                if tok[2] != e:
                    self._wait(e, key, tok)
        for w in pwrites:
            for key, tok in w.r.items():
                if tok[2] != e:
                    self._wait(e, key, tok)
            for key, tok in w.p.items():
                if tok[2] != e:
                    self._wait(e, key, tok)
            for key in w.fk:
                tok = w.w.get(key)
                if tok is not None and tok[2] != e:
                    self._wait(e, key, tok)

    def _commit(self, key, tok, reads, writes, pwrites):
        for r in reads:
            r.r[key] = tok
        for w in writes:
            prev = dict(w.w)
            for k2, t2 in w.r.items():
                if k2 not in prev or prev[k2][1] < t2[1]:
                    prev[k2] = t2
            w.p = prev
            w.w = {key: tok}
            w.fk = {key}
            w.r = {}
        for w in pwrites:
            w.w[key] = tok

    def op(self, e, fn, reads=(), writes=(), pwrites=(), inc=True):
        self._deps(e, reads, writes, pwrites)
        ins = fn(self.E[e])
        self.nins += 1
        if inc:
            self.cnt[e] += 1
            ins.then_inc(self.sem[e], 1)
            tok = (self.sem[e], self.cnt[e], e)
        else:
            tok = (self.sem[e], self.cnt[e] + 1, e)
        self._commit("c_" + e, tok, reads, writes, pwrites)
        return ins

    def dma(self, q, out, in_, reads=(), writes=(), pwrites=(), sem_res=None, **kw):
        self._deps(q, reads, writes, pwrites)
        ins = self.E[q].dma_start(out=out, in_=in_, **kw)
        self.nins += 1
        if sem_res is None:
            for c in list(writes) + list(pwrites) + list(reads):
                if c.dsem is not None:
                    sem_res = c
                    break
        assert sem_res is not None and sem_res.dsem is not None
        sem_res.dcnt += 16
        ins.then_inc(sem_res.dsem, 16)
        tok = (sem_res.dsem, sem_res.dcnt, "dma")
        self._commit("d_" + sem_res.name, tok, reads, writes, pwrites)
        return ins

    def barrier(self, engines=("pe", "dve", "act", "pool", "sp")):
        for e in engines:
            for f in ("pe", "dve", "act", "pool"):
                if f != e and self.cnt[f] > 0:
                    self._wait(e, "c_" + f, (self.sem[f], self.cnt[f], f))
            for r in self.dsems:
                if r.dcnt > 0:
                    self._wait(e, "d_" + r.name, (r.dsem, r.dcnt, "dma"))

    def finish(self):
        self.barrier(engines=("sp",))


class KB:
    def __init__(self, nc, st):
        self.nc = nc
        self.S = Sched(nc, st)
        self.top = st
        self.uid = 0

    def scope(self):
        return _Scope(self)


class _Scope:
    def __init__(self, kb):
        self.kb = kb
        self.st = ExitStack()

    def __enter__(self):
        self.st.__enter__()
        return self

    def __exit__(self, *a):
        self.kb.S.barrier()
        return self.st.__exit__(*a)

    def sb(self, name, shape, dt, dma=False):
        self.kb.uid += 1
        t = self.st.enter_context(self.kb.nc.sbuf_tensor(f"{name}_{self.kb.uid}", shape, dt))
        return t, self.kb.S.res(name, dma)

    def ps(self, name, shape, dt):
        self.kb.uid += 1
        t = self.st.enter_context(self.kb.nc.psum_tensor(f"{name}_{self.kb.uid}", shape, dt))
        r = self.kb.S.res(name)
        r.excl = True
        return t, r


def make_ident(S, sc):
    identf, r_if = sc.sb("identf", [128, 128], F32)
    ident, r_id = sc.sb("ident", [128, 128], BF16)
    S.op("pool", lambda e: e.memset(identf[:], 0.0), writes=[r_if])
    S.op("pool", lambda e: e.affine_select(
        out=identf[:], in_=identf[:], pattern=[[-1, 128]], compare_op=ALU.not_equal,
        fill=1.0, base=0, channel_multiplier=1), reads=[r_if], writes=[r_if])
    S.op("dve", lambda e: e.tensor_copy(out=ident[:], in_=identf[:]), reads=[r_if], writes=[r_id])
    return ident, r_id, identf, r_if


def phase1(kb, x_src, nw_ap, groups):
    S = kb.S
    with kb.scope() as sc:
        ident, r_id, _, _ = make_ident(S, sc)
        nwb, r_nwb = sc.sb("p1_nwb", [128, D], F32, dma=True)
        xt = [sc.sb(f"p1_xt{i}", [128, D], F32, dma=True) for i in range(2)]
        xn = [sc.sb(f"p1_xn{i}", [128, D], BF16) for i in range(2)]
        ss = [sc.sb(f"p1_ss{i}", [128, 2], F32) for i in range(2)]
        hT, r_hT = sc.sb("p1_hT", [128, 32, P1TT], BF16)
        NWB = 2
        wb = [sc.sb(f"p1_wb{i}", [128, 32, 256], BF16, dma=True) for i in range(NWB)]
        NOB = 4
        ob = [sc.sb(f"p1_ob{i}", [128, 512], F32, dma=True) for i in range(NOB)]
        obh = [sc.sb(f"p1_obh{i}", [128, 512], BF16, dma=True) for i in range(NOB)]
        pt = [sc.ps(f"p1_pt{i}", [128, 1024], BF16) for i in range(2)]
        pm = [sc.ps(f"p1_pm{i}", [128, 512], F32) for i in range(4)]
        S.dma("sp", nwb[:], nw_ap.partition_broadcast(128), writes=[r_nwb])
        wi = 0
        oi = 0
        pi = 0
        ev = 0
        for T in range(SEQ // P1TT):
            for s in range(P1TT // 128):
                b = s % 2
                t0 = T * P1TT + s * 128
                xt_t, xt_r = xt[b]
                xn_t, xn_r = xn[b]
                ss_t, ss_r = ss[b]
                S.dma("sp", xt_t[:], x_src[t0:t0 + 128, :], writes=[xt_r])
                S.op("act", lambda e: e.activation(out=xn_t[:], in_=xt_t[:], func=AF.Square,
                                                   accum_out=ss_t[:, 0:1]),
                     reads=[xt_r], writes=[xn_r, ss_r])
                S.op("act", lambda e: e.activation(out=ss_t[:, 1:2], in_=ss_t[:, 0:1], func=AF.Sqrt,
                                                   scale=1.0 / D, bias=EPS),
                     reads=[ss_r], writes=[ss_r])
                S.op("dve", lambda e: e.reciprocal(out=ss_t[:, 0:1], in_=ss_t[:, 1:2]),
                     reads=[ss_r], writes=[ss_r])
                S.op("dve", lambda e: e.scalar_tensor_tensor(out=xn_t[:], in0=xt_t[:], scalar=ss_t[:, 0:1],
                                                             in1=nwb[:], op0=ALU.mult, op1=ALU.mult),
                     reads=[xt_r, ss_r, r_nwb], writes=[xn_r])
                for g4 in range(4):
                    pt_t, pt_r = pt[g4 % 2]
                    for j in range(8):
                        kc = g4 * 8 + j
                        S.op("pe", lambda e: e.transpose(out=pt_t[:, j * 128:(j + 1) * 128],
                                                         in_=xn_t[:, kc * 128:(kc + 1) * 128], identity=ident[:]),
                             reads=[xn_r, r_id], writes=[pt_r] if j == 0 else [],
                             pwrites=[] if j == 0 else [pt_r], inc=(j == 7))
                    dst = hT[:, g4 * 8:(g4 + 1) * 8, s * 128:(s + 1) * 128]
                    src = pt_t[:].rearrange("p (j t) -> p j t", j=8)
                    if g4 % 2 == 0:
                        S.op("dve", lambda e: e.tensor_copy(out=dst, in_=src), reads=[pt_r], pwrites=[r_hT])
                    else:
                        S.op("act", lambda e: e.copy(out=dst, in_=src), reads=[pt_r], pwrites=[r_hT])
            for g in groups:
                W = g["W"]
                odt = g["dst"].dtype
                for c in range(g["nchunk"]):
                    wb_t, wb_r = wb[wi % NWB]
                    wi += 1
                    S.dma("pool", wb_t[:, :, :W], g["w"][c], writes=[wb_r])
                    if g["layout"] == "fm":
                        for hh in range(P1TT // 512):
                            pm_t, pm_r = pm[pi % 4]
                            pi += 1
                            for kc in range(32):
                                S.op("pe", lambda e: e.matmul(pm_t[:W, :512], lhsT=wb_t[:, kc, :W],
                                                              rhs=hT[:, kc, hh * 512:(hh + 1) * 512],
                                                              start=(kc == 0), stop=(kc == 31)),
                                     reads=[wb_r, r_hT], writes=[pm_r] if kc == 0 else [],
                                     pwrites=[] if kc == 0 else [pm_r], inc=(kc == 31))
                            ob_t, ob_r = (ob if odt == F32 else obh)[oi % NOB]
                            oi += 1
                            if ev % 2 == 0:
                                S.op("dve", lambda e: e.tensor_copy(out=ob_t[:W, :512], in_=pm_t[:W, :512]),
                                     reads=[pm_r], writes=[ob_r])
                            else:
                                S.op("act", lambda e: e.copy(out=ob_t[:W, :512], in_=pm_t[:W, :512]),
                                     reads=[pm_r], writes=[ob_r])
                            ev += 1
                            c0 = T * P1TT + hh * 512
                            S.dma("sp", g["dst"][c * W:(c + 1) * W, c0:c0 + 512], ob_t[:W, :512],
                                  reads=[ob_r], pwrites=[g["rdst"]])
                    else:
                        for s in range(P1TT // 128):
                            pm_t, pm_r = pm[pi % 4]
                            pi += 1
                            for kc in range(32):
                                S.op("pe", lambda e: e.matmul(pm_t[:, :W], lhsT=hT[:, kc, s * 128:(s + 1) * 128],
                                                              rhs=wb_t[:, kc, :W],
                                                              start=(kc == 0), stop=(kc == 31)),
                                     reads=[wb_r, r_hT], writes=[pm_r] if kc == 0 else [],
                                     pwrites=[] if kc == 0 else [pm_r], inc=(kc == 31))
                            ob_t, ob_r = (ob if odt == F32 else obh)[oi % NOB]
                            oi += 1
                            if ev % 2 == 0:
                                S.op("dve", lambda e: e.tensor_copy(out=ob_t[:, :W], in_=pm_t[:, :W]),
                                     reads=[pm_r], writes=[ob_r])
                            else:
                                S.op("act", lambda e: e.copy(out=ob_t[:, :W], in_=pm_t[:, :W]),
                                     reads=[pm_r], writes=[ob_r])
                            ev += 1
                            t0 = T * P1TT + s * 128
                            S.dma("sp", g["dst"][t0:t0 + 128, c * W:(c + 1) * W], ob_t[:, :W],
                                  reads=[ob_r], pwrites=[g["rdst"]])


def phase3(kb, o_scr, r_o, P, n_ic, wout_ap, y_dst, r_y, accum=None):
    S = kb.S
    with kb.scope() as sc:
        w, r_w = sc.sb("p3_w", [P, n_ic, D], BF16, dma=True)
        for ic in range(n_ic):
            S.dma("pool", w[:, ic, :], wout_ap[:, ic, :], pwrites=[r_w])
        ot = [sc.sb(f"p3_ot{i}", [P, n_ic, TT], BF16, dma=True) for i in range(2)]
        yb = [sc.sb(f"p3_yb{i}", [128, D], F32, dma=True) for i in range(2)]
        if accum is not None:
            xds = [sc.sb(f"p3_xd{i}", [128, D], F32, dma=True) for i in range(2 if n_ic <= 8 else 1)]
        pm = [sc.ps(f"p3_pm{i}", [128, 512], F32) for i in range(4)]
        pi = 0
        yi = 0
        o_v = o_scr.rearrange("(ic p) t -> p ic t", p=P)
        for T in range(NTT):
            ot_t, ot_r = ot[T % 2]
            S.dma("sp", ot_t[:], o_v[:, :, T * TT:(T + 1) * TT], reads=[r_o], writes=[ot_r])
            for s in range(TT // 128):
                yb_t, yb_r = yb[yi % 2]
                yi += 1
                for cc in range(D // 512):
                    pm_t, pm_r = pm[pi % 4]
                    pi += 1
                    for ic in range(n_ic):
                        S.op("pe", lambda e: e.matmul(pm_t[:, :], lhsT=ot_t[:, ic, s * 128:(s + 1) * 128],
                                                      rhs=w[:, ic, cc * 512:(cc + 1) * 512],
                                                      start=(ic == 0), stop=(ic == n_ic - 1)),
                             reads=[ot_r, r_w], writes=[pm_r] if ic == 0 else [],
                             pwrites=[] if ic == 0 else [pm_r], inc=(ic == n_ic - 1))
                    if cc % 2 == 0:
                        S.op("dve", lambda e: e.tensor_copy(out=yb_t[:, cc * 512:(cc + 1) * 512], in_=pm_t[:, :]),
                             reads=[pm_r], writes=[yb_r] if cc == 0 else [], pwrites=[] if cc == 0 else [yb_r])
                    else:
                        S.op("act", lambda e: e.copy(out=yb_t[:, cc * 512:(cc + 1) * 512], in_=pm_t[:, :]),
                             reads=[pm_r], pwrites=[yb_r])
                t0 = T * TT + s * 128
                if accum is not None:
                    xd, r_xd = xds[yi % len(xds)]
                    S.dma("sp", xd[:], accum[t0:t0 + 128, :], writes=[r_xd], sem_res=r_xd)
                    hD = D // 2
                    S.op("dve", lambda e: e.tensor_tensor(out=yb_t[:, 0:hD], in0=yb_t[:, 0:hD], in1=xd[:, 0:hD], op=ALU.add),
                         reads=[yb_r, r_xd], writes=[yb_r])
                    S.op("pool", lambda e: e.tensor_tensor(out=yb_t[:, hD:D], in0=yb_t[:, hD:D], in1=xd[:, hD:D], op=ALU.add),
                         reads=[yb_r, r_xd], writes=[yb_r])
                S.dma("sp", y_dst[t0:t0 + 128, :], yb_t[:], reads=[yb_r], pwrites=[r_y], sem_res=yb_r)


C_P = 112
C_NCH = 12


def mixer_rglru(kb, xbT, r_xb, zT, r_z, prm, o_scr, r_o):
    S = kb.S
    P = C_P
    with kb.scope() as sc:
        cw, r_cw = sc.sb("c_cw", [P, C_NCH, 4], F32, dma=True)
        vec, r_vec = sc.sb("c_vec", [P, 4, C_NCH], F32, dma=True)
        c1, r_c1 = sc.sb("c_c1", [P, C_NCH], F32)
        wa, r_wa = sc.sb("c_wa", [P, 4, 3, 336], BF16, dma=True)
        wx, r_wx = sc.sb("c_wx", [P, 4, 3, 336], BF16, dma=True)
        S.dma("sp", cw[:], prm["cw"], writes=[r_cw])
        S.dma("sp", vec[:], prm["vec"], writes=[r_vec])
        S.dma("pool", wa[:], prm["wa"], writes=[r_wa])
        S.dma("pool", wx[:], prm["wx"], writes=[r_wx])
        S.op("act", lambda e: e.activation(out=c1[:], in_=vec[:, 3, :], func=AF.Exp, scale=-1.0),
             reads=[r_vec], writes=[r_c1])
        S.op("act", lambda e: e.activation(out=c1[:], in_=c1[:], func=AF.Ln, bias=1.0, scale=1.0),
             reads=[r_c1], writes=[r_c1])
        S.op("dve", lambda e: e.tensor_scalar(out=c1[:], in0=c1[:], scalar1=-8.0, scalar2=None, op0=ALU.mult),
             reads=[r_c1], writes=[r_c1])
        xb, r_xbs = sc.sb("c_xb", [P, SEQ + 4], F32, dma=True)
        xc = [sc.sb(f"c_xc{i}", [P, SEQ], F32) for i in range(3)]
        xcb, r_xcb = sc.sb("c_xcb", [P, 3, SEQ], BF16)
        ra, r_ra = sc.sb("c_ra", [P, SEQ], F32)
        gi, r_gi = sc.sb("c_gi", [P, SEQ], F32)
        tmp, r_tmp = sc.sb("c_tmp", [P, SEQ], F32)
        zt, r_zt = sc.sb("c_zt", [P, SEQ], BF16, dma=True)
        ob, r_ob = sc.sb("c_ob", [P, SEQ], BF16, dma=True)
        pm = [sc.ps(f"c_pm{i}", [128, 512], F32) for i in range(4)]
        pi = 0
        S.op("pool", lambda e: e.memset(xb[:, 0:4], 0.0), writes=[r_xbs])
        for n in range(4):
            for c in range(3):
                ch = n * 3 + c
                xc_t, xc_r = xc[c]
                S.dma("sp", xb[:, 4:], xbT[ch * P:(ch + 1) * P, :], reads=[r_xb], writes=[r_xbs])
                S.op("dve", lambda e: e.tensor_scalar(out=xc_t[:], in0=xb[:, 1:1 + SEQ], scalar1=cw[:, ch, 0:1],
                                                      scalar2=vec[:, 0, ch:ch + 1], op0=ALU.mult, op1=ALU.add),
                     reads=[r_xbs, r_cw, r_vec], writes=[xc_r])
                for j in range(1, 4):
                    S.op("dve", lambda e: e.scalar_tensor_tensor(out=xc_t[:], in0=xb[:, 1 + j:1 + j + SEQ],
                                                                 scalar=cw[:, ch, j:j + 1], in1=xc_t[:],
                                                                 op0=ALU.mult, op1=ALU.add),
                         reads=[r_xbs, r_cw, xc_r], writes=[xc_r])
                S.op("act", lambda e: e.copy(out=xcb[:, c, :], in_=xc_t[:]), reads=[xc_r],
                     writes=[r_xcb] if c == 0 else [], pwrites=[] if c == 0 else [r_xcb])
            for d in range(3):
                ch = n * 3 + d
                xc_t, xc_r = xc[d]
                S.dma("sp", zt[:], zT[ch * P:(ch + 1) * P, :], reads=[r_z], writes=[r_zt])
                for (wt, wr, dst, dr, bi) in ((wa, r_wa, ra, r_ra, 1), (wx, r_wx, gi, r_gi, 2)):
                    for T in range(NTT):
                        pm_t, pm_r = pm[pi % 4]
                        pi += 1
                        for c in range(3):
                            S.op("pe", lambda e: e.matmul(pm_t[:P, :TT], lhsT=wt[:, n, c, d * P:(d + 1) * P],
                                                          rhs=xcb[:, c, T * TT:(T + 1) * TT],
                                                          start=(c == 0), stop=(c == 2)),
                                 reads=[wr, r_xcb], writes=[pm_r] if c == 0 else [],
                                 pwrites=[] if c == 0 else [pm_r], inc=(c == 2))
                        S.op("act", lambda e: e.activation(out=dst[:, T * TT:(T + 1) * TT], in_=pm_t[:P, :TT],
                                                           func=AF.Sigmoid, bias=vec[:, bi, ch:ch + 1], scale=1.0),
                             reads=[pm_r, r_vec], writes=[dr] if T == 0 else [], pwrites=[] if T == 0 else [dr])
                S.op("act", lambda e: e.activation(out=ra[:], in_=ra[:], func=AF.Exp, scale=c1[:, ch:ch + 1]),
                     reads=[r_ra, r_c1], writes=[r_ra])
                S.op("pool", lambda e: e.tensor_tensor(out=tmp[:], in0=ra[:], in1=ra[:], op=ALU.mult),
                     reads=[r_ra], writes=[r_tmp])
                S.op("act", lambda e: e.activation(out=tmp[:], in_=tmp[:], func=AF.Sqrt, scale=-1.0, bias=1.0),
                     reads=[r_tmp], writes=[r_tmp])
                S.op("dve", lambda e: e.tensor_tensor(out=gi[:], in0=gi[:], in1=xc_t[:], op=ALU.mult),
                     reads=[r_gi, xc_r], writes=[r_gi])
                S.op("pool", lambda e: e.tensor_tensor(out=gi[:], in0=gi[:], in1=tmp[:], op=ALU.mult),
                     reads=[r_gi, r_tmp], writes=[r_gi])
                S.op("dve", lambda e: e.tensor_tensor_scan(out=tmp[:], data0=ra[:], data1=gi[:], initial=0.0,
                                                           op0=ALU.mult, op1=ALU.add),
                     reads=[r_ra, r_gi], writes=[r_tmp])
                S.op("act", lambda e: e.activation(out=zt[:], in_=zt[:], func=AF.Silu), reads=[r_zt], writes=[r_zt])
                S.op("pool", lambda e: e.tensor_tensor(out=ob[:], in0=tmp[:], in1=zt[:], op=ALU.mult),
                     reads=[r_tmp, r_zt], writes=[r_ob])
                S.dma("sp", o_scr[ch * P:(ch + 1) * P, :], ob[:], reads=[r_ob], pwrites=[r_o])


D_PATS = ((128, 1), (512, 4), (2048, 16))


def mixer_dilated(kb, u, r_u, prm, o_scr, r_o, nd_scr, r_nd):
    S = kb.S
    scale = 128.0 ** -0.5
    with kb.scope() as sc:
        ident, r_id, _, _ = make_ident(S, sc)
        wqk, r_wqk = sc.sb("d_wqk", [128, 2, 128], F32, dma=True)
        bm, r_bm = sc.sb("d_bm", [128, 3, 2, 512], F32, dma=True)
        S.dma("sp", wqk[:].rearrange("p a d -> p (a d)"), prm["qkw"].partition_broadcast(128), writes=[r_wqk])
        S.dma("sp", bm[:].rearrange("p a b c -> p (a b c)"), prm["bm"], writes=[r_bm])
        blk = [sc.sb(f"d_blk{i}", [128, 1536], BF16, dma=True) for i in range(2)]
        sq, r_sq = sc.sb("d_sq", [128, 1024], F32)
        ssq, r_ssq = sc.sb("d_ssq", [128, 8], F32)
        rstd, r_rstd = sc.sb("d_rstd", [128, 8], F32)
        qkn, r_qkn = sc.sb("d_qkn", [128, 1024], BF16)
        qT, r_qT = sc.sb("d_qT", [128, 512], BF16)
        kT = [sc.sb(f"d_kT{i}", [128, 512], BF16) for i in range(2)]
        va = [sc.sb(f"d_va{i}", [128, 4, 129], BF16) for i in range(2)]
        pex = [sc.sb(f"d_pex{i}", [128, 512], F32) for i in range(2)]
        ptm = [sc.sb(f"d_ptm{i}", [128, 512], BF16) for i in range(2)]
        ndst = [sc.sb(f"d_ndst{i}", [128, 4, 129], F32, dma=True) for i in range(2)]
        pt, r_pt = sc.ps("d_pt", [128, 1024], BF16)
        pss = [sc.ps(f"d_pss{i}", [128, 512], F32) for i in range(4)]
        accs = [sc.ps(f"d_acc{i}", [128, 512], F32) for i in range(2)]
        for i in range(2):
            S.op("pool", lambda e: e.memset(va[i][0][:], 1.0), writes=[va[i][1]])
        bi = 0
        si = 0
        for p, (window, dil) in enumerate(D_PATS):
            uv = u.rearrange("(n dl) c -> dl n c", dl=dil)
            ndv = nd_scr[p].rearrange("(n dl) c -> dl n c", dl=dil)
            for r in range(dil):
                for i in range(SEQ // dil // 128):
                    blk_t, blk_r = blk[bi % 2]
                    nd_t, nd_r = ndst[bi % 2]
                    bi += 1
                    cur = i % 2
                    kT_t, kT_r = kT[cur]
                    va_t, va_r = va[cur]
                    S.dma("sp", blk_t[:], uv[r, i * 128:(i + 1) * 128, p * 1536:(p + 1) * 1536],
                          reads=[r_u], writes=[blk_r])
                    S.op("dve", lambda e: e.tensor_tensor(out=sq[:], in0=blk_t[:, 0:1024], in1=blk_t[:, 0:1024],
                                                          op=ALU.mult), reads=[blk_r], writes=[r_sq])
                    S.op("dve", lambda e: e.tensor_reduce(out=ssq[:], in_=sq[:].rearrange("p (h d) -> p h d", h=8),
                                                          axis=AX.X, op=ALU.add), reads=[r_sq], writes=[r_ssq])
                    S.op("act", lambda e: e.activation(out=rstd[:], in_=ssq[:], func=AF.Sqrt, scale=1.0 / 128, bias=EPS),
                         reads=[r_ssq], writes=[r_rstd])
                    S.op("dve", lambda e: e.reciprocal(out=rstd[:], in_=rstd[:]), reads=[r_rstd], writes=[r_rstd])
                    S.op("dve", lambda e: e.tensor_tensor(
                        out=sq[:].rearrange("p (h d) -> p h d", h=8),
                        in0=blk_t[:, 0:1024].rearrange("p (h d) -> p h d", h=8),
                        in1=rstd[:, :].unsqueeze(2).to_broadcast([128, 8, 128]), op=ALU.mult),
                        reads=[blk_r, r_rstd], writes=[r_sq])
                    S.op("pool", lambda e: e.tensor_tensor(
                        out=qkn[:].rearrange("p (a h d) -> p a h d", a=2, h=4),
                        in0=sq[:].rearrange("p (a h d) -> p a h d", a=2, h=4),
                        in1=wqk[:, :, :].unsqueeze(2).to_broadcast([128, 2, 4, 128]), op=ALU.mult),
                        reads=[r_sq, r_wqk], writes=[r_qkn])
                    for j in range(8):
                        S.op("pe", lambda e: e.transpose(out=pt[:, j * 128:(j + 1) * 128],
                                                         in_=qkn[:, j * 128:(j + 1) * 128], identity=ident[:]),
                             reads=[r_qkn, r_id], writes=[r_pt] if j == 0 else [],
                             pwrites=[] if j == 0 else [r_pt], inc=(j == 7))
                    S.op("dve", lambda e: e.tensor_copy(out=qT[:], in_=pt[:, 0:512]), reads=[r_pt], writes=[r_qT])
                    S.op("act", lambda e: e.copy(out=kT_t[:], in_=pt[:, 512:1024]), reads=[r_pt], writes=[kT_r])
                    S.op("pool", lambda e: e.tensor_copy(out=va_t[:, :, 0:128],
                                                         in_=blk_t[:, 1024:1536].rearrange("p (h d) -> p h d", h=4)),
                         reads=[blk_r], writes=[va_r])
                    tiles = ([(1 - cur, 0)] if i > 0 else []) + [(cur, 1)]
                    pts = []
                    for ti, (slot, kind) in enumerate(tiles):
                        ps_t, ps_r = pss[si % 4]
                        pe_t, pe_r = pex[si % 2]
                        pm_t, pm_r = ptm[si % 2]
                        si += 1
                        for h in range(4):
                            S.op("pe", lambda e: e.matmul(ps_t[:, h * 128:(h + 1) * 128],
                                                          lhsT=kT[slot][0][:, h * 128:(h + 1) * 128],
                                                          rhs=qT[:, h * 128:(h + 1) * 128], start=True, stop=True),
                                 reads=[kT[slot][1], r_qT], writes=[ps_r] if h == 0 else [],
                                 pwrites=[] if h == 0 else [ps_r], inc=(h == 3))
                        S.op("act", lambda e: e.activation(out=pe_t[:], in_=ps_t[:], func=AF.Exp, scale=scale),
                             reads=[ps_r], writes=[pe_r])
                        S.op("dve", lambda e: e.tensor_tensor(out=pm_t[:], in0=pe_t[:], in1=bm[:, p, kind, :],
                                                              op=ALU.mult), reads=[pe_r, r_bm], writes=[pm_r])
                        pts.append((pm_t, pm_r, slot))
                    for h in range(4):
                        off = (h % 2) * 129
                        acc, r_acc = accs[h // 2]
                        for ti, (pm_t, pm_r, slot) in enumerate(pts):
                            last = (ti == len(pts) - 1)
                            S.op("pe", lambda e: e.matmul(acc[:, off:off + 129], lhsT=pm_t[:, h * 128:(h + 1) * 128],
                                                          rhs=va[slot][0][:, h, :], start=(ti == 0), stop=last),
                                 reads=[pm_r, va[slot][1]],
                                 writes=[r_acc] if (ti == 0 and h % 2 == 0) else [],
                                 pwrites=[] if (ti == 0 and h % 2 == 0) else [r_acc],
                                 inc=(last and h % 2 == 1))
                    S.op("dve", lambda e: e.tensor_copy(
                        out=nd_t[:, 0:2, :], in_=accs[0][0][:, 0:258].rearrange("p (h c) -> p h c", h=2)),
                        reads=[accs[0][1]], writes=[nd_r])
                    S.op("act", lambda e: e.copy(
                        out=nd_t[:, 2:4, :], in_=accs[1][0][:, 0:258].rearrange("p (h c) -> p h c", h=2)),
                        reads=[accs[1][1]], pwrites=[nd_r])
                    S.dma("sp", ndv[r, i * 128:(i + 1) * 128, :], nd_t[:].rearrange("p h c -> p (h c)"),
                          reads=[nd_r], pwrites=[r_nd])
    with kb.scope() as sc:
        ident, r_id, _, _ = make_ident(S, sc)
        nds = [[sc.sb(f"d2_nd{i}_{j}", [128, 4, 129], F32, dma=True) for j in range(3)] for i in range(2)]
        zt = [sc.sb(f"d2_z{i}", [128, 512], BF16, dma=True) for i in range(2)]
        rden, r_rden = sc.sb("d2_rden", [128, 4], F32)
        of, r_of = sc.sb("d2_of", [128, 4, 128], F32)
        zs, r_zs = sc.sb("d2_zs", [128, 512], F32)
        ob, r_ob = sc.sb("d2_ob", [128, 512], BF16)
        oT = [sc.sb(f"d2_oT{i}", [128, 4, TT], BF16, dma=True) for i in range(2)]
        pt, r_pt = sc.ps("d2_pt", [128, 1024], BF16)
        for t in range(SEQ // 128):
            a = nds[t % 2]
            z_t, z_r = zt[t % 2]
            oT_t, oT_r = oT[(t // 4) % 2]
            for j in range(3):
                S.dma("sp", a[j][0][:].rearrange("p h c -> p (h c)"), nd_scr[j, t * 128:(t + 1) * 128, :],
                      reads=[r_nd], writes=[a[j][1]])
            S.dma("sp", z_t[:], u[t * 128:(t + 1) * 128, 4608:5120], reads=[r_u], writes=[z_r])
            s_t, s_r = a[0]
            S.op("dve", lambda e: e.tensor_tensor(out=s_t[:], in0=s_t[:], in1=a[1][0][:], op=ALU.add),
                 reads=[s_r, a[1][1]], writes=[s_r])
            S.op("pool", lambda e: e.tensor_tensor(out=s_t[:], in0=s_t[:], in1=a[2][0][:], op=ALU.add),
                 reads=[s_r, a[2][1]], writes=[s_r])
            S.op("dve", lambda e: e.reciprocal(out=rden[:], in_=s_t[:, :, 128]), reads=[s_r], writes=[r_rden])
            S.op("dve", lambda e: e.tensor_tensor(out=of[:], in0=s_t[:, :, 0:128],
                                                  in1=rden[:, :].unsqueeze(2).to_broadcast([128, 4, 128]), op=ALU.mult),
                 reads=[s_r, r_rden], writes=[r_of])
            S.op("act", lambda e: e.activation(out=zs[:], in_=z_t[:], func=AF.Silu), reads=[z_r], writes=[r_zs])
            S.op("pool", lambda e: e.tensor_tensor(out=ob[:], in0=of[:].rearrange("p h d -> p (h d)"), in1=zs[:],
                                                   op=ALU.mult), reads=[r_of, r_zs], writes=[r_ob])
            for h in range(4):
                S.op("pe", lambda e: e.transpose(out=pt[:, h * 128:(h + 1) * 128], in_=ob[:, h * 128:(h + 1) * 128],
                                                 identity=ident[:]),
                     reads=[r_ob, r_id], writes=[r_pt] if h == 0 else [], pwrites=[] if h == 0 else [r_pt],
                     inc=(h == 3))
            q4 = t % 4
            S.op("act", lambda e: e.copy(out=oT_t[:, :, q4 * 128:(q4 + 1) * 128],
                                         in_=pt[:, 0:512].rearrange("p (h t) -> p h t", h=4)),
                 reads=[r_pt], writes=[oT_r] if q4 == 0 else [], pwrites=[] if q4 == 0 else [oT_r])
            if q4 == 3:
                T = t // 4
                S.dma("sp", o_scr.rearrange("(h p) t -> p h t", p=128)[:, :, T * TT:(T + 1) * TT], oT_t[:],
                      reads=[oT_r], pwrites=[r_o])


def mixer_mlstm(kb, qkT, r_qk, utm, r_utm, gates, r_g, prm, o_scr, r_o):
    S = kb.S
    NCH = SEQ // 128
    with kb.scope() as sc0:
        qkb, r_qkb = sc0.sb("a_qkb", [128, 8, SEQ], BF16)
        with kb.scope() as sc:
            cw, r_cw = sc.sb("a_cw", [128, 8, 4], F32, dma=True)
            cb, r_cb = sc.sb("a_cb", [128, 8], F32, dma=True)
            S.dma("sp", cw[:], prm["cw"], writes=[r_cw])
            S.dma("sp", cb[:], prm["cb"], writes=[r_cb])
            xb = [sc.sb(f"a_xb{i}", [128, SEQ + 4], F32, dma=True) for i in range(2)]
            xc, r_xc = sc.sb("a_xc", [128, SEQ], F32)
            for i in range(2):
                S.op("pool", lambda e: e.memset(xb[i][0][:, 0:4], 0.0), writes=[xb[i][1]])
            for ch in range(8):
                xb_t, xb_r = xb[ch % 2]
                S.dma("sp", xb_t[:, 4:], qkT[ch * 128:(ch + 1) * 128, :], reads=[r_qk], writes=[xb_r])
                S.op("dve", lambda e: e.tensor_scalar(out=xc[:], in0=xb_t[:, 1:1 + SEQ], scalar1=cw[:, ch, 0:1],
                                                      scalar2=cb[:, ch:ch + 1], op0=ALU.mult, op1=ALU.add),
                     reads=[xb_r, r_cw, r_cb], writes=[r_xc])
                for j in range(1, 4):
                    S.op("dve", lambda e: e.scalar_tensor_tensor(out=xc[:], in0=xb_t[:, 1 + j:1 + j + SEQ],
                                                                 scalar=cw[:, ch, j:j + 1], in1=xc[:],
                                                                 op0=ALU.mult, op1=ALU.add),
                         reads=[xb_r, r_cw, r_xc], writes=[r_xc])
                S.op("act", lambda e: e.activation(out=qkb[:, ch, :], in_=xc[:], func=AF.Silu),
                     reads=[r_xc], writes=[r_qkb] if ch == 0 else [], pwrites=[] if ch == 0 else [r_qkb])
        with kb.scope() as sc:
            ident, r_id, identf, r_if = make_ident(S, sc)
            triu, r_tri = sc.sb("a_triu", [128, 128], F32)
            ones, r_ones = sc.sb("a_ones", [128, 128], F32)
            onesb, r_onesb = sc.sb("a_onesb", [128, 1], BF16)
            S.op("pool", lambda e: e.memset(triu[:], 1.0), writes=[r_tri])
            S.op("pool", lambda e: e.affine_select(out=triu[:], in_=triu[:], pattern=[[1, 128]],
                                                   compare_op=ALU.is_ge, fill=0.0, base=0, channel_multiplier=-1),
                 reads=[r_tri], writes=[r_tri])
            S.op("pool", lambda e: e.memset(ones[:], 1.0), writes=[r_ones])
            S.op("pool", lambda e: e.memset(onesb[:], 1.0), writes=[r_onesb])
            onw, r_onw = sc.sb("a_onw", [128, 1024], F32, dma=True)
            S.dma("sp", onw[:], prm["onw"].partition_broadcast(128), writes=[r_onw])
            gb, r_gb = sc.sb("a_gb", [128, 4], F32, dma=True)
            S.dma("sp", gb[:], prm["gb"].partition_broadcast(128), writes=[r_gb])
            G, r_G = sc.sb("a_G", [128, NCH, 4], F32, dma=True)
            gv = gates.rearrange("(c p) g -> p c g", p=128)
            for i4 in range(4):
                S.dma("sp", G[:, i4 * 8:(i4 + 1) * 8, :], gv[:, i4 * 8:(i4 + 1) * 8, :], reads=[r_g],
                      writes=[r_G] if i4 == 0 else [], pwrites=[] if i4 == 0 else [r_G])
            S.op("dve", lambda e: e.tensor_tensor(out=G[:], in0=G[:], in1=gb[:, :].unsqueeze(1).to_broadcast([128, NCH, 4]),
                                                  op=ALU.add), reads=[r_G, r_gb], writes=[r_G])
            lf, r_lf = sc.sb("a_lf", [128, NCH, 2], F32)
            S.op("act", lambda e: e.activation(out=lf[:], in_=G[:, :, 2:4], func=AF.Exp, scale=-1.0),
                 reads=[r_G], writes=[r_lf])
            S.op("act", lambda e: e.activation(out=lf[:], in_=lf[:], func=AF.Ln, bias=1.0, scale=1.0),
                 reads=[r_lf], writes=[r_lf])
            S.op("dve", lambda e: e.tensor_scalar(out=lf[:], in0=lf[:], scalar1=-1.0, scalar2=None, op0=ALU.mult),
                 reads=[r_lf], writes=[r_lf])
            ps_b, r_psb = sc.ps("a_psb", [128, 512], F32)
            S.op("pe", lambda e: e.matmul(ps_b[:, 0:2 * NCH], lhsT=triu[:], rhs=lf[:].rearrange("p c h -> p (c h)"),
                                          start=True, stop=True), reads=[r_tri, r_lf], writes=[r_psb])
            cs, r_cs = sc.sb("a_cs", [128, NCH, 2], F32)
            ecs, r_ecs = sc.sb("a_ecs", [128, NCH, 2], F32)
            S.op("dve", lambda e: e.tensor_tensor(out=cs[:], in0=G[:, :, 0:2],
                                                  in1=ps_b[:, 0:2 * NCH].rearrange("p (c h) -> p c h", h=2),
                                                  op=ALU.subtract), reads=[r_G, r_psb], writes=[r_cs])
            S.op("act", lambda e: e.activation(out=ecs[:], in_=cs[:], func=AF.Exp), reads=[r_cs], writes=[r_ecs])
            Cst, r_C = sc.sb("a_C", [128, 2, 2, 512], F32)
            Cb, r_Cb = sc.sb("a_Cb", [128, 2, 2, 512], BF16)
            nst, r_n = sc.sb("a_n", [128, 2, 2], F32)
            nb, r_nb = sc.sb("a_nb", [128, 2, 2], BF16)
            S.op("pool", lambda e: e.memset(Cst[:], 0.0), writes=[r_C])
            S.op("pool", lambda e: e.memset(nst[:], 0.0), writes=[r_n])
            LT, r_LT = sc.sb("a_LT", [128, 4, 128], F32)
            EB = [sc.sb(f"a_EB{i}", [128, 4, 128], F32) for i in range(2)]
            DT = [sc.sb(f"a_DT{i}", [128, 128], F32) for i in range(4)]
            ws, r_ws = sc.sb("a_ws", [128, 4], F32)
            wsb, r_wsb = sc.sb("a_wsb", [128, 4], BF16)
            vt = [sc.sb(f"a_vt{i}", [128, 3072], BF16, dma=True) for i in range(2)]
            aT, r_aT = sc.sb("a_aT", [128, 128], BF16)
            qp, r_qp = sc.sb("a_qp", [128, 2, 128], BF16)
            vp, r_vp = sc.sb("a_vp", [128, 512], BF16)
            ktm, r_ktm = sc.sb("a_ktm", [128, 256], BF16)
            den, r_den = sc.sb("a_den", [128, 4], F32)
            hc, r_hc = sc.sb("a_hc", [128, 512], F32)
            junk, r_junk = sc.sb("a_junk", [128, 512], F32)
            t1, r_t1 = sc.sb("a_t1", [128, 512], F32)
            sg, r_sg = sc.sb("a_sg", [128, 512], F32)
            sz, r_sz = sc.sb("a_sz", [128, 512], F32)
            yb, r_yb = sc.sb("a_yb", [128, 512], BF16)
            oT = [sc.sb(f"a_oT{i}", [128, 8, TT], BF16, dma=True) for i in range(2)]
            ps_row, r_prow = sc.ps("a_prow", [128, 512], F32)
            ps_st, r_pst = sc.ps("a_pst", [128, 512], F32)
            ps_num, r_pnum = sc.ps("a_pnum", [128, 512], F32)
            ps_sm, r_psm = sc.ps("a_psm", [128, 512], F32)
            ps_cu = [sc.ps(f"a_pcu{i}", [128, 512], F32) for i in range(2)]
            ps_kt, r_pkt = sc.ps("a_pkt", [128, 1024], BF16)
            o_v = o_scr.rearrange("(ic p) t -> p ic t", p=128)
            for c in range(NCH):
                vt_t, vt_r = vt[c % 2]
                S.dma("sp", vt_t[:], utm[c * 128:(c + 1) * 128, :], reads=[r_utm], writes=[vt_r])
                if c % 2 == 0:
                    EB_t, EB_r = EB[(c // 2) % 2]
                    S.op("dve", lambda e: e.tensor_tensor(
                        out=LT[:], in0=triu[:, :].unsqueeze(1).to_broadcast([128, 4, 128]),
                        in1=lf[:, c:c + 2, :].rearrange("p c h -> p (c h)").unsqueeze(2).to_broadcast([128, 4, 128]),
                        op=ALU.mult), reads=[r_tri, r_lf], writes=[r_LT])
                    S.op("pe", lambda e: e.matmul(ps_row[:, :], lhsT=ones[:], rhs=LT[:].rearrange("p a t -> p (a t)"),
                                                  start=True, stop=True), reads=[r_ones, r_LT], writes=[r_prow])
                    S.op("act", lambda e: e.activation(out=EB_t[:].rearrange("p a t -> p (a t)"), in_=ps_row[:, :],
                                                       func=AF.Exp), reads=[r_prow], writes=[EB_r])
                    for pr in range(4):
                        idx = c * 2 + pr
                        S.op("act", lambda e: e.activation(out=DT[pr][0][:], in_=ps_row[:, pr * 128:(pr + 1) * 128],
                                                           func=AF.Exp,
                                                           bias=cs[:, idx // 2, (idx % 2):(idx % 2) + 1], scale=1.0),
                             reads=[r_prow, r_cs], writes=[DT[pr][1]])
                        S.op("pool", lambda e: e.tensor_tensor(out=DT[pr][0][:], in0=DT[pr][0][:], in1=triu[:],
                                                               op=ALU.mult), reads=[DT[pr][1], r_tri], writes=[DT[pr][1]])
                    S.op("dve", lambda e: e.tensor_tensor(out=ws[:], in0=ecs[:, c:c + 2, :].rearrange("p c h -> p (c h)"),
                                                          in1=EB_t[:, :, 127], op=ALU.mult),
                         reads=[r_ecs, EB_r], writes=[r_ws])
                    S.op("act", lambda e: e.copy(out=wsb[:], in_=ws[:]), reads=[r_ws], writes=[r_wsb])
                EB_t, EB_r = EB[(c // 2) % 2]
                oT_t, oT_r = oT[(c // 4) % 2]
                for h in range(2):
                    pr = (c % 2) * 2 + h
                    DT_t, DT_r = DT[pr]
                    qo = h * 2
                    ko = 4 + h * 2
                    tok = slice(c * 128, (c + 1) * 128)
                    for dc in range(2):
                        S.op("pe", lambda e: e.transpose(out=ps_kt[:, dc * 128:(dc + 1) * 128],
                                                         in_=qkb[:, ko + dc, tok], identity=ident[:]),
                             reads=[r_qkb, r_id], writes=[r_pkt] if dc == 0 else [], pwrites=[] if dc == 0 else [r_pkt],
                             inc=(dc == 1))
                    S.op("act", lambda e: e.copy(out=ktm[:], in_=ps_kt[:, 0:256]), reads=[r_pkt], writes=[r_ktm])
                    for dc in range(2):
                        S.op("pe", lambda e: e.matmul(ps_st[:, 0:128], lhsT=qkb[:, ko + dc, tok], rhs=qkb[:, qo + dc, tok],
                                                      start=(dc == 0), stop=(dc == 1)),
                             reads=[r_qkb], writes=[r_pst] if dc == 0 else [], pwrites=[] if dc == 0 else [r_pst],
                             inc=(dc == 1))
                    S.op("dve", lambda e: e.scalar_tensor_tensor(out=aT[:], in0=ps_st[:, 0:128], scalar=1.0 / 16.0,
                                                                 in1=DT_t[:], op0=ALU.mult, op1=ALU.mult),
                         reads=[r_pst, DT_r], writes=[r_aT])
                    S.op("dve", lambda e: e.scalar_tensor_tensor(
                        out=qp[:], in0=qkb[:, qo:qo + 2, tok], scalar=1.0 / 16.0,
                        in1=EB_t[:, pr, :].unsqueeze(1).to_broadcast([128, 2, 128]), op0=ALU.mult, op1=ALU.mult),
                        reads=[r_qkb, EB_r], writes=[r_qp])
                    vs = vt_t[:, h * 512:(h + 1) * 512]
                    nmm = 1 if c == 0 else 3
                    S.op("pe", lambda e: e.matmul(ps_num[:, :], lhsT=aT[:], rhs=vs, start=True, stop=(nmm == 1)),
                         reads=[r_aT, vt_r], writes=[r_pnum], inc=(nmm == 1))
                    if c > 0:
                        for dc in range(2):
                            S.op("pe", lambda e: e.matmul(ps_num[:, :], lhsT=qp[:, dc, :], rhs=Cb[:, h, dc, :],
                                                          start=False, stop=(dc == 1)),
                                 reads=[r_qp, r_Cb], pwrites=[r_pnum], inc=(dc == 1))
                    S.op("pe", lambda e: e.matmul(ps_sm[:, 0:1], lhsT=aT[:], rhs=onesb[:], start=True, stop=(nmm == 1)),
                         reads=[r_aT, r_onesb], writes=[r_psm], inc=(nmm == 1))
                    if c > 0:
                        for dc in range(2):
                            S.op("pe", lambda e: e.matmul(ps_sm[:, 0:1], lhsT=qp[:, dc, :], rhs=nb[:, h, dc:dc + 1],
                                                          start=False, stop=(dc == 1)),
                                 reads=[r_qp, r_nb], pwrites=[r_psm], inc=(dc == 1))
                    S.op("act", lambda e: e.activation(out=den[:, 0:1], in_=ps_sm[:, 0:1], func=AF.Abs),
                         reads=[r_psm], writes=[r_den])
                    S.op("dve", lambda e: e.tensor_scalar(out=den[:, 0:1], in0=den[:, 0:1], scalar1=1.0, scalar2=None,
                                                          op0=ALU.max), reads=[r_den], writes=[r_den])
                    S.op("dve", lambda e: e.reciprocal(out=den[:, 1:2], in_=den[:, 0:1]), reads=[r_den], writes=[r_den])
                    S.op("dve", lambda e: e.tensor_scalar(out=hc[:], in0=ps_num[:, :], scalar1=den[:, 1:2], scalar2=None,
                                                          op0=ALU.mult), reads=[r_pnum, r_den], writes=[r_hc])
                    S.op("pool", lambda e: e.tensor_scalar(out=vp[:], in0=vs, scalar1=ws[:, pr:pr + 1], scalar2=None,
                                                           op0=ALU.mult), reads=[vt_r, r_ws], writes=[r_vp])
                    for dc in range(2):
                        S.op("pe", lambda e: e.matmul(ps_cu[dc][0][:, :], lhsT=ktm[:, dc * 128:(dc + 1) * 128], rhs=vp[:],
                                                      start=True, stop=True),
                             reads=[r_ktm, r_vp], writes=[ps_cu[dc][1]])
                    for dc in range(2):
                        S.op("pe", lambda e: e.matmul(ps_sm[:, 2 + dc:3 + dc], lhsT=ktm[:, dc * 128:(dc + 1) * 128],
                                                      rhs=wsb[:, pr:pr + 1], start=True, stop=True),
                             reads=[r_ktm, r_wsb], pwrites=[r_psm])
                    dec = EB_t[:, pr, 127:128]
                    for dc in range(2):
                        S.op("dve", lambda e: e.scalar_tensor_tensor(out=Cst[:, h, dc, :], in0=Cst[:, h, dc, :], scalar=dec,
                                                                     in1=ps_cu[dc][0][:, :], op0=ALU.mult, op1=ALU.add),
                             reads=[r_C, EB_r, ps_cu[dc][1]], writes=[r_C])
                    S.op("act", lambda e: e.copy(out=Cb[:, h, :, :], in_=Cst[:, h, :, :]), reads=[r_C], writes=[r_Cb])
                    S.op("dve", lambda e: e.scalar_tensor_tensor(out=nst[:, h, :], in0=nst[:, h, :], scalar=dec,
                                                                 in1=ps_sm[:, 2:4], op0=ALU.mult, op1=ALU.add),
                         reads=[r_n, EB_r, r_psm], writes=[r_n])
                    S.op("act", lambda e: e.copy(out=nb[:, h, :], in_=nst[:, h, :]), reads=[r_n], writes=[r_nb])
                    S.op("act", lambda e: e.activation(out=junk[:], in_=hc[:], func=AF.Square, accum_out=den[:, 2:3]),
                         reads=[r_hc], writes=[r_junk, r_den])
                    S.op("act", lambda e: e.activation(out=den[:, 3:4], in_=den[:, 2:3], func=AF.Sqrt, scale=1.0 / 512,
                                                       bias=EPS), reads=[r_den], writes=[r_den])
                    S.op("dve", lambda e: e.reciprocal(out=den[:, 2:3], in_=den[:, 3:4]), reads=[r_den], writes=[r_den])
                    S.op("dve", lambda e: e.scalar_tensor_tensor(out=t1[:], in0=hc[:], scalar=den[:, 2:3],
                                                                 in1=onw[:, h * 512:(h + 1) * 512], op0=ALU.mult,
                                                                 op1=ALU.mult), reads=[r_hc, r_den, r_onw], writes=[r_t1])
                    S.op("act", lambda e: e.activation(out=sg[:], in_=vt_t[:, 1024 + h * 512:1024 + (h + 1) * 512],
                                                       func=AF.Sigmoid), reads=[vt_r], writes=[r_sg])
                    S.op("act", lambda e: e.activation(out=sz[:], in_=vt_t[:, 2048 + h * 512:2048 + (h + 1) * 512],
                                                       func=AF.Silu), reads=[vt_r], writes=[r_sz])
                    S.op("pool", lambda e: e.tensor_tensor(out=sg[:], in0=sg[:], in1=sz[:], op=ALU.mult),
                         reads=[r_sg, r_sz], writes=[r_sg])
                    S.op("pool", lambda e: e.tensor_tensor(out=yb[:], in0=t1[:], in1=sg[:], op=ALU.mult),
                         reads=[r_t1, r_sg], writes=[r_yb])
                    for ic in range(4):
                        S.op("pe", lambda e: e.transpose(out=ps_kt[:, 512 + ic * 128:512 + (ic + 1) * 128],
                                                         in_=yb[:, ic * 128:(ic + 1) * 128], identity=ident[:]),
                             reads=[r_yb, r_id], pwrites=[r_pkt], inc=(ic == 3))
                    q4 = c % 4
                    first = (q4 == 0 and h == 0)
                    S.op("act", lambda e: e.copy(out=oT_t[:, h * 4:(h + 1) * 4, q4 * 128:(q4 + 1) * 128],
                                                 in_=ps_kt[:, 512:1024].rearrange("p (i t) -> p i t", i=4)),
                         reads=[r_pkt], writes=[oT_r] if first else [], pwrites=[] if first else [oT_r])
                if c % 4 == 3:
                    T = c // 4
                    S.dma("sp", o_v[:, :, T * TT:(T + 1) * TT], oT_t[:], reads=[oT_r], pwrites=[r_o])


B_FORCE = 1e4
B_NEG = -1e30


def _alibi(n):
    return 2.0 ** (-8.0 * np.arange(1, n + 1) / n)


def mixer_nsa(kb, utm, r_utm, kv0T, r_kv0, ug, r_ug, prm, o_scr, r_o, G=3):
    S = kb.S
    scale = 128.0 ** -0.5
    NQ = SEQ // 128
    min_slope = float(_alibi(32)[8 * G + 7])
    with kb.scope() as sc0:
        kselT, r_kselT = sc0.sb("b_kselT", [128, SEQ], BF16)
        kwinT, r_kwinT = sc0.sb("b_kwinT", [128, SEQ], BF16)
        vsel, r_vsel = sc0.sb("b_vsel", [128, NQ, 129], BF16)
        vwin, r_vwin = sc0.sb("b_vwin", [128, NQ, 129], BF16)
        kcmpT, r_kcmpT = sc0.sb("b_kcmpT", [128, 256], BF16)
        vcmp, r_vcmp = sc0.sb("b_vcmp", [128, 2, 129], BF16)
        ovl, r_ovl = sc0.sb("b_ovl", [128, 2, 64], BF16, dma=True)
        S.op("pool", lambda e: e.memset(vsel[:], 1.0), writes=[r_vsel])
        S.op("pool", lambda e: e.memset(vwin[:], 1.0), writes=[r_vwin])
        S.op("pool", lambda e: e.memset(kcmpT[:], 0.0), writes=[r_kcmpT])
        S.op("pool", lambda e: e.memset(vcmp[:], 0.0), writes=[r_vcmp])
        S.op("pool", lambda e: e.memset(vcmp[:, :, 128:129], 1.0), writes=[r_vcmp])
        S.dma("pool", ovl[:], prm["ovl"], writes=[r_ovl])
        with kb.scope() as sc:
            ident, r_id, _, _ = make_ident(S, sc)
            knw, r_knw = sc.sb("b_knw", [128, 3, 128], F32, dma=True)
            S.dma("sp", knw[:].rearrange("p a d -> p (a d)"), prm["knw"].partition_broadcast(128), writes=[r_knw])
            kvt = [sc.sb(f"b_kvt{i}", [128, 512], BF16, dma=True) for i in range(2)]
            sq, r_sq = sc.sb("b_sq", [128, 2, 128], F32)
            ssq, r_ssq = sc.sb("b_ssq", [128, 2], F32)
            rstd, r_rstd = sc.sb("b_rstd", [128, 2], F32)
            kn, r_kn = sc.sb("b_kn", [128, 2, 128], BF16)
            pt, r_pt = sc.ps("b_pt", [128, 1024], BF16)
            for kt in range(NQ):
                kv_t, kv_r = kvt[kt % 2]
                S.dma("sp", kv_t[:], utm[kt * 128:(kt + 1) * 128, 2304:2816], reads=[r_utm], writes=[kv_r])
                kview = kv_t[:].rearrange("p (a b d) -> p a b d", a=2, b=2)
                S.op("dve", lambda e: e.tensor_tensor(out=sq[:], in0=kview[:, :, 0, :], in1=kview[:, :, 0, :], op=ALU.mult),
                     reads=[kv_r], writes=[r_sq])
                S.op("dve", lambda e: e.tensor_reduce(out=ssq[:], in_=sq[:], axis=AX.X, op=ALU.add),
                     reads=[r_sq], writes=[r_ssq])
                S.op("act", lambda e: e.activation(out=rstd[:], in_=ssq[:], func=AF.Sqrt, scale=1.0 / 128, bias=EPS),
                     reads=[r_ssq], writes=[r_rstd])
                S.op("dve", lambda e: e.reciprocal(out=rstd[:], in_=rstd[:]), reads=[r_rstd], writes=[r_rstd])
                S.op("dve", lambda e: e.tensor_tensor(out=sq[:], in0=kview[:, :, 0, :],
                                                      in1=rstd[:, :].unsqueeze(2).to_broadcast([128, 2, 128]), op=ALU.mult),
                     reads=[kv_r, r_rstd], writes=[r_sq])
                S.op("pool", lambda e: e.tensor_tensor(out=kn[:], in0=sq[:], in1=knw[:, 1:3, :], op=ALU.mult),
                     reads=[r_sq, r_knw], writes=[r_kn])
                for a in range(2):
                    S.op("pe", lambda e: e.transpose(out=pt[:, a * 128:(a + 1) * 128], in_=kn[:, a, :], identity=ident[:]),
                         reads=[r_kn, r_id], writes=[r_pt] if a == 0 else [], pwrites=[] if a == 0 else [r_pt], inc=(a == 1))
                S.op("dve", lambda e: e.tensor_copy(out=kselT[:, kt * 128:(kt + 1) * 128], in_=pt[:, 0:128]),
                     reads=[r_pt], pwrites=[r_kselT])
                S.op("act", lambda e: e.copy(out=kwinT[:, kt * 128:(kt + 1) * 128], in_=pt[:, 128:256]),
                     reads=[r_pt], pwrites=[r_kwinT])
                S.op("pool", lambda e: e.tensor_copy(out=vsel[:, kt, 0:128], in_=kview[:, 0, 1, :]),
                     reads=[kv_r], pwrites=[r_vsel])
                S.op("pool", lambda e: e.tensor_copy(out=vwin[:, kt, 0:128], in_=kview[:, 1, 1, :]),
                     reads=[kv_r], pwrites=[r_vwin])
            k0, r_k0 = sc.sb("b_k0", [128, 2, SEQ], BF16, dma=True)
            S.dma("sp", k0[:], kv0T.rearrange("(a p) t -> p a t", p=128), reads=[r_kv0], writes=[r_k0])
            peT, r_peT = sc.sb("b_peT", [128, 2, 32], F32, dma=True)
            S.dma("sp", peT[:], prm["peT"], writes=[r_peT])
            wkv, r_wkv = sc.sb("b_wkv", [128, 2, 32, 128], BF16, dma=True)
            S.dma("pool", wkv[:, 0], prm["wk"], writes=[r_wkv])
            S.dma("pool", wkv[:, 1], prm["wv"], pwrites=[r_wkv])
            kg, r_kg = sc.sb("b_kg", [128, 2, 32, 256], BF16)
            S.op("pool", lambda e: e.memset(kg[:], 0.0), writes=[r_kg])
            for a in range(2):
                for l in range(32):
                    eng = "dve" if (l % 2 == 0) else "pool"
                    S.op(eng, lambda e: e.tensor_scalar(out=kg[:, a, l, 0:255], in0=k0[:, a, l:l + 16 * 254 + 1:16],
                                                        scalar1=peT[:, a, l:l + 1], scalar2=None, op0=ALU.add),
                         reads=[r_k0, r_peT], pwrites=[r_kg])
            pc = [sc.ps(f"b_pc{i}", [128, 512], F32) for i in range(2)]
            for ct in range(2):
                M = 128 if ct == 0 else 127
                for a in range(2):
                    pc_t, pc_r = pc[a]
                    for l in range(32):
                        S.op("pe", lambda e: e.matmul(pc_t[:M, 0:128], lhsT=kg[:, a, l, ct * 128:ct * 128 + M],
                                                      rhs=wkv[:, a, l, :], start=(l == 0), stop=(l == 31)),
                             reads=[r_kg, r_wkv], writes=[pc_r] if l == 0 else [], pwrites=[] if l == 0 else [pc_r],
                             inc=(l == 31))
                pk, pk_r = pc[0]
                S.op("act", lambda e: e.activation(out=sq[:M, 0, :], in_=pk[:M, 0:128], func=AF.Square,
                                                   accum_out=ssq[:M, 0:1]), reads=[pk_r], writes=[r_sq, r_ssq])
                S.op("act", lambda e: e.activation(out=rstd[:M, 0:1], in_=ssq[:M, 0:1], func=AF.Sqrt, scale=1.0 / 128,
                                                   bias=EPS), reads=[r_ssq], writes=[r_rstd])
                S.op("dve", lambda e: e.reciprocal(out=rstd[:M, 0:1], in_=rstd[:M, 0:1]), reads=[r_rstd], writes=[r_rstd])
                S.op("dve", lambda e: e.scalar_tensor_tensor(out=kn[:M, 0, :], in0=pk[:M, 0:128], scalar=rstd[:M, 0:1],
                                                             in1=knw[:M, 0, :], op0=ALU.mult, op1=ALU.mult),
                     reads=[pk_r, r_rstd, r_knw], writes=[r_kn])
                S.op("pe", lambda e: e.transpose(out=pt[:, 0:M], in_=kn[:M, 0, :], identity=ident[:M, :M]),
                     reads=[r_kn, r_id], writes=[r_pt])
                S.op("dve", lambda e: e.tensor_copy(out=kcmpT[:, ct * 128:ct * 128 + M], in_=pt[:, 0:M]),
                     reads=[r_pt], pwrites=[r_kcmpT])
                S.op("act", lambda e: e.copy(out=vcmp[:M, ct, 0:128], in_=pc[1][0][:M, 0:128]),
                     reads=[pc[1][1]], pwrites=[r_vcmp])
        with kb.scope() as sc:
            ident, r_id, _, _ = make_ident(S, sc)
            qnw, r_qnw = sc.sb("b_qnw", [128, 128], F32, dma=True)
            S.dma("sp", qnw[:], prm["qnw"].partition_broadcast(128), writes=[r_qnw])
            BQ, r_BQ = sc.sb("b_BQ", [8, 1024], BF16, dma=True)
            AK, r_AK = sc.sb("b_AK", [8, 2, 32, 128], BF16, dma=True)
            Eall, r_E = sc.sb("b_E", [64, SEQ], BF16, dma=True)
            Wadd, r_Wadd = sc.sb("b_Wadd", [128, 128], F32, dma=True)
            Wkeep, r_Wkeep = sc.sb("b_Wkeep", [128, 128], F32, dma=True)
            S.dma("pool", BQ[:], prm["BQ"], writes=[r_BQ])
            S.dma("pool", AK[:], prm["AK"], writes=[r_AK])
            S.dma("pool", Eall[:], prm["Eall"], writes=[r_E])
            S.dma("sp", Wadd[:], prm["Wadd"], writes=[r_Wadd])
            S.dma("sp", Wkeep[:], prm["Wkeep"], writes=[r_Wkeep])
            qt = [sc.sb(f"b_qt{i}", [128, 2048], BF16, dma=True) for i in range(2)]
            gt = [sc.sb(f"b_gt{i}", [128, 24], F32, dma=True) for i in range(2)]
            sq, r_sq = sc.sb("b2_sq", [128, 1024], F32)
            ssq, r_ssq = sc.sb("b2_ssq", [128, 8], F32)
            rstd, r_rstd = sc.sb("b2_rstd", [128, 8], F32)
            qn, r_qn = sc.sb("b2_qn", [128, 1024], BF16)
            qT, r_qT = sc.sb("b2_qT", [128, 1024], BF16)
            PT = [sc.sb(f"b2_PT{i}", [128, 1024], BF16) for i in range(2)]
            msk, r_msk = sc.sb("b2_msk", [128, 128], BF16)
            oacc, r_oacc = sc.sb("b2_oacc", [128, 8, 128], F32)
            imp, r_imp = sc.sb("b2_imp", [128, 64], F32)
            imp2, r_imp2 = sc.sb("b2_imp2", [128, 64], F32)
            imp3, r_imp3 = sc.sb("b2_imp3", [128, 64], F32)
            m8, r_m8 = sc.sb("b2_m8", [128, 16], F32)
            selb, r_selb = sc.sb("b2_selb", [128, 64], BF16)
            selT, r_selT = sc.sb("b2_selT", [64, 128], BF16)
            den, r_den = sc.sb("b2_den", [128, 8], F32)
            rg, r_rg = sc.sb("b2_rg", [128, 8], F32)
            zs, r_zs = sc.sb("b2_zs", [128, 1024], F32)
            yb, r_yb = sc.sb("b2_yb", [128, 1024], BF16)
            oT = [sc.sb(f"b2_oT{i}", [128, 8, TT], BF16, dma=True) for i in range(2)]
            pst, r_pst = sc.ps("b2_pst", [128, 1024], F32)
            acc = [sc.ps(f"b2_acc{i}", [128, 512], F32) for i in range(3)]
            pmk, r_pmk = sc.ps("b2_pmk", [128, 512], F32)
            ptr, r_ptr = sc.ps("b2_ptr", [128, 1024], BF16)
            pti = 0

            def acc_of(h):
                return acc[h // 3][0][:, (h % 3) * 129:(h % 3) * 129 + 129], acc[h // 3][1]

            def attend(tiles, br, first_branch, gt_t, gt_r, want_imp):
                nonlocal pti
                nt = len(tiles)

                def stage_a(tl):
                    nonlocal pti
                    PT_t, PT_r = PT[pti % 2]
                    pti += 1
                    for half in range(2):
                        S.op("pe", lambda e: e.matmul(pst[:, half * 512:(half + 1) * 512], lhsT=tl["kT"],
                                                      rhs=qT[:, half * 512:(half + 1) * 512], start=True, stop=False),
                             reads=[tl["r_k"], r_qT], writes=[r_pst] if half == 0 else [],
                             pwrites=[] if half == 0 else [r_pst], inc=False)
                        S.op("pe", lambda e: e.matmul(pst[:, half * 512:(half + 1) * 512], lhsT=AK[:, tl["ak"], tl["m"], :],
                                                      rhs=BQ[:, half * 512:(half + 1) * 512], start=False, stop=True),
                             reads=[r_AK, r_BQ], pwrites=[r_pst], inc=(half == 1))
                    S.op("act", lambda e: e.activation(out=PT_t[:], in_=pst[:, :], func=AF.Exp, scale=scale),
                         reads=[r_pst], writes=[PT_r])
                    if tl["aff"] is not None:
                        pat, cm, base = tl["aff"]
                        S.op("pool", lambda e: e.affine_select(out=PT_t[:].rearrange("p (h q) -> p h q", h=8),
                                                               in_=PT_t[:].rearrange("p (h q) -> p h q", h=8),
                                                               pattern=[[0, 8], [pat, 128]], compare_op=ALU.is_ge, fill=0.0,
                                                               base=base, channel_multiplier=cm),
                             reads=[PT_r], writes=[PT_r])
                    return PT_t, PT_r

                def stage_b(ti, tl, PT_t, PT_r):
                    if tl["selmask_kt"] is not None:
                        kt = tl["selmask_kt"]
                        S.op("pe", lambda e: e.matmul(pmk[:, 0:128], lhsT=Eall[:, kt * 128:(kt + 1) * 128], rhs=selT[:, :],
                                                      start=True, stop=True), reads=[r_E, r_selT], writes=[r_pmk])
                        S.op("dve", lambda e: e.tensor_tensor(out=PT_t[:].rearrange("p (h q) -> p h q", h=8),
                                                              in0=PT_t[:].rearrange("p (h q) -> p h q", h=8),
                                                              in1=pmk[:, 0:128].unsqueeze(1).to_broadcast([128, 8, 128]),
                                                              op=ALU.mult), reads=[PT_r, r_pmk], writes=[PT_r])
                    for h in range(8):
                        a_ap, a_r = acc_of(h)
                        first_in_bank = (ti == 0 and h % 3 == 0)
                        last = (ti == nt - 1)
                        S.op("pe", lambda e: e.matmul(a_ap, lhsT=PT_t[:, h * 128:(h + 1) * 128], rhs=tl["vaug"],
                                                      start=first_in_bank, stop=last, skip_group_check=True),
                             reads=[PT_r, tl["r_v"]], writes=[a_r] if first_in_bank else [],
                             pwrites=[] if first_in_bank else [a_r], inc=(last and (h % 3 == 2 or h == 7)))
                    if want_imp:
                        for h in range(8):
                            S.op("pe", lambda e: e.matmul(pmk[:, h * 64:(h + 1) * 64], lhsT=PT_t[:, h * 128:(h + 1) * 128],
                                                          rhs=tl["ovl"], start=(ti == 0 and h == 0), stop=(ti == nt - 1),
                                                          skip_group_check=True),
                                 reads=[PT_r, r_ovl], writes=[r_pmk] if (ti == 0 and h == 0) else [],
                                 pwrites=[] if (ti == 0 and h == 0) else [r_pmk], inc=(ti == nt - 1 and h == 7))

                pend = stage_a(tiles[0])
                for ti in range(nt):
                    nxt = stage_a(tiles[ti + 1]) if ti + 1 < nt else None
                    stage_b(ti, tiles[ti], *pend)
                    pend = nxt
                for bk in range(3):
                    nh = 3 if bk < 2 else 2
                    S.op("dve", lambda e: e.tensor_scalar(
                        out=den[:, bk * 3:bk * 3 + nh],
                        in0=acc[bk][0][:, 0:nh * 129].rearrange("p (h c) -> p h c", c=129)[:, :, 128],
                        scalar1=1e-30, scalar2=None, op0=ALU.max), reads=[acc[bk][1]],
                        writes=[r_den] if bk == 0 else [], pwrites=[] if bk == 0 else [r_den])
                S.op("dve", lambda e: e.reciprocal(out=den[:], in_=den[:]), reads=[r_den], writes=[r_den])
                S.op("dve", lambda e: e.tensor_tensor(out=rg[:], in0=den[:],
                                                      in1=gt_t[:].rearrange("p (h b) -> p h b", b=3)[:, :, br],
                                                      op=ALU.mult), reads=[r_den, gt_r], writes=[r_rg])
                for h in range(8):
                    a_ap, a_r = acc_of(h)
                    if first_branch:
                        S.op("dve", lambda e: e.tensor_scalar(out=oacc[:, h, :], in0=a_ap[:, 0:128], scalar1=rg[:, h:h + 1],
                                                              scalar2=None, op0=ALU.mult),
                             reads=[a_r, r_rg], writes=[r_oacc] if h == 0 else [], pwrites=[] if h == 0 else [r_oacc])
                    else:
                        S.op("dve", lambda e: e.scalar_tensor_tensor(out=oacc[:, h, :], in0=a_ap[:, 0:128],
                                                                     scalar=rg[:, h:h + 1], in1=oacc[:, h, :],
                                                                     op0=ALU.mult, op1=ALU.add),
                             reads=[a_r, r_rg, r_oacc], writes=[r_oacc])
                    if want_imp:
                        if h == 0:
                            S.op("dve", lambda e: e.tensor_scalar(out=imp[:], in0=pmk[:, 0:64], scalar1=den[:, 0:1],
                                                                  scalar2=None, op0=ALU.mult),
                                 reads=[r_pmk, r_den], writes=[r_imp])
                        else:
                            S.op("dve", lambda e: e.scalar_tensor_tensor(out=imp[:], in0=pmk[:, h * 64:(h + 1) * 64],
                                                                         scalar=den[:, h:h + 1], in1=imp[:],
                                                                         op0=ALU.mult, op1=ALU.add),
                                 reads=[r_pmk, r_den, r_imp], writes=[r_imp])

            for i in range(NQ):
                t0 = i * 128
                q_t, q_r = qt[i % 2]
                gt_t, gt_r = gt[i % 2]
                oT_t, oT_r = oT[(i // 4) % 2]
                S.dma("sp", q_t[:], utm[t0:t0 + 128, 0:2048], reads=[r_utm], writes=[q_r])
                S.dma("sp", gt_t[:], ug[t0:t0 + 128, :], reads=[r_ug], writes=[gt_r])
                S.op("act", lambda e: e.activation(out=gt_t[:], in_=gt_t[:], func=AF.Sigmoid), reads=[gt_r], writes=[gt_r])
                S.op("dve", lambda e: e.tensor_tensor(out=sq[:], in0=q_t[:, 0:1024], in1=q_t[:, 0:1024], op=ALU.mult),
                     reads=[q_r], writes=[r_sq])
                S.op("dve", lambda e: e.tensor_reduce(out=ssq[:], in_=sq[:].rearrange("p (h d) -> p h d", h=8), axis=AX.X,
                                                      op=ALU.add), reads=[r_sq], writes=[r_ssq])
                S.op("act", lambda e: e.activation(out=rstd[:], in_=ssq[:], func=AF.Sqrt, scale=1.0 / 128, bias=EPS),
                     reads=[r_ssq], writes=[r_rstd])
                S.op("dve", lambda e: e.reciprocal(out=rstd[:], in_=rstd[:]), reads=[r_rstd], writes=[r_rstd])
                S.op("dve", lambda e: e.tensor_tensor(out=sq[:].rearrange("p (h d) -> p h d", h=8),
                                                      in0=q_t[:, 0:1024].rearrange("p (h d) -> p h d", h=8),
                                                      in1=rstd[:, :].unsqueeze(2).to_broadcast([128, 8, 128]), op=ALU.mult),
                     reads=[q_r, r_rstd], writes=[r_sq])
                S.op("pool", lambda e: e.tensor_tensor(out=qn[:].rearrange("p (h d) -> p h d", h=8),
                                                       in0=sq[:].rearrange("p (h d) -> p h d", h=8),
                                                       in1=qnw[:, :].unsqueeze(1).to_broadcast([128, 8, 128]), op=ALU.mult),
                     reads=[r_sq, r_qnw], writes=[r_qn])
                for h in range(8):
                    S.op("pe", lambda e: e.transpose(out=ptr[:, h * 128:(h + 1) * 128], in_=qn[:, h * 128:(h + 1) * 128],
                                                     identity=ident[:]),
                         reads=[r_qn, r_id], writes=[r_ptr] if h == 0 else [], pwrites=[] if h == 0 else [r_ptr],
                         inc=(h == 7))
                S.op("dve", lambda e: e.tensor_copy(out=qT[:], in_=ptr[:, :]), reads=[r_ptr], writes=[r_qT])
                tiles = []
                for ct in range(2):
                    P0 = 2048 * ct + 31
                    if t0 + 127 < P0:
                        continue
                    tiles.append(dict(kT=kcmpT[:, ct * 128:(ct + 1) * 128], r_k=r_kcmpT, vaug=vcmp[:, ct, :], r_v=r_vcmp,
                                      ak=1, m=i - 16 * ct, aff=(1, -16, t0 - P0), selmask_kt=None, ovl=ovl[:, ct, :]))
                attend(tiles, 0, True, gt_t, gt_r, True)
                c0 = 62 - 2 * i
                S.op("dve", lambda e: e.tensor_tensor(out=imp2[:], in0=imp[:], in1=Wkeep[:, c0:c0 + 64], op=ALU.mult),
                     reads=[r_imp, r_Wkeep], writes=[r_imp2])
                S.op("dve", lambda e: e.tensor_tensor(out=imp2[:], in0=imp2[:], in1=Wadd[:, c0:c0 + 64], op=ALU.add),
                     reads=[r_imp2, r_Wadd], writes=[r_imp2])
                if i >= 1:
                    S.op("dve", lambda e: e.tensor_scalar(out=imp2[:, 0:1], in0=imp2[:, 0:1], scalar1=B_FORCE, scalar2=None,
                                                          op0=ALU.add), reads=[r_imp2], writes=[r_imp2])
                S.op("dve", lambda e: e.max(out=m8[:, 0:8], in_=imp2[:]), reads=[r_imp2], writes=[r_m8])
                S.op("dve", lambda e: e.match_replace(out=imp3[:], in_to_replace=m8[:, 0:8], in_values=imp2[:],
                                                      imm_value=-3.0e38), reads=[r_m8, r_imp2], writes=[r_imp3])
                S.op("dve", lambda e: e.max(out=m8[:, 8:16], in_=imp3[:]), reads=[r_imp3], writes=[r_m8])
                S.op("dve", lambda e: e.tensor_scalar(out=imp3[:], in0=imp2[:], scalar1=m8[:, 15:16], scalar2=None,
                                                      op0=ALU.is_ge), reads=[r_imp2, r_m8], writes=[r_imp3])
                S.op("dve", lambda e: e.tensor_tensor(out=selb[:], in0=imp3[:], in1=Wkeep[:, c0:c0 + 64], op=ALU.mult),
                     reads=[r_imp3, r_Wkeep], writes=[r_selb])
                S.op("pe", lambda e: e.transpose(out=ptr[:64, 0:128], in_=selb[:, :], identity=ident[:]),
                     reads=[r_selb, r_id], writes=[r_ptr])
                S.op("act", lambda e: e.copy(out=selT[:], in_=ptr[:64, 0:128]), reads=[r_ptr], writes=[r_selT])
                tiles = []
                for kt in range(i + 1):
                    if (i - kt - 1) * 128 * min_slope > 160.0:
                        continue
                    tiles.append(dict(kT=kselT[:, kt * 128:(kt + 1) * 128], r_k=r_kselT, vaug=vsel[:, kt, :], r_v=r_vsel,
                                      ak=0, m=i - kt, aff=((1, -1, 0) if kt == i else None), selmask_kt=kt,
                                      ovl=None))
                attend(tiles, 1, False, gt_t, gt_r, False)
                tiles = []
                for kt in range(max(0, i - 4), i + 1):
                    aff = None
                    if kt == i:
                        aff = (1, -1, 0)
                    elif kt == i - 4:
                        aff = (-1, 1, -1)
                    tiles.append(dict(kT=kwinT[:, kt * 128:(kt + 1) * 128], r_k=r_kwinT, vaug=vwin[:, kt, :], r_v=r_vwin,
                                      ak=0, m=i - kt, aff=aff, selmask_kt=None, ovl=None))
                attend(tiles, 2, False, gt_t, gt_r, False)
                S.op("act", lambda e: e.activation(out=zs[:], in_=q_t[:, 1024:2048], func=AF.Silu), reads=[q_r], writes=[r_zs])
                S.op("pool", lambda e: e.tensor_tensor(out=yb[:], in0=oacc[:].rearrange("p h d -> p (h d)"), in1=zs[:],
                                                       op=ALU.mult), reads=[r_oacc, r_zs], writes=[r_yb])
                for h in range(8):
                    S.op("pe", lambda e: e.transpose(out=ptr[:, h * 128:(h + 1) * 128], in_=yb[:, h * 128:(h + 1) * 128],
                                                     identity=ident[:]),
                         reads=[r_yb, r_id], writes=[r_ptr] if h == 0 else [], pwrites=[] if h == 0 else [r_ptr],
                         inc=(h == 7))
                q4 = i % 4
                S.op("act", lambda e: e.copy(out=oT_t[:, :, q4 * 128:(q4 + 1) * 128],
                                             in_=ptr[:, :].rearrange("p (h t) -> p h t", h=8)),
                     reads=[r_ptr], writes=[oT_r] if q4 == 0 else [], pwrites=[] if q4 == 0 else [oT_r])
                if q4 == 3:
                    T = i // 4
                    S.dma("sp", o_scr.rearrange("(h p) t -> p h t", p=128)[:, :, T * TT:(T + 1) * TT], oT_t[:],
                          reads=[oT_r], pwrites=[r_o])


def _dram_in(nc, name, shape, dt=F32):
    return nc.dram_tensor(name, list(shape), dt, kind="ExternalInput").ap()


_SCRATCH = {
    0: (("s_qkT", [1024, SEQ], F32), ("s_utm", [SEQ, 3072], BF16), ("s_gates", [SEQ, 4], F32), ("s_o", [1024, SEQ], BF16)),
    1: (("s_utm", [SEQ, 2816], BF16), ("s_kv0T", [256, SEQ], BF16), ("s_ug", [SEQ, 24], F32), ("s_o", [1024, SEQ], BF16)),
    2: (("s_xbT", [1344, SEQ], F32), ("s_zT", [1344, SEQ], BF16), ("s_o", [1344, SEQ], BF16)),
    3: (("s_u", [SEQ, 5120], BF16), ("s_nd", [3, SEQ, 516], F32), ("s_o", [512, SEQ], BF16)),
}


def alloc_scratch(nc, S, kind, tag=""):
    scr = {}
    for name, shape, dt in _SCRATCH[kind]:
        scr[name] = (nc.dram_tensor(f"{name}{tag}", shape, dt, kind="Internal").ap(), S.res(f"{name}{tag}"))
    return scr


def layer_groups(kind, win, scr):
    if kind == 2:
        (xbT, r_xb), (zT, r_z) = scr["s_xbT"], scr["s_zT"]
        return [dict(layout="fm", w=win["w_xb"], W=112, nchunk=12, dst=xbT, rdst=r_xb),
                dict(layout="fm", w=win["w_z"], W=112, nchunk=12, dst=zT, rdst=r_z)]
    if kind == 0:
        (qkT, r_qk), (utm, r_utm), (gts, r_g) = scr["s_qkT"], scr["s_utm"], scr["s_gates"]
        return [dict(layout="fm", w=win["w_qk"], W=128, nchunk=8, dst=qkT, rdst=r_qk),
                dict(layout="tm", w=win["w_vgz"], W=256, nchunk=12, dst=utm, rdst=r_utm),
                dict(layout="tm", w=win["w_g"], W=4, nchunk=1, dst=gts, rdst=r_g)]
    if kind == 1:
        (utm, r_utm), (kv0T, r_kv0), (ug, r_ug) = scr["s_utm"], scr["s_kv0T"], scr["s_ug"]
        return [dict(layout="tm", w=win["w_tm"], W=256, nchunk=11, dst=utm, rdst=r_utm),
                dict(layout="fm", w=win["w_kv0"], W=128, nchunk=2, dst=kv0T, rdst=r_kv0),
                dict(layout="tm", w=win["w_g"], W=24, nchunk=1, dst=ug, rdst=r_ug)]
    if kind == 3:
        (u, r_u) = scr["s_u"]
        return [dict(layout="tm", w=win["w_u"], W=256, nchunk=20, dst=u, rdst=r_u)]
    raise NotImplementedError


def layer_mix(kb, kind, win, y, r_y, scr, accum=None, G=3):
    (o_scr, r_o) = scr["s_o"]
    if kind == 2:
        (xbT, r_xb), (zT, r_z) = scr["s_xbT"], scr["s_zT"]
        mixer_rglru(kb, xbT, r_xb, zT, r_z, win, o_scr, r_o)
        phase3(kb, o_scr, r_o, 112, 12, win["w_out"], y, r_y, accum)
    elif kind == 0:
        (qkT, r_qk), (utm, r_utm), (gts, r_g) = scr["s_qkT"], scr["s_utm"], scr["s_gates"]
        mixer_mlstm(kb, qkT, r_qk, utm, r_utm, gts, r_g, win, o_scr, r_o)
        phase3(kb, o_scr, r_o, 128, 8, win["w_out"], y, r_y, accum)
    elif kind == 1:
        (utm, r_utm), (kv0T, r_kv0), (ug, r_ug) = scr["s_utm"], scr["s_kv0T"], scr["s_ug"]
        mixer_nsa(kb, utm, r_utm, kv0T, r_kv0, ug, r_ug, win, o_scr, r_o, G)
        phase3(kb, o_scr, r_o, 128, 8, win["w_out"], y, r_y, accum)
    elif kind == 3:
        (u, r_u), (nd, r_nd) = scr["s_u"], scr["s_nd"]
        mixer_dilated(kb, u, r_u, win, o_scr, r_o, nd, r_nd)
        phase3(kb, o_scr, r_o, 128, 4, win["w_out"], y, r_y, accum)
    else:
        raise NotImplementedError


def emit_layer(kb, kind, x, nw, win, y, r_y, scr, accum=None, G=3):
    phase1(kb, x, nw, layer_groups(kind, win, scr))
    layer_mix(kb, kind, win, y, r_y, scr, accum, G)


def build_layer(kind, shapes):
    nc = bass.Bass("TRN2", target_bir_lowering=False)
    x = _dram_in(nc, "x", [SEQ, D])
    nw = _dram_in(nc, "nw", [1, D])
    win = {k: _dram_in(nc, k, v) for k, v in shapes.items()}
    y = nc.dram_tensor("y", [SEQ, D], F32, kind="ExternalOutput").ap()
    with ExitStack() as st:
        kb = KB(nc, st)
        S = kb.S
        emit_layer(kb, kind, x, nw, win, y, S.res("y_out"), alloc_scratch(nc, S, kind))
        S.finish()
    return nc


def build_fused(shapes, layers=(0, 1, 2, 3), groups=(0, 1, 2, 3)):
    nc = bass.Bass("TRN2", target_bir_lowering=False)
    x = _dram_in(nc, "x", [SEQ, D])
    nwall = _dram_in(nc, "nw", [4, D])
    out = nc.dram_tensor("out", [SEQ, D], F32, kind="ExternalOutput").ap()
    xs = [nc.dram_tensor(f"s_x{i}", [SEQ, D], F32, kind="Internal").ap() for i in range(2)]
    with ExitStack() as st:
        kb = KB(nc, st)
        S = kb.S
        for li, L in enumerate(layers):
            src_ap = x if li == 0 else xs[(li - 1) % 2]
            dst_ap = out if li == len(layers) - 1 else xs[li % 2]
            r_dst = S.res(f"xres{L}")
            scrs, wins, allg = {}, {}, []
            for g in groups:
                scrs[g] = alloc_scratch(nc, S, L, tag=f"_L{L}g{g}")
                wins[g] = {k: _dram_in(nc, f"L{L}g{g}_{k}", v) for k, v in shapes[(L, g)].items()}
                allg += layer_groups(L, wins[g], scrs[g])
            phase1(kb, src_ap, nwall[L:L + 1, :], allg)
            for gi, g in enumerate(groups):
                layer_mix(kb, L, wins[g], dst_ap, r_dst, scrs[g], accum=(src_ap if gi == 0 else dst_ap), G=g)
        S.finish()
        print("fused program: nins", S.nins, "nwaits", S.nwaits, flush=True)
    return nc


def build_reduce():
    nc = bass.Bass("TRN2", target_bir_lowering=False)
    R = 1024
    x = _dram_in(nc, "x", [R, D])
    ys = [_dram_in(nc, f"y{i}", [R, D]) for i in range(4)]
    out = nc.dram_tensor("out", [R, D], F32, kind="ExternalOutput").ap()
    with ExitStack() as st:
        kb = KB(nc, st)
        S = kb.S
        r_out = S.res("out")
        with kb.scope() as sc:
            bufs = [[sc.sb(f"r_b{i}_{j}", [128, D], F32, dma=True) for j in range(5)] for i in range(2)]
            for t in range(R // 128):
                bb = bufs[t % 2]
                for j, src in enumerate([x] + ys):
                    S.dma("sp" if j % 2 == 0 else "act", bb[j][0][:], src[t * 128:(t + 1) * 128, :], writes=[bb[j][1]])
                a_t, a_r = bb[0]
                for j in range(1, 5):
                    eng = "dve" if j % 2 == 1 else "pool"
                    S.op(eng, lambda e: e.tensor_tensor(out=a_t[:], in0=a_t[:], in1=bb[j][0][:], op=ALU.add),
                         reads=[a_r, bb[j][1]], writes=[a_r])
                S.dma("sp", out[t * 128:(t + 1) * 128, :], a_t[:], reads=[a_r], pwrites=[r_out])
        S.finish()
    return nc


def _chunk_w(wcols, W):
    n = wcols.shape[1] // W
    a = wcols.reshape(32, 128, n, W).transpose(2, 1, 0, 3)
    return np.ascontiguousarray(a)


def _layer2_inputs(inp, g):
    CW = 5376
    lo, hi = g * 1344, (g + 1) * 1344
    w_in = inp["c_w_in"][0]
    d = {}
    d["w_xb"] = _chunk_w(w_in[:, lo:hi], 112)
    d["w_z"] = _chunk_w(w_in[:, CW + lo:CW + hi], 112)
    d["cw"] = np.ascontiguousarray(inp["c_conv_w"][0][:, lo:hi].reshape(4, 12, 112).transpose(2, 1, 0))
    vec = np.stack([inp["c_conv_b"][0][lo:hi], inp["c_b_a"][0][lo:hi], inp["c_b_x"][0][lo:hi],
                    inp["c_lambda"][0][lo:hi]], axis=0)
    d["vec"] = np.ascontiguousarray(vec.reshape(4, 12, 112).transpose(2, 0, 1))
    for nm, key in (("wa", "c_w_a"), ("wx", "c_w_x")):
        wblk = inp[key][0][4 * g:4 * g + 4]
        d[nm] = np.ascontiguousarray(wblk.reshape(4, 3, 112, 336).transpose(2, 0, 1, 3))
    d["w_out"] = np.ascontiguousarray(inp["c_w_out"][0][lo:hi, :].reshape(12, 112, D).transpose(1, 0, 2))
    return d


def _layer0_inputs(inp, g):
    w_in = inp["a_w_in"][0]
    hs = (2 * g, 2 * g + 1)
    d = {}
    qk_cols = [w_in[:, h * 256:(h + 1) * 256] for h in hs] + [w_in[:, 2048 + h * 256:2048 + (h + 1) * 256] for h in hs]
    d["w_qk"] = _chunk_w(np.concatenate(qk_cols, axis=1), 128)
    vgz = []
    for base in (4096, 8192, 12288):
        for h in hs:
            vgz.append(w_in[:, base + h * 512:base + (h + 1) * 512])
    d["w_vgz"] = _chunk_w(np.concatenate(vgz, axis=1), 256)
    gcols = [w_in[:, 16384 + h:16385 + h] for h in hs] + [w_in[:, 16392 + h:16393 + h] for h in hs]
    d["w_g"] = _chunk_w(np.concatenate(gcols, axis=1), 4)
    idx = np.concatenate([np.arange(h * 256, (h + 1) * 256) for h in hs] +
                         [2048 + np.arange(h * 256, (h + 1) * 256) for h in hs])
    d["cw"] = np.ascontiguousarray(inp["a_conv_w"][0][:, idx].reshape(4, 8, 128).transpose(2, 1, 0))
    d["cb"] = np.ascontiguousarray(inp["a_conv_b"][0][idx].reshape(8, 128).T)
    gbv = inp["a_gate_b"][0]
    d["gb"] = np.ascontiguousarray(np.array([[gbv[0, hs[0]], gbv[0, hs[1]], gbv[1, hs[0]], gbv[1, hs[1]]]], np.float32))
    d["onw"] = np.ascontiguousarray(inp["a_out_norm_w"][0][g * 1024:(g + 1) * 1024][None, :])
    d["w_out"] = np.ascontiguousarray(inp["a_w_out"][0][g * 1024:(g + 1) * 1024, :].reshape(8, 128, D).transpose(1, 0, 2))
    return d


def _bf16_split(x):
    import ml_dtypes
    x = np.asarray(x, np.float32)
    hi = x.astype(ml_dtypes.bfloat16).astype(np.float32)
    lo = (x - hi).astype(ml_dtypes.bfloat16).astype(np.float32)
    return hi, lo


def _layer1_inputs(inp, g):
    w_in = inp["b_w_in"][0]
    d = {}
    kvc = lambda br, kvt: w_in[:, 8192 + ((br * 2 + kvt) * 4 + g) * 128: 8192 + ((br * 2 + kvt) * 4 + g) * 128 + 128]
    cols = [w_in[:, g * 1024:(g + 1) * 1024], w_in[:, 4096 + g * 1024:4096 + (g + 1) * 1024]]
    cols += [kvc(br, kvt) for br in range(3) for kvt in range(2)]
    d["w_tm"] = _chunk_w(np.concatenate(cols, axis=1), 256)
    d["w_kv0"] = _chunk_w(np.concatenate([kvc(0, 0), kvc(0, 1)], axis=1), 128)
    d["w_g"] = _chunk_w(w_in[:, 11264 + g * 24:11264 + (g + 1) * 24], 24)
    d["knw"] = np.ascontiguousarray(inp["b_k_norm_w"][0].reshape(1, 384))
    d["qnw"] = np.ascontiguousarray(inp["b_q_norm_w"][0].reshape(1, 128))
    d["peT"] = np.ascontiguousarray(inp["b_cmp_pe"][0].transpose(2, 0, 1))
    d["wk"] = np.ascontiguousarray(inp["b_cmp_wk"][0].reshape(32, 128, 128).transpose(1, 0, 2))
    d["wv"] = np.ascontiguousarray(inp["b_cmp_wv"][0].reshape(32, 128, 128).transpose(1, 0, 2))
    d["w_out"] = np.ascontiguousarray(inp["b_w_out"][0][g * 1024:(g + 1) * 1024, :].reshape(8, 128, D).transpose(1, 0, 2))
    scale = 128.0 ** -0.5
    slopes = _alibi(32)[8 * g:8 * g + 8]
    sl = (slopes / scale).astype(np.float32)
    qq = np.arange(128, dtype=np.float32)
    BQ = np.zeros((8, 8, 128), np.float32)
    for h in range(8):
        hi, lo = _bf16_split(sl[h])
        BQ[0, h, :] = hi
        BQ[1, h, :] = lo
        phi, plo = _bf16_split(sl[h] * qq)
        BQ[2, h, :] = -phi
        BQ[3, h, :] = -plo
        BQ[4, h, :] = -128.0 * hi
        BQ[5, h, :] = -128.0 * lo
        chi, clo = _bf16_split(31.0 * sl[h])
        BQ[6, h, :] = chi
        BQ[7, h, :] = clo
    d["BQ"] = np.ascontiguousarray(BQ.reshape(8, 1024))
    AK = np.zeros((8, 2, 32, 128), np.float32)
    AK[0, 0] = AK[1, 0] = np.arange(128)[None, :]
    AK[0, 1] = AK[1, 1] = 16 * np.arange(128)[None, :]
    AK[2:4] = 1.0
    AK[4] = AK[5] = np.arange(32, dtype=np.float32)[None, :, None]
    AK[6:8, 1] = 1.0
    d["AK"] = AK
    n_cmp = 255
    cmp_start = np.arange(n_cmp) * 16
    sel_start = np.arange(64) * 64
    ov = np.clip(np.minimum(cmp_start[:, None] + 32, sel_start[None, :] + 64)
                 - np.maximum(cmp_start[:, None], sel_start[None, :]), 0, None) / 32.0
    ovl = np.zeros((256, 64), np.float32)
    ovl[:255] = ov
    d["ovl"] = np.ascontiguousarray(ovl.reshape(2, 128, 64).transpose(1, 0, 2))
    E = np.zeros((64, SEQ), np.float32)
    E[np.arange(SEQ) // 64, np.arange(SEQ)] = 1.0
    d["Eall"] = E
    jj = np.arange(128)[None, :] - 62
    cur = (np.arange(128)[:, None] >= 64).astype(np.int64)
    keep = (jj <= cur)
    forced = (jj == cur) | (jj == cur - 1)
    d["Wkeep"] = keep.astype(np.float32)
    d["Wadd"] = np.where(~keep, B_NEG, np.where(forced, B_FORCE, 0.0)).astype(np.float32)
    return d


def _layer3_inputs(inp, g):
    w_in = inp["d_w_in"][0]
    cols = []
    for pat in range(3):
        for typ in range(3):
            c0 = ((pat * 3 + typ) * 16 + 4 * g) * 128
            cols.append(w_in[:, c0:c0 + 512])
    z0 = 9 * 2048 + 4 * g * 128
    cols.append(w_in[:, z0:z0 + 512])
    d = {}
    d["w_u"] = _chunk_w(np.concatenate(cols, axis=1), 256)
    d["qkw"] = np.ascontiguousarray(np.concatenate([inp["d_q_norm_w"][0], inp["d_k_norm_w"][0]])[None, :])
    slopes = _alibi(16)[4 * g:4 * g + 4]
    kk = np.arange(128)[:, None]
    qq = np.arange(128)[None, :]
    bm = np.zeros((128, 3, 2, 4, 128), np.float64)
    for p, (window, dil) in enumerate(D_PATS):
        for kind in range(2):
            steps = qq + 128 - kk if kind == 0 else qq - kk
            valid = (steps >= 0) & (steps <= 128)
            for h in range(4):
                bm[:, p, kind, h, :] = np.where(valid, np.exp(-slopes[h] * np.where(valid, steps, 0) * dil), 0.0)
    d["bm"] = np.ascontiguousarray(bm.reshape(128, -1).astype(np.float32))
    d["w_out"] = np.ascontiguousarray(inp["d_w_out"][0][g * 512:(g + 1) * 512, :].reshape(4, 128, D).transpose(1, 0, 2))
    return d


_LAYER_INPUTS = {0: _layer0_inputs, 1: _layer1_inputs, 2: _layer2_inputs, 3: _layer3_inputs}


def run_layer(kind, xcur, inp, layer_idx):
    fn = _LAYER_INPUTS[kind]
    maps = []
    nw = np.ascontiguousarray(inp["norm_w"][layer_idx][None, :])
    for c in range(NCORES):
        b, g = c // 4, c % 4
        m = fn(inp, g)
        m["x"] = xcur[b]
        m["nw"] = nw
        maps.append(m)
    shapes = {k: v.shape for k, v in maps[0].items() if k not in ("x", "nw")}
    nc = build_layer(kind, shapes)
    res = run_bass_kernel_spmd(nc, maps, core_ids=list(range(NCORES)))
    return [r["y"] for r in res.results]


def run_reduce(xcur, ys):
    flat = xcur.reshape(2 * SEQ, D)
    maps = []
    for c in range(NCORES):
        b, q = c // 4, c % 4
        m = {"x": flat[c * 1024:(c + 1) * 1024]}
        for j in range(4):
            m[f"y{j}"] = ys[b * 4 + j][q * 1024:(q + 1) * 1024]
        maps.append(m)
    nc = build_reduce()
    res = run_bass_kernel_spmd(nc, maps, core_ids=list(range(NCORES)))
    return np.concatenate([r["out"] for r in res.results], axis=0).reshape(2, SEQ, D)


def run_fused(inp, layers=(0, 1, 2, 3), groups=(0, 1, 2, 3)):
    x = np.ascontiguousarray(inp["x"], dtype=np.float32)
    nw = np.ascontiguousarray(inp["norm_w"], dtype=np.float32)
    base = {}
    for L in layers:
        for g in range(4):
            for k, v in _LAYER_INPUTS[L](inp, g).items():
                base[f"L{L}g{g}_{k}"] = v
    shapes = {(L, g): {k[len(f"L{L}g{g}_"):]: v.shape for k, v in base.items() if k.startswith(f"L{L}g{g}_")}
              for L in layers for g in range(4)}
    nc = build_fused(shapes, layers, groups)
    maps = []
    for b in range(2):
        m = dict(base)
        m["x"] = x[b]
        m["nw"] = nw
        maps.append(m)
    res = run_bass_kernel_spmd(nc, maps, core_ids=[0, 1])
    return np.stack([res.results[b]["out"] for b in range(2)], axis=0)


def kernel(**inputs):
    return kernel_unfused(**inputs)


def kernel_unfused(**inputs):
    inp = {k: np.asarray(v) for k, v in inputs.items()}
    x = np.ascontiguousarray(inp["x"], dtype=np.float32)
    for layer in range(4):
        ys = run_layer(layer % 4, x, inp, layer)
        x = run_reduce(x, ys)
    return x
```
